# Optimizing a Trainium2 kernel written in Bass

```python
import math
import jax, jax.numpy as jnp
from jax import lax
import numpy as np

D_MODEL = 2048
BATCH = 1
SEQ = 16384
DEPTH = 2

GRID_W = 64
CTX_LEN = 256

HEAD_DIM = 128
ATTN_WIDTH = D_MODEL // 2
N_Q_HEADS = ATTN_WIDTH // HEAD_DIM
N_KV_HEADS = 2
Q_PER_KV = N_Q_HEADS // N_KV_HEADS
KV_WIDTH = N_KV_HEADS * HEAD_DIM
Q_BLOCK = 128
ROPE_THETA = 10000.0
ROPE_PAIRS = HEAD_DIM // 4

SSM_WIDTH = D_MODEL // 4
SSM_GROUP = 16
N_SSM_GROUPS = SSM_WIDTH // SSM_GROUP
SSM_STATE = 64
DT_MIN = 0.001
DT_MAX = 0.1

FOURIER_WIDTH = D_MODEL - ATTN_WIDTH - SSM_WIDTH
FOURIER_HEADS = 4
FOURIER_HEAD_DIM = FOURIER_WIDTH // FOURIER_HEADS

Q_END = ATTN_WIDTH
K_END = Q_END + KV_WIDTH
V_END = K_END + KV_WIDTH
S_END = V_END + SSM_WIDTH
IN_WIDTH = S_END + FOURIER_WIDTH

FFN_HIDDEN = ((8 * D_MODEL // 3 + 255) // 256) * 256
NORM_EPS = 1e-6

kernel_name = 'hybrid_gqa_s5_fnet_diffusion_block'


def rms_norm(x, g):
    x32 = x.astype(jnp.float32)
    r = x32 * lax.rsqrt(jnp.mean(x32 * x32, axis=-1, keepdims=True) + NORM_EPS)
    return (r * g.astype(jnp.float32)).astype(x.dtype)


def modulate(h, shift, scale):
    return h * (1 + scale) + shift


def axial_rope_tables(rows):
    row = jnp.repeat(jnp.arange(rows, dtype=jnp.float32), GRID_W)
    col = jnp.tile(jnp.arange(GRID_W, dtype=jnp.float32), rows)
    freqs = ROPE_THETA ** (-jnp.arange(ROPE_PAIRS, dtype=jnp.float32) / ROPE_PAIRS)
    ang_r = (row[:, None] * freqs)[:, None, :]
    ang_c = (col[:, None] * freqs)[:, None, :]
    return (jnp.cos(ang_r), jnp.sin(ang_r), jnp.cos(ang_c), jnp.sin(ang_c))


def apply_axial_rope(x, tabs):
    cr, sr, cc, sc = [t.astype(x.dtype) for t in tabs]
    x1, x2, x3, x4 = jnp.split(x, 4, axis=-1)
    return jnp.concatenate([x1 * cr - x2 * sr, x2 * cr + x1 * sr,
                            x3 * cc - x4 * sc, x4 * cc + x3 * sc], axis=-1)


def attend(q, k, v):
    s = jnp.einsum('bqkgd,bskd->bkgqs', q, k).astype(jnp.float32) * (HEAD_DIM ** -0.5)
    p = jax.nn.softmax(s, axis=-1).astype(v.dtype)
    return jnp.einsum('bkgqs,bskd->bqkgd', p, v)


def latent_attention(q, k_all, v_all):
    b, l = q.shape[:2]
    nb = l // Q_BLOCK
    qb = jnp.moveaxis(q.reshape(b, nb, Q_BLOCK, N_KV_HEADS, Q_PER_KV, HEAD_DIM), 1, 0)
    o = lax.map(lambda qblk: attend(qblk, k_all, v_all), qb)
    return jnp.moveaxis(o, 0, 1).reshape(b, l, ATTN_WIDTH)


def s5_discretize(lam_re, lam_im, log_dt, b_re, b_im):
    lr = lam_re.astype(jnp.float32)
    li = lam_im.astype(jnp.float32)
    dt = jnp.exp(log_dt.astype(jnp.float32))[:, None]
    mag = jnp.exp(lr * dt)
    ar = mag * jnp.cos(li * dt)
    ai = mag * jnp.sin(li * dt)
    den = lr * lr + li * li
    cr = ((ar - 1.0) * lr + ai * li) / den
    ci = (ai * lr - (ar - 1.0) * li) / den
    br = b_re.astype(jnp.float32)
    bi = b_im.astype(jnp.float32)
    bbr = cr[..., None] * br - ci[..., None] * bi
    bbi = cr[..., None] * bi + ci[..., None] * br
    return ar, ai, bbr, bbi


def s5_combine(e1, e2):
    a1r, a1i, b1r, b1i = e1
    a2r, a2i, b2r, b2i = e2
    return (a2r * a1r - a2i * a1i, a2r * a1i + a2i * a1r,
            a2r * b1r - a2i * b1i + b2r, a2r * b1i + a2i * b1r + b2i)


def s5_scan(u, ar, ai, bbr, bbi, h0, reverse):
    l = u.shape[1]
    bu_r = jnp.einsum('blgh,gph->blgp', u, bbr)
    bu_i = jnp.einsum('blgh,gph->blgp', u, bbi)
    if h0 is not None:
        h0r, h0i = h0
        edge = l - 1 if reverse else 0
        bu_r = bu_r.at[:, edge].add(ar * h0r - ai * h0i)
        bu_i = bu_i.at[:, edge].add(ar * h0i + ai * h0r)
    shape = bu_r.shape
    elems = (jnp.broadcast_to(ar, shape), jnp.broadcast_to(ai, shape), bu_r, bu_i)
    _, _, hr, hi = lax.associative_scan(s5_combine, elems, reverse=reverse, axis=1)
    return hr, hi


def s5_readout(hr, hi, c_re, c_im):
    return (jnp.einsum('blgp,ghp->blgh', hr, c_re.astype(jnp.float32))
            - jnp.einsum('blgp,ghp->blgh', hi, c_im.astype(jnp.float32)))


def s5_bidirectional(u, uc, lam_re, lam_im, log_dt, b_re, b_im, c_re, c_im, d, need_ctx):
    b, l = u.shape[:2]
    u32 = u.astype(jnp.float32).reshape(b, l, N_SSM_GROUPS, SSM_GROUP)
    uc32 = uc.astype(jnp.float32).reshape(b, uc.shape[1], N_SSM_GROUPS, SSM_GROUP)
    dd = d.astype(jnp.float32).reshape(N_SSM_GROUPS, SSM_GROUP)
    y = dd * u32
    yc = dd * uc32 if need_ctx else None
    for direction in range(2):
        rev = direction == 1
        ar, ai, bbr, bbi = s5_discretize(lam_re[direction], lam_im[direction],
                                         log_dt[direction], b_re[direction], b_im[direction])
        hcr, hci = s5_scan(uc32, ar, ai, bbr, bbi, None, rev)
        end = 0 if rev else -1
        hr, hi = s5_scan(u32, ar, ai, bbr, bbi, (hcr[:, end], hci[:, end]), rev)
        y = y + s5_readout(hr, hi, c_re[direction], c_im[direction])
        if need_ctx:
            yc = yc + s5_readout(hcr, hci, c_re[direction], c_im[direction])
    y = y.reshape(b, l, SSM_WIDTH).astype(u.dtype)
    if need_ctx:
        yc = yc.reshape(b, uc.shape[1], SSM_WIDTH).astype(uc.dtype)
    return y, yc


def s5_glu(y, w, bias):
    g = jax.nn.gelu(y)
    return g * jax.nn.sigmoid(g @ w + bias)


def fourier_mix(u):
    b, l = u.shape[:2]
    u32 = u.astype(jnp.float32).reshape(b, l, FOURIER_HEADS, FOURIER_HEAD_DIM)
    f = jnp.fft.fftn(u32, axes=(1, 3), norm='ortho').real
    return f.reshape(b, l, FOURIER_WIDTH).astype(u.dtype)


def swiglu(h, w_gate, w_up, w_down):
    return (jax.nn.silu(h @ w_gate) * (h @ w_up)) @ w_down


def mixer(h, hc, tabs, w_in, q_norm, k_norm, lam_re, lam_im, log_dt, b_re, b_im,
          c_re, c_im, ssm_d, glu_w, glu_b, fourier_w, w_out, need_ctx):
    b, l, _ = h.shape
    n_ctx = hc.shape[1]
    proj = h @ w_in
    q = rms_norm(proj[..., :Q_END].reshape(b, l, N_Q_HEADS, HEAD_DIM), q_norm)
    q = apply_axial_rope(q, tabs).reshape(b, l, N_KV_HEADS, Q_PER_KV, HEAD_DIM)
    k = apply_axial_rope(rms_norm(proj[..., Q_END:K_END].reshape(b, l, N_KV_HEADS, HEAD_DIM), k_norm), tabs)
    v = proj[..., K_END:V_END].reshape(b, l, N_KV_HEADS, HEAD_DIM)
    u_s = proj[..., V_END:S_END]
    u_f = proj[..., S_END:]
    proj_c = hc @ (w_in if need_ctx else w_in[:, Q_END:S_END])
    off = 0 if need_ctx else Q_END
    kc = rms_norm(proj_c[..., Q_END - off:K_END - off].reshape(b, n_ctx, N_KV_HEADS, HEAD_DIM), k_norm)
    vc = proj_c[..., K_END - off:V_END - off].reshape(b, n_ctx, N_KV_HEADS, HEAD_DIM)
    uc_s = proj_c[..., V_END - off:S_END - off]

    attn = latent_attention(q, jnp.concatenate([k, kc], axis=1), jnp.concatenate([v, vc], axis=1))
    ys, ysc = s5_bidirectional(u_s, uc_s, lam_re, lam_im, log_dt, b_re, b_im, c_re, c_im, ssm_d, need_ctx)
    ssm_out = s5_glu(ys, glu_w, glu_b)
    four = fourier_mix(u_f) @ fourier_w
    y = jnp.concatenate([attn, ssm_out, four], axis=-1) @ w_out
    if not need_ctx:
        return y, None
    qc = rms_norm(proj_c[..., :Q_END].reshape(b, n_ctx, N_Q_HEADS, HEAD_DIM), q_norm)
    qc = qc.reshape(b, n_ctx, N_KV_HEADS, Q_PER_KV, HEAD_DIM)
    attn_c = attend(qc, kc, vc).reshape(b, n_ctx, ATTN_WIDTH)
    four_c = fourier_mix(proj_c[..., S_END:]) @ fourier_w
    yc = jnp.concatenate([attn_c, s5_glu(ysc, glu_w, glu_b), four_c], axis=-1) @ w_out
    return y, yc


def setup_inputs(seed: int = 0) -> dict:
    key = jax.random.key(seed)
    ks = jax.random.split(key, 28)
    f32 = jnp.float32

    def nrm(k, shape, scale):
        return jax.random.normal(k, shape, f32) * scale

    g_p = (DEPTH, 2, N_SSM_GROUPS, SSM_STATE)
    n_idx = jnp.arange(SSM_STATE, dtype=f32)
    return {
        'x': nrm(ks[0], (BATCH, SEQ, D_MODEL), 1.0),
        'c': nrm(ks[1], (BATCH, D_MODEL), 1.0),
        'ctx': nrm(ks[2], (BATCH, CTX_LEN, D_MODEL), 1.0),
        'c_ctx': nrm(ks[3], (D_MODEL,), 1.0),
        'ada_w': nrm(ks[4], (DEPTH, D_MODEL, 6 * D_MODEL), 0.5 * D_MODEL ** -0.5),
        'ada_b': nrm(ks[5], (DEPTH, 6 * D_MODEL), 0.01),
        'norm_mix_pre': 1 + nrm(ks[6], (DEPTH, D_MODEL), 0.05),
        'norm_mix_post': 1 + nrm(ks[7], (DEPTH, D_MODEL), 0.05),
        'norm_ffn_pre': 1 + nrm(ks[8], (DEPTH, D_MODEL), 0.05),
        'norm_ffn_post': 1 + nrm(ks[9], (DEPTH, D_MODEL), 0.05),
        'w_in': nrm(ks[10], (DEPTH, D_MODEL, IN_WIDTH), D_MODEL ** -0.5),
        'q_norm': 1 + nrm(ks[11], (DEPTH, HEAD_DIM), 0.05),
        'k_norm': 1 + nrm(ks[12], (DEPTH, HEAD_DIM), 0.05),
        'ssm_lam_re': -0.5 + nrm(ks[13], g_p, 0.02),
        'ssm_lam_im': math.pi * n_idx + nrm(ks[14], g_p, 0.02),
        'ssm_log_dt': jax.random.uniform(ks[15], (DEPTH, 2, N_SSM_GROUPS), f32,
                                         math.log(DT_MIN), math.log(DT_MAX)),
        'ssm_b_re': nrm(ks[16], (DEPTH, 2, N_SSM_GROUPS, SSM_STATE, SSM_GROUP), (2 * SSM_GROUP) ** -0.5),
        'ssm_b_im': nrm(ks[17], (DEPTH, 2, N_SSM_GROUPS, SSM_STATE, SSM_GROUP), (2 * SSM_GROUP) ** -0.5),
        'ssm_c_re': nrm(ks[18], (DEPTH, 2, N_SSM_GROUPS, SSM_GROUP, SSM_STATE), (2 * SSM_STATE) ** -0.5),
        'ssm_c_im': nrm(ks[19], (DEPTH, 2, N_SSM_GROUPS, SSM_GROUP, SSM_STATE), (2 * SSM_STATE) ** -0.5),
        'ssm_d': nrm(ks[20], (DEPTH, SSM_WIDTH), 1.0),
        'ssm_glu_w': nrm(ks[21], (DEPTH, SSM_WIDTH, SSM_WIDTH), SSM_WIDTH ** -0.5),
        'ssm_glu_b': nrm(ks[22], (DEPTH, SSM_WIDTH), 0.01),
        'fourier_w': nrm(ks[23], (DEPTH, FOURIER_WIDTH, FOURIER_WIDTH), FOURIER_WIDTH ** -0.5),
        'w_out': nrm(ks[24], (DEPTH, D_MODEL, D_MODEL), D_MODEL ** -0.5),
        'ffn_w_gate': nrm(ks[25], (DEPTH, D_MODEL, FFN_HIDDEN), D_MODEL ** -0.5),
        'ffn_w_up': nrm(ks[26], (DEPTH, D_MODEL, FFN_HIDDEN), D_MODEL ** -0.5),
        'ffn_w_down': nrm(ks[27], (DEPTH, FFN_HIDDEN, D_MODEL), FFN_HIDDEN ** -0.5),
    }


def reference(x, c, ctx, c_ctx, ada_w, ada_b, norm_mix_pre, norm_mix_post, norm_ffn_pre,
              norm_ffn_post, w_in, q_norm, k_norm, ssm_lam_re, ssm_lam_im, ssm_log_dt,
              ssm_b_re, ssm_b_im, ssm_c_re, ssm_c_im, ssm_d, ssm_glu_w, ssm_glu_b,
              fourier_w, w_out, ffn_w_gate, ffn_w_up, ffn_w_down):
    n_tok = x.shape[1]
    ROWS = n_tok // GRID_W
    tabs = axial_rope_tables(ROWS)
    xc = ctx
    for layer in range(DEPTH):
        need_ctx = layer < DEPTH - 1
        mod = jax.nn.silu(c) @ ada_w[layer] + ada_b[layer]
        sh_m, sc_m, g_m, sh_f, sc_f, g_f = jnp.split(mod[:, None, :], 6, axis=-1)
        mod_c = jax.nn.silu(c_ctx) @ ada_w[layer] + ada_b[layer]
        shc_m, scc_m, gc_m, shc_f, scc_f, gc_f = jnp.split(mod_c, 6, axis=-1)

        h = modulate(rms_norm(x, norm_mix_pre[layer]), sh_m, sc_m)
        hc = modulate(rms_norm(xc, norm_mix_pre[layer]), shc_m, scc_m)
        y, yc = mixer(h, hc, tabs, w_in[layer], q_norm[layer], k_norm[layer],
                      ssm_lam_re[layer], ssm_lam_im[layer], ssm_log_dt[layer],
                      ssm_b_re[layer], ssm_b_im[layer], ssm_c_re[layer], ssm_c_im[layer],
                      ssm_d[layer], ssm_glu_w[layer], ssm_glu_b[layer], fourier_w[layer],
                      w_out[layer], need_ctx)
        x = x + g_m * rms_norm(y, norm_mix_post[layer])
        h = modulate(rms_norm(x, norm_ffn_pre[layer]), sh_f, sc_f)
        x = x + g_f * rms_norm(swiglu(h, ffn_w_gate[layer], ffn_w_up[layer], ffn_w_down[layer]),
                               norm_ffn_post[layer])
        if need_ctx:
            xc = xc + gc_m * rms_norm(yc, norm_mix_post[layer])
            hc = modulate(rms_norm(xc, norm_ffn_pre[layer]), shc_f, scc_f)
            xc = xc + gc_f * rms_norm(swiglu(hc, ffn_w_gate[layer], ffn_w_up[layer], ffn_w_down[layer]),
                                      norm_ffn_post[layer])
    return x
```

```python
from contextlib import ExitStack
import numpy as np
import concourse.bass as bass
import concourse.mybir as mybir
from concourse.bass_utils import run_bass_kernel_spmd

F32 = mybir.dt.float32
BF16 = mybir.dt.bfloat16
I32 = mybir.dt.int32
AF = mybir.ActivationFunctionType
ALU = mybir.AluOpType
AX = mybir.AxisListType


STRICT_WAR = True


class Prog:
    def __init__(self):
        self.nc = bass.Bass("TRN2", target_bir_lowering=False)
        self.es = ExitStack()
        self.ops = []
        self.last_w = {}
        self.readers = {}
        self.n_t = 0
        self.psum_keys = set()

    def din(self, name, shape, dt=F32):
        return self.nc.dram_tensor(name, list(shape), dt, kind="ExternalInput").ap()

    def dout(self, name, shape, dt=F32):
        return self.nc.dram_tensor(name, list(shape), dt, kind="ExternalOutput").ap()

    def sb(self, shape, dt=F32, name=None):
        self.n_t += 1
        return self.es.enter_context(self.nc.sbuf_tensor(name or f"sb{self.n_t}", list(shape), dt))

    def ps(self, shape, dt=F32, name=None):
        self.n_t += 1
        return self.es.enter_context(self.nc.psum_tensor(name or f"ps{self.n_t}", list(shape), dt))

    def op(self, eng, fn, r=(), w=(), dma=False, semkey=None):
        i = len(self.ops)
        raw = set()
        war = set()
        if eng == "pe" and not dma:
            self.psum_keys.update(w)
        for k in r:
            if k in self.last_w:
                raw.add(self.last_w[k])
            if k in self.psum_keys:
                for j in self.readers.get(k, ()):
                    if self.ops[j]["eng"] != eng:
                        raw.add(j)
        for k in w:
            if k in self.last_w:
                raw.add(self.last_w[k])
            war.update(self.readers.get(k, ()))
        self.ops.append(dict(eng=eng, fn=fn, raw=raw, war=war - raw, dma=dma, semkey=semkey))
        for k in r:
            self.readers.setdefault(k, []).append(i)
        for k in w:
            self.last_w[k] = i
            self.readers[k] = []
        return i

    def dma(self, q, out, in_, r=(), w=(), semkey=None):
        if semkey is None:
            semkey = ("ld", w[0]) if w else ("st", r[0])
        return self.op(q, lambda e: e.dma_start(out=out, in_=in_), r=r, w=w, dma=True, semkey=semkey)

    def build(self):
        nc = self.nc
        ops = self.ops
        n = len(ops)
        for i, o in enumerate(ops):
            deps = set()
            for j in o["raw"]:
                pj = ops[j]
                if (not pj["dma"]) and pj["eng"] == o["eng"] and o["eng"] == "pe" and not o["dma"]:
                    continue
                deps.add(j)
            for j in o["war"]:
                pj = ops[j]
                if (not pj["dma"]) and pj["eng"] == o["eng"] and not o["dma"] and (o["eng"] == "pe" or not STRICT_WAR):
                    continue
                if (not pj["dma"]) and pj["eng"] == o["eng"] and o["dma"]:
                    pass
                deps.add(j)
            o["deps"] = deps
        has_dep = [False] * n
        for o in ops:
            for j in o["deps"]:
                has_dep[j] = True
        engs = ["pe", "act", "dve", "pool", "sp"]
        sems = {e: self.es.enter_context(nc.semaphore(f"s_{e}")) for e in engs}
        cnt = {e: 0 for e in engs}
        dsem = {}
        for i, o in enumerate(ops):
            if o["dma"]:
                k = o["semkey"]
                if k not in dsem:
                    dsem[k] = [self.es.enter_context(nc.semaphore(f"d{len(dsem)}")), 0]
                dsem[k][1] += 16
                o["sem"] = dsem[k][0]
                o["ticket"] = dsem[k][1]
                o["inc"] = 16
            elif has_dep[i]:
                cnt[o["eng"]] += 1
                o["sem"] = sems[o["eng"]]
                o["ticket"] = cnt[o["eng"]]
                o["inc"] = 1
            else:
                o["sem"] = None
        self.n_sems = len(dsem) + 5
        per = {e: [i for i, o in enumerate(ops) if o["eng"] == e] for e in engs}

        def replay(e):
            def run(eng):
                seen = {}
                for i in per[e]:
                    o = ops[i]
                    need = {}
                    for j in o["deps"]:
                        pj = ops[j]
                        s = pj["sem"]
                        key = id(s)
                        if seen.get(key, 0) >= pj["ticket"]:
                            continue
                        if key not in need or need[key][1] < pj["ticket"]:
                            need[key] = (s, pj["ticket"])
                    for key, (s, t) in need.items():
                        eng.wait_ge(s, t)
                        seen[key] = t
                    ins = o["fn"](eng)
                    if o["sem"] is not None:
                        ins.then_inc(o["sem"], o["inc"])
                if e == "sp":
                    for k, (s, t) in dsem.items():
                        if seen.get(id(s), 0) < t:
                            eng.wait_ge(s, t)
            return run

        with nc.Block() as block:
            block.sync(replay("sp"))
            block.tensor(replay("pe"))
            block.scalar(replay("act"))
            block.vector(replay("dve"))
            block.gpsimd(replay("pool"))
        self.es.close()
        return nc


def build_cast(N, CH=4096):
    P = Prog()
    src = P.din("src", [128, N], F32)
    dst = P.dout("dst", [128, N], BF16)
    nb = N // CH
    NS = 3
    fin = [P.sb([128, CH], F32) for _ in range(NS)]
    fout = [P.sb([128, CH], BF16) for _ in range(NS)]
    engs = ["dve", "act", "pool"]
    for b in range(nb):
        s = b % NS
        P.dma("sp", fin[s][:], src[:, b * CH:(b + 1) * CH], w=[("fin", s)])
        e = engs[b % 3]
        if e == "act":
            P.op("act", lambda g, s=s: g.copy(out=fout[s][:], in_=fin[s][:]), r=[("fin", s)], w=[("fout", s)])
        else:
            P.op(e, lambda g, s=s: g.tensor_copy(out=fout[s][:], in_=fin[s][:]), r=[("fin", s)], w=[("fout", s)])
        P.dma("sp", dst[:, b * CH:(b + 1) * CH], fout[s][:], r=[("fout", s)])
    return P.build()


def mm(P, out, lhsT, rhs, start, stop, r, w):
    P.op("pe", lambda g: g.matmul(out, lhsT, rhs, start=start, stop=stop), r=r, w=w)


def act(P, out, in_, func, r, w, bias=None, scale=None):
    kw = {}
    if bias is not None:
        kw["bias"] = bias
    if scale is not None:
        kw["scale"] = scale
    P.op("act", lambda g: g.activation(out=out, in_=in_, func=func, **kw), r=r, w=w)


def tt(P, eng, out, a, b, op, r, w):
    P.op(eng, lambda g: g.tensor_tensor(out=out, in0=a, in1=b, op=op), r=r, w=w)


def stt(P, out, in0, scalar, in1, op0, op1, r, w):
    P.op("dve", lambda g: g.scalar_tensor_tensor(out=out, in0=in0, scalar=scalar, in1=in1, op0=op0, op1=op1), r=r, w=w)


def ts(P, eng, out, in0, s1, s2, op0, op1, r, w):
    if op1 is None:
        P.op(eng, lambda g: g.tensor_scalar(out=out, in0=in0, scalar1=s1, scalar2=None, op0=op0), r=r, w=w)
    else:
        P.op(eng, lambda g: g.tensor_scalar(out=out, in0=in0, scalar1=s1, scalar2=s2, op0=op0, op1=op1), r=r, w=w)


def rstd_from_ps(P, out_sb, in_ps, n, eps, r, w, tmpkey):
    act(P, out_sb, in_ps, AF.Ln, r=list(r) + ["epsc"], w=w, bias=eps_ap(P, eps), scale=1.0 / n)
    act(P, out_sb, out_sb, AF.Exp, r=w, w=w, scale=-0.5)


def eps_ap(P, eps):
    if not hasattr(P, "_eps"):
        P._eps = P.sb([128, 1], F32, name="epsc")
        P.op("pool", lambda g: g.memset(P._eps[:], float(eps)), w=["epsc"])
    return P._eps[:]


def build_adaln(NL=2, D=2048, CW=1536):
    P = Prog()
    KC = D // 128
    c2T = P.din("c2T", [128, KC, 2], F32)
    w = P.din("w", [NL, D, CW], F32)
    b2 = P.din("b2", [NL, 2, CW], F32)
    mod = P.dout("mod", [NL, 2, CW], F32)
    cs = P.sb([128, KC, 2], F32)
    ss = P.sb([128, KC, 2], F32)
    bt = P.sb([2, NL, CW], F32)
    P.dma("sp", cs[:], c2T[:], w=["cs"])
    for l in range(NL):
        P.dma("sp", bt[:, l, :], b2[l], w=[("bt", l)])
    act(P, ss[:], cs[:], AF.Silu, r=["cs"], w=["ss"])
    wt = [P.sb([128, KC, 512], F32) for _ in range(2)]
    pst = [P.ps([128, 512], F32) for _ in range(2)]
    ot = [P.sb([2, 512], F32) for _ in range(2)]
    it = 0
    for l in range(NL):
        wv = w[l].rearrange("(kc p) c -> p kc c", p=128)
        for cb in range(CW // 512):
            s = it % 2
            P.dma("sp", wt[s][:], wv[:, :, cb * 512:(cb + 1) * 512], w=[("wt", s)])
            for kc in range(KC):
                mm(P, pst[s][0:2, :], ss[:, kc, :], wt[s][:, kc, :], kc == 0, kc == KC - 1,
                   r=["ss", ("wt", s)], w=[("ps", s)])
            tt(P, "dve", ot[s][:], pst[s][0:2, :], bt[:, l, cb * 512:(cb + 1) * 512], ALU.add,
               r=[("ps", s), ("bt", l)], w=[("ot", s)])
            P.dma("sp", mod[l, :, cb * 512:(cb + 1) * 512], ot[s][:], r=[("ot", s)])
            it += 1
    return P.build()


D_MODEL = 2048
IN_W = 2560
EPS = 1e-6


def build_p1(NB, TB):
    P = Prog()
    Tc = NB * TB
    KC = 16
    xT = P.din("xT", [D_MODEL, Tc], F32)
    gam = P.din("gam", [128, KC], F32)
    scv = P.din("scv", [128, KC], F32)
    shv = P.din("shv", [128, KC], F32)
    w_in = P.din("w_in", [D_MODEL, IN_W], BF16)
    qn = P.din("qn", [128, 1], F32)
    kn = P.din("kn", [128, 1], F32)
    cosT = P.din("cosT", [128, Tc], F32)
    sinT = P.din("sinT", [128, Tc], F32)
    rotT = P.din("rotT", [128, 128], F32)
    cd = P.din("cd", [128, 128], F32)
    sdn = P.din("sdn", [128, 128], F32)
    qT = P.dout("qT", [1024, Tc], BF16)
    kT = P.dout("kT", [256, Tc], BF16)
    vT = P.dout("vT", [256, Tc], BF16)
    usT = P.dout("usT", [512, Tc], F32)
    zrT = P.dout("zrT", [512, Tc], F32)
    ziT = P.dout("ziT", [512, Tc], F32)

    W = P.sb([128, KC, IN_W], BF16)
    wv = w_in.rearrange("(kc p) c -> p kc c", p=128)
    for i in range(4):
        P.dma("sp", W[:, i * 4:(i + 1) * 4, :], wv[:, i * 4:(i + 1) * 4, :], w=[("W", i)])
    Wk = [("W", i) for i in range(4)]
    g_sb = P.sb([128, KC], F32); sc_sb = P.sb([128, KC], F32); sh_sb = P.sb([128, KC], F32)
    G = P.sb([128, KC], F32)
    qn_sb = P.sb([128, 1], F32); kn_sb = P.sb([128, 1], F32)
    rot_sb = P.sb([128, 128], F32); cd_sb = P.sb([128, 128], F32); sd_sb = P.sb([128, 128], F32)
    ones = P.sb([128, 128], BF16)
    P.dma("sp", g_sb[:], gam[:], w=["g"]); P.dma("sp", sc_sb[:], scv[:], w=["sc"]); P.dma("sp", sh_sb[:], shv[:], w=["sh"])
    P.dma("sp", qn_sb[:], qn[:], w=["qn"]); P.dma("sp", kn_sb[:], kn[:], w=["kn"])
    P.dma("sp", rot_sb[:], rotT[:], w=["rot"]); P.dma("sp", cd_sb[:], cd[:], w=["cd"]); P.dma("sp", sd_sb[:], sdn[:], w=["sd"])
    P.op("pool", lambda g: g.memset(ones[:], 1.0), w=["ones"])
    stt(P, G[:], sc_sb[:], 1.0, g_sb[:], ALU.add, ALU.mult, r=["g", "sc"], w=["G"])

    x_sb = P.sb([128, KC, TB], F32)
    sq = P.sb([128, KC, TB], BF16)
    h = P.sb([128, KC, TB], BF16)
    tmp = [P.sb([128, TB], F32) for _ in range(2)]
    rstd = P.sb([128, TB], F32)
    cs_sb = P.sb([128, TB], F32); sn_sb = P.sb([128, TB], F32)
    sqh = P.sb([128, TB], BF16)
    r2 = P.sb([128, TB], F32)
    qf = P.sb([128, TB], F32)
    t1 = P.sb([128, TB], F32); t2 = P.sb([128, TB], F32)
    ob = [P.sb([128, TB], BF16) for _ in range(2)]
    of = [P.sb([128, TB], F32) for _ in range(3)]
    acc = [P.ps([128, 512], F32) for _ in range(2)]
    ps_ss = P.ps([128, 512], F32); ps_s2 = P.ps([128, 512], F32); ps_rot = P.ps([128, 512], F32)
    ps_z = [P.ps([128, 512], F32) for _ in range(2)]
    xv = xT.rearrange("(kc p) t -> p kc t", p=128)
    nb_o = 0; nf_o = 0; na = 0; nz = 0
    for b in range(NB):
        tsl = slice(b * TB, (b + 1) * TB)
        P.dma("sp", x_sb[:], xv[:, :, tsl], w=["x"])
        P.dma("sp", cs_sb[:], cosT[:, tsl], w=["cs"]); P.dma("sp", sn_sb[:], sinT[:, tsl], w=["sn"])
        act(P, sq[:], x_sb[:], AF.Square, r=["x"], w=["sq"])
        for kc in range(KC):
            mm(P, ps_ss[:, :TB], ones[:], sq[:, kc, :], kc == 0, kc == KC - 1, r=["ones", "sq"], w=["ps_ss"])
        rstd_from_ps(P, rstd[:], ps_ss[:, :TB], D_MODEL, EPS, r=["ps_ss"], w=["rstd"], tmpkey=None)
        for kc in range(KC):
            s = kc % 2
            stt(P, tmp[s][:], x_sb[:, kc, :], G[:, kc:kc + 1], rstd[:], ALU.mult, ALU.mult,
                r=["x", "G", "rstd"], w=[("tmp", s)])
            act(P, h[:, kc, :], tmp[s][:], AF.Identity, r=[("tmp", s), "sh"], w=[("h", kc)], bias=sh_sb[:, kc:kc + 1])
        hk = [("h", kc) for kc in range(KC)]
        for cc in range(IN_W // 128):
            a = na % 2; na += 1
            for kc in range(KC):
                mm(P, acc[a][:, :TB], W[:, kc, cc * 128:(cc + 1) * 128], h[:, kc, :], kc == 0, kc == KC - 1,
                   r=[Wk[kc // 4], ("h", kc)], w=[("acc", a)])
            A = acc[a][:, :TB]
            if cc < 10:
                gn, gk = (qn_sb, "qn") if cc < 8 else (kn_sb, "kn")
                act(P, sqh[:], A, AF.Square, r=[("acc", a)], w=["sqh"])
                mm(P, ps_s2[:, :TB], ones[:], sqh[:], True, True, r=["ones", "sqh"], w=["ps_s2"])
                rstd_from_ps(P, r2[:], ps_s2[:, :TB], 128, EPS, r=["ps_s2"], w=["r2"], tmpkey=None)
                stt(P, qf[:], A, gn[:, 0:1], r2[:], ALU.mult, ALU.mult, r=[("acc", a), gk, "r2"], w=["qf"])
                mm(P, ps_rot[:, :TB], rot_sb[:], qf[:], True, True, r=["rot", "qf"], w=["ps_rot"])
                tt(P, "dve", t1[:], qf[:], cs_sb[:], ALU.mult, r=["qf", "cs"], w=["t1"])
                tt(P, "dve", t2[:], ps_rot[:, :TB], sn_sb[:], ALU.mult, r=["ps_rot", "sn"], w=["t2"])
                o = nb_o % 2; nb_o += 1
                tt(P, "pool", ob[o][:], t1[:], t2[:], ALU.add, r=["t1", "t2"], w=[("ob", o)])
                dst = qT[cc * 128:(cc + 1) * 128, tsl] if cc < 8 else kT[(cc - 8) * 128:(cc - 7) * 128, tsl]
                P.dma("sp", dst, ob[o][:], r=[("ob", o)])
            elif cc < 12:
                o = nb_o % 2; nb_o += 1
                P.op("act", lambda g, o=o, A=A: g.copy(out=ob[o][:], in_=A), r=[("acc", a)], w=[("ob", o)])
                P.dma("sp", vT[(cc - 10) * 128:(cc - 9) * 128, tsl], ob[o][:], r=[("ob", o)])
            elif cc < 16:
                o = nf_o % 3; nf_o += 1
                P.op("act", lambda g, o=o, A=A: g.copy(out=of[o][:], in_=A), r=[("acc", a)], w=[("of", o)])
                P.dma("sp", usT[(cc - 12) * 128:(cc - 11) * 128, tsl], of[o][:], r=[("of", o)])
            else:
                o = nf_o % 3; nf_o += 1
                P.op("act", lambda g, o=o, A=A: g.copy(out=of[o][:], in_=A), r=[("acc", a)], w=[("of", o)])
                for mat, mk, dstT in ((cd_sb, "cd", zrT), (sd_sb, "sd", ziT)):
                    z = nz % 2; nz += 1
                    mm(P, ps_z[z][:, :TB], mat[:], of[o][:], True, True, r=[mk, ("of", o)], w=[("ps_z", z)])
                    o2 = nf_o % 3; nf_o += 1
                    P.op("dve", lambda g, o2=o2, z=z: g.tensor_copy(out=of[o2][:], in_=ps_z[z][:, :TB]),
                         r=[("ps_z", z)], w=[("of", o2)])
                    P.dma("sp", dstT[(cc - 16) * 128:(cc - 15) * 128, tsl], of[o2][:], r=[("of", o2)])
    return P.build()


def rope_tables(rows_tok):
    f = (10000.0 ** (-np.arange(32, dtype=np.float32) / 32)).astype(np.float32)
    t = np.asarray(rows_tok)
    row = (t // 64).astype(np.float32); col = (t % 64).astype(np.float32)
    ar = (row[None, :] * f[:, None]).astype(np.float32)
    ac = (col[None, :] * f[:, None]).astype(np.float32)
    cosT = np.concatenate([np.cos(ar), np.cos(ar), np.cos(ac), np.cos(ac)], 0).astype(np.float32)
    sinT = np.concatenate([np.sin(ar), np.sin(ar), np.sin(ac), np.sin(ac)], 0).astype(np.float32)
    return cosT, sinT


def rope_rotT():
    R = np.zeros((128, 128), np.float32)
    for base in (0, 64):
        for i in range(32):
            R[base + i, base + i + 32] = -1.0
            R[base + i + 32, base + i] = 1.0
    return np.ascontiguousarray(R.T)


def chan_dft():
    j = np.arange(128)
    ang = 2 * np.pi * np.outer(j, j) / 128
    return (np.cos(ang) / np.sqrt(128)).astype(np.float32), (-np.sin(ang) / np.sqrt(128)).astype(np.float32)


ATT_VARIANT = ""


def build_attn(NQ, NK, QB):
    P = Prog()
    NQB = NQ // QB
    NKT = NK // 128
    qT = P.din("qT", [128, NQ], BF16)
    kT = P.din("kT", [128, NK], BF16)
    v = P.din("v", [NK, 128], BF16)
    oT = P.dout("oT", [128, NQ], BF16)
    q_sb = P.sb([128, NQ], BF16); k_sb = P.sb([128, NK], BF16); v_sb = P.sb([128, NKT, 128], BF16)
    KCH = 13 if NKT % 13 == 0 else NKT
    vv = v.rearrange("(kt p) d -> p kt d", p=128)
    QCH = 4 if NQB % 4 == 0 else NQB
    P.dma("sp", q_sb[:, 0:QCH * QB], qT[:, 0:QCH * QB], w=[("Q", 0)])
    for c in range(NKT // KCH):
        P.dma("sp", k_sb[:, c * KCH * 128:(c + 1) * KCH * 128], kT[:, c * KCH * 128:(c + 1) * KCH * 128], w=[("K", c)])
        P.dma("sp", v_sb[:, c * KCH:(c + 1) * KCH, :], vv[:, c * KCH:(c + 1) * KCH, :], w=[("V", c)])
    for c in range(1, NQB // QCH):
        P.dma("sp", q_sb[:, c * QCH * QB:(c + 1) * QCH * QB], qT[:, c * QCH * QB:(c + 1) * QCH * QB], w=[("Q", c)])
    ones_b = P.sb([128, 1], BF16); ones_f = P.sb([1, 128], F32)
    P.op("pool", lambda g: g.memset(ones_b[:], 1.0), w=["ones_b"])
    P.op("pool", lambda g: g.memset(ones_f[:], 1.0), w=["ones_f"])
    S = [P.ps([128, 512], F32) for _ in range(3)]
    O = [P.ps([128, 512], F32) for _ in range(2)]
    SM = [P.ps([128, 512], F32) for _ in range(2)]
    BC = P.ps([128, 512], F32)
    PT = [P.sb([128, QB], BF16) for _ in range(3)]
    rs = P.sb([1, QB], F32); bc_sb = P.sb([128, QB], F32)
    o_sb = [P.sb([128, QB], BF16) for _ in range(2)]
    scale = 128 ** -0.5
    items = [(qb, kt) for qb in range(NQB) for kt in range(NKT)]

    def emit_S(i):
        qb, kt = items[i]
        s = i % 3
        mm(P, S[s][:, :QB], k_sb[:, kt * 128:(kt + 1) * 128], q_sb[:, qb * QB:(qb + 1) * QB], True, True,
           r=[("K", kt // KCH), ("Q", qb // QCH)], w=[("S", s)])

    emit_S(0)
    for i, (qb, kt) in enumerate(items):
        if i + 1 < len(items):
            emit_S(i + 1)
        s = i % 3
        o = qb % 2
        act(P, PT[s][:], S[s][:, :QB], AF.Exp, r=[("S", s)], w=[("PT", s)], scale=scale)
        if ATT_VARIANT != "B" or kt in (0, NKT - 1):
            mm(P, O[o][:, :QB], v_sb[:, kt, :], PT[s][:], kt == 0, kt == NKT - 1, r=[("V", kt // KCH), ("PT", s)], w=[("O", o)])
        if ATT_VARIANT == "" or kt in (0, NKT - 1):
            mm(P, SM[o][0:1, :QB], ones_b[:], PT[s][:], kt == 0, kt == NKT - 1, r=["ones_b", ("PT", s)], w=[("SM", o)])
        if kt == NKT - 1:
            P.op("dve", lambda g, o=o: g.reciprocal(out=rs[:], in_=SM[o][0:1, :QB]), r=[("SM", o)], w=["rs"])
            mm(P, BC[:, :QB], ones_f[:], rs[:], True, True, r=["ones_f", "rs"], w=["BC"])
            P.op("act", lambda g: g.copy(out=bc_sb[:], in_=BC[:, :QB]), r=["BC"], w=["bc_sb"])
            tt(P, "dve", o_sb[o][:], O[o][:, :QB], bc_sb[:], ALU.mult, r=[("O", o), "bc_sb"], w=[("o_sb", o)])
            P.dma("sp", oT[:, qb * QB:(qb + 1) * QB], o_sb[o][:], r=[("o_sb", o)])
    return P.build()


def const_ap(P, val):
    if not hasattr(P, "_consts"):
        P._consts = {}
    if val not in P._consts:
        t = P.sb([128, 1], F32, name=f"const{len(P._consts)}")
        key = ("const", len(P._consts))
        P.op("pool", lambda g: g.memset(t[:], float(val)), w=[key])
        P._consts[val] = (t, key)
    return P._consts[val]


def emit_sin(P, out, u, shape, r, w, tag):
    if not hasattr(P, "_sin_tmp"):
        P._sin_tmp = {}
    sk = tuple(shape)
    if sk not in P._sin_tmp:
        P._sin_tmp[sk] = (P.sb(shape, I32), P.sb(shape, F32), P.sb(shape, F32), P.sb(shape, F32))
    ki, kf, fr, mk = P._sin_tmp[sk]
    k = lambda n: ("sintmp", sk, n)
    P.op("dve", lambda g: g.tensor_copy(out=ki[:], in_=u), r=r, w=[k("ki")])
    P.op("dve", lambda g: g.tensor_copy(out=kf[:], in_=ki[:]), r=[k("ki")], w=[k("kf")])
    tt(P, "dve", fr[:], u, kf[:], ALU.subtract, r=r + [k("kf")], w=[k("fr")])
    ts(P, "dve", mk[:], fr[:], 0.0, None, ALU.is_lt, None, r=[k("fr")], w=[k("mk")])
    tt(P, "dve", fr[:], fr[:], mk[:], ALU.add, r=[k("fr"), k("mk")], w=[k("fr")])
    cb, cbk = const_ap(P, -np.pi)
    act(P, out, fr[:], AF.Sin, r=[k("fr"), cbk], w=w, bias=cb[:], scale=2 * np.pi)


def build_ssm(L, TBL=512):
    P = Prog()
    NCT = 4
    u = P.din("u", [128, L], F32)
    lam_re = P.din("lam_re", [128, NCT], F32)
    lam_im = P.din("lam_im", [128, NCT], F32)
    log_dt = P.din("log_dt", [128, NCT], F32)
    b_re = P.din("b_re", [NCT, 128, 128], F32)
    b_im = P.din("b_im", [NCT, 128, 128], F32)
    c_re = P.din("c_re", [NCT, 128, 128], F32)
    c_im = P.din("c_im", [NCT, 128, 128], F32)
    tvec = P.din("tvec", [128, TBL], F32)
    ident = P.din("ident", [128, 128], F32)
    y = P.dout("y", [128, L], F32)

    U = P.sb([128, L], F32)
    nch = 8
    bounds = [(i * L) // nch for i in range(nch + 1)]
    def ukey(t0):
        for i in range(nch):
            if bounds[i] <= t0 < bounds[i + 1]:
                return ("U", i)
    lr = P.sb([128, NCT]); li = P.sb([128, NCT]); ld = P.sb([128, NCT])
    P.dma("sp", lr[:], lam_re[:], w=["lr"]); P.dma("sp", li[:], lam_im[:], w=["li"]); P.dma("sp", ld[:], log_dt[:], w=["ld"])
    tv = P.sb([128, TBL]); idn = P.sb([128, 128])
    P.dma("sp", tv[:], tvec[:], w=["tv"]); P.dma("sp", idn[:], ident[:], w=["idn"])
    Bre = P.sb([128, NCT, 128]); Bim = P.sb([128, NCT, 128]); Cre = P.sb([128, NCT, 128]); Cim = P.sb([128, NCT, 128])
    for c in range(NCT):
        P.dma("sp", Bre[:, c, :], b_re[c], w=[("Bre", c)]); P.dma("sp", Bim[:, c, :], b_im[c], w=[("Bim", c)])
        P.dma("sp", Cre[:, c, :], c_re[c], w=[("Cre", c)]); P.dma("sp", Cim[:, c, :], c_im[c], w=[("Cim", c)])
    for i in range(nch):
        P.dma("sp", U[:, bounds[i]:bounds[i + 1]], u[:, bounds[i]:bounds[i + 1]], w=[("U", i)])
    S4 = [128, NCT]
    dt = P.sb(S4); mag = P.sb(S4); th = P.sb(S4); tmp = P.sb(S4); us = P.sb(S4); uc = P.sb(S4)
    sn = P.sb(S4); cs = P.sb(S4); ar1 = P.sb(S4); ai = P.sb(S4); den = P.sb(S4); t2 = P.sb(S4)
    cr = P.sb(S4); ci = P.sb(S4); nci = P.sb(S4); th2 = P.sb(S4)
    act(P, dt[:], ld[:], AF.Exp, r=["ld"], w=["dt"])
    tt(P, "dve", tmp[:], lr[:], dt[:], ALU.mult, r=["lr", "dt"], w=["tmp"])
    act(P, mag[:], tmp[:], AF.Exp, r=["tmp"], w=["mag"])
    tt(P, "dve", th[:], li[:], dt[:], ALU.mult, r=["li", "dt"], w=["th"])
    ts(P, "dve", th2[:], th[:], float(1.0 / (2 * np.pi)), None, ALU.mult, None, r=["th"], w=["th2"])
    ts(P, "dve", us[:], th2[:], 0.5, None, ALU.add, None, r=["th2"], w=["us"])
    ts(P, "dve", uc[:], th2[:], 0.75, None, ALU.add, None, r=["th2"], w=["uc"])
    emit_sin(P, sn[:], us[:], S4, ["us"], ["sn"], "rs")
    emit_sin(P, cs[:], uc[:], S4, ["uc"], ["cs"], "rc")
    tt(P, "dve", ar1[:], mag[:], cs[:], ALU.mult, r=["mag", "cs"], w=["ar1"])
    ts(P, "dve", ar1[:], ar1[:], -1.0, None, ALU.add, None, r=["ar1"], w=["ar1"])
    tt(P, "dve", ai[:], mag[:], sn[:], ALU.mult, r=["mag", "sn"], w=["ai"])
    tt(P, "dve", den[:], lr[:], lr[:], ALU.mult, r=["lr"], w=["den"])
    tt(P, "dve", t2[:], li[:], li[:], ALU.mult, r=["li"], w=["t2"])
    tt(P, "dve", den[:], den[:], t2[:], ALU.add, r=["den", "t2"], w=["den"])
    P.op("dve", lambda g: g.reciprocal(out=den[:], in_=den[:]), r=["den"], w=["den"])
    tt(P, "dve", cr[:], ar1[:], lr[:], ALU.mult, r=["ar1", "lr"], w=["cr"])
    tt(P, "dve", t2[:], ai[:], li[:], ALU.mult, r=["ai", "li"], w=["t2"])
    tt(P, "dve", cr[:], cr[:], t2[:], ALU.add, r=["cr", "t2"], w=["cr"])
    tt(P, "dve", cr[:], cr[:], den[:], ALU.mult, r=["cr", "den"], w=["cr"])
    tt(P, "dve", ci[:], ai[:], lr[:], ALU.mult, r=["ai", "lr"], w=["ci"])
    tt(P, "dve", t2[:], ar1[:], li[:], ALU.mult, r=["ar1", "li"], w=["t2"])
    tt(P, "dve", ci[:], ci[:], t2[:], ALU.subtract, r=["ci", "t2"], w=["ci"])
    tt(P, "dve", ci[:], ci[:], den[:], ALU.mult, r=["ci", "den"], w=["ci"])
    ts(P, "dve", nci[:], ci[:], -1.0, None, ALU.mult, None, r=["ci"], w=["nci"])
    BrT = P.sb([128, NCT, 128]); BiT = P.sb([128, NCT, 128]); Cimn = P.sb([128, NCT, 128])
    bt = P.sb([128, 128]); bb = [P.sb([128, 128]) for _ in range(2)]
    pst = P.ps([128, 512], F32)
    nb = 0
    for c in range(NCT):
        for which in range(2):
            s = nb % 2; nb += 1
            if which == 0:
                ts(P, "dve", bt[:], Bre[:, c, :], cr[:, c:c + 1], None, ALU.mult, None, r=[("Bre", c), "cr"], w=["bt"])
                stt(P, bb[s][:], Bim[:, c, :], nci[:, c:c + 1], bt[:], ALU.mult, ALU.add, r=[("Bim", c), "nci", "bt"], w=[("bb", s)])
                dst = BrT
            else:
                ts(P, "dve", bt[:], Bim[:, c, :], cr[:, c:c + 1], None, ALU.mult, None, r=[("Bim", c), "cr"], w=["bt"])
                stt(P, bb[s][:], Bre[:, c, :], ci[:, c:c + 1], bt[:], ALU.mult, ALU.add, r=[("Bre", c), "ci", "bt"], w=[("bb", s)])
                dst = BiT
            P.op("pe", lambda g, s=s: g.transpose(pst[:, 0:128], bb[s][:], idn[:]), r=[("bb", s), "idn"], w=["pst"])
            P.op("act", lambda g, dst=dst, c=c: g.copy(out=dst[:, c, :], in_=pst[:, 0:128]), r=["pst"], w=[("BT", c, which)])
        P.op("act", lambda g, c=c: g.mul(out=Cimn[:, c, :], in_=Cim[:, c, :], mul=-1.0), r=[("Cim", c)], w=[("Cimn", c)])
    ct = P.sb([128, NCT, TBL]); st = P.sb([128, NCT, TBL]); magT = P.sb([128, NCT, TBL])
    ut = P.sb([128, TBL])
    for c in range(NCT):
        for which, dstT, off in ((0, st, 0.5), (1, ct, 0.75)):
            ts(P, "dve", ut[:], tv[:], th2[:, c:c + 1], off, ALU.mult, ALU.add, r=["tv", "th2"], w=["ut"])
            emit_sin(P, dstT[:, c, :], ut[:], [128, TBL], ["ut"], [("tab", c, which)], ("tb", c, which))
        ts(P, "dve", magT[:, c, :], tv[:], 0.0, mag[:, c:c + 1], ALU.mult, ALU.add, r=["tv", "mag"], w=[("magT", c)])
    blocks = []
    t0 = 0
    while t0 < L:
        n = min(TBL, L - t0); blocks.append((t0, n)); t0 += n
    psb = [[P.ps([128, 512], F32), P.ps([128, 512], F32)] for _ in range(2)]
    psy = [P.ps([128, 512], F32) for _ in range(2)]
    m = [P.sb([128, TBL]) for _ in range(4)]
    gin = [P.sb([128, TBL]) for _ in range(2)]
    g_ = [P.sb([128, TBL]) for _ in range(2)]
    hh = [[[P.sb([128, TBL]), P.sb([128, TBL])] for _ in range(2)] for _ in range(NCT)]
    ysb = [P.sb([128, TBL]) for _ in range(2)]
    it = 0
    for j, (t0, n) in enumerate(blocks):
        yb = j % 2
        hb = j % 2
        uk = [ukey(t0), ukey(t0 + n - 1)]
        for c in range(NCT):
            pb = it % 2; it += 1
            br_ps, bi_ps = psb[pb][0][:, :n], psb[pb][1][:, :n]
            mm(P, br_ps, BrT[:, c, :], U[:, t0:t0 + n], True, True, r=[("BT", c, 0)] + uk, w=[("psb", pb, 0)])
            mm(P, bi_ps, BiT[:, c, :], U[:, t0:t0 + n], True, True, r=[("BT", c, 1)] + uk, w=[("psb", pb, 1)])
            ctc, stc = ct[:, c, :n], st[:, c, :n]
            tk = [("tab", c, 0), ("tab", c, 1)]
            tt(P, "dve", m[0][:, :n], br_ps, ctc, ALU.mult, r=[("psb", pb, 0)] + tk, w=["m0"])
            tt(P, "dve", m[1][:, :n], bi_ps, stc, ALU.mult, r=[("psb", pb, 1)] + tk, w=["m1"])
            tt(P, "dve", m[2][:, :n], bi_ps, ctc, ALU.mult, r=[("psb", pb, 1)] + tk, w=["m2"])
            tt(P, "dve", m[3][:, :n], br_ps, stc, ALU.mult, r=[("psb", pb, 0)] + tk, w=["m3"])
            tt(P, "dve", gin[0][:, :n], m[0][:, :n], m[1][:, :n], ALU.add, r=["m0", "m1"], w=["gin0"])
            tt(P, "dve", gin[1][:, :n], m[2][:, :n], m[3][:, :n], ALU.subtract, r=["m2", "m3"], w=["gin1"])
            for ri in range(2):
                if j == 0:
                    init = 0.0; rk = []
                else:
                    pn = blocks[j - 1][1]
                    init = hh[c][1 - hb][ri][:, pn - 1:pn]; rk = [("hh", c, 1 - hb, ri)]
                P.op("dve", lambda g, ri=ri, init=init, c=c, n=n: g.tensor_tensor_scan(
                    out=g_[ri][:, :n], data0=magT[:, c, :n], data1=gin[ri][:, :n], initial=init,
                    op0=ALU.mult, op1=ALU.add), r=[("magT", c), f"gin{ri}"] + rk, w=[f"g{ri}"])
            tt(P, "pool", m[0][:, :n], g_[0][:, :n], ctc, ALU.mult, r=["g0"] + tk, w=["m0"])
            tt(P, "pool", m[1][:, :n], g_[1][:, :n], stc, ALU.mult, r=["g1"] + tk, w=["m1"])
            tt(P, "pool", m[2][:, :n], g_[0][:, :n], stc, ALU.mult, r=["g0"] + tk, w=["m2"])
            tt(P, "pool", m[3][:, :n], g_[1][:, :n], ctc, ALU.mult, r=["g1"] + tk, w=["m3"])
            hr, hi = hh[c][hb][0], hh[c][hb][1]
            tt(P, "dve", hr[:, :n], m[0][:, :n], m[1][:, :n], ALU.subtract, r=["m0", "m1"], w=[("hh", c, hb, 0)])
            tt(P, "dve", hi[:, :n], m[2][:, :n], m[3][:, :n], ALU.add, r=["m2", "m3"], w=[("hh", c, hb, 1)])
            mm(P, psy[yb][:, :n], Cre[:, c, :], hr[:, :n], c == 0, False, r=[("Cre", c), ("hh", c, hb, 0)], w=[("psy", yb)])
            mm(P, psy[yb][:, :n], Cimn[:, c, :], hi[:, :n], False, c == NCT - 1, r=[("Cimn", c), ("hh", c, hb, 1)], w=[("psy", yb)])
        P.op("act", lambda g, yb=yb, n=n: g.copy(out=ysb[yb][:, :n], in_=psy[yb][:, :n]), r=[("psy", yb)], w=[("ysb", yb)])
        P.dma("sp", y[:, t0:t0 + n], ysb[yb][:, :n], r=[("ysb", yb)])
    return P.build()


def ssm_pack(lam_re, lam_im, log_dt, b_re, b_im, c_re, c_im, TBL=512):
    f = np.float32
    lr = np.zeros((128, 4), f); li = np.zeros((128, 4), f); ld = np.zeros((128, 4), f)
    Bre = np.zeros((4, 128, 128), f); Bim = np.zeros((4, 128, 128), f)
    Cre = np.zeros((4, 128, 128), f); Cim = np.zeros((4, 128, 128), f)
    for c in range(4):
        for gi in range(2):
            g = 2 * c + gi
            sl = slice(gi * 64, (gi + 1) * 64)
            lr[sl, c] = lam_re[g]; li[sl, c] = lam_im[g]; ld[sl, c] = log_dt[g]
            cs = slice(c * 32 + gi * 16, c * 32 + gi * 16 + 16)
            Bre[c, sl, cs] = b_re[g]; Bim[c, sl, cs] = b_im[g]
            Cre[c, sl, cs] = c_re[g].T; Cim[c, sl, cs] = c_im[g].T
    tvec = np.tile(np.arange(1, TBL + 1, dtype=f)[None, :], (128, 1))
    return dict(lam_re=lr, lam_im=li, log_dt=ld, b_re=Bre, b_im=Bim, c_re=Cre, c_im=Cim, tvec=tvec,
                ident=np.eye(128, dtype=f))


def build_pf1(NT2=16, C=512):
    P = Prog()
    zr = P.din("zr", [128, NT2, C], F32); zi = P.din("zi", [128, NT2, C], F32)
    mc = P.din("mc", [128, 128], F32); ms = P.din("ms", [128, 128], F32); mns = P.din("mns", [128, 128], F32)
    twc = P.din("twc", [128, NT2], F32); tws = P.din("tws", [128, NT2], F32); ntws = P.din("ntws", [128, NT2], F32)
    ar = P.dout("ar", [128, NT2, C], F32); ai = P.dout("ai", [128, NT2, C], F32)
    zr_sb = P.sb([128, NT2, C]); zi_sb = P.sb([128, NT2, C])
    mc_sb = P.sb([128, 128]); ms_sb = P.sb([128, 128]); mns_sb = P.sb([128, 128])
    twc_sb = P.sb([128, NT2]); tws_sb = P.sb([128, NT2]); ntws_sb = P.sb([128, NT2])
    for t, d, k in ((mc_sb, mc, "mc"), (ms_sb, ms, "ms"), (mns_sb, mns, "mns"), (twc_sb, twc, "twc"), (tws_sb, tws, "tws"), (ntws_sb, ntws, "ntws")):
        P.dma("sp", t[:], d[:], w=[k])
    H = NT2 // 2
    for hf in range(2):
        P.dma("sp", zr_sb[:, hf * H:(hf + 1) * H, :], zr[:, hf * H:(hf + 1) * H, :], w=[("zr", hf)])
        P.dma("sp", zi_sb[:, hf * H:(hf + 1) * H, :], zi[:, hf * H:(hf + 1) * H, :], w=[("zi", hf)])
    pr = [P.ps([128, 512]) for _ in range(2)]; pi = [P.ps([128, 512]) for _ in range(2)]
    t1 = [P.sb([128, C]) for _ in range(2)]; t2 = [P.sb([128, C]) for _ in range(2)]
    orr = [P.sb([128, C]) for _ in range(2)]; oi = [P.sb([128, C]) for _ in range(2)]
    for j in range(NT2):
        s = j % 2; hf = j // H
        mm(P, pr[s][:, :C], mc_sb[:], zr_sb[:, j, :], True, False, r=["mc", ("zr", hf)], w=[("pr", s)])
        mm(P, pr[s][:, :C], ms_sb[:], zi_sb[:, j, :], False, True, r=["ms", ("zi", hf)], w=[("pr", s)])
        mm(P, pi[s][:, :C], mc_sb[:], zi_sb[:, j, :], True, False, r=["mc", ("zi", hf)], w=[("pi", s)])
        mm(P, pi[s][:, :C], mns_sb[:], zr_sb[:, j, :], False, True, r=["mns", ("zr", hf)], w=[("pi", s)])
        ts(P, "dve", t1[s][:], pi[s][:, :C], tws_sb[:, j:j + 1], None, ALU.mult, None, r=[("pi", s), "tws"], w=[("t1", s)])
        stt(P, orr[s][:], pr[s][:, :C], twc_sb[:, j:j + 1], t1[s][:], ALU.mult, ALU.add, r=[("pr", s), "twc", ("t1", s)], w=[("or", s)])
        ts(P, "dve", t2[s][:], pr[s][:, :C], ntws_sb[:, j:j + 1], None, ALU.mult, None, r=[("pr", s), "ntws"], w=[("t2", s)])
        stt(P, oi[s][:], pi[s][:, :C], twc_sb[:, j:j + 1], t2[s][:], ALU.mult, ALU.add, r=[("pi", s), "twc", ("t2", s)], w=[("oi", s)])
        P.dma("sp", ar[:, j, :], orr[s][:], r=[("or", s)])
        P.dma("sp", ai[:, j, :], oi[s][:], r=[("oi", s)])
    return P.build()


def build_pf2(NK1=16, C=512, NCTX=256):
    P = Prog()
    ar = P.din("ar", [128, NK1, C], F32); ai = P.din("ai", [128, NK1, C], F32)
    mc = P.din("mc", [128, 128], F32); ms = P.din("ms", [128, 128], F32)
    zcr = P.din("zcr", [NCTX, C], F32); zci = P.din("zci", [NCTX, C], F32)
    cc = P.din("cc", [NCTX, NCTX], F32); cs = P.din("cs", [NCTX, NCTX], F32)
    xr = P.dout("xr", [128, NK1, C], F32)
    xc = P.dout("xc", [NCTX, C], F32)
    ar_sb = P.sb([128, NK1, C]); ai_sb = P.sb([128, NK1, C])
    mc_sb = P.sb([128, 128]); ms_sb = P.sb([128, 128])
    P.dma("sp", mc_sb[:], mc[:], w=["mc"]); P.dma("sp", ms_sb[:], ms[:], w=["ms"])
    H = NK1 // 2
    for hf in range(2):
        P.dma("sp", ar_sb[:, hf * H:(hf + 1) * H, :], ar[:, hf * H:(hf + 1) * H, :], w=[("ar", hf)])
        P.dma("sp", ai_sb[:, hf * H:(hf + 1) * H, :], ai[:, hf * H:(hf + 1) * H, :], w=[("ai", hf)])
    NTC = NCTX // 128
    zcr_sb = P.sb([128, NTC, C]); zci_sb = P.sb([128, NTC, C]); cc_sb = P.sb([128, NTC, NCTX]); cs_sb = P.sb([128, NTC, NCTX])
    P.dma("sp", zcr_sb[:], zcr.rearrange("(a p) c -> p a c", p=128), w=["zcr"])
    P.dma("sp", zci_sb[:], zci.rearrange("(a p) c -> p a c", p=128), w=["zci"])
    P.dma("sp", cc_sb[:], cc.rearrange("(a p) c -> p a c", p=128), w=["cc"])
    P.dma("sp", cs_sb[:], cs.rearrange("(a p) c -> p a c", p=128), w=["cs"])
    px = [P.ps([128, 512]) for _ in range(2)]
    ox = [P.sb([128, C]) for _ in range(2)]
    it = 0
    for j in range(NK1):
        s = it % 2; it += 1; hf = j // H
        mm(P, px[s][:, :C], mc_sb[:], ar_sb[:, j, :], True, False, r=["mc", ("ar", hf)], w=[("px", s)])
        mm(P, px[s][:, :C], ms_sb[:], ai_sb[:, j, :], False, True, r=["ms", ("ai", hf)], w=[("px", s)])
        P.op("act", lambda g, s=s: g.copy(out=ox[s][:], in_=px[s][:, :C]), r=[("px", s)], w=[("ox", s)])
        P.dma("sp", xr[:, j, :], ox[s][:], r=[("ox", s)])
    for kc in range(NTC):
        s = it % 2; it += 1
        n = 0
        for mat, mk, z, zk in ((cc_sb, "cc", zcr_sb, "zcr"), (cs_sb, "cs", zci_sb, "zci")):
            for tcn in range(NTC):
                mm(P, px[s][:, :C], mat[:, tcn, kc * 128:(kc + 1) * 128], z[:, tcn, :], n == 0, n == 2 * NTC - 1,
                   r=[mk, zk], w=[("px", s)])
                n += 1
        P.op("act", lambda g, s=s: g.copy(out=ox[s][:], in_=px[s][:, :C]), r=[("px", s)], w=[("ox", s)])
        P.dma("sp", xc[kc * 128:(kc + 1) * 128, :], ox[s][:], r=[("ox", s)])
    return P.build()


def dft_consts():
    j = np.arange(128)
    ang = 2 * np.pi * np.outer(j, j) / 128
    c128 = np.cos(ang); s128 = np.sin(ang)
    k1 = np.arange(128)[:, None]; t2 = np.arange(128)[None, :]
    ph = 2 * np.pi * (k1 * t2) / 16384.0
    j2 = np.arange(256)
    a2 = 2 * np.pi * np.outer(j2, j2) / 256
    f = np.float32
    return dict(c1=(c128 / 128).astype(f), s1=(s128 / 128).astype(f), ns1=(-s128 / 128).astype(f),
                c2=c128.astype(f), s2=s128.astype(f), twc=np.cos(ph).astype(f), tws=np.sin(ph).astype(f),
                cc=(np.cos(a2) / 16).astype(f), cs=(np.sin(a2) / 16).astype(f))


def run_fourier(run, zr, zi, zcr, zci):
    K = dft_consts()
    Zr = zr.reshape(128, 128, 512); Zi = zi.reshape(128, 128, 512)
    ins = []
    for i in range(8):
        sl = slice(16 * i, 16 * (i + 1))
        ins.append(dict(zr=np.ascontiguousarray(Zr[:, sl]), zi=np.ascontiguousarray(Zi[:, sl]), mc=K["c1"], ms=K["s1"], mns=K["ns1"],
                        twc=np.ascontiguousarray(K["twc"][:, sl]), tws=np.ascontiguousarray(K["tws"][:, sl]),
                        ntws=np.ascontiguousarray(-K["tws"][:, sl])))
    res = run("pf1", lambda: build_pf1(), ins)
    Ar = np.concatenate([r["ar"] for r in res], axis=1)
    Ai = np.concatenate([r["ai"] for r in res], axis=1)
    ArT = Ar.transpose(1, 0, 2); AiT = Ai.transpose(1, 0, 2)
    ins = []
    for i in range(8):
        sl = slice(16 * i, 16 * (i + 1))
        ins.append(dict(ar=np.ascontiguousarray(ArT[:, sl]), ai=np.ascontiguousarray(AiT[:, sl]), mc=K["c2"], ms=K["s2"],
                        zcr=zcr, zci=zci, cc=K["cc"], cs=K["cs"]))
    res = run("pf2", lambda: build_pf2(), ins)
    X = np.concatenate([r["xr"] for r in res], axis=1).reshape(16384, 512)
    return X, res[0]["xc"]


_NC_CACHE = {}


def run_prog(name, builder, ins):
    if name not in _NC_CACHE:
        _NC_CACHE[name] = builder()
    res = run_bass_kernel_spmd(_NC_CACHE[name], ins, core_ids=list(range(len(ins))))
    return [{k: np.asarray(v) for k, v in r.items()} for r in res.results]


FFN_H = 5632


def build_p3(NB, TB):
    P = Prog()
    Tc = NB * TB
    KC = 16
    HC = FFN_H // 128
    xT = P.din("xT", [D_MODEL, Tc], F32)
    attnT = P.din("attnT", [1024, Tc], BF16)
    yfT = P.din("yfT", [512, Tc], F32); ybT = P.din("ybT", [512, Tc], F32); usT = P.din("usT", [512, Tc], F32)
    xfT = P.din("xfT", [512, Tc], F32)
    glu_w = P.din("glu_w", [512, 512], BF16); fw = P.din("fw", [512, 512], BF16)
    w_out = P.din("w_out", [D_MODEL, D_MODEL], BF16)
    w_gate = P.din("w_gate", [D_MODEL, FFN_H], BF16); w_up = P.din("w_up", [D_MODEL, FFN_H], BF16)
    w_down = P.din("w_down", [FFN_H, D_MODEL], BF16)
    vnames = ["gpm", "gm", "gpf", "scf", "shf", "gof", "gf"]
    vin = {n: P.din(n, [128, KC], F32) for n in vnames}
    ssm_d = P.din("ssm_d", [128, 4], F32); glu_b = P.din("glu_b", [128, 4], F32)
    xoT = P.dout("xoT", [D_MODEL, Tc], F32)

    vs = {}
    for n in vnames:
        vs[n] = P.sb([128, KC]); P.dma("sp", vs[n][:], vin[n][:], w=[n])
    d_sb = P.sb([128, 4]); gb_sb = P.sb([128, 4])
    P.dma("sp", d_sb[:], ssm_d[:], w=["ssm_d"]); P.dma("sp", gb_sb[:], glu_b[:], w=["glu_b"])
    GM = P.sb([128, KC]); G2 = P.sb([128, KC]); GF = P.sb([128, KC])
    tt(P, "dve", GM[:], vs["gpm"][:], vs["gm"][:], ALU.mult, r=["gpm", "gm"], w=["GM"])
    stt(P, G2[:], vs["scf"][:], 1.0, vs["gpf"][:], ALU.add, ALU.mult, r=["scf", "gpf"], w=["G2"])
    tt(P, "dve", GF[:], vs["gof"][:], vs["gf"][:], ALU.mult, r=["gof", "gf"], w=["GF"])
    ones = P.sb([128, 128], BF16)
    P.op("pool", lambda g: g.memset(ones[:], 1.0), w=["ones"])

    x_sb = P.sb([128, KC, TB], F32)
    cat = P.sb([128, KC, TB], BF16)
    yb_sb = P.sb([128, KC, TB], BF16)
    a_sb = P.sb([128, HC, TB], BF16)
    WB = [P.sb([128, KC, 512], BF16) for _ in range(3)]
    s_us = [P.sb([128, TB]) for _ in range(1)]; s_yf = [P.sb([128, TB]) for _ in range(1)]; s_yb = [P.sb([128, TB]) for _ in range(1)]
    s_xf = [P.sb([128, TB]) for _ in range(1)]
    e1 = P.sb([128, TB]); e2 = P.sb([128, TB]); e3 = P.sb([128, TB])
    gfp = P.sb([128, 4, TB], F32); gbf = P.sb([128, 4, TB], BF16); xfb = P.sb([128, 4, TB], BF16)
    sqc = [P.sb([128, TB], BF16) for _ in range(2)]
    rstd = P.sb([128, TB]); tmpf = [P.sb([128, TB]) for _ in range(2)]
    sg = [P.sb([128, TB]) for _ in range(2)]
    xo = [P.sb([128, TB]) for _ in range(2)]
    PS = [P.ps([128, 512]) for _ in range(8)]
    xv = xT.rearrange("(kc p) t -> p kc t", p=128)
    av = attnT.rearrange("(kc p) t -> p kc t", p=128)
    wcnt = [0]

    def wload(src_ap, shape_kc):
        s = wcnt[0] % 3; wcnt[0] += 1
        P.dma("sp", WB[s][:, :shape_kc, :], src_ap, w=[("WB", s)])
        return s

    def rms_bc(nsq_key):
        rstd_from_ps(P, rstd[:], PS[4][:, :TB], D_MODEL, EPS, r=["ps4"], w=["rstd"], tmpkey=None)

    nsq = [0]

    def sq_acc(src_ps, src_key, idx, n):
        s = nsq[0] % 2; nsq[0] += 1
        act(P, sqc[s][:], src_ps, AF.Square, r=[src_key], w=[("sqc", s)])
        mm(P, PS[4][:, :TB], ones[:], sqc[s][:], idx == 0, idx == n - 1, r=["ones", ("sqc", s)], w=["ps4"])

    for b in range(NB):
        tsl = slice(b * TB, (b + 1) * TB)
        P.dma("sp", x_sb[:], xv[:, :, tsl], w=["x"])
        P.dma("sp", cat[:, 0:8, :], av[:, :, tsl], w=[("cat", k) for k in range(8)])
        sgw = wload(glu_w.rearrange("(kc p) c -> p kc c", p=128), 4)
        sfw = wload(fw.rearrange("(kc p) c -> p kc c", p=128), 4)
        for ch in range(4):
            s = 0
            fsl = slice(ch * 128, (ch + 1) * 128)
            P.dma("sp", s_us[s][:], usT[fsl, tsl], w=[("us", s)])
            P.dma("sp", s_yf[s][:], yfT[fsl, tsl], w=[("yf", s)])
            P.dma("sp", s_yb[s][:], ybT[fsl, tsl], w=[("yb", s)])
            P.dma("sp", s_xf[s][:], xfT[fsl, tsl], w=[("xf", s)])
            stt(P, e1[:], s_us[s][:], d_sb[:, ch:ch + 1], s_yf[s][:], ALU.mult, ALU.add, r=[("us", s), "ssm_d", ("yf", s)], w=["e1"])
            tt(P, "dve", e1[:], e1[:], s_yb[s][:], ALU.add, r=["e1", ("yb", s)], w=["e1"])
            tt(P, "pool", e2[:], e1[:], e1[:], ALU.mult, r=["e1"], w=["e2"])
            ts(P, "dve", e2[:], e2[:], 0.044715, 1.0, ALU.mult, ALU.add, r=["e2"], w=["e2"])
            tt(P, "pool", e3[:], e2[:], e1[:], ALU.mult, r=["e2", "e1"], w=["e3"])
            act(P, e3[:], e3[:], AF.Sigmoid, r=["e3"], w=["e3"], scale=1.5957691216057308)
            tt(P, "dve", gfp[:, ch, :], e3[:], e1[:], ALU.mult, r=["e3", "e1"], w=[("gfp", ch)])
            P.op("act", lambda g, ch=ch: g.copy(out=gbf[:, ch, :], in_=gfp[:, ch, :]), r=[("gfp", ch)], w=[("gbf", ch)])
            P.op("act", lambda g, ch=ch, s=s: g.copy(out=xfb[:, ch, :], in_=s_xf[s][:]), r=[("xf", s)], w=[("xfb", ch)])
        for co in range(4):
            pz = PS[5 + co % 2]; pk = ("ps", 5 + co % 2)
            for ci in range(4):
                mm(P, pz[:, :TB], WB[sgw][:, ci, co * 128:(co + 1) * 128], gbf[:, ci, :], ci == 0, ci == 3,
                   r=[("WB", sgw), ("gbf", ci)], w=[pk])
            act(P, e2[:], pz[:, :TB], AF.Sigmoid, r=[pk, "glu_b"], w=["e2"], bias=gb_sb[:, co:co + 1])
            tt(P, "dve", cat[:, 8 + co, :], e2[:], gfp[:, co, :], ALU.mult, r=["e2", ("gfp", co)], w=[("cat", 8 + co)])
        for co in range(4):
            pz = PS[5 + co % 2]; pk = ("ps", 5 + co % 2)
            for ci in range(4):
                mm(P, pz[:, :TB], WB[sfw][:, ci, co * 128:(co + 1) * 128], xfb[:, ci, :], ci == 0, ci == 3,
                   r=[("WB", sfw), ("xfb", ci)], w=[pk])
            P.op("act", lambda g, co=co, pz=pz: g.copy(out=cat[:, 12 + co, :], in_=pz[:, :TB]), r=[pk], w=[("cat", 12 + co)])
        wov = w_out.rearrange("(kc p) c -> p kc c", p=128)
        for og in range(4):
            sw = wload(wov[:, :, og * 512:(og + 1) * 512], KC)
            for o4 in range(4):
                oc = og * 4 + o4
                pa = PS[oc % 2]; pk = ("ps", oc % 2)
                for kc in range(KC):
                    mm(P, pa[:, :TB], WB[sw][:, kc, o4 * 128:(o4 + 1) * 128], cat[:, kc, :], kc == 0, kc == KC - 1,
                       r=[("WB", sw), ("cat", kc)], w=[pk])
                P.op("dve", lambda g, oc=oc, pa=pa: g.tensor_copy(out=yb_sb[:, oc, :], in_=pa[:, :TB]), r=[pk], w=[("yb_sb", oc)])
                sq_acc(pa[:, :TB], pk, oc, KC)
        rms_bc(None)
        for oc in range(KC):
            s = oc % 2
            stt(P, tmpf[s][:], yb_sb[:, oc, :], GM[:, oc:oc + 1], rstd[:], ALU.mult, ALU.mult,
                r=[("yb_sb", oc), "GM", "rstd"], w=[("tmpf", s)])
            tt(P, "pool", x_sb[:, oc, :], x_sb[:, oc, :], tmpf[s][:], ALU.add, r=["x", ("tmpf", s)], w=[("x1", oc)])
        for kc in range(KC):
            sq_acc(x_sb[:, kc, :], ("x1", kc), kc, KC)
        rms_bc(None)
        for kc in range(KC):
            s = kc % 2
            stt(P, tmpf[s][:], x_sb[:, kc, :], G2[:, kc:kc + 1], rstd[:], ALU.mult, ALU.mult,
                r=[("x1", kc), "G2", "rstd"], w=[("tmpf", s)])
            act(P, cat[:, kc, :], tmpf[s][:], AF.Identity, r=[("tmpf", s), "shf"], w=[("cat", kc)], bias=vs["shf"][:, kc:kc + 1])
        wgv = w_gate.rearrange("(kc p) c -> p kc c", p=128); wuv = w_up.rearrange("(kc p) c -> p kc c", p=128)
        for hg in range(HC // 4):
            swg = wload(wgv[:, :, hg * 512:(hg + 1) * 512], KC)
            swu = wload(wuv[:, :, hg * 512:(hg + 1) * 512], KC)
            for h4 in range(4):
                hc = hg * 4 + h4
                pg = PS[(hc % 2) * 2]; pgk = ("ps", (hc % 2) * 2)
                pu = PS[(hc % 2) * 2 + 1]; puk = ("ps", (hc % 2) * 2 + 1)
                for kc in range(KC):
                    mm(P, pg[:, :TB], WB[swg][:, kc, h4 * 128:(h4 + 1) * 128], cat[:, kc, :], kc == 0, kc == KC - 1,
                       r=[("WB", swg), ("cat", kc)], w=[pgk])
                for kc in range(KC):
                    mm(P, pu[:, :TB], WB[swu][:, kc, h4 * 128:(h4 + 1) * 128], cat[:, kc, :], kc == 0, kc == KC - 1,
                       r=[("WB", swu), ("cat", kc)], w=[puk])
                s = hc % 2
                act(P, sg[s][:], pg[:, :TB], AF.Silu, r=[pgk], w=[("sg", s)])
                tt(P, "dve", a_sb[:, hc, :], pu[:, :TB], sg[s][:], ALU.mult, r=[puk, ("sg", s)], w=[("a", hc)])
        wdv = w_down.rearrange("(hc p) c -> p hc c", p=128)
        for og in range(4):
            for q in range(4):
                sw = wload(wdv[:, q * 11:(q + 1) * 11, og * 512:(og + 1) * 512], 11)
                for h in range(11):
                    hc = q * 11 + h
                    for o4 in range(4):
                        mm(P, PS[o4][:, :TB], WB[sw][:, h, o4 * 128:(o4 + 1) * 128], a_sb[:, hc, :], hc == 0, hc == HC - 1,
                           r=[("WB", sw), ("a", hc)], w=[("ps", o4)])
            for o4 in range(4):
                oc = og * 4 + o4
                P.op("dve", lambda g, oc=oc, o4=o4: g.tensor_copy(out=yb_sb[:, oc, :], in_=PS[o4][:, :TB]), r=[("ps", o4)], w=[("yb_sb", oc)])
                sq_acc(PS[o4][:, :TB], ("ps", o4), oc, KC)
        rms_bc(None)
        for oc in range(KC):
            s = oc % 2
            stt(P, tmpf[s][:], yb_sb[:, oc, :], GF[:, oc:oc + 1], rstd[:], ALU.mult, ALU.mult,
                r=[("yb_sb", oc), "GF", "rstd"], w=[("tmpf", s)])
            tt(P, "pool", xo[s][:], x_sb[:, oc, :], tmpf[s][:], ALU.add, r=[("x1", oc), ("tmpf", s)], w=[("xo", s)])
            P.dma("sp", xoT[oc * 128:(oc + 1) * 128, tsl], xo[s][:], r=[("xo", s)])
    return P.build()


def _v2(v):
    return np.ascontiguousarray(np.asarray(v, np.float32).reshape(-1, 128).T)


def _cat(res, name, axis):
    return np.concatenate([r[name] for r in res], axis=axis)


BIG_W = ["w_in", "ssm_glu_w", "fourier_w", "w_out", "ffn_w_gate", "ffn_w_up", "ffn_w_down"]


def cast_weights(inp):
    parts = []
    shapes = []
    for l in range(2):
        for n in BIG_W:
            a = np.asarray(inp[n][l], np.float32)
            parts.append(a.ravel()); shapes.append((l, n, a.shape))
    flat = np.concatenate(parts)
    tot = flat.size
    per = tot // (8 * 128)
    assert per * 8 * 128 == tot and per % 5120 == 0, (tot, per)
    flat = flat.reshape(8, 128, per)
    res = run_prog("cast", lambda: build_cast(per, 5120), [{"src": flat[i]} for i in range(8)])
    out = np.concatenate([r["dst"].reshape(-1) for r in res])
    W = [dict(), dict()]
    off = 0
    for (l, n, shp) in shapes:
        sz = int(np.prod(shp))
        W[l][n] = out[off:off + sz].reshape(shp); off += sz
    return W


def kernel(**inp):
    f = np.float32
    x = np.asarray(inp["x"], f)[0]
    xc = np.asarray(inp["ctx"], f)[0]
    T = x.shape[0]; TC = xc.shape[0]
    Wb = cast_weights(inp)
    c2 = np.stack([np.asarray(inp["c"], f)[0], np.asarray(inp["c_ctx"], f)], 0)
    c2T = np.ascontiguousarray(c2.reshape(2, 16, 128).transpose(2, 1, 0))
    ada_w = np.asarray(inp["ada_w"], f); ada_b = np.asarray(inp["ada_b"], f)
    CW = 1536
    ins = [dict(c2T=c2T, w=np.ascontiguousarray(ada_w[:, :, i * CW:(i + 1) * CW]),
                b2=np.ascontiguousarray(np.repeat(ada_b[:, None, i * CW:(i + 1) * CW], 2, axis=1))) for i in range(8)]
    mod = _cat(run_prog("adaln", lambda: build_adaln(), ins), "mod", 2)
    cosT, sinT = rope_tables(np.arange(T))
    cosC = np.ones((128, TC), f); sinC = np.zeros((128, TC), f)
    rotT = rope_rotT(); cd, sdn = chan_dft()
    Tc = T // 8; Cc = TC // 8
    for l in range(2):
        need_ctx = l == 0
        m = mod[l]
        sh_m, sc_m, g_m, sh_f, sc_f, g_f = [m[:, k * 2048:(k + 1) * 2048] for k in range(6)]
        qn = np.asarray(inp["q_norm"][l], f)[:, None].copy(); kn = np.asarray(inp["k_norm"][l], f)[:, None].copy()
        com = dict(gam=_v2(inp["norm_mix_pre"][l]), w_in=Wb[l]["w_in"], qn=qn, kn=kn, rotT=rotT, cd=cd, sdn=sdn)
        ins = [dict(com, xT=np.ascontiguousarray(x[i * Tc:(i + 1) * Tc].T), scv=_v2(sc_m[0]), shv=_v2(sh_m[0]),
                    cosT=np.ascontiguousarray(cosT[:, i * Tc:(i + 1) * Tc]), sinT=np.ascontiguousarray(sinT[:, i * Tc:(i + 1) * Tc]))
               for i in range(8)]
        r1 = run_prog("p1", lambda: build_p1(4, 512), ins)
        ins = [dict(com, xT=np.ascontiguousarray(xc[i * Cc:(i + 1) * Cc].T), scv=_v2(sc_m[1]), shv=_v2(sh_m[1]),
                    cosT=np.ascontiguousarray(cosC[:, i * Cc:(i + 1) * Cc]), sinT=np.ascontiguousarray(sinC[:, i * Cc:(i + 1) * Cc]))
               for i in range(8)]
        r1c = run_prog("p1c", lambda: build_p1(1, 32), ins)
        L1 = {n: _cat(r1, n, 1) for n in ("qT", "kT", "vT", "usT", "zrT", "ziT")}
        C1 = {n: _cat(r1c, n, 1) for n in ("qT", "kT", "vT", "usT", "zrT", "ziT")}
        ins = []
        for i in range(8):
            j = i // 4
            ins.append(dict(qT=np.ascontiguousarray(L1["qT"][i * 128:(i + 1) * 128]),
                            kT=np.ascontiguousarray(np.concatenate([L1["kT"][j * 128:(j + 1) * 128], C1["kT"][j * 128:(j + 1) * 128]], 1)),
                            v=np.ascontiguousarray(np.concatenate([L1["vT"][j * 128:(j + 1) * 128], C1["vT"][j * 128:(j + 1) * 128]], 1).T)))
        attnT = np.ascontiguousarray(np.concatenate([r["o"] for r in run_prog("pa", lambda: build_attn2(T, T + TC, 512), ins)], 1).T)
        if need_ctx:
            ins = []
            for i in range(8):
                j = i // 4
                ins.append(dict(qT=np.ascontiguousarray(C1["qT"][i * 128:(i + 1) * 128]),
                                kT=np.ascontiguousarray(C1["kT"][j * 128:(j + 1) * 128]),
                                v=np.ascontiguousarray(C1["vT"][j * 128:(j + 1) * 128].T)))
            attncT = np.ascontiguousarray(np.concatenate([r["o"] for r in run_prog("pac", lambda: build_attn2(TC, TC, TC), ins)], 1).T)
        us = L1["usT"]; usc = C1["usT"]
        fwd = np.concatenate([usc, us], 1)
        bwd = np.concatenate([usc[:, ::-1], us[:, ::-1]], 1)
        ins = []
        for i in range(8):
            rows = slice(64 * i, 64 * (i + 1))
            u_core = np.ascontiguousarray(np.concatenate([fwd[rows], bwd[rows]], 0))
            idx = [(c // 2, 4 * i + 2 * (c % 2) + gi) for c in range(4) for gi in range(2)]
            pk = ssm_pack(np.stack([np.asarray(inp["ssm_lam_re"], f)[l, d, g] for d, g in idx]),
                          np.stack([np.asarray(inp["ssm_lam_im"], f)[l, d, g] for d, g in idx]),
                          np.array([np.asarray(inp["ssm_log_dt"], f)[l, d, g] for d, g in idx], f),
                          np.stack([np.asarray(inp["ssm_b_re"], f)[l, d, g] for d, g in idx]),
                          np.stack([np.asarray(inp["ssm_b_im"], f)[l, d, g] for d, g in idx]),
                          np.stack([np.asarray(inp["ssm_c_re"], f)[l, d, g] for d, g in idx]),
                          np.stack([np.asarray(inp["ssm_c_im"], f)[l, d, g] for d, g in idx]))
            ins.append(dict(pk, u=u_core))
        rs = run_prog("ps", lambda: build_ssm(T + TC, 512), ins)
        yf_all = np.concatenate([r["y"][0:64] for r in rs], 0)
        yb_all = np.concatenate([r["y"][64:128] for r in rs], 0)
        yfT, yfcT = yf_all[:, TC:], yf_all[:, :TC]
        ybT, ybcT = yb_all[:, TC:][:, ::-1], yb_all[:, :TC][:, ::-1]
        X, Xc = run_fourier(run_prog, np.ascontiguousarray(L1["zrT"].T), np.ascontiguousarray(L1["ziT"].T),
                            np.ascontiguousarray(C1["zrT"].T), np.ascontiguousarray(C1["ziT"].T))
        xfT = X.T; xfcT = Xc.T
        comw = dict(glu_w=Wb[l]["ssm_glu_w"], fw=Wb[l]["fourier_w"], w_out=Wb[l]["w_out"], w_gate=Wb[l]["ffn_w_gate"],
                    w_up=Wb[l]["ffn_w_up"], w_down=Wb[l]["ffn_w_down"], gpm=_v2(inp["norm_mix_post"][l]),
                    gpf=_v2(inp["norm_ffn_pre"][l]), gof=_v2(inp["norm_ffn_post"][l]),
                    ssm_d=_v2(inp["ssm_d"][l]), glu_b=_v2(inp["ssm_glu_b"][l]))

        def p3_ins(xs, at, yf_, yb_, us_, xf_, n, v):
            out = []
            for i in range(8):
                sl = slice(i * n, (i + 1) * n)
                out.append(dict(comw, xT=np.ascontiguousarray(xs[sl].T), attnT=np.ascontiguousarray(at[:, sl]),
                                yfT=np.ascontiguousarray(yf_[:, sl]), ybT=np.ascontiguousarray(yb_[:, sl]),
                                usT=np.ascontiguousarray(us_[:, sl]), xfT=np.ascontiguousarray(xf_[:, sl]),
                                gm=_v2(g_m[v]), scf=_v2(sc_f[v]), shf=_v2(sh_f[v]), gf=_v2(g_f[v])))
            return out
        r3 = run_prog("p3", lambda: build_p3(4, 512), p3_ins(x, attnT, yfT, ybT, us, xfT, Tc, 0))
        x_new = np.ascontiguousarray(_cat(r3, "xoT", 1).T)
        if need_ctx:
            r3c = run_prog("p3c", lambda: build_p3(1, 32), p3_ins(xc, attncT, yfcT, ybcT, usc, xfcT, Cc, 1))
            xc = np.ascontiguousarray(_cat(r3c, "xoT", 1).T)
        x = x_new
    return x[None].astype(np.float32)


def build_attn2(NQ, NK, QB):
    P = Prog()
    NQB = NQ // QB
    NKT = NK // 128
    NPR = NKT // 2
    QC = QB // 128
    qT = P.din("qT", [128, NQ], BF16)
    kT = P.din("kT", [128, NK], BF16)
    v = P.din("v", [NK, 128], BF16)
    o = P.dout("o", [NQ, 128], BF16)
    q_sb = P.sb([128, NQ], BF16); k_sb = P.sb([128, NK], BF16); v_sb = P.sb([128, NKT, 130], BF16)
    KCH = 13 if NKT % 13 == 0 else NKT
    vv = v.rearrange("(kt p) d -> p kt d", p=128)
    QCH = 4 if NQB % 4 == 0 else NQB
    P.dma("sp", q_sb[:, 0:QCH * QB], qT[:, 0:QCH * QB], w=[("Q", 0)])
    for c in range(NKT // KCH):
        P.dma("sp", k_sb[:, c * KCH * 128:(c + 1) * KCH * 128], kT[:, c * KCH * 128:(c + 1) * KCH * 128], w=[("K", c)])
        P.op("pool", lambda g, c=c: g.memset(v_sb[:, c * KCH:(c + 1) * KCH, 128:130], 1.0), w=[("V1", c)])
        P.dma("sp", v_sb[:, c * KCH:(c + 1) * KCH, 0:128], vv[:, c * KCH:(c + 1) * KCH, :], w=[("V", c)])
    for c in range(1, NQB // QCH):
        P.dma("sp", q_sb[:, c * QCH * QB:(c + 1) * QCH * QB], qT[:, c * QCH * QB:(c + 1) * QCH * QB], w=[("Q", c)])
    S = [P.ps([128, 2, 512], F32) for _ in range(2)]
    O = [[P.ps([128, 512], F32) for _ in range(2)] for _ in range(2)]
    PT = [P.sb([128, 2, QB], BF16) for _ in range(2)]
    rs = P.sb([128, 4], F32)
    o_sb = [P.sb([128, QC, 128], BF16) for _ in range(2)]
    ov = o.rearrange("(qb qc p) d -> qb p qc d", qc=QC, p=128)
    scale = 128 ** -0.5
    items = [(qb, pr) for qb in range(NQB) for pr in range(NPR)]

    def emit_S(i):
        qb, pr = items[i]
        s = i % 2
        for j in range(2):
            kt = 2 * pr + j
            mm(P, S[s][:, j, :QB], k_sb[:, kt * 128:(kt + 1) * 128], q_sb[:, qb * QB:(qb + 1) * QB], True, True,
               r=[("K", kt // KCH), ("Q", qb // QCH)], w=[("S", s)])

    emit_S(0)
    for i, (qb, pr) in enumerate(items):
        if i + 1 < len(items):
            emit_S(i + 1)
        s = i % 2
        ob = qb % 2
        act(P, PT[s][:, :, :], S[s][:, :, :QB], AF.Exp, r=[("S", s)], w=[("PT", s)], scale=scale)
        for j in range(2):
            kt = 2 * pr + j
            for qc in range(QC):
                bank = O[ob][qc // 2]
                col = (qc % 2) * 129
                first = (kt == 0 and qc % 2 == 0)
                P.op("pe", lambda g, bank=bank, col=col, s=s, j=j, qc=qc, kt=kt, first=first: g.matmul(
                    bank[:, col:col + 129], PT[s][:, j, qc * 128:(qc + 1) * 128], v_sb[:, kt, 0:129],
                    start=first, stop=(kt == NKT - 1), skip_group_check=True),
                    r=[("V", kt // KCH), ("V1", kt // KCH), ("PT", s)], w=[("O", ob, qc // 2)])
        if pr == NPR - 1:
            for qc in range(QC):
                bank = O[ob][qc // 2]
                col = (qc % 2) * 129
                bk = ("O", ob, qc // 2)
                P.op("dve", lambda g, bank=bank, col=col, qc=qc: g.reciprocal(out=rs[:, qc:qc + 1], in_=bank[:, col + 128:col + 129]),
                     r=[bk], w=[("rs", qc)])
                ts(P, "dve", o_sb[ob][:, qc, :], bank[:, col:col + 128], rs[:, qc:qc + 1], None, ALU.mult, None,
                   r=[bk, ("rs", qc)], w=[("o_sb", ob)])
            P.dma("sp", ov[qb], o_sb[ob][:], r=[("o_sb", ob)])
    return P.build()
```

```python
from contextlib import ExitStack
import numpy as np
import concourse.bass as bass
import concourse.mybir as mybir
from concourse.bass_utils import run_bass_kernel_spmd

F32 = mybir.dt.float32
BF16 = mybir.dt.bfloat16
I32 = mybir.dt.int32
AF = mybir.ActivationFunctionType
ALU = mybir.AluOpType
AX = mybir.AxisListType


STRICT_WAR = True


class Prog:
    def __init__(self):
        self.nc = bass.Bass("TRN2", target_bir_lowering=False)
        self.es = ExitStack()
        self.ops = []
        self.last_w = {}
        self.readers = {}
        self.n_t = 0
        self.psum_keys = set()

    def din(self, name, shape, dt=F32):
        return self.nc.dram_tensor(name, list(shape), dt, kind="ExternalInput").ap()

    def dout(self, name, shape, dt=F32):
        return self.nc.dram_tensor(name, list(shape), dt, kind="ExternalOutput").ap()

    def sb(self, shape, dt=F32, name=None):
        self.n_t += 1
        return self.es.enter_context(self.nc.sbuf_tensor(name or f"sb{self.n_t}", list(shape), dt))

    def ps(self, shape, dt=F32, name=None):
        self.n_t += 1
        return self.es.enter_context(self.nc.psum_tensor(name or f"ps{self.n_t}", list(shape), dt))

    def op(self, eng, fn, r=(), w=(), dma=False, semkey=None):
        i = len(self.ops)
        raw = set()
        war = set()
        if eng == "pe" and not dma:
            self.psum_keys.update(w)
        for k in r:
            if k in self.last_w:
                raw.add(self.last_w[k])
            if k in self.psum_keys:
                for j in self.readers.get(k, ()):
                    if self.ops[j]["eng"] != eng:
                        raw.add(j)
        for k in w:
            if k in self.last_w:
                raw.add(self.last_w[k])
            war.update(self.readers.get(k, ()))
        self.ops.append(dict(eng=eng, fn=fn, raw=raw, war=war - raw, dma=dma, semkey=semkey))
        for k in r:
            self.readers.setdefault(k, []).append(i)
        for k in w:
            self.last_w[k] = i
            self.readers[k] = []
        return i

    def dma(self, q, out, in_, r=(), w=(), semkey=None):
        if semkey is None:
            semkey = ("ld", w[0]) if w else ("st", r[0])
        return self.op(q, lambda e: e.dma_start(out=out, in_=in_), r=r, w=w, dma=True, semkey=semkey)

    def build(self):
        nc = self.nc
        ops = self.ops
        n = len(ops)
        for i, o in enumerate(ops):
            deps = set()
            for j in o["raw"]:
                pj = ops[j]
                if (not pj["dma"]) and pj["eng"] == o["eng"] and o["eng"] == "pe" and not o["dma"]:
                    continue
                deps.add(j)
            for j in o["war"]:
                pj = ops[j]
                if (not pj["dma"]) and pj["eng"] == o["eng"] and not o["dma"] and (o["eng"] == "pe" or not STRICT_WAR):
                    continue
                if (not pj["dma"]) and pj["eng"] == o["eng"] and o["dma"]:
                    pass
                deps.add(j)
            o["deps"] = deps
        has_dep = [False] * n
        for o in ops:
            for j in o["deps"]:
                has_dep[j] = True
        engs = ["pe", "act", "dve", "pool", "sp"]
        sems = {e: self.es.enter_context(nc.semaphore(f"s_{e}")) for e in engs}
        cnt = {e: 0 for e in engs}
        dsem = {}
        for i, o in enumerate(ops):
            if o["dma"]:
                k = o["semkey"]
                if k not in dsem:
                    dsem[k] = [self.es.enter_context(nc.semaphore(f"d{len(dsem)}")), 0]
                dsem[k][1] += 16
                o["sem"] = dsem[k][0]
                o["ticket"] = dsem[k][1]
                o["inc"] = 16
            elif has_dep[i]:
                cnt[o["eng"]] += 1
                o["sem"] = sems[o["eng"]]
                o["ticket"] = cnt[o["eng"]]
                o["inc"] = 1
            else:
                o["sem"] = None
        self.n_sems = len(dsem) + 5
        per = {e: [i for i, o in enumerate(ops) if o["eng"] == e] for e in engs}

        def replay(e):
            def run(eng):
                seen = {}
                for i in per[e]:
                    o = ops[i]
                    need = {}
                    for j in o["deps"]:
                        pj = ops[j]
                        s = pj["sem"]
                        key = id(s)
                        if seen.get(key, 0) >= pj["ticket"]:
                            continue
                        if key not in need or need[key][1] < pj["ticket"]:
                            need[key] = (s, pj["ticket"])
                    for key, (s, t) in need.items():
                        eng.wait_ge(s, t)
                        seen[key] = t
                    ins = o["fn"](eng)
                    if o["sem"] is not None:
                        ins.then_inc(o["sem"], o["inc"])
                if e == "sp":
                    for k, (s, t) in dsem.items():
                        if seen.get(id(s), 0) < t:
                            eng.wait_ge(s, t)
            return run

        with nc.Block() as block:
            block.sync(replay("sp"))
            block.tensor(replay("pe"))
            block.scalar(replay("act"))
            block.vector(replay("dve"))
            block.gpsimd(replay("pool"))
        self.es.close()
        return nc


def build_cast(N, CH=4096):
    P = Prog()
    src = P.din("src", [128, N], F32)
    dst = P.dout("dst", [128, N], BF16)
    nb = N // CH
    NS = 3
    fin = [P.sb([128, CH], F32) for _ in range(NS)]
    fout = [P.sb([128, CH], BF16) for _ in range(NS)]
    engs = ["dve", "act", "pool"]
    for b in range(nb):
        s = b % NS
        P.dma("sp", fin[s][:], src[:, b * CH:(b + 1) * CH], w=[("fin", s)])
        e = engs[b % 3]
        if e == "act":
            P.op("act", lambda g, s=s: g.copy(out=fout[s][:], in_=fin[s][:]), r=[("fin", s)], w=[("fout", s)])
        else:
            P.op(e, lambda g, s=s: g.tensor_copy(out=fout[s][:], in_=fin[s][:]), r=[("fin", s)], w=[("fout", s)])
        P.dma("sp", dst[:, b * CH:(b + 1) * CH], fout[s][:], r=[("fout", s)])
    return P.build()


def mm(P, out, lhsT, rhs, start, stop, r, w):
    P.op("pe", lambda g: g.matmul(out, lhsT, rhs, start=start, stop=stop), r=r, w=w)


def act(P, out, in_, func, r, w, bias=None, scale=None):
    kw = {}
    if bias is not None:
        kw["bias"] = bias
    if scale is not None:
        kw["scale"] = scale
    P.op("act", lambda g: g.activation(out=out, in_=in_, func=func, **kw), r=r, w=w)


def tt(P, eng, out, a, b, op, r, w):
    P.op(eng, lambda g: g.tensor_tensor(out=out, in0=a, in1=b, op=op), r=r, w=w)


def stt(P, out, in0, scalar, in1, op0, op1, r, w):
    P.op("dve", lambda g: g.scalar_tensor_tensor(out=out, in0=in0, scalar=scalar, in1=in1, op0=op0, op1=op1), r=r, w=w)


def ts(P, eng, out, in0, s1, s2, op0, op1, r, w):
    if op1 is None:
        P.op(eng, lambda g: g.tensor_scalar(out=out, in0=in0, scalar1=s1, scalar2=None, op0=op0), r=r, w=w)
    else:
        P.op(eng, lambda g: g.tensor_scalar(out=out, in0=in0, scalar1=s1, scalar2=s2, op0=op0, op1=op1), r=r, w=w)


def rstd_from_ps(P, out_sb, in_ps, n, eps, r, w, tmpkey):
    act(P, out_sb, in_ps, AF.Ln, r=list(r) + ["epsc"], w=w, bias=eps_ap(P, eps), scale=1.0 / n)
    act(P, out_sb, out_sb, AF.Exp, r=w, w=w, scale=-0.5)


def eps_ap(P, eps):
    if not hasattr(P, "_eps"):
        P._eps = P.sb([128, 1], F32, name="epsc")
        P.op("pool", lambda g: g.memset(P._eps[:], float(eps)), w=["epsc"])
    return P._eps[:]


def build_adaln(NL=2, D=2048, CW=1536):
    P = Prog()
    KC = D // 128
    c2T = P.din("c2T", [128, KC, 2], F32)
    w = P.din("w", [NL, D, CW], F32)
    b2 = P.din("b2", [NL, 2, CW], F32)
    mod = P.dout("mod", [NL, 2, CW], F32)
    cs = P.sb([128, KC, 2], F32)
    ss = P.sb([128, KC, 2], F32)
    bt = P.sb([2, NL, CW], F32)
    P.dma("sp", cs[:], c2T[:], w=["cs"])
    for l in range(NL):
        P.dma("sp", bt[:, l, :], b2[l], w=[("bt", l)])
    act(P, ss[:], cs[:], AF.Silu, r=["cs"], w=["ss"])
    wt = [P.sb([128, KC, 512], F32) for _ in range(2)]
    pst = [P.ps([128, 512], F32) for _ in range(2)]
    ot = [P.sb([2, 512], F32) for _ in range(2)]
    it = 0
    for l in range(NL):
        wv = w[l].rearrange("(kc p) c -> p kc c", p=128)
        for cb in range(CW // 512):
            s = it % 2
            P.dma("sp", wt[s][:], wv[:, :, cb * 512:(cb + 1) * 512], w=[("wt", s)])
            for kc in range(KC):
                mm(P, pst[s][0:2, :], ss[:, kc, :], wt[s][:, kc, :], kc == 0, kc == KC - 1,
                   r=["ss", ("wt", s)], w=[("ps", s)])
            tt(P, "dve", ot[s][:], pst[s][0:2, :], bt[:, l, cb * 512:(cb + 1) * 512], ALU.add,
               r=[("ps", s), ("bt", l)], w=[("ot", s)])
            P.dma("sp", mod[l, :, cb * 512:(cb + 1) * 512], ot[s][:], r=[("ot", s)])
            it += 1
    return P.build()


D_MODEL = 2048
IN_W = 2560
EPS = 1e-6


def build_p1(NB, TB):
    P = Prog()
    Tc = NB * TB
    KC = 16
    xT = P.din("xT", [D_MODEL, Tc], F32)
    gam = P.din("gam", [128, KC], F32)
    scv = P.din("scv", [128, KC], F32)
    shv = P.din("shv", [128, KC], F32)
    w_in = P.din("w_in", [D_MODEL, IN_W], BF16)
    qn = P.din("qn", [128, 1], F32)
    kn = P.din("kn", [128, 1], F32)
    cosT = P.din("cosT", [128, Tc], F32)
    sinT = P.din("sinT", [128, Tc], F32)
    rotT = P.din("rotT", [128, 128], F32)
    cd = P.din("cd", [128, 128], F32)
    sdn = P.din("sdn", [128, 128], F32)
    qT = P.dout("qT", [1024, Tc], BF16)
    kT = P.dout("kT", [256, Tc], BF16)
    vT = P.dout("vT", [256, Tc], BF16)
    usT = P.dout("usT", [512, Tc], F32)
    zrT = P.dout("zrT", [512, Tc], F32)
    ziT = P.dout("ziT", [512, Tc], F32)

    W = P.sb([128, KC, IN_W], BF16)
    wv = w_in.rearrange("(kc p) c -> p kc c", p=128)
    for i in range(4):
        P.dma("sp", W[:, i * 4:(i + 1) * 4, :], wv[:, i * 4:(i + 1) * 4, :], w=[("W", i)])
    Wk = [("W", i) for i in range(4)]
    g_sb = P.sb([128, KC], F32); sc_sb = P.sb([128, KC], F32); sh_sb = P.sb([128, KC], F32)
    G = P.sb([128, KC], F32)
    qn_sb = P.sb([128, 1], F32); kn_sb = P.sb([128, 1], F32)
    rot_sb = P.sb([128, 128], F32); cd_sb = P.sb([128, 128], F32); sd_sb = P.sb([128, 128], F32)
    ones = P.sb([128, 128], BF16)
    P.dma("sp", g_sb[:], gam[:], w=["g"]); P.dma("sp", sc_sb[:], scv[:], w=["sc"]); P.dma("sp", sh_sb[:], shv[:], w=["sh"])
    P.dma("sp", qn_sb[:], qn[:], w=["qn"]); P.dma("sp", kn_sb[:], kn[:], w=["kn"])
    P.dma("sp", rot_sb[:], rotT[:], w=["rot"]); P.dma("sp", cd_sb[:], cd[:], w=["cd"]); P.dma("sp", sd_sb[:], sdn[:], w=["sd"])
    P.op("pool", lambda g: g.memset(ones[:], 1.0), w=["ones"])
    stt(P, G[:], sc_sb[:], 1.0, g_sb[:], ALU.add, ALU.mult, r=["g", "sc"], w=["G"])

    x_sb = P.sb([128, KC, TB], F32)
    sq = P.sb([128, KC, TB], BF16)
    h = P.sb([128, KC, TB], BF16)
    tmp = [P.sb([128, TB], F32) for _ in range(2)]
    rstd = P.sb([128, TB], F32)
    cs_sb = P.sb([128, TB], F32); sn_sb = P.sb([128, TB], F32)
    sqh = P.sb([128, TB], BF16)
    r2 = P.sb([128, TB], F32)
    qf = P.sb([128, TB], F32)
    t1 = P.sb([128, TB], F32); t2 = P.sb([128, TB], F32)
    ob = [P.sb([128, TB], BF16) for _ in range(2)]
    of = [P.sb([128, TB], F32) for _ in range(3)]
    acc = [P.ps([128, 512], F32) for _ in range(2)]
    ps_ss = P.ps([128, 512], F32); ps_s2 = P.ps([128, 512], F32); ps_rot = P.ps([128, 512], F32)
    ps_z = [P.ps([128, 512], F32) for _ in range(2)]
    xv = xT.rearrange("(kc p) t -> p kc t", p=128)
    nb_o = 0; nf_o = 0; na = 0; nz = 0
    for b in range(NB):
        tsl = slice(b * TB, (b + 1) * TB)
        P.dma("sp", x_sb[:], xv[:, :, tsl], w=["x"])
        P.dma("sp", cs_sb[:], cosT[:, tsl], w=["cs"]); P.dma("sp", sn_sb[:], sinT[:, tsl], w=["sn"])
        act(P, sq[:], x_sb[:], AF.Square, r=["x"], w=["sq"])
        for kc in range(KC):
            mm(P, ps_ss[:, :TB], ones[:], sq[:, kc, :], kc == 0, kc == KC - 1, r=["ones", "sq"], w=["ps_ss"])
        rstd_from_ps(P, rstd[:], ps_ss[:, :TB], D_MODEL, EPS, r=["ps_ss"], w=["rstd"], tmpkey=None)
        for kc in range(KC):
            s = kc % 2
            stt(P, tmp[s][:], x_sb[:, kc, :], G[:, kc:kc + 1], rstd[:], ALU.mult, ALU.mult,
                r=["x", "G", "rstd"], w=[("tmp", s)])
            act(P, h[:, kc, :], tmp[s][:], AF.Identity, r=[("tmp", s), "sh"], w=[("h", kc)], bias=sh_sb[:, kc:kc + 1])
        hk = [("h", kc) for kc in range(KC)]
        for cc in range(IN_W // 128):
            a = na % 2; na += 1
            for kc in range(KC):
                mm(P, acc[a][:, :TB], W[:, kc, cc * 128:(cc + 1) * 128], h[:, kc, :], kc == 0, kc == KC - 1,
                   r=[Wk[kc // 4], ("h", kc)], w=[("acc", a)])
            A = acc[a][:, :TB]
            if cc < 10:
                gn, gk = (qn_sb, "qn") if cc < 8 else (kn_sb, "kn")
                act(P, sqh[:], A, AF.Square, r=[("acc", a)], w=["sqh"])
                mm(P, ps_s2[:, :TB], ones[:], sqh[:], True, True, r=["ones", "sqh"], w=["ps_s2"])
                rstd_from_ps(P, r2[:], ps_s2[:, :TB], 128, EPS, r=["ps_s2"], w=["r2"], tmpkey=None)
                stt(P, qf[:], A, gn[:, 0:1], r2[:], ALU.mult, ALU.mult, r=[("acc", a), gk, "r2"], w=["qf"])
                mm(P, ps_rot[:, :TB], rot_sb[:], qf[:], True, True, r=["rot", "qf"], w=["ps_rot"])
                tt(P, "dve", t1[:], qf[:], cs_sb[:], ALU.mult, r=["qf", "cs"], w=["t1"])
                tt(P, "dve", t2[:], ps_rot[:, :TB], sn_sb[:], ALU.mult, r=["ps_rot", "sn"], w=["t2"])
                o = nb_o % 2; nb_o += 1
                tt(P, "pool", ob[o][:], t1[:], t2[:], ALU.add, r=["t1", "t2"], w=[("ob", o)])
                dst = qT[cc * 128:(cc + 1) * 128, tsl] if cc < 8 else kT[(cc - 8) * 128:(cc - 7) * 128, tsl]
                P.dma("sp", dst, ob[o][:], r=[("ob", o)])
            elif cc < 12:
                o = nb_o % 2; nb_o += 1
                P.op("act", lambda g, o=o, A=A: g.copy(out=ob[o][:], in_=A), r=[("acc", a)], w=[("ob", o)])
                P.dma("sp", vT[(cc - 10) * 128:(cc - 9) * 128, tsl], ob[o][:], r=[("ob", o)])
            elif cc < 16:
                o = nf_o % 3; nf_o += 1
                P.op("act", lambda g, o=o, A=A: g.copy(out=of[o][:], in_=A), r=[("acc", a)], w=[("of", o)])
                P.dma("sp", usT[(cc - 12) * 128:(cc - 11) * 128, tsl], of[o][:], r=[("of", o)])
            else:
                o = nf_o % 3; nf_o += 1
                P.op("act", lambda g, o=o, A=A: g.copy(out=of[o][:], in_=A), r=[("acc", a)], w=[("of", o)])
                for mat, mk, dstT in ((cd_sb, "cd", zrT), (sd_sb, "sd", ziT)):
                    z = nz % 2; nz += 1
                    mm(P, ps_z[z][:, :TB], mat[:], of[o][:], True, True, r=[mk, ("of", o)], w=[("ps_z", z)])
                    o2 = nf_o % 3; nf_o += 1
                    P.op("dve", lambda g, o2=o2, z=z: g.tensor_copy(out=of[o2][:], in_=ps_z[z][:, :TB]),
                         r=[("ps_z", z)], w=[("of", o2)])
                    P.dma("sp", dstT[(cc - 16) * 128:(cc - 15) * 128, tsl], of[o2][:], r=[("of", o2)])
    return P.build()


def rope_tables(rows_tok):
    f = (10000.0 ** (-np.arange(32, dtype=np.float32) / 32)).astype(np.float32)
    t = np.asarray(rows_tok)
    row = (t // 64).astype(np.float32); col = (t % 64).astype(np.float32)
    ar = (row[None, :] * f[:, None]).astype(np.float32)
    ac = (col[None, :] * f[:, None]).astype(np.float32)
    cosT = np.concatenate([np.cos(ar), np.cos(ar), np.cos(ac), np.cos(ac)], 0).astype(np.float32)
    sinT = np.concatenate([np.sin(ar), np.sin(ar), np.sin(ac), np.sin(ac)], 0).astype(np.float32)
    return cosT, sinT


def rope_rotT():
    R = np.zeros((128, 128), np.float32)
    for base in (0, 64):
        for i in range(32):
            R[base + i, base + i + 32] = -1.0
            R[base + i + 32, base + i] = 1.0
    return np.ascontiguousarray(R.T)


def chan_dft():
    j = np.arange(128)
    ang = 2 * np.pi * np.outer(j, j) / 128
    return (np.cos(ang) / np.sqrt(128)).astype(np.float32), (-np.sin(ang) / np.sqrt(128)).astype(np.float32)


ATT_VARIANT = ""


def build_attn(NQ, NK, QB):
    P = Prog()
    NQB = NQ // QB
    NKT = NK // 128
    qT = P.din("qT", [128, NQ], BF16)
    kT = P.din("kT", [128, NK], BF16)
    v = P.din("v", [NK, 128], BF16)
    oT = P.dout("oT", [128, NQ], BF16)
    q_sb = P.sb([128, NQ], BF16); k_sb = P.sb([128, NK], BF16); v_sb = P.sb([128, NKT, 128], BF16)
    KCH = 13 if NKT % 13 == 0 else NKT
    vv = v.rearrange("(kt p) d -> p kt d", p=128)
    QCH = 4 if NQB % 4 == 0 else NQB
    P.dma("sp", q_sb[:, 0:QCH * QB], qT[:, 0:QCH * QB], w=[("Q", 0)])
    for c in range(NKT // KCH):
        P.dma("sp", k_sb[:, c * KCH * 128:(c + 1) * KCH * 128], kT[:, c * KCH * 128:(c + 1) * KCH * 128], w=[("K", c)])
        P.dma("sp", v_sb[:, c * KCH:(c + 1) * KCH, :], vv[:, c * KCH:(c + 1) * KCH, :], w=[("V", c)])
    for c in range(1, NQB // QCH):
        P.dma("sp", q_sb[:, c * QCH * QB:(c + 1) * QCH * QB], qT[:, c * QCH * QB:(c + 1) * QCH * QB], w=[("Q", c)])
    ones_b = P.sb([128, 1], BF16); ones_f = P.sb([1, 128], F32)
    P.op("pool", lambda g: g.memset(ones_b[:], 1.0), w=["ones_b"])
    P.op("pool", lambda g: g.memset(ones_f[:], 1.0), w=["ones_f"])
    S = [P.ps([128, 512], F32) for _ in range(3)]
    O = [P.ps([128, 512], F32) for _ in range(2)]
    SM = [P.ps([128, 512], F32) for _ in range(2)]
    BC = P.ps([128, 512], F32)
    PT = [P.sb([128, QB], BF16) for _ in range(3)]
    rs = P.sb([1, QB], F32); bc_sb = P.sb([128, QB], F32)
    o_sb = [P.sb([128, QB], BF16) for _ in range(2)]
    scale = 128 ** -0.5
    items = [(qb, kt) for qb in range(NQB) for kt in range(NKT)]

    def emit_S(i):
        qb, kt = items[i]
        s = i % 3
        mm(P, S[s][:, :QB], k_sb[:, kt * 128:(kt + 1) * 128], q_sb[:, qb * QB:(qb + 1) * QB], True, True,
           r=[("K", kt // KCH), ("Q", qb // QCH)], w=[("S", s)])

    emit_S(0)
    for i, (qb, kt) in enumerate(items):
        if i + 1 < len(items):
            emit_S(i + 1)
        s = i % 3
        o = qb % 2
        act(P, PT[s][:], S[s][:, :QB], AF.Exp, r=[("S", s)], w=[("PT", s)], scale=scale)
        if ATT_VARIANT != "B" or kt in (0, NKT - 1):
            mm(P, O[o][:, :QB], v_sb[:, kt, :], PT[s][:], kt == 0, kt == NKT - 1, r=[("V", kt // KCH), ("PT", s)], w=[("O", o)])
        if ATT_VARIANT == "" or kt in (0, NKT - 1):
            mm(P, SM[o][0:1, :QB], ones_b[:], PT[s][:], kt == 0, kt == NKT - 1, r=["ones_b", ("PT", s)], w=[("SM", o)])
        if kt == NKT - 1:
            P.op("dve", lambda g, o=o: g.reciprocal(out=rs[:], in_=SM[o][0:1, :QB]), r=[("SM", o)], w=["rs"])
            mm(P, BC[:, :QB], ones_f[:], rs[:], True, True, r=["ones_f", "rs"], w=["BC"])
            P.op("act", lambda g: g.copy(out=bc_sb[:], in_=BC[:, :QB]), r=["BC"], w=["bc_sb"])
            tt(P, "dve", o_sb[o][:], O[o][:, :QB], bc_sb[:], ALU.mult, r=[("O", o), "bc_sb"], w=[("o_sb", o)])
            P.dma("sp", oT[:, qb * QB:(qb + 1) * QB], o_sb[o][:], r=[("o_sb", o)])
    return P.build()


def const_ap(P, val):
    if not hasattr(P, "_consts"):
        P._consts = {}
    if val not in P._consts:
        t = P.sb([128, 1], F32, name=f"const{len(P._consts)}")
        key = ("const", len(P._consts))
        P.op("pool", lambda g: g.memset(t[:], float(val)), w=[key])
        P._consts[val] = (t, key)
    return P._consts[val]


def emit_sin(P, out, u, shape, r, w, tag):
    if not hasattr(P, "_sin_tmp"):
        P._sin_tmp = {}
    sk = tuple(shape)
    if sk not in P._sin_tmp:
        P._sin_tmp[sk] = (P.sb(shape, I32), P.sb(shape, F32), P.sb(shape, F32), P.sb(shape, F32))
    ki, kf, fr, mk = P._sin_tmp[sk]
    k = lambda n: ("sintmp", sk, n)
    P.op("dve", lambda g: g.tensor_copy(out=ki[:], in_=u), r=r, w=[k("ki")])
    P.op("dve", lambda g: g.tensor_copy(out=kf[:], in_=ki[:]), r=[k("ki")], w=[k("kf")])
    tt(P, "dve", fr[:], u, kf[:], ALU.subtract, r=r + [k("kf")], w=[k("fr")])
    ts(P, "dve", mk[:], fr[:], 0.0, None, ALU.is_lt, None, r=[k("fr")], w=[k("mk")])
    tt(P, "dve", fr[:], fr[:], mk[:], ALU.add, r=[k("fr"), k("mk")], w=[k("fr")])
    cb, cbk = const_ap(P, -np.pi)
    act(P, out, fr[:], AF.Sin, r=[k("fr"), cbk], w=w, bias=cb[:], scale=2 * np.pi)


def build_ssm(L, TBL=512):
    P = Prog()
    NCT = 4
    u = P.din("u", [128, L], F32)
    lam_re = P.din("lam_re", [128, NCT], F32)
    lam_im = P.din("lam_im", [128, NCT], F32)
    log_dt = P.din("log_dt", [128, NCT], F32)
    b_re = P.din("b_re", [NCT, 128, 128], F32)
    b_im = P.din("b_im", [NCT, 128, 128], F32)
    c_re = P.din("c_re", [NCT, 128, 128], F32)
    c_im = P.din("c_im", [NCT, 128, 128], F32)
    tvec = P.din("tvec", [128, TBL], F32)
    ident = P.din("ident", [128, 128], F32)
    y = P.dout("y", [128, L], F32)

    U = P.sb([128, L], F32)
    nch = 8
    bounds = [(i * L) // nch for i in range(nch + 1)]
    def ukey(t0):
        for i in range(nch):
            if bounds[i] <= t0 < bounds[i + 1]:
                return ("U", i)
    lr = P.sb([128, NCT]); li = P.sb([128, NCT]); ld = P.sb([128, NCT])
    P.dma("sp", lr[:], lam_re[:], w=["lr"]); P.dma("sp", li[:], lam_im[:], w=["li"]); P.dma("sp", ld[:], log_dt[:], w=["ld"])
    tv = P.sb([128, TBL]); idn = P.sb([128, 128])
    P.dma("sp", tv[:], tvec[:], w=["tv"]); P.dma("sp", idn[:], ident[:], w=["idn"])
    Bre = P.sb([128, NCT, 128]); Bim = P.sb([128, NCT, 128]); Cre = P.sb([128, NCT, 128]); Cim = P.sb([128, NCT, 128])
    for c in range(NCT):
        P.dma("sp", Bre[:, c, :], b_re[c], w=[("Bre", c)]); P.dma("sp", Bim[:, c, :], b_im[c], w=[("Bim", c)])
        P.dma("sp", Cre[:, c, :], c_re[c], w=[("Cre", c)]); P.dma("sp", Cim[:, c, :], c_im[c], w=[("Cim", c)])
    for i in range(nch):
        P.dma("sp", U[:, bounds[i]:bounds[i + 1]], u[:, bounds[i]:bounds[i + 1]], w=[("U", i)])
    S4 = [128, NCT]
    dt = P.sb(S4); mag = P.sb(S4); th = P.sb(S4); tmp = P.sb(S4); us = P.sb(S4); uc = P.sb(S4)
    sn = P.sb(S4); cs = P.sb(S4); ar1 = P.sb(S4); ai = P.sb(S4); den = P.sb(S4); t2 = P.sb(S4)
    cr = P.sb(S4); ci = P.sb(S4); nci = P.sb(S4); th2 = P.sb(S4)
    act(P, dt[:], ld[:], AF.Exp, r=["ld"], w=["dt"])
    tt(P, "dve", tmp[:], lr[:], dt[:], ALU.mult, r=["lr", "dt"], w=["tmp"])
    act(P, mag[:], tmp[:], AF.Exp, r=["tmp"], w=["mag"])
    tt(P, "dve", th[:], li[:], dt[:], ALU.mult, r=["li", "dt"], w=["th"])
    ts(P, "dve", th2[:], th[:], float(1.0 / (2 * np.pi)), None, ALU.mult, None, r=["th"], w=["th2"])
    ts(P, "dve", us[:], th2[:], 0.5, None, ALU.add, None, r=["th2"], w=["us"])
    ts(P, "dve", uc[:], th2[:], 0.75, None, ALU.add, None, r=["th2"], w=["uc"])
    emit_sin(P, sn[:], us[:], S4, ["us"], ["sn"], "rs")
    emit_sin(P, cs[:], uc[:], S4, ["uc"], ["cs"], "rc")
    tt(P, "dve", ar1[:], mag[:], cs[:], ALU.mult, r=["mag", "cs"], w=["ar1"])
    ts(P, "dve", ar1[:], ar1[:], -1.0, None, ALU.add, None, r=["ar1"], w=["ar1"])
    tt(P, "dve", ai[:], mag[:], sn[:], ALU.mult, r=["mag", "sn"], w=["ai"])
    tt(P, "dve", den[:], lr[:], lr[:], ALU.mult, r=["lr"], w=["den"])
    tt(P, "dve", t2[:], li[:], li[:], ALU.mult, r=["li"], w=["t2"])
    tt(P, "dve", den[:], den[:], t2[:], ALU.add, r=["den", "t2"], w=["den"])
    P.op("dve", lambda g: g.reciprocal(out=den[:], in_=den[:]), r=["den"], w=["den"])
    tt(P, "dve", cr[:], ar1[:], lr[:], ALU.mult, r=["ar1", "lr"], w=["cr"])
    tt(P, "dve", t2[:], ai[:], li[:], ALU.mult, r=["ai", "li"], w=["t2"])
    tt(P, "dve", cr[:], cr[:], t2[:], ALU.add, r=["cr", "t2"], w=["cr"])
    tt(P, "dve", cr[:], cr[:], den[:], ALU.mult, r=["cr", "den"], w=["cr"])
    tt(P, "dve", ci[:], ai[:], lr[:], ALU.mult, r=["ai", "lr"], w=["ci"])
    tt(P, "dve", t2[:], ar1[:], li[:], ALU.mult, r=["ar1", "li"], w=["t2"])
    tt(P, "dve", ci[:], ci[:], t2[:], ALU.subtract, r=["ci", "t2"], w=["ci"])
    tt(P, "dve", ci[:], ci[:], den[:], ALU.mult, r=["ci", "den"], w=["ci"])
    ts(P, "dve", nci[:], ci[:], -1.0, None, ALU.mult, None, r=["ci"], w=["nci"])
    BrT = P.sb([128, NCT, 128]); BiT = P.sb([128, NCT, 128]); Cimn = P.sb([128, NCT, 128])
    bt = P.sb([128, 128]); bb = [P.sb([128, 128]) for _ in range(2)]
    pst = P.ps([128, 512], F32)
    nb = 0
    for c in range(NCT):
        for which in range(2):
            s = nb % 2; nb += 1
            if which == 0:
                ts(P, "dve", bt[:], Bre[:, c, :], cr[:, c:c + 1], None, ALU.mult, None, r=[("Bre", c), "cr"], w=["bt"])
                stt(P, bb[s][:], Bim[:, c, :], nci[:, c:c + 1], bt[:], ALU.mult, ALU.add, r=[("Bim", c), "nci", "bt"], w=[("bb", s)])
                dst = BrT
            else:
                ts(P, "dve", bt[:], Bim[:, c, :], cr[:, c:c + 1], None, ALU.mult, None, r=[("Bim", c), "cr"], w=["bt"])
                stt(P, bb[s][:], Bre[:, c, :], ci[:, c:c + 1], bt[:], ALU.mult, ALU.add, r=[("Bre", c), "ci", "bt"], w=[("bb", s)])
                dst = BiT
            P.op("pe", lambda g, s=s: g.transpose(pst[:, 0:128], bb[s][:], idn[:]), r=[("bb", s), "idn"], w=["pst"])
            P.op("act", lambda g, dst=dst, c=c: g.copy(out=dst[:, c, :], in_=pst[:, 0:128]), r=["pst"], w=[("BT", c, which)])
        P.op("act", lambda g, c=c: g.mul(out=Cimn[:, c, :], in_=Cim[:, c, :], mul=-1.0), r=[("Cim", c)], w=[("Cimn", c)])
    ct = P.sb([128, NCT, TBL]); st = P.sb([128, NCT, TBL]); magT = P.sb([128, NCT, TBL])
    ut = P.sb([128, TBL])
    for c in range(NCT):
        for which, dstT, off in ((0, st, 0.5), (1, ct, 0.75)):
            ts(P, "dve", ut[:], tv[:], th2[:, c:c + 1], off, ALU.mult, ALU.add, r=["tv", "th2"], w=["ut"])
            emit_sin(P, dstT[:, c, :], ut[:], [128, TBL], ["ut"], [("tab", c, which)], ("tb", c, which))
        ts(P, "dve", magT[:, c, :], tv[:], 0.0, mag[:, c:c + 1], ALU.mult, ALU.add, r=["tv", "mag"], w=[("magT", c)])
    blocks = []
    t0 = 0
    while t0 < L:
        n = min(TBL, L - t0); blocks.append((t0, n)); t0 += n
    psb = [[P.ps([128, 512], F32), P.ps([128, 512], F32)] for _ in range(2)]
    psy = [P.ps([128, 512], F32) for _ in range(2)]
    mi = [P.sb([128, TBL]) for _ in range(4)]
    mo = [[P.sb([128, TBL]) for _ in range(4)] for _ in range(2)]
    gin = [P.sb([128, TBL]) for _ in range(2)]
    g_ = [[P.sb([128, TBL]) for _ in range(2)] for _ in range(2)]
    hh = [[[P.sb([128, TBL]), P.sb([128, TBL])] for _ in range(2)] for _ in range(NCT)]
    ysb = [P.sb([128, TBL]) for _ in range(2)]
    items = [(j, c) for j in range(len(blocks)) for c in range(NCT)]

    def tabs(c, n):
        return ct[:, c, :n], st[:, c, :n], [("tab", c, 0), ("tab", c, 1)]

    def stA(k):
        j, c = items[k]; t0, n = blocks[j]; pb = k % 2
        uk = [ukey(t0), ukey(t0 + n - 1)]
        br_ps, bi_ps = psb[pb][0][:, :n], psb[pb][1][:, :n]
        mm(P, br_ps, BrT[:, c, :], U[:, t0:t0 + n], True, True, r=[("BT", c, 0)] + uk, w=[("psb", pb, 0)])
        mm(P, bi_ps, BiT[:, c, :], U[:, t0:t0 + n], True, True, r=[("BT", c, 1)] + uk, w=[("psb", pb, 1)])
        ctc, stc, tk = tabs(c, n)
        tt(P, "dve", mi[0][:, :n], br_ps, ctc, ALU.mult, r=[("psb", pb, 0)] + tk, w=["mi0"])
        tt(P, "dve", mi[1][:, :n], bi_ps, stc, ALU.mult, r=[("psb", pb, 1)] + tk, w=["mi1"])
        tt(P, "dve", mi[2][:, :n], bi_ps, ctc, ALU.mult, r=[("psb", pb, 1)] + tk, w=["mi2"])
        tt(P, "dve", mi[3][:, :n], br_ps, stc, ALU.mult, r=[("psb", pb, 0)] + tk, w=["mi3"])
        tt(P, "dve", gin[0][:, :n], mi[0][:, :n], mi[1][:, :n], ALU.add, r=["mi0", "mi1"], w=["gin0"])
        tt(P, "pool", gin[1][:, :n], mi[2][:, :n], mi[3][:, :n], ALU.subtract, r=["mi2", "mi3"], w=["gin1"])

    def stB(k):
        j, c = items[k]; t0, n = blocks[j]; hb = j % 2; gb = k % 2
        for ri in range(2):
            if j == 0:
                init = 0.0; rk = []
            else:
                pn = blocks[j - 1][1]
                init = hh[c][1 - hb][ri][:, pn - 1:pn]; rk = [("hh", c, 1 - hb, ri)]
            P.op("dve", lambda g, ri=ri, init=init, c=c, n=n, gb=gb: g.tensor_tensor_scan(
                out=g_[gb][ri][:, :n], data0=magT[:, c, :n], data1=gin[ri][:, :n], initial=init,
                op0=ALU.mult, op1=ALU.add), r=[("magT", c), f"gin{ri}"] + rk, w=[("g", gb, ri)])

    def stC(k):
        j, c = items[k]; t0, n = blocks[j]; gb = k % 2
        ctc, stc, tk = tabs(c, n)
        tt(P, "pool", mo[gb][0][:, :n], g_[gb][0][:, :n], ctc, ALU.mult, r=[("g", gb, 0)] + tk, w=[("mo", gb, 0)])
        tt(P, "pool", mo[gb][2][:, :n], g_[gb][0][:, :n], stc, ALU.mult, r=[("g", gb, 0)] + tk, w=[("mo", gb, 2)])
        tt(P, "pool", mo[gb][1][:, :n], g_[gb][1][:, :n], stc, ALU.mult, r=[("g", gb, 1)] + tk, w=[("mo", gb, 1)])
        tt(P, "pool", mo[gb][3][:, :n], g_[gb][1][:, :n], ctc, ALU.mult, r=[("g", gb, 1)] + tk, w=[("mo", gb, 3)])

    def stD(k):
        j, c = items[k]; t0, n = blocks[j]; hb = j % 2; gb = k % 2; yb = j % 2
        hr, hi = hh[c][hb][0], hh[c][hb][1]
        tt(P, "dve", hr[:, :n], mo[gb][0][:, :n], mo[gb][1][:, :n], ALU.subtract, r=[("mo", gb, 0), ("mo", gb, 1)], w=[("hh", c, hb, 0)])
        tt(P, "dve", hi[:, :n], mo[gb][2][:, :n], mo[gb][3][:, :n], ALU.add, r=[("mo", gb, 2), ("mo", gb, 3)], w=[("hh", c, hb, 1)])
        mm(P, psy[yb][:, :n], Cre[:, c, :], hr[:, :n], c == 0, False, r=[("Cre", c), ("hh", c, hb, 0)], w=[("psy", yb)])
        mm(P, psy[yb][:, :n], Cimn[:, c, :], hi[:, :n], False, c == NCT - 1, r=[("Cimn", c), ("hh", c, hb, 1)], w=[("psy", yb)])
        if c == NCT - 1:
            P.op("act", lambda g, yb=yb, n=n: g.copy(out=ysb[yb][:, :n], in_=psy[yb][:, :n]), r=[("psy", yb)], w=[("ysb", yb)])
            P.dma("sp", y[:, t0:t0 + n], ysb[yb][:, :n], r=[("ysb", yb)])

    stA(0); stB(0); stC(0)
    for k in range(len(items)):
        if k + 1 < len(items):
            stA(k + 1)
        stD(k)
        if k + 1 < len(items):
            stB(k + 1); stC(k + 1)
    return P.build()


def ssm_pack(lam_re, lam_im, log_dt, b_re, b_im, c_re, c_im, TBL=512):
    f = np.float32
    lr = np.zeros((128, 4), f); li = np.zeros((128, 4), f); ld = np.zeros((128, 4), f)
    Bre = np.zeros((4, 128, 128), f); Bim = np.zeros((4, 128, 128), f)
    Cre = np.zeros((4, 128, 128), f); Cim = np.zeros((4, 128, 128), f)
    for c in range(4):
        for gi in range(2):
            g = 2 * c + gi
            sl = slice(gi * 64, (gi + 1) * 64)
            lr[sl, c] = lam_re[g]; li[sl, c] = lam_im[g]; ld[sl, c] = log_dt[g]
            cs = slice(c * 32 + gi * 16, c * 32 + gi * 16 + 16)
            Bre[c, sl, cs] = b_re[g]; Bim[c, sl, cs] = b_im[g]
            Cre[c, sl, cs] = c_re[g].T; Cim[c, sl, cs] = c_im[g].T
    tvec = np.tile(np.arange(1, TBL + 1, dtype=f)[None, :], (128, 1))
    return dict(lam_re=lr, lam_im=li, log_dt=ld, b_re=Bre, b_im=Bim, c_re=Cre, c_im=Cim, tvec=tvec,
                ident=np.eye(128, dtype=f))


def build_pf1(NT2=16, C=512):
    P = Prog()
    zr = P.din("zr", [128, NT2, C], F32); zi = P.din("zi", [128, NT2, C], F32)
    mc = P.din("mc", [128, 128], F32); ms = P.din("ms", [128, 128], F32); mns = P.din("mns", [128, 128], F32)
    twc = P.din("twc", [128, NT2], F32); tws = P.din("tws", [128, NT2], F32); ntws = P.din("ntws", [128, NT2], F32)
    ar = P.dout("ar", [128, NT2, C], F32); ai = P.dout("ai", [128, NT2, C], F32)
    zr_sb = P.sb([128, NT2, C]); zi_sb = P.sb([128, NT2, C])
    mc_sb = P.sb([128, 128]); ms_sb = P.sb([128, 128]); mns_sb = P.sb([128, 128])
    twc_sb = P.sb([128, NT2]); tws_sb = P.sb([128, NT2]); ntws_sb = P.sb([128, NT2])
    for t, d, k in ((mc_sb, mc, "mc"), (ms_sb, ms, "ms"), (mns_sb, mns, "mns"), (twc_sb, twc, "twc"), (tws_sb, tws, "tws"), (ntws_sb, ntws, "ntws")):
        P.dma("sp", t[:], d[:], w=[k])
    H = NT2 // 2
    for hf in range(2):
        P.dma("sp", zr_sb[:, hf * H:(hf + 1) * H, :], zr[:, hf * H:(hf + 1) * H, :], w=[("zr", hf)])
        P.dma("sp", zi_sb[:, hf * H:(hf + 1) * H, :], zi[:, hf * H:(hf + 1) * H, :], w=[("zi", hf)])
    pr = [P.ps([128, 512]) for _ in range(2)]; pi = [P.ps([128, 512]) for _ in range(2)]
    t1 = [P.sb([128, C]) for _ in range(2)]; t2 = [P.sb([128, C]) for _ in range(2)]
    orr = [P.sb([128, C]) for _ in range(2)]; oi = [P.sb([128, C]) for _ in range(2)]
    for j in range(NT2):
        s = j % 2; hf = j // H
        mm(P, pr[s][:, :C], mc_sb[:], zr_sb[:, j, :], True, False, r=["mc", ("zr", hf)], w=[("pr", s)])
        mm(P, pr[s][:, :C], ms_sb[:], zi_sb[:, j, :], False, True, r=["ms", ("zi", hf)], w=[("pr", s)])
        mm(P, pi[s][:, :C], mc_sb[:], zi_sb[:, j, :], True, False, r=["mc", ("zi", hf)], w=[("pi", s)])
        mm(P, pi[s][:, :C], mns_sb[:], zr_sb[:, j, :], False, True, r=["mns", ("zr", hf)], w=[("pi", s)])
        ts(P, "dve", t1[s][:], pi[s][:, :C], tws_sb[:, j:j + 1], None, ALU.mult, None, r=[("pi", s), "tws"], w=[("t1", s)])
        stt(P, orr[s][:], pr[s][:, :C], twc_sb[:, j:j + 1], t1[s][:], ALU.mult, ALU.add, r=[("pr", s), "twc", ("t1", s)], w=[("or", s)])
        ts(P, "dve", t2[s][:], pr[s][:, :C], ntws_sb[:, j:j + 1], None, ALU.mult, None, r=[("pr", s), "ntws"], w=[("t2", s)])
        stt(P, oi[s][:], pi[s][:, :C], twc_sb[:, j:j + 1], t2[s][:], ALU.mult, ALU.add, r=[("pi", s), "twc", ("t2", s)], w=[("oi", s)])
        P.dma("sp", ar[:, j, :], orr[s][:], r=[("or", s)])
        P.dma("sp", ai[:, j, :], oi[s][:], r=[("oi", s)])
    return P.build()


def build_pf2(NK1=16, C=512, NCTX=256):
    P = Prog()
    ar = P.din("ar", [128, NK1, C], F32); ai = P.din("ai", [128, NK1, C], F32)
    mc = P.din("mc", [128, 128], F32); ms = P.din("ms", [128, 128], F32)
    zcr = P.din("zcr", [NCTX, C], F32); zci = P.din("zci", [NCTX, C], F32)
    cc = P.din("cc", [NCTX, NCTX], F32); cs = P.din("cs", [NCTX, NCTX], F32)
    xr = P.dout("xr", [128, NK1, C], F32)
    xc = P.dout("xc", [NCTX, C], F32)
    ar_sb = P.sb([128, NK1, C]); ai_sb = P.sb([128, NK1, C])
    mc_sb = P.sb([128, 128]); ms_sb = P.sb([128, 128])
    P.dma("sp", mc_sb[:], mc[:], w=["mc"]); P.dma("sp", ms_sb[:], ms[:], w=["ms"])
    H = NK1 // 2
    for hf in range(2):
        P.dma("sp", ar_sb[:, hf * H:(hf + 1) * H, :], ar[:, hf * H:(hf + 1) * H, :], w=[("ar", hf)])
        P.dma("sp", ai_sb[:, hf * H:(hf + 1) * H, :], ai[:, hf * H:(hf + 1) * H, :], w=[("ai", hf)])
    NTC = NCTX // 128
    zcr_sb = P.sb([128, NTC, C]); zci_sb = P.sb([128, NTC, C]); cc_sb = P.sb([128, NTC, NCTX]); cs_sb = P.sb([128, NTC, NCTX])
    P.dma("sp", zcr_sb[:], zcr.rearrange("(a p) c -> p a c", p=128), w=["zcr"])
    P.dma("sp", zci_sb[:], zci.rearrange("(a p) c -> p a c", p=128), w=["zci"])
    P.dma("sp", cc_sb[:], cc.rearrange("(a p) c -> p a c", p=128), w=["cc"])
    P.dma("sp", cs_sb[:], cs.rearrange("(a p) c -> p a c", p=128), w=["cs"])
    px = [P.ps([128, 512]) for _ in range(2)]
    ox = [P.sb([128, C]) for _ in range(2)]
    it = 0
    for j in range(NK1):
        s = it % 2; it += 1; hf = j // H
        mm(P, px[s][:, :C], mc_sb[:], ar_sb[:, j, :], True, False, r=["mc", ("ar", hf)], w=[("px", s)])
        mm(P, px[s][:, :C], ms_sb[:], ai_sb[:, j, :], False, True, r=["ms", ("ai", hf)], w=[("px", s)])
        P.op("act", lambda g, s=s: g.copy(out=ox[s][:], in_=px[s][:, :C]), r=[("px", s)], w=[("ox", s)])
        P.dma("sp", xr[:, j, :], ox[s][:], r=[("ox", s)])
    for kc in range(NTC):
        s = it % 2; it += 1
        n = 0
        for mat, mk, z, zk in ((cc_sb, "cc", zcr_sb, "zcr"), (cs_sb, "cs", zci_sb, "zci")):
            for tcn in range(NTC):
                mm(P, px[s][:, :C], mat[:, tcn, kc * 128:(kc + 1) * 128], z[:, tcn, :], n == 0, n == 2 * NTC - 1,
                   r=[mk, zk], w=[("px", s)])
                n += 1
        P.op("act", lambda g, s=s: g.copy(out=ox[s][:], in_=px[s][:, :C]), r=[("px", s)], w=[("ox", s)])
        P.dma("sp", xc[kc * 128:(kc + 1) * 128, :], ox[s][:], r=[("ox", s)])
    return P.build()


def dft_consts():
    j = np.arange(128)
    ang = 2 * np.pi * np.outer(j, j) / 128
    c128 = np.cos(ang); s128 = np.sin(ang)
    k1 = np.arange(128)[:, None]; t2 = np.arange(128)[None, :]
    ph = 2 * np.pi * (k1 * t2) / 16384.0
    j2 = np.arange(256)
    a2 = 2 * np.pi * np.outer(j2, j2) / 256
    f = np.float32
    return dict(c1=(c128 / 128).astype(f), s1=(s128 / 128).astype(f), ns1=(-s128 / 128).astype(f),
                c2=c128.astype(f), s2=s128.astype(f), twc=np.cos(ph).astype(f), tws=np.sin(ph).astype(f),
                cc=(np.cos(a2) / 16).astype(f), cs=(np.sin(a2) / 16).astype(f))


def run_fourier(run, zr, zi, zcr, zci):
    K = dft_consts()
    Zr = zr.reshape(128, 128, 512); Zi = zi.reshape(128, 128, 512)
    ins = []
    for i in range(8):
        sl = slice(16 * i, 16 * (i + 1))
        ins.append(dict(zr=np.ascontiguousarray(Zr[:, sl]), zi=np.ascontiguousarray(Zi[:, sl]), mc=K["c1"], ms=K["s1"], mns=K["ns1"],
                        twc=np.ascontiguousarray(K["twc"][:, sl]), tws=np.ascontiguousarray(K["tws"][:, sl]),
                        ntws=np.ascontiguousarray(-K["tws"][:, sl])))
    res = run("pf1", lambda: build_pf1(), ins)
    Ar = np.concatenate([r["ar"] for r in res], axis=1)
    Ai = np.concatenate([r["ai"] for r in res], axis=1)
    ArT = Ar.transpose(1, 0, 2); AiT = Ai.transpose(1, 0, 2)
    ins = []
    for i in range(8):
        sl = slice(16 * i, 16 * (i + 1))
        ins.append(dict(ar=np.ascontiguousarray(ArT[:, sl]), ai=np.ascontiguousarray(AiT[:, sl]), mc=K["c2"], ms=K["s2"],
                        zcr=zcr, zci=zci, cc=K["cc"], cs=K["cs"]))
    res = run("pf2", lambda: build_pf2(), ins)
    X = np.concatenate([r["xr"] for r in res], axis=1).reshape(16384, 512)
    return X, res[0]["xc"]


_NC_CACHE = {}


def run_prog(name, builder, ins):
    if name not in _NC_CACHE:
        _NC_CACHE[name] = builder()
    res = run_bass_kernel_spmd(_NC_CACHE[name], ins, core_ids=list(range(len(ins))))
    return [{k: np.asarray(v) for k, v in r.items()} for r in res.results]


FFN_H = 5632


def build_p3(NB, TB):
    P = Prog()
    Tc = NB * TB
    KC = 16
    HC = FFN_H // 128
    xT = P.din("xT", [D_MODEL, Tc], F32)
    attnT = P.din("attnT", [1024, Tc], BF16)
    yfT = P.din("yfT", [512, Tc], F32); ybT = P.din("ybT", [512, Tc], F32); usT = P.din("usT", [512, Tc], F32)
    xfT = P.din("xfT", [512, Tc], F32)
    glu_w = P.din("glu_w", [512, 512], BF16); fw = P.din("fw", [512, 512], BF16)
    w_out = P.din("w_out", [D_MODEL, D_MODEL], BF16)
    w_gate = P.din("w_gate", [D_MODEL, FFN_H], BF16); w_up = P.din("w_up", [D_MODEL, FFN_H], BF16)
    w_down = P.din("w_down", [FFN_H, D_MODEL], BF16)
    vnames = ["gpm", "gm", "gpf", "scf", "shf", "gof", "gf"]
    vin = {n: P.din(n, [128, KC], F32) for n in vnames}
    ssm_d = P.din("ssm_d", [128, 4], F32); glu_b = P.din("glu_b", [128, 4], F32)
    xoT = P.dout("xoT", [D_MODEL, Tc], F32)

    vs = {}
    for n in vnames:
        vs[n] = P.sb([128, KC]); P.dma("sp", vs[n][:], vin[n][:], w=[n])
    d_sb = P.sb([128, 4]); gb_sb = P.sb([128, 4])
    P.dma("sp", d_sb[:], ssm_d[:], w=["ssm_d"]); P.dma("sp", gb_sb[:], glu_b[:], w=["glu_b"])
    GM = P.sb([128, KC]); G2 = P.sb([128, KC]); GF = P.sb([128, KC])
    tt(P, "dve", GM[:], vs["gpm"][:], vs["gm"][:], ALU.mult, r=["gpm", "gm"], w=["GM"])
    stt(P, G2[:], vs["scf"][:], 1.0, vs["gpf"][:], ALU.add, ALU.mult, r=["scf", "gpf"], w=["G2"])
    tt(P, "dve", GF[:], vs["gof"][:], vs["gf"][:], ALU.mult, r=["gof", "gf"], w=["GF"])
    ones = P.sb([128, 128], BF16)
    P.op("pool", lambda g: g.memset(ones[:], 1.0), w=["ones"])

    x_sb = P.sb([128, KC, TB], F32)
    cat = P.sb([128, KC, TB], BF16)
    yb_sb = P.sb([128, KC, TB], BF16)
    a_sb = P.sb([128, HC, TB], BF16)
    WB = [P.sb([128, KC, 512], BF16) for _ in range(3)]
    s_us = [P.sb([128, TB]) for _ in range(1)]; s_yf = [P.sb([128, TB]) for _ in range(1)]; s_yb = [P.sb([128, TB]) for _ in range(1)]
    s_xf = [P.sb([128, TB]) for _ in range(1)]
    e1 = P.sb([128, TB]); e2 = P.sb([128, TB]); e3 = P.sb([128, TB])
    gfp = P.sb([128, 4, TB], F32); gbf = P.sb([128, 4, TB], BF16); xfb = P.sb([128, 4, TB], BF16)
    sqc = [P.sb([128, TB], BF16) for _ in range(2)]
    rstd = P.sb([128, TB]); tmpf = [P.sb([128, TB]) for _ in range(2)]
    sg = [P.sb([128, TB]) for _ in range(2)]
    xo = [P.sb([128, TB]) for _ in range(2)]
    PS = [P.ps([128, 512]) for _ in range(8)]
    xv = xT.rearrange("(kc p) t -> p kc t", p=128)
    av = attnT.rearrange("(kc p) t -> p kc t", p=128)
    wcnt = [0]

    def wload(src_ap, shape_kc):
        s = wcnt[0] % 3; wcnt[0] += 1
        P.dma("sp", WB[s][:, :shape_kc, :], src_ap, w=[("WB", s)])
        return s

    def rms_bc(nsq_key):
        rstd_from_ps(P, rstd[:], PS[4][:, :TB], D_MODEL, EPS, r=["ps4"], w=["rstd"], tmpkey=None)

    nsq = [0]

    def sq_acc(src_ps, src_key, idx, n):
        s = nsq[0] % 2; nsq[0] += 1
        act(P, sqc[s][:], src_ps, AF.Square, r=[src_key], w=[("sqc", s)])
        mm(P, PS[4][:, :TB], ones[:], sqc[s][:], idx == 0, idx == n - 1, r=["ones", ("sqc", s)], w=["ps4"])

    for b in range(NB):
        tsl = slice(b * TB, (b + 1) * TB)
        P.dma("sp", x_sb[:], xv[:, :, tsl], w=["x"])
        P.dma("sp", cat[:, 0:8, :], av[:, :, tsl], w=[("cat", k) for k in range(8)])
        sgw = wload(glu_w.rearrange("(kc p) c -> p kc c", p=128), 4)
        sfw = wload(fw.rearrange("(kc p) c -> p kc c", p=128), 4)
        for ch in range(4):
            s = 0
            fsl = slice(ch * 128, (ch + 1) * 128)
            P.dma("sp", s_us[s][:], usT[fsl, tsl], w=[("us", s)])
            P.dma("sp", s_yf[s][:], yfT[fsl, tsl], w=[("yf", s)])
            P.dma("sp", s_yb[s][:], ybT[fsl, tsl], w=[("yb", s)])
            P.dma("sp", s_xf[s][:], xfT[fsl, tsl], w=[("xf", s)])
            stt(P, e1[:], s_us[s][:], d_sb[:, ch:ch + 1], s_yf[s][:], ALU.mult, ALU.add, r=[("us", s), "ssm_d", ("yf", s)], w=["e1"])
            tt(P, "dve", e1[:], e1[:], s_yb[s][:], ALU.add, r=["e1", ("yb", s)], w=["e1"])
            tt(P, "pool", e2[:], e1[:], e1[:], ALU.mult, r=["e1"], w=["e2"])
            ts(P, "dve", e2[:], e2[:], 0.044715, 1.0, ALU.mult, ALU.add, r=["e2"], w=["e2"])
            tt(P, "pool", e3[:], e2[:], e1[:], ALU.mult, r=["e2", "e1"], w=["e3"])
            act(P, e3[:], e3[:], AF.Sigmoid, r=["e3"], w=["e3"], scale=1.5957691216057308)
            tt(P, "dve", gfp[:, ch, :], e3[:], e1[:], ALU.mult, r=["e3", "e1"], w=[("gfp", ch)])
            P.op("act", lambda g, ch=ch: g.copy(out=gbf[:, ch, :], in_=gfp[:, ch, :]), r=[("gfp", ch)], w=[("gbf", ch)])
            P.op("act", lambda g, ch=ch, s=s: g.copy(out=xfb[:, ch, :], in_=s_xf[s][:]), r=[("xf", s)], w=[("xfb", ch)])
        for co in range(4):
            pz = PS[5 + co % 2]; pk = ("ps", 5 + co % 2)
            for ci in range(4):
                mm(P, pz[:, :TB], WB[sgw][:, ci, co * 128:(co + 1) * 128], gbf[:, ci, :], ci == 0, ci == 3,
                   r=[("WB", sgw), ("gbf", ci)], w=[pk])
            act(P, e2[:], pz[:, :TB], AF.Sigmoid, r=[pk, "glu_b"], w=["e2"], bias=gb_sb[:, co:co + 1])
            tt(P, "dve", cat[:, 8 + co, :], e2[:], gfp[:, co, :], ALU.mult, r=["e2", ("gfp", co)], w=[("cat", 8 + co)])
        for co in range(4):
            pz = PS[5 + co % 2]; pk = ("ps", 5 + co % 2)
            for ci in range(4):
                mm(P, pz[:, :TB], WB[sfw][:, ci, co * 128:(co + 1) * 128], xfb[:, ci, :], ci == 0, ci == 3,
                   r=[("WB", sfw), ("xfb", ci)], w=[pk])
            P.op("act", lambda g, co=co, pz=pz: g.copy(out=cat[:, 12 + co, :], in_=pz[:, :TB]), r=[pk], w=[("cat", 12 + co)])
        wov = w_out.rearrange("(kc p) c -> p kc c", p=128)
        for og in range(4):
            sw = wload(wov[:, :, og * 512:(og + 1) * 512], KC)
            for o4 in range(4):
                oc = og * 4 + o4
                pa = PS[oc % 2]; pk = ("ps", oc % 2)
                for kc in range(KC):
                    mm(P, pa[:, :TB], WB[sw][:, kc, o4 * 128:(o4 + 1) * 128], cat[:, kc, :], kc == 0, kc == KC - 1,
                       r=[("WB", sw), ("cat", kc)], w=[pk])
                P.op("dve", lambda g, oc=oc, pa=pa: g.tensor_copy(out=yb_sb[:, oc, :], in_=pa[:, :TB]), r=[pk], w=[("yb_sb", oc)])
                sq_acc(pa[:, :TB], pk, oc, KC)
        rms_bc(None)
        for oc in range(KC):
            s = oc % 2
            stt(P, tmpf[s][:], yb_sb[:, oc, :], GM[:, oc:oc + 1], rstd[:], ALU.mult, ALU.mult,
                r=[("yb_sb", oc), "GM", "rstd"], w=[("tmpf", s)])
            tt(P, "pool", x_sb[:, oc, :], x_sb[:, oc, :], tmpf[s][:], ALU.add, r=["x", ("tmpf", s)], w=[("x1", oc)])
        for kc in range(KC):
            sq_acc(x_sb[:, kc, :], ("x1", kc), kc, KC)
        rms_bc(None)
        for kc in range(KC):
            s = kc % 2
            stt(P, tmpf[s][:], x_sb[:, kc, :], G2[:, kc:kc + 1], rstd[:], ALU.mult, ALU.mult,
                r=[("x1", kc), "G2", "rstd"], w=[("tmpf", s)])
            act(P, cat[:, kc, :], tmpf[s][:], AF.Identity, r=[("tmpf", s), "shf"], w=[("cat", kc)], bias=vs["shf"][:, kc:kc + 1])
        wgv = w_gate.rearrange("(kc p) c -> p kc c", p=128); wuv = w_up.rearrange("(kc p) c -> p kc c", p=128)
        for hg in range(HC // 4):
            swg = wload(wgv[:, :, hg * 512:(hg + 1) * 512], KC)
            swu = wload(wuv[:, :, hg * 512:(hg + 1) * 512], KC)
            for h4 in range(4):
                hc = hg * 4 + h4
                pg = PS[(hc % 2) * 2]; pgk = ("ps", (hc % 2) * 2)
                pu = PS[(hc % 2) * 2 + 1]; puk = ("ps", (hc % 2) * 2 + 1)
                for kc in range(KC):
                    mm(P, pg[:, :TB], WB[swg][:, kc, h4 * 128:(h4 + 1) * 128], cat[:, kc, :], kc == 0, kc == KC - 1,
                       r=[("WB", swg), ("cat", kc)], w=[pgk])
                for kc in range(KC):
                    mm(P, pu[:, :TB], WB[swu][:, kc, h4 * 128:(h4 + 1) * 128], cat[:, kc, :], kc == 0, kc == KC - 1,
                       r=[("WB", swu), ("cat", kc)], w=[puk])
                s = hc % 2
                act(P, sg[s][:], pg[:, :TB], AF.Silu, r=[pgk], w=[("sg", s)])
                tt(P, "dve", a_sb[:, hc, :], pu[:, :TB], sg[s][:], ALU.mult, r=[puk, ("sg", s)], w=[("a", hc)])
        wdv = w_down.rearrange("(hc p) c -> p hc c", p=128)
        for og in range(4):
            for q in range(4):
                sw = wload(wdv[:, q * 11:(q + 1) * 11, og * 512:(og + 1) * 512], 11)
                for h in range(11):
                    hc = q * 11 + h
                    for o4 in range(4):
                        mm(P, PS[o4][:, :TB], WB[sw][:, h, o4 * 128:(o4 + 1) * 128], a_sb[:, hc, :], hc == 0, hc == HC - 1,
                           r=[("WB", sw), ("a", hc)], w=[("ps", o4)])
            for o4 in range(4):
                oc = og * 4 + o4
                P.op("dve", lambda g, oc=oc, o4=o4: g.tensor_copy(out=yb_sb[:, oc, :], in_=PS[o4][:, :TB]), r=[("ps", o4)], w=[("yb_sb", oc)])
                sq_acc(PS[o4][:, :TB], ("ps", o4), oc, KC)
        rms_bc(None)
        for oc in range(KC):
            s = oc % 2
            stt(P, tmpf[s][:], yb_sb[:, oc, :], GF[:, oc:oc + 1], rstd[:], ALU.mult, ALU.mult,
                r=[("yb_sb", oc), "GF", "rstd"], w=[("tmpf", s)])
            tt(P, "pool", xo[s][:], x_sb[:, oc, :], tmpf[s][:], ALU.add, r=[("x1", oc), ("tmpf", s)], w=[("xo", s)])
            P.dma("sp", xoT[oc * 128:(oc + 1) * 128, tsl], xo[s][:], r=[("xo", s)])
    return P.build()


def _v2(v):
    return np.ascontiguousarray(np.asarray(v, np.float32).reshape(-1, 128).T)


def _cat(res, name, axis):
    return np.concatenate([r[name] for r in res], axis=axis)


BIG_W = ["w_in", "ssm_glu_w", "fourier_w", "w_out", "ffn_w_gate", "ffn_w_up", "ffn_w_down"]


def cast_weights(inp):
    parts = []
    shapes = []
    for l in range(2):
        for n in BIG_W:
            a = np.asarray(inp[n][l], np.float32)
            parts.append(a.ravel()); shapes.append((l, n, a.shape))
    flat = np.concatenate(parts)
    tot = flat.size
    per = tot // (8 * 128)
    assert per * 8 * 128 == tot and per % 5120 == 0, (tot, per)
    flat = flat.reshape(8, 128, per)
    res = run_prog("cast", lambda: build_cast(per, 5120), [{"src": flat[i]} for i in range(8)])
    out = np.concatenate([r["dst"].reshape(-1) for r in res])
    W = [dict(), dict()]
    off = 0
    for (l, n, shp) in shapes:
        sz = int(np.prod(shp))
        W[l][n] = out[off:off + sz].reshape(shp); off += sz
    return W


def kernel(**inp):
    f = np.float32
    x = np.asarray(inp["x"], f)[0]
    xc = np.asarray(inp["ctx"], f)[0]
    T = x.shape[0]; TC = xc.shape[0]
    Wb = cast_weights(inp)
    c2 = np.stack([np.asarray(inp["c"], f)[0], np.asarray(inp["c_ctx"], f)], 0)
    c2T = np.ascontiguousarray(c2.reshape(2, 16, 128).transpose(2, 1, 0))
    ada_w = np.asarray(inp["ada_w"], f); ada_b = np.asarray(inp["ada_b"], f)
    CW = 1536
    ins = [dict(c2T=c2T, w=np.ascontiguousarray(ada_w[:, :, i * CW:(i + 1) * CW]),
                b2=np.ascontiguousarray(np.repeat(ada_b[:, None, i * CW:(i + 1) * CW], 2, axis=1))) for i in range(8)]
    mod = _cat(run_prog("adaln", lambda: build_adaln(), ins), "mod", 2)
    cosT, sinT = rope_tables(np.arange(T))
    cosC = np.ones((128, TC), f); sinC = np.zeros((128, TC), f)
    rotT = rope_rotT(); cd, sdn = chan_dft()
    Tc = T // 8; Cc = TC // 8
    for l in range(2):
        need_ctx = l == 0
        m = mod[l]
        sh_m, sc_m, g_m, sh_f, sc_f, g_f = [m[:, k * 2048:(k + 1) * 2048] for k in range(6)]
        qn = np.asarray(inp["q_norm"][l], f)[:, None].copy(); kn = np.asarray(inp["k_norm"][l], f)[:, None].copy()
        com = dict(gam=_v2(inp["norm_mix_pre"][l]), w_in=Wb[l]["w_in"], qn=qn, kn=kn, rotT=rotT, cd=cd, sdn=sdn)
        ins = [dict(com, xT=np.ascontiguousarray(x[i * Tc:(i + 1) * Tc].T), scv=_v2(sc_m[0]), shv=_v2(sh_m[0]),
                    cosT=np.ascontiguousarray(cosT[:, i * Tc:(i + 1) * Tc]), sinT=np.ascontiguousarray(sinT[:, i * Tc:(i + 1) * Tc]))
               for i in range(8)]
        r1 = run_prog("p1", lambda: build_p1(4, 512), ins)
        ins = [dict(com, xT=np.ascontiguousarray(xc[i * Cc:(i + 1) * Cc].T), scv=_v2(sc_m[1]), shv=_v2(sh_m[1]),
                    cosT=np.ascontiguousarray(cosC[:, i * Cc:(i + 1) * Cc]), sinT=np.ascontiguousarray(sinC[:, i * Cc:(i + 1) * Cc]))
               for i in range(8)]
        r1c = run_prog("p1c", lambda: build_p1(1, 32), ins)
        L1 = {n: _cat(r1, n, 1) for n in ("qT", "kT", "vT", "usT", "zrT", "ziT")}
        C1 = {n: _cat(r1c, n, 1) for n in ("qT", "kT", "vT", "usT", "zrT", "ziT")}
        ins = []
        for i in range(8):
            j = i // 4
            ins.append(dict(qT=np.ascontiguousarray(L1["qT"][i * 128:(i + 1) * 128]),
                            kT=np.ascontiguousarray(np.concatenate([L1["kT"][j * 128:(j + 1) * 128], C1["kT"][j * 128:(j + 1) * 128]], 1)),
                            v=np.ascontiguousarray(np.concatenate([L1["vT"][j * 128:(j + 1) * 128], C1["vT"][j * 128:(j + 1) * 128]], 1).T)))
        attnT = np.ascontiguousarray(np.concatenate([r["o"] for r in run_prog("pa", lambda: build_attn2(T, T + TC, 512), ins)], 1).T)
        if need_ctx:
            ins = []
            for i in range(8):
                j = i // 4
                ins.append(dict(qT=np.ascontiguousarray(C1["qT"][i * 128:(i + 1) * 128]),
                                kT=np.ascontiguousarray(C1["kT"][j * 128:(j + 1) * 128]),
                                v=np.ascontiguousarray(C1["vT"][j * 128:(j + 1) * 128].T)))
            attncT = np.ascontiguousarray(np.concatenate([r["o"] for r in run_prog("pac", lambda: build_attn2(TC, TC, TC), ins)], 1).T)
        us = L1["usT"]; usc = C1["usT"]
        fwd = np.concatenate([usc, us], 1)
        bwd = np.concatenate([usc[:, ::-1], us[:, ::-1]], 1)
        ins = []
        for i in range(8):
            rows = slice(64 * i, 64 * (i + 1))
            u_core = np.ascontiguousarray(np.concatenate([fwd[rows], bwd[rows]], 0))
            idx = [(c // 2, 4 * i + 2 * (c % 2) + gi) for c in range(4) for gi in range(2)]
            pk = ssm_pack(np.stack([np.asarray(inp["ssm_lam_re"], f)[l, d, g] for d, g in idx]),
                          np.stack([np.asarray(inp["ssm_lam_im"], f)[l, d, g] for d, g in idx]),
                          np.array([np.asarray(inp["ssm_log_dt"], f)[l, d, g] for d, g in idx], f),
                          np.stack([np.asarray(inp["ssm_b_re"], f)[l, d, g] for d, g in idx]),
                          np.stack([np.asarray(inp["ssm_b_im"], f)[l, d, g] for d, g in idx]),
                          np.stack([np.asarray(inp["ssm_c_re"], f)[l, d, g] for d, g in idx]),
                          np.stack([np.asarray(inp["ssm_c_im"], f)[l, d, g] for d, g in idx]))
            ins.append(dict(pk, u=u_core))
        rs = run_prog("ps", lambda: build_ssm(T + TC, 512), ins)
        yf_all = np.concatenate([r["y"][0:64] for r in rs], 0)
        yb_all = np.concatenate([r["y"][64:128] for r in rs], 0)
        yfT, yfcT = yf_all[:, TC:], yf_all[:, :TC]
        ybT, ybcT = yb_all[:, TC:][:, ::-1], yb_all[:, :TC][:, ::-1]
        X, Xc = run_fourier(run_prog, np.ascontiguousarray(L1["zrT"].T), np.ascontiguousarray(L1["ziT"].T),
                            np.ascontiguousarray(C1["zrT"].T), np.ascontiguousarray(C1["ziT"].T))
        xfT = X.T; xfcT = Xc.T
        comw = dict(glu_w=Wb[l]["ssm_glu_w"], fw=Wb[l]["fourier_w"], w_out=Wb[l]["w_out"], w_gate=Wb[l]["ffn_w_gate"],
                    w_up=Wb[l]["ffn_w_up"], w_down=Wb[l]["ffn_w_down"], gpm=_v2(inp["norm_mix_post"][l]),
                    gpf=_v2(inp["norm_ffn_pre"][l]), gof=_v2(inp["norm_ffn_post"][l]),
                    ssm_d=_v2(inp["ssm_d"][l]), glu_b=_v2(inp["ssm_glu_b"][l]))

        def p3_ins(xs, at, yf_, yb_, us_, xf_, n, v):
            out = []
            for i in range(8):
                sl = slice(i * n, (i + 1) * n)
                out.append(dict(comw, xT=np.ascontiguousarray(xs[sl].T), attnT=np.ascontiguousarray(at[:, sl]),
                                yfT=np.ascontiguousarray(yf_[:, sl]), ybT=np.ascontiguousarray(yb_[:, sl]),
                                usT=np.ascontiguousarray(us_[:, sl]), xfT=np.ascontiguousarray(xf_[:, sl]),
                                gm=_v2(g_m[v]), scf=_v2(sc_f[v]), shf=_v2(sh_f[v]), gf=_v2(g_f[v])))
            return out
        r3 = run_prog("p3", lambda: build_p3(4, 512), p3_ins(x, attnT, yfT, ybT, us, xfT, Tc, 0))
        x_new = np.ascontiguousarray(_cat(r3, "xoT", 1).T)
        if need_ctx:
            r3c = run_prog("p3c", lambda: build_p3(1, 32), p3_ins(xc, attncT, yfcT, ybcT, usc, xfcT, Cc, 1))
            xc = np.ascontiguousarray(_cat(r3c, "xoT", 1).T)
        x = x_new
    return x[None].astype(np.float32)


def build_attn2(NQ, NK, QB):
    P = Prog()
    NQB = NQ // QB
    NKT = NK // 128
    NPR = NKT // 2
    QC = QB // 128
    qT = P.din("qT", [128, NQ], BF16)
    kT = P.din("kT", [128, NK], BF16)
    v = P.din("v", [NK, 128], BF16)
    o = P.dout("o", [NQ, 128], BF16)
    q_sb = P.sb([128, NQ], BF16); k_sb = P.sb([128, NK], BF16); v_sb = P.sb([128, NKT, 130], BF16)
    KCH = 13 if NKT % 13 == 0 else NKT
    vv = v.rearrange("(kt p) d -> p kt d", p=128)
    QCH = 4 if NQB % 4 == 0 else NQB
    P.dma("sp", q_sb[:, 0:QCH * QB], qT[:, 0:QCH * QB], w=[("Q", 0)])
    for c in range(NKT // KCH):
        P.dma("sp", k_sb[:, c * KCH * 128:(c + 1) * KCH * 128], kT[:, c * KCH * 128:(c + 1) * KCH * 128], w=[("K", c)])
        P.op("pool", lambda g, c=c: g.memset(v_sb[:, c * KCH:(c + 1) * KCH, 128:130], 1.0), w=[("V1", c)])
        P.dma("sp", v_sb[:, c * KCH:(c + 1) * KCH, 0:128], vv[:, c * KCH:(c + 1) * KCH, :], w=[("V", c)])
    for c in range(1, NQB // QCH):
        P.dma("sp", q_sb[:, c * QCH * QB:(c + 1) * QCH * QB], qT[:, c * QCH * QB:(c + 1) * QCH * QB], w=[("Q", c)])
    S = [P.ps([128, 2, 512], F32) for _ in range(2)]
    O = [[P.ps([128, 512], F32) for _ in range(2)] for _ in range(2)]
    PT = [P.sb([128, 2, QB], BF16) for _ in range(2)]
    rs = P.sb([128, 4], F32)
    o_sb = [P.sb([128, QC, 128], BF16) for _ in range(2)]
    ov = o.rearrange("(qb qc p) d -> qb p qc d", qc=QC, p=128)
    scale = 128 ** -0.5
    items = [(qb, pr) for qb in range(NQB) for pr in range(NPR)]

    def emit_S(i):
        qb, pr = items[i]
        s = i % 2
        for j in range(2):
            kt = 2 * pr + j
            mm(P, S[s][:, j, :QB], k_sb[:, kt * 128:(kt + 1) * 128], q_sb[:, qb * QB:(qb + 1) * QB], True, True,
               r=[("K", kt // KCH), ("Q", qb // QCH)], w=[("S", s)])

    emit_S(0)
    for i, (qb, pr) in enumerate(items):
        if i + 1 < len(items):
            emit_S(i + 1)
        s = i % 2
        ob = qb % 2
        act(P, PT[s][:, :, :], S[s][:, :, :QB], AF.Exp, r=[("S", s)], w=[("PT", s)], scale=scale)
        for j in range(2):
            kt = 2 * pr + j
            for qc in range(QC):
                bank = O[ob][qc // 2]
                col = (qc % 2) * 129
                first = (kt == 0 and qc % 2 == 0)
                P.op("pe", lambda g, bank=bank, col=col, s=s, j=j, qc=qc, kt=kt, first=first: g.matmul(
                    bank[:, col:col + 129], PT[s][:, j, qc * 128:(qc + 1) * 128], v_sb[:, kt, 0:129],
                    start=first, stop=(kt == NKT - 1), skip_group_check=True),
                    r=[("V", kt // KCH), ("V1", kt // KCH), ("PT", s)], w=[("O", ob, qc // 2)])
        if pr == NPR - 1:
            for qc in range(QC):
                bank = O[ob][qc // 2]
                col = (qc % 2) * 129
                bk = ("O", ob, qc // 2)
                P.op("dve", lambda g, bank=bank, col=col, qc=qc: g.reciprocal(out=rs[:, qc:qc + 1], in_=bank[:, col + 128:col + 129]),
                     r=[bk], w=[("rs", qc)])
                ts(P, "dve", o_sb[ob][:, qc, :], bank[:, col:col + 128], rs[:, qc:qc + 1], None, ALU.mult, None,
                   r=[bk, ("rs", qc)], w=[("o_sb", ob)])
            P.dma("sp", ov[qb], o_sb[ob][:], r=[("o_sb", ob)])
    return P.build()
```

```python
from contextlib import ExitStack
import numpy as np
import concourse.bass as bass
import concourse.mybir as mybir
from concourse.bass_utils import run_bass_kernel_spmd

F32 = mybir.dt.float32
BF16 = mybir.dt.bfloat16
I32 = mybir.dt.int32
AF = mybir.ActivationFunctionType
ALU = mybir.AluOpType
AX = mybir.AxisListType


STRICT_WAR = True


class Prog:
    def __init__(self):
        self.nc = bass.Bass("TRN2", target_bir_lowering=False)
        self.es = ExitStack()
        self.ops = []
        self.last_w = {}
        self.readers = {}
        self.n_t = 0
        self.psum_keys = set()

    def din(self, name, shape, dt=F32):
        return self.nc.dram_tensor(name, list(shape), dt, kind="ExternalInput").ap()

    def dout(self, name, shape, dt=F32):
        return self.nc.dram_tensor(name, list(shape), dt, kind="ExternalOutput").ap()

    def sb(self, shape, dt=F32, name=None):
        self.n_t += 1
        return self.es.enter_context(self.nc.sbuf_tensor(name or f"sb{self.n_t}", list(shape), dt))

    def ps(self, shape, dt=F32, name=None):
        self.n_t += 1
        return self.es.enter_context(self.nc.psum_tensor(name or f"ps{self.n_t}", list(shape), dt))

    def op(self, eng, fn, r=(), w=(), dma=False, semkey=None):
        i = len(self.ops)
        raw = set()
        war = set()
        if eng == "pe" and not dma:
            self.psum_keys.update(w)
        for k in r:
            if k in self.last_w:
                raw.add(self.last_w[k])
            if k in self.psum_keys:
                for j in self.readers.get(k, ()):
                    if self.ops[j]["eng"] != eng:
                        raw.add(j)
        for k in w:
            if k in self.last_w:
                raw.add(self.last_w[k])
            war.update(self.readers.get(k, ()))
        self.ops.append(dict(eng=eng, fn=fn, raw=raw, war=war - raw, dma=dma, semkey=semkey))
        for k in r:
            self.readers.setdefault(k, []).append(i)
        for k in w:
            self.last_w[k] = i
            self.readers[k] = []
        return i

    def dma(self, q, out, in_, r=(), w=(), semkey=None):
        if semkey is None:
            semkey = ("ld", w[0]) if w else ("st", r[0])
        return self.op(q, lambda e: e.dma_start(out=out, in_=in_), r=r, w=w, dma=True, semkey=semkey)

    def build(self):
        nc = self.nc
        ops = self.ops
        n = len(ops)
        for i, o in enumerate(ops):
            deps = set()
            for j in o["raw"]:
                pj = ops[j]
                if (not pj["dma"]) and pj["eng"] == o["eng"] and o["eng"] == "pe" and not o["dma"]:
                    continue
                deps.add(j)
            for j in o["war"]:
                pj = ops[j]
                if (not pj["dma"]) and pj["eng"] == o["eng"] and not o["dma"] and (o["eng"] == "pe" or not STRICT_WAR):
                    continue
                if (not pj["dma"]) and pj["eng"] == o["eng"] and o["dma"]:
                    pass
                deps.add(j)
            o["deps"] = deps
        has_dep = [False] * n
        for o in ops:
            for j in o["deps"]:
                has_dep[j] = True
        engs = ["pe", "act", "dve", "pool", "sp"]
        sems = {e: self.es.enter_context(nc.semaphore(f"s_{e}")) for e in engs}
        cnt = {e: 0 for e in engs}
        dsem = {}
        for i, o in enumerate(ops):
            if o["dma"]:
                k = o["semkey"]
                if k not in dsem:
                    dsem[k] = [self.es.enter_context(nc.semaphore(f"d{len(dsem)}")), 0]
                dsem[k][1] += 16
                o["sem"] = dsem[k][0]
                o["ticket"] = dsem[k][1]
                o["inc"] = 16
            elif has_dep[i]:
                cnt[o["eng"]] += 1
                o["sem"] = sems[o["eng"]]
                o["ticket"] = cnt[o["eng"]]
                o["inc"] = 1
            else:
                o["sem"] = None
        self.n_sems = len(dsem) + 5
        per = {e: [i for i, o in enumerate(ops) if o["eng"] == e] for e in engs}

        def replay(e):
            def run(eng):
                seen = {}
                for i in per[e]:
                    o = ops[i]
                    need = {}
                    for j in o["deps"]:
                        pj = ops[j]
                        s = pj["sem"]
                        key = id(s)
                        if seen.get(key, 0) >= pj["ticket"]:
                            continue
                        if key not in need or need[key][1] < pj["ticket"]:
                            need[key] = (s, pj["ticket"])
                    for key, (s, t) in need.items():
                        eng.wait_ge(s, t)
                        seen[key] = t
                    ins = o["fn"](eng)
                    if o["sem"] is not None:
                        ins.then_inc(o["sem"], o["inc"])
                if e == "sp":
                    for k, (s, t) in dsem.items():
                        if seen.get(id(s), 0) < t:
                            eng.wait_ge(s, t)
            return run

        with nc.Block() as block:
            block.sync(replay("sp"))
            block.tensor(replay("pe"))
            block.scalar(replay("act"))
            block.vector(replay("dve"))
            block.gpsimd(replay("pool"))
        self.es.close()
        return nc


def build_cast(N, CH=4096):
    P = Prog()
    src = P.din("src", [128, N], F32)
    dst = P.dout("dst", [128, N], BF16)
    nb = N // CH
    NS = 3
    fin = [P.sb([128, CH], F32) for _ in range(NS)]
    fout = [P.sb([128, CH], BF16) for _ in range(NS)]
    engs = ["dve", "act", "pool"]
    for b in range(nb):
        s = b % NS
        P.dma("sp", fin[s][:], src[:, b * CH:(b + 1) * CH], w=[("fin", s)])
        e = engs[b % 3]
        if e == "act":
            P.op("act", lambda g, s=s: g.copy(out=fout[s][:], in_=fin[s][:]), r=[("fin", s)], w=[("fout", s)])
        else:
            P.op(e, lambda g, s=s: g.tensor_copy(out=fout[s][:], in_=fin[s][:]), r=[("fin", s)], w=[("fout", s)])
        P.dma("sp", dst[:, b * CH:(b + 1) * CH], fout[s][:], r=[("fout", s)])
    return P.build()


def mm(P, out, lhsT, rhs, start, stop, r, w):
    P.op("pe", lambda g: g.matmul(out, lhsT, rhs, start=start, stop=stop), r=r, w=w)


def act(P, out, in_, func, r, w, bias=None, scale=None):
    kw = {}
    if bias is not None:
        kw["bias"] = bias
    if scale is not None:
        kw["scale"] = scale
    P.op("act", lambda g: g.activation(out=out, in_=in_, func=func, **kw), r=r, w=w)


def tt(P, eng, out, a, b, op, r, w):
    P.op(eng, lambda g: g.tensor_tensor(out=out, in0=a, in1=b, op=op), r=r, w=w)


def stt(P, out, in0, scalar, in1, op0, op1, r, w):
    P.op("dve", lambda g: g.scalar_tensor_tensor(out=out, in0=in0, scalar=scalar, in1=in1, op0=op0, op1=op1), r=r, w=w)


def ts(P, eng, out, in0, s1, s2, op0, op1, r, w):
    if op1 is None:
        P.op(eng, lambda g: g.tensor_scalar(out=out, in0=in0, scalar1=s1, scalar2=None, op0=op0), r=r, w=w)
    else:
        P.op(eng, lambda g: g.tensor_scalar(out=out, in0=in0, scalar1=s1, scalar2=s2, op0=op0, op1=op1), r=r, w=w)


def rstd_from_ps(P, out_sb, in_ps, n, eps, r, w, tmpkey):
    act(P, out_sb, in_ps, AF.Ln, r=list(r) + ["epsc"], w=w, bias=eps_ap(P, eps), scale=1.0 / n)
    act(P, out_sb, out_sb, AF.Exp, r=w, w=w, scale=-0.5)


def eps_ap(P, eps):
    if not hasattr(P, "_eps"):
        P._eps = P.sb([128, 1], F32, name="epsc")
        P.op("pool", lambda g: g.memset(P._eps[:], float(eps)), w=["epsc"])
    return P._eps[:]


def build_adaln(NL=2, D=2048, CW=1536):
    P = Prog()
    KC = D // 128
    c2T = P.din("c2T", [128, KC, 2], F32)
    w = P.din("w", [NL, D, CW], F32)
    b2 = P.din("b2", [NL, 2, CW], F32)
    mod = P.dout("mod", [NL, 2, CW], F32)
    cs = P.sb([128, KC, 2], F32)
    ss = P.sb([128, KC, 2], F32)
    bt = P.sb([2, NL, CW], F32)
    P.dma("sp", cs[:], c2T[:], w=["cs"])
    for l in range(NL):
        P.dma("sp", bt[:, l, :], b2[l], w=[("bt", l)])
    act(P, ss[:], cs[:], AF.Silu, r=["cs"], w=["ss"])
    wt = [P.sb([128, KC, 512], F32) for _ in range(2)]
    pst = [P.ps([128, 512], F32) for _ in range(2)]
    ot = [P.sb([2, 512], F32) for _ in range(2)]
    it = 0
    for l in range(NL):
        wv = w[l].rearrange("(kc p) c -> p kc c", p=128)
        for cb in range(CW // 512):
            s = it % 2
            P.dma("sp", wt[s][:], wv[:, :, cb * 512:(cb + 1) * 512], w=[("wt", s)])
            for kc in range(KC):
                mm(P, pst[s][0:2, :], ss[:, kc, :], wt[s][:, kc, :], kc == 0, kc == KC - 1,
                   r=["ss", ("wt", s)], w=[("ps", s)])
            tt(P, "dve", ot[s][:], pst[s][0:2, :], bt[:, l, cb * 512:(cb + 1) * 512], ALU.add,
               r=[("ps", s), ("bt", l)], w=[("ot", s)])
            P.dma("sp", mod[l, :, cb * 512:(cb + 1) * 512], ot[s][:], r=[("ot", s)])
            it += 1
    return P.build()


D_MODEL = 2048
IN_W = 2560
EPS = 1e-6


def build_p1(NB, TB):
    P = Prog()
    Tc = NB * TB
    KC = 16
    xT = P.din("xT", [D_MODEL, Tc], F32)
    gam = P.din("gam", [128, KC], F32)
    scv = P.din("scv", [128, KC], F32)
    shv = P.din("shv", [128, KC], F32)
    w_in = P.din("w_in", [D_MODEL, IN_W], BF16)
    qn = P.din("qn", [128, 1], F32)
    kn = P.din("kn", [128, 1], F32)
    cosT = P.din("cosT", [128, Tc], F32)
    sinT = P.din("sinT", [128, Tc], F32)
    rotT = P.din("rotT", [128, 128], F32)
    cd = P.din("cd", [128, 128], F32)
    sdn = P.din("sdn", [128, 128], F32)
    qT = P.dout("qT", [1024, Tc], BF16)
    kT = P.dout("kT", [256, Tc], BF16)
    vT = P.dout("vT", [256, Tc], BF16)
    usT = P.dout("usT", [512, Tc], F32)
    zrT = P.dout("zrT", [512, Tc], F32)
    ziT = P.dout("ziT", [512, Tc], F32)

    W = P.sb([128, KC, IN_W], BF16)
    wv = w_in.rearrange("(kc p) c -> p kc c", p=128)
    for i in range(4):
        P.dma("sp", W[:, i * 4:(i + 1) * 4, :], wv[:, i * 4:(i + 1) * 4, :], w=[("W", i)])
    Wk = [("W", i) for i in range(4)]
    g_sb = P.sb([128, KC], F32); sc_sb = P.sb([128, KC], F32); sh_sb = P.sb([128, KC], F32)
    G = P.sb([128, KC], F32)
    qn_sb = P.sb([128, 1], F32); kn_sb = P.sb([128, 1], F32)
    rot_sb = P.sb([128, 128], F32); cd_sb = P.sb([128, 128], F32); sd_sb = P.sb([128, 128], F32)
    ones = P.sb([128, 128], BF16)
    P.dma("sp", g_sb[:], gam[:], w=["g"]); P.dma("sp", sc_sb[:], scv[:], w=["sc"]); P.dma("sp", sh_sb[:], shv[:], w=["sh"])
    P.dma("sp", qn_sb[:], qn[:], w=["qn"]); P.dma("sp", kn_sb[:], kn[:], w=["kn"])
    P.dma("sp", rot_sb[:], rotT[:], w=["rot"]); P.dma("sp", cd_sb[:], cd[:], w=["cd"]); P.dma("sp", sd_sb[:], sdn[:], w=["sd"])
    P.op("pool", lambda g: g.memset(ones[:], 1.0), w=["ones"])
    stt(P, G[:], sc_sb[:], 1.0, g_sb[:], ALU.add, ALU.mult, r=["g", "sc"], w=["G"])

    x_sb = [P.sb([128, KC, TB], F32) for _ in range(2)]
    sqs = [P.sb([128, TB], BF16) for _ in range(2)]
    h = P.sb([128, KC, TB], BF16)
    tmp = [P.sb([128, TB], F32) for _ in range(2)]
    rstd = [P.sb([128, TB], F32) for _ in range(2)]
    cs_b = [P.sb([128, TB], F32) for _ in range(2)]; sn_b = [P.sb([128, TB], F32) for _ in range(2)]
    sqh = P.sb([128, TB], BF16)
    r2 = P.sb([128, TB], F32)
    qf = P.sb([128, TB], F32)
    t1 = P.sb([128, TB], F32); t2 = P.sb([128, TB], F32)
    ob = [P.sb([128, TB], BF16) for _ in range(2)]
    of = [P.sb([128, TB], F32) for _ in range(3)]
    ufb = [P.sb([128, TB], F32) for _ in range(2)]
    acc = [P.ps([128, 512], F32) for _ in range(3)]
    ps_ss = P.ps([128, 512], F32); ps_s2 = P.ps([128, 512], F32); ps_rot = P.ps([128, 512], F32)
    ps_z = [P.ps([128, 512], F32) for _ in range(2)]
    xv = xT.rearrange("(kc p) t -> p kc t", p=128)
    nb_o = 0; nf_o = 0; na = 0; nz = 0
    nsq = [0]

    def p_load(b):
        tsl = slice(b * TB, (b + 1) * TB); pb = b % 2
        P.dma("sp", x_sb[pb][:], xv[:, :, tsl], w=[("x", pb)])
        P.dma("sp", cs_b[pb][:], cosT[:, tsl], w=[("cs", pb)]); P.dma("sp", sn_b[pb][:], sinT[:, tsl], w=[("sn", pb)])

    def p_stats(b):
        pb = b % 2
        for kc in range(KC):
            q = nsq[0] % 2; nsq[0] += 1
            act(P, sqs[q][:], x_sb[pb][:, kc, :], AF.Square, r=[("x", pb)], w=[("sqs", q)])
            mm(P, ps_ss[:, :TB], ones[:], sqs[q][:], kc == 0, kc == KC - 1, r=["ones", ("sqs", q)], w=["ps_ss"])
        rstd_from_ps(P, rstd[pb][:], ps_ss[:, :TB], D_MODEL, EPS, r=["ps_ss"], w=[("rstd", pb)], tmpkey=None)

    def p_h(b):
        pb = b % 2
        for kc in range(KC):
            s_ = kc % 2
            stt(P, tmp[s_][:], x_sb[pb][:, kc, :], G[:, kc:kc + 1], rstd[pb][:], ALU.mult, ALU.mult,
                r=[("x", pb), "G", ("rstd", pb)], w=[("tmp", s_)])
            act(P, h[:, kc, :], tmp[s_][:], AF.Identity, r=[("tmp", s_), "sh"], w=[("h", kc)], bias=sh_sb[:, kc:kc + 1])

    p_load(0); p_stats(0); p_h(0)
    for b in range(NB):
        tsl = slice(b * TB, (b + 1) * TB)
        cs_sb = cs_b[b % 2]; sn_sb = sn_b[b % 2]; csk = ("cs", b % 2); snk = ("sn", b % 2)
        if b + 1 < NB:
            p_load(b + 1)
        hk = [("h", kc) for kc in range(KC)]
        NCC = IN_W // 128
        st = {}

        def stage1(cc):
            nonlocal na, nb_o, nf_o, nz
            a = na % 3; na += 1
            for kc in range(KC):
                mm(P, acc[a][:, :TB], W[:, kc, cc * 128:(cc + 1) * 128], h[:, kc, :], kc == 0, kc == KC - 1,
                   r=[Wk[kc // 4], ("h", kc)], w=[("acc", a)])
            A = acc[a][:, :TB]
            st[cc] = (a, A)
            if cc < 10:
                return
            if cc < 12:
                o = nb_o % 2; nb_o += 1
                P.op("act", lambda g, o=o, A=A: g.copy(out=ob[o][:], in_=A), r=[("acc", a)], w=[("ob", o)])
                P.dma("sp", vT[(cc - 10) * 128:(cc - 9) * 128, tsl], ob[o][:], r=[("ob", o)])
            elif cc < 16:
                o = nf_o % 3; nf_o += 1
                P.op("act", lambda g, o=o, A=A: g.copy(out=of[o][:], in_=A), r=[("acc", a)], w=[("of", o)])
                P.dma("sp", usT[(cc - 12) * 128:(cc - 11) * 128, tsl], of[o][:], r=[("of", o)])
            else:
                o = cc % 2
                P.op("act", lambda g, o=o, A=A: g.copy(out=ufb[o][:], in_=A), r=[("acc", a)], w=[("ufb", o)])
                st[cc] = (a, A, o)

        def stage2(cc):
            nonlocal nf_o, nz
            if cc < 10:
                a, A = st[cc]
                gn, gk = (qn_sb, "qn") if cc < 8 else (kn_sb, "kn")
                act(P, sqh[:], A, AF.Square, r=[("acc", a)], w=["sqh"])
                mm(P, ps_s2[:, :TB], ones[:], sqh[:], True, True, r=["ones", "sqh"], w=["ps_s2"])
                rstd_from_ps(P, r2[:], ps_s2[:, :TB], 128, EPS, r=["ps_s2"], w=["r2"], tmpkey=None)
                stt(P, qf[:], A, gn[:, 0:1], r2[:], ALU.mult, ALU.mult, r=[("acc", a), gk, "r2"], w=["qf"])
            elif cc >= 16:
                a, A, o = st[cc]
                for mat, mk, dstT in ((cd_sb, "cd", zrT), (sd_sb, "sd", ziT)):
                    z = nz % 2; nz += 1
                    mm(P, ps_z[z][:, :TB], mat[:], ufb[o][:], True, True, r=[mk, ("ufb", o)], w=[("ps_z", z)])
                    o2 = nf_o % 3; nf_o += 1
                    P.op("dve", lambda g, o2=o2, z=z: g.tensor_copy(out=of[o2][:], in_=ps_z[z][:, :TB]),
                         r=[("ps_z", z)], w=[("of", o2)])
                    P.dma("sp", dstT[(cc - 16) * 128:(cc - 15) * 128, tsl], of[o2][:], r=[("of", o2)])

        def stage3(cc):
            nonlocal nb_o
            if cc >= 10:
                return
            mm(P, ps_rot[:, :TB], rot_sb[:], qf[:], True, True, r=["rot", "qf"], w=["ps_rot"])
            tt(P, "dve", t1[:], qf[:], cs_sb[:], ALU.mult, r=["qf", csk], w=["t1"])
            tt(P, "dve", t2[:], ps_rot[:, :TB], sn_sb[:], ALU.mult, r=["ps_rot", snk], w=["t2"])
            o = nb_o % 2; nb_o += 1
            tt(P, "pool", ob[o][:], t1[:], t2[:], ALU.add, r=["t1", "t2"], w=[("ob", o)])
            dst = qT[cc * 128:(cc + 1) * 128, tsl] if cc < 8 else kT[(cc - 8) * 128:(cc - 7) * 128, tsl]
            P.dma("sp", dst, ob[o][:], r=[("ob", o)])

        for i in range(NCC + 2):
            if i == 3 and b + 1 < NB:
                p_stats(b + 1)
            if i < NCC:
                stage1(i)
            if 0 <= i - 2 < NCC:
                stage3(i - 2)
            if 0 <= i - 1 < NCC:
                stage2(i - 1)
        if b + 1 < NB:
            p_h(b + 1)
    return P.build()


def rope_tables(rows_tok):
    f = (10000.0 ** (-np.arange(32, dtype=np.float32) / 32)).astype(np.float32)
    t = np.asarray(rows_tok)
    row = (t // 64).astype(np.float32); col = (t % 64).astype(np.float32)
    ar = (row[None, :] * f[:, None]).astype(np.float32)
    ac = (col[None, :] * f[:, None]).astype(np.float32)
    cosT = np.concatenate([np.cos(ar), np.cos(ar), np.cos(ac), np.cos(ac)], 0).astype(np.float32)
    sinT = np.concatenate([np.sin(ar), np.sin(ar), np.sin(ac), np.sin(ac)], 0).astype(np.float32)
    return cosT, sinT


def rope_rotT():
    R = np.zeros((128, 128), np.float32)
    for base in (0, 64):
        for i in range(32):
            R[base + i, base + i + 32] = -1.0
            R[base + i + 32, base + i] = 1.0
    return np.ascontiguousarray(R.T)


def chan_dft():
    j = np.arange(128)
    ang = 2 * np.pi * np.outer(j, j) / 128
    return (np.cos(ang) / np.sqrt(128)).astype(np.float32), (-np.sin(ang) / np.sqrt(128)).astype(np.float32)


ATT_VARIANT = ""


def build_attn(NQ, NK, QB):
    P = Prog()
    NQB = NQ // QB
    NKT = NK // 128
    qT = P.din("qT", [128, NQ], BF16)
    kT = P.din("kT", [128, NK], BF16)
    v = P.din("v", [NK, 128], BF16)
    oT = P.dout("oT", [128, NQ], BF16)
    q_sb = P.sb([128, NQ], BF16); k_sb = P.sb([128, NK], BF16); v_sb = P.sb([128, NKT, 128], BF16)
    KCH = 13 if NKT % 13 == 0 else NKT
    vv = v.rearrange("(kt p) d -> p kt d", p=128)
    QCH = 4 if NQB % 4 == 0 else NQB
    P.dma("sp", q_sb[:, 0:QCH * QB], qT[:, 0:QCH * QB], w=[("Q", 0)])
    for c in range(NKT // KCH):
        P.dma("sp", k_sb[:, c * KCH * 128:(c + 1) * KCH * 128], kT[:, c * KCH * 128:(c + 1) * KCH * 128], w=[("K", c)])
        P.dma("sp", v_sb[:, c * KCH:(c + 1) * KCH, :], vv[:, c * KCH:(c + 1) * KCH, :], w=[("V", c)])
    for c in range(1, NQB // QCH):
        P.dma("sp", q_sb[:, c * QCH * QB:(c + 1) * QCH * QB], qT[:, c * QCH * QB:(c + 1) * QCH * QB], w=[("Q", c)])
    ones_b = P.sb([128, 1], BF16); ones_f = P.sb([1, 128], F32)
    P.op("pool", lambda g: g.memset(ones_b[:], 1.0), w=["ones_b"])
    P.op("pool", lambda g: g.memset(ones_f[:], 1.0), w=["ones_f"])
    S = [P.ps([128, 512], F32) for _ in range(3)]
    O = [P.ps([128, 512], F32) for _ in range(2)]
    SM = [P.ps([128, 512], F32) for _ in range(2)]
    BC = P.ps([128, 512], F32)
    PT = [P.sb([128, QB], BF16) for _ in range(3)]
    rs = P.sb([1, QB], F32); bc_sb = P.sb([128, QB], F32)
    o_sb = [P.sb([128, QB], BF16) for _ in range(2)]
    scale = 128 ** -0.5
    items = [(qb, kt) for qb in range(NQB) for kt in range(NKT)]

    def emit_S(i):
        qb, kt = items[i]
        s = i % 3
        mm(P, S[s][:, :QB], k_sb[:, kt * 128:(kt + 1) * 128], q_sb[:, qb * QB:(qb + 1) * QB], True, True,
           r=[("K", kt // KCH), ("Q", qb // QCH)], w=[("S", s)])

    emit_S(0)
    for i, (qb, kt) in enumerate(items):
        if i + 1 < len(items):
            emit_S(i + 1)
        s = i % 3
        o = qb % 2
        act(P, PT[s][:], S[s][:, :QB], AF.Exp, r=[("S", s)], w=[("PT", s)], scale=scale)
        if ATT_VARIANT != "B" or kt in (0, NKT - 1):
            mm(P, O[o][:, :QB], v_sb[:, kt, :], PT[s][:], kt == 0, kt == NKT - 1, r=[("V", kt // KCH), ("PT", s)], w=[("O", o)])
        if ATT_VARIANT == "" or kt in (0, NKT - 1):
            mm(P, SM[o][0:1, :QB], ones_b[:], PT[s][:], kt == 0, kt == NKT - 1, r=["ones_b", ("PT", s)], w=[("SM", o)])
        if kt == NKT - 1:
            P.op("dve", lambda g, o=o: g.reciprocal(out=rs[:], in_=SM[o][0:1, :QB]), r=[("SM", o)], w=["rs"])
            mm(P, BC[:, :QB], ones_f[:], rs[:], True, True, r=["ones_f", "rs"], w=["BC"])
            P.op("act", lambda g: g.copy(out=bc_sb[:], in_=BC[:, :QB]), r=["BC"], w=["bc_sb"])
            tt(P, "dve", o_sb[o][:], O[o][:, :QB], bc_sb[:], ALU.mult, r=[("O", o), "bc_sb"], w=[("o_sb", o)])
            P.dma("sp", oT[:, qb * QB:(qb + 1) * QB], o_sb[o][:], r=[("o_sb", o)])
    return P.build()


def const_ap(P, val):
    if not hasattr(P, "_consts"):
        P._consts = {}
    if val not in P._consts:
        t = P.sb([128, 1], F32, name=f"const{len(P._consts)}")
        key = ("const", len(P._consts))
        P.op("pool", lambda g: g.memset(t[:], float(val)), w=[key])
        P._consts[val] = (t, key)
    return P._consts[val]


def emit_sin(P, out, u, shape, r, w, tag):
    if not hasattr(P, "_sin_tmp"):
        P._sin_tmp = {}
    sk = tuple(shape)
    if sk not in P._sin_tmp:
        P._sin_tmp[sk] = (P.sb(shape, I32), P.sb(shape, F32), P.sb(shape, F32), P.sb(shape, F32))
    ki, kf, fr, mk = P._sin_tmp[sk]
    k = lambda n: ("sintmp", sk, n)
    P.op("dve", lambda g: g.tensor_copy(out=ki[:], in_=u), r=r, w=[k("ki")])
    P.op("dve", lambda g: g.tensor_copy(out=kf[:], in_=ki[:]), r=[k("ki")], w=[k("kf")])
    tt(P, "dve", fr[:], u, kf[:], ALU.subtract, r=r + [k("kf")], w=[k("fr")])
    ts(P, "dve", mk[:], fr[:], 0.0, None, ALU.is_lt, None, r=[k("fr")], w=[k("mk")])
    tt(P, "dve", fr[:], fr[:], mk[:], ALU.add, r=[k("fr"), k("mk")], w=[k("fr")])
    cb, cbk = const_ap(P, -np.pi)
    act(P, out, fr[:], AF.Sin, r=[k("fr"), cbk], w=w, bias=cb[:], scale=2 * np.pi)


def build_ssm(L, TBL=512):
    P = Prog()
    NCT = 4
    u = P.din("u", [128, L], F32)
    lam_re = P.din("lam_re", [128, NCT], F32)
    lam_im = P.din("lam_im", [128, NCT], F32)
    log_dt = P.din("log_dt", [128, NCT], F32)
    b_re = P.din("b_re", [NCT, 128, 128], F32)
    b_im = P.din("b_im", [NCT, 128, 128], F32)
    c_re = P.din("c_re", [NCT, 128, 128], F32)
    c_im = P.din("c_im", [NCT, 128, 128], F32)
    tvec = P.din("tvec", [128, TBL], F32)
    ident = P.din("ident", [128, 128], F32)
    y = P.dout("y", [128, L], F32)

    U = P.sb([128, L], F32)
    nch = 8
    bounds = [(i * L) // nch for i in range(nch + 1)]
    def ukey(t0):
        for i in range(nch):
            if bounds[i] <= t0 < bounds[i + 1]:
                return ("U", i)
    lr = P.sb([128, NCT]); li = P.sb([128, NCT]); ld = P.sb([128, NCT])
    P.dma("sp", lr[:], lam_re[:], w=["lr"]); P.dma("sp", li[:], lam_im[:], w=["li"]); P.dma("sp", ld[:], log_dt[:], w=["ld"])
    tv = P.sb([128, TBL]); idn = P.sb([128, 128])
    P.dma("sp", tv[:], tvec[:], w=["tv"]); P.dma("sp", idn[:], ident[:], w=["idn"])
    Bre = P.sb([128, NCT, 128]); Bim = P.sb([128, NCT, 128]); Cre = P.sb([128, NCT, 128]); Cim = P.sb([128, NCT, 128])
    for c in range(NCT):
        P.dma("sp", Bre[:, c, :], b_re[c], w=[("Bre", c)]); P.dma("sp", Bim[:, c, :], b_im[c], w=[("Bim", c)])
        P.dma("sp", Cre[:, c, :], c_re[c], w=[("Cre", c)]); P.dma("sp", Cim[:, c, :], c_im[c], w=[("Cim", c)])
    for i in range(nch):
        P.dma("sp", U[:, bounds[i]:bounds[i + 1]], u[:, bounds[i]:bounds[i + 1]], w=[("U", i)])
    S4 = [128, NCT]
    dt = P.sb(S4); mag = P.sb(S4); th = P.sb(S4); tmp = P.sb(S4); us = P.sb(S4); uc = P.sb(S4)
    sn = P.sb(S4); cs = P.sb(S4); ar1 = P.sb(S4); ai = P.sb(S4); den = P.sb(S4); t2 = P.sb(S4)
    cr = P.sb(S4); ci = P.sb(S4); nci = P.sb(S4); th2 = P.sb(S4)
    act(P, dt[:], ld[:], AF.Exp, r=["ld"], w=["dt"])
    tt(P, "dve", tmp[:], lr[:], dt[:], ALU.mult, r=["lr", "dt"], w=["tmp"])
    act(P, mag[:], tmp[:], AF.Exp, r=["tmp"], w=["mag"])
    tt(P, "dve", th[:], li[:], dt[:], ALU.mult, r=["li", "dt"], w=["th"])
    ts(P, "dve", th2[:], th[:], float(1.0 / (2 * np.pi)), None, ALU.mult, None, r=["th"], w=["th2"])
    ts(P, "dve", us[:], th2[:], 0.5, None, ALU.add, None, r=["th2"], w=["us"])
    ts(P, "dve", uc[:], th2[:], 0.75, None, ALU.add, None, r=["th2"], w=["uc"])
    emit_sin(P, sn[:], us[:], S4, ["us"], ["sn"], "rs")
    emit_sin(P, cs[:], uc[:], S4, ["uc"], ["cs"], "rc")
    tt(P, "dve", ar1[:], mag[:], cs[:], ALU.mult, r=["mag", "cs"], w=["ar1"])
    ts(P, "dve", ar1[:], ar1[:], -1.0, None, ALU.add, None, r=["ar1"], w=["ar1"])
    tt(P, "dve", ai[:], mag[:], sn[:], ALU.mult, r=["mag", "sn"], w=["ai"])
    tt(P, "dve", den[:], lr[:], lr[:], ALU.mult, r=["lr"], w=["den"])
    tt(P, "dve", t2[:], li[:], li[:], ALU.mult, r=["li"], w=["t2"])
    tt(P, "dve", den[:], den[:], t2[:], ALU.add, r=["den", "t2"], w=["den"])
    P.op("dve", lambda g: g.reciprocal(out=den[:], in_=den[:]), r=["den"], w=["den"])
    tt(P, "dve", cr[:], ar1[:], lr[:], ALU.mult, r=["ar1", "lr"], w=["cr"])
    tt(P, "dve", t2[:], ai[:], li[:], ALU.mult, r=["ai", "li"], w=["t2"])
    tt(P, "dve", cr[:], cr[:], t2[:], ALU.add, r=["cr", "t2"], w=["cr"])
    tt(P, "dve", cr[:], cr[:], den[:], ALU.mult, r=["cr", "den"], w=["cr"])
    tt(P, "dve", ci[:], ai[:], lr[:], ALU.mult, r=["ai", "lr"], w=["ci"])
    tt(P, "dve", t2[:], ar1[:], li[:], ALU.mult, r=["ar1", "li"], w=["t2"])
    tt(P, "dve", ci[:], ci[:], t2[:], ALU.subtract, r=["ci", "t2"], w=["ci"])
    tt(P, "dve", ci[:], ci[:], den[:], ALU.mult, r=["ci", "den"], w=["ci"])
    ts(P, "dve", nci[:], ci[:], -1.0, None, ALU.mult, None, r=["ci"], w=["nci"])
    BrT = P.sb([128, NCT, 128]); BiT = P.sb([128, NCT, 128]); Cimn = P.sb([128, NCT, 128])
    bt = P.sb([128, 128]); bb = [P.sb([128, 128]) for _ in range(2)]
    pst = P.ps([128, 512], F32)
    nb = 0
    for c in range(NCT):
        for which in range(2):
            s = nb % 2; nb += 1
            if which == 0:
                ts(P, "dve", bt[:], Bre[:, c, :], cr[:, c:c + 1], None, ALU.mult, None, r=[("Bre", c), "cr"], w=["bt"])
                stt(P, bb[s][:], Bim[:, c, :], nci[:, c:c + 1], bt[:], ALU.mult, ALU.add, r=[("Bim", c), "nci", "bt"], w=[("bb", s)])
                dst = BrT
            else:
                ts(P, "dve", bt[:], Bim[:, c, :], cr[:, c:c + 1], None, ALU.mult, None, r=[("Bim", c), "cr"], w=["bt"])
                stt(P, bb[s][:], Bre[:, c, :], ci[:, c:c + 1], bt[:], ALU.mult, ALU.add, r=[("Bre", c), "ci", "bt"], w=[("bb", s)])
                dst = BiT
            P.op("pe", lambda g, s=s: g.transpose(pst[:, 0:128], bb[s][:], idn[:]), r=[("bb", s), "idn"], w=["pst"])
            P.op("act", lambda g, dst=dst, c=c: g.copy(out=dst[:, c, :], in_=pst[:, 0:128]), r=["pst"], w=[("BT", c, which)])
        P.op("act", lambda g, c=c: g.mul(out=Cimn[:, c, :], in_=Cim[:, c, :], mul=-1.0), r=[("Cim", c)], w=[("Cimn", c)])
    ct = P.sb([128, NCT, TBL]); st = P.sb([128, NCT, TBL]); magT = P.sb([128, NCT, TBL])
    ut = P.sb([128, TBL])
    for c in range(NCT):
        for which, dstT, off in ((0, st, 0.5), (1, ct, 0.75)):
            ts(P, "dve", ut[:], tv[:], th2[:, c:c + 1], off, ALU.mult, ALU.add, r=["tv", "th2"], w=["ut"])
            emit_sin(P, dstT[:, c, :], ut[:], [128, TBL], ["ut"], [("tab", c, which)], ("tb", c, which))
        ts(P, "dve", magT[:, c, :], tv[:], 0.0, mag[:, c:c + 1], ALU.mult, ALU.add, r=["tv", "mag"], w=[("magT", c)])
    blocks = []
    t0 = 0
    while t0 < L:
        n = min(TBL, L - t0); blocks.append((t0, n)); t0 += n
    psb = [[P.ps([128, 512], F32), P.ps([128, 512], F32)] for _ in range(2)]
    psy = [P.ps([128, 512], F32) for _ in range(2)]
    mi = [P.sb([128, TBL]) for _ in range(4)]
    mo = [[P.sb([128, TBL]) for _ in range(4)] for _ in range(2)]
    gin = [P.sb([128, TBL]) for _ in range(2)]
    g_ = [[P.sb([128, TBL]) for _ in range(2)] for _ in range(2)]
    hh = [[[P.sb([128, TBL]), P.sb([128, TBL])] for _ in range(2)] for _ in range(NCT)]
    ysb = [P.sb([128, TBL]) for _ in range(2)]
    items = [(j, c) for j in range(len(blocks)) for c in range(NCT)]

    def tabs(c, n):
        return ct[:, c, :n], st[:, c, :n], [("tab", c, 0), ("tab", c, 1)]

    def stA(k):
        j, c = items[k]; t0, n = blocks[j]; pb = k % 2
        uk = [ukey(t0), ukey(t0 + n - 1)]
        br_ps, bi_ps = psb[pb][0][:, :n], psb[pb][1][:, :n]
        mm(P, br_ps, BrT[:, c, :], U[:, t0:t0 + n], True, True, r=[("BT", c, 0)] + uk, w=[("psb", pb, 0)])
        mm(P, bi_ps, BiT[:, c, :], U[:, t0:t0 + n], True, True, r=[("BT", c, 1)] + uk, w=[("psb", pb, 1)])
        ctc, stc, tk = tabs(c, n)
        tt(P, "dve", mi[0][:, :n], br_ps, ctc, ALU.mult, r=[("psb", pb, 0)] + tk, w=["mi0"])
        tt(P, "dve", mi[1][:, :n], bi_ps, stc, ALU.mult, r=[("psb", pb, 1)] + tk, w=["mi1"])
        tt(P, "dve", mi[2][:, :n], bi_ps, ctc, ALU.mult, r=[("psb", pb, 1)] + tk, w=["mi2"])
        tt(P, "dve", mi[3][:, :n], br_ps, stc, ALU.mult, r=[("psb", pb, 0)] + tk, w=["mi3"])
        tt(P, "dve", gin[0][:, :n], mi[0][:, :n], mi[1][:, :n], ALU.add, r=["mi0", "mi1"], w=["gin0"])
        tt(P, "pool", gin[1][:, :n], mi[2][:, :n], mi[3][:, :n], ALU.subtract, r=["mi2", "mi3"], w=["gin1"])

    def stB(k):
        j, c = items[k]; t0, n = blocks[j]; hb = j % 2; gb = k % 2
        for ri in range(2):
            if j == 0:
                init = 0.0; rk = []
            else:
                pn = blocks[j - 1][1]
                init = hh[c][1 - hb][ri][:, pn - 1:pn]; rk = [("hh", c, 1 - hb, ri)]
            P.op("dve", lambda g, ri=ri, init=init, c=c, n=n, gb=gb: g.tensor_tensor_scan(
                out=g_[gb][ri][:, :n], data0=magT[:, c, :n], data1=gin[ri][:, :n], initial=init,
                op0=ALU.mult, op1=ALU.add), r=[("magT", c), f"gin{ri}"] + rk, w=[("g", gb, ri)])

    def stC(k):
        j, c = items[k]; t0, n = blocks[j]; gb = k % 2
        ctc, stc, tk = tabs(c, n)
        tt(P, "pool", mo[gb][0][:, :n], g_[gb][0][:, :n], ctc, ALU.mult, r=[("g", gb, 0)] + tk, w=[("mo", gb, 0)])
        tt(P, "pool", mo[gb][2][:, :n], g_[gb][0][:, :n], stc, ALU.mult, r=[("g", gb, 0)] + tk, w=[("mo", gb, 2)])
        tt(P, "pool", mo[gb][1][:, :n], g_[gb][1][:, :n], stc, ALU.mult, r=[("g", gb, 1)] + tk, w=[("mo", gb, 1)])
        tt(P, "pool", mo[gb][3][:, :n], g_[gb][1][:, :n], ctc, ALU.mult, r=[("g", gb, 1)] + tk, w=[("mo", gb, 3)])

    def stD(k):
        j, c = items[k]; t0, n = blocks[j]; hb = j % 2; gb = k % 2; yb = j % 2
        hr, hi = hh[c][hb][0], hh[c][hb][1]
        tt(P, "dve", hr[:, :n], mo[gb][0][:, :n], mo[gb][1][:, :n], ALU.subtract, r=[("mo", gb, 0), ("mo", gb, 1)], w=[("hh", c, hb, 0)])
        tt(P, "dve", hi[:, :n], mo[gb][2][:, :n], mo[gb][3][:, :n], ALU.add, r=[("mo", gb, 2), ("mo", gb, 3)], w=[("hh", c, hb, 1)])
        mm(P, psy[yb][:, :n], Cre[:, c, :], hr[:, :n], c == 0, False, r=[("Cre", c), ("hh", c, hb, 0)], w=[("psy", yb)])
        mm(P, psy[yb][:, :n], Cimn[:, c, :], hi[:, :n], False, c == NCT - 1, r=[("Cimn", c), ("hh", c, hb, 1)], w=[("psy", yb)])
        if c == NCT - 1:
            P.op("act", lambda g, yb=yb, n=n: g.copy(out=ysb[yb][:, :n], in_=psy[yb][:, :n]), r=[("psy", yb)], w=[("ysb", yb)])
            P.dma("sp", y[:, t0:t0 + n], ysb[yb][:, :n], r=[("ysb", yb)])

    stA(0); stB(0); stC(0)
    for k in range(len(items)):
        if k + 1 < len(items):
            stA(k + 1)
        stD(k)
        if k + 1 < len(items):
            stB(k + 1); stC(k + 1)
    return P.build()


def ssm_pack(lam_re, lam_im, log_dt, b_re, b_im, c_re, c_im, TBL=512):
    f = np.float32
    lr = np.zeros((128, 4), f); li = np.zeros((128, 4), f); ld = np.zeros((128, 4), f)
    Bre = np.zeros((4, 128, 128), f); Bim = np.zeros((4, 128, 128), f)
    Cre = np.zeros((4, 128, 128), f); Cim = np.zeros((4, 128, 128), f)
    for c in range(4):
        for gi in range(2):
            g = 2 * c + gi
            sl = slice(gi * 64, (gi + 1) * 64)
            lr[sl, c] = lam_re[g]; li[sl, c] = lam_im[g]; ld[sl, c] = log_dt[g]
            cs = slice(c * 32 + gi * 16, c * 32 + gi * 16 + 16)
            Bre[c, sl, cs] = b_re[g]; Bim[c, sl, cs] = b_im[g]
            Cre[c, sl, cs] = c_re[g].T; Cim[c, sl, cs] = c_im[g].T
    tvec = np.tile(np.arange(1, TBL + 1, dtype=f)[None, :], (128, 1))
    return dict(lam_re=lr, lam_im=li, log_dt=ld, b_re=Bre, b_im=Bim, c_re=Cre, c_im=Cim, tvec=tvec,
                ident=np.eye(128, dtype=f))


def build_pf1(NT2=16, C=512):
    P = Prog()
    zr = P.din("zr", [128, NT2, C], F32); zi = P.din("zi", [128, NT2, C], F32)
    mc = P.din("mc", [128, 128], F32); ms = P.din("ms", [128, 128], F32); mns = P.din("mns", [128, 128], F32)
    twc = P.din("twc", [128, NT2], F32); tws = P.din("tws", [128, NT2], F32); ntws = P.din("ntws", [128, NT2], F32)
    ar = P.dout("ar", [128, NT2, C], F32); ai = P.dout("ai", [128, NT2, C], F32)
    zr_sb = P.sb([128, NT2, C]); zi_sb = P.sb([128, NT2, C])
    mc_sb = P.sb([128, 128]); ms_sb = P.sb([128, 128]); mns_sb = P.sb([128, 128])
    twc_sb = P.sb([128, NT2]); tws_sb = P.sb([128, NT2]); ntws_sb = P.sb([128, NT2])
    for t, d, k in ((mc_sb, mc, "mc"), (ms_sb, ms, "ms"), (mns_sb, mns, "mns"), (twc_sb, twc, "twc"), (tws_sb, tws, "tws"), (ntws_sb, ntws, "ntws")):
        P.dma("sp", t[:], d[:], w=[k])
    H = NT2 // 2
    for hf in range(2):
        P.dma("sp", zr_sb[:, hf * H:(hf + 1) * H, :], zr[:, hf * H:(hf + 1) * H, :], w=[("zr", hf)])
        P.dma("sp", zi_sb[:, hf * H:(hf + 1) * H, :], zi[:, hf * H:(hf + 1) * H, :], w=[("zi", hf)])
    pr = [P.ps([128, 512]) for _ in range(2)]; pi = [P.ps([128, 512]) for _ in range(2)]
    t1 = [P.sb([128, C]) for _ in range(2)]; t2 = [P.sb([128, C]) for _ in range(2)]
    orr = [P.sb([128, C]) for _ in range(2)]; oi = [P.sb([128, C]) for _ in range(2)]
    for j in range(NT2):
        s = j % 2; hf = j // H
        mm(P, pr[s][:, :C], mc_sb[:], zr_sb[:, j, :], True, False, r=["mc", ("zr", hf)], w=[("pr", s)])
        mm(P, pr[s][:, :C], ms_sb[:], zi_sb[:, j, :], False, True, r=["ms", ("zi", hf)], w=[("pr", s)])
        mm(P, pi[s][:, :C], mc_sb[:], zi_sb[:, j, :], True, False, r=["mc", ("zi", hf)], w=[("pi", s)])
        mm(P, pi[s][:, :C], mns_sb[:], zr_sb[:, j, :], False, True, r=["mns", ("zr", hf)], w=[("pi", s)])
        ts(P, "dve", t1[s][:], pi[s][:, :C], tws_sb[:, j:j + 1], None, ALU.mult, None, r=[("pi", s), "tws"], w=[("t1", s)])
        stt(P, orr[s][:], pr[s][:, :C], twc_sb[:, j:j + 1], t1[s][:], ALU.mult, ALU.add, r=[("pr", s), "twc", ("t1", s)], w=[("or", s)])
        ts(P, "dve", t2[s][:], pr[s][:, :C], ntws_sb[:, j:j + 1], None, ALU.mult, None, r=[("pr", s), "ntws"], w=[("t2", s)])
        stt(P, oi[s][:], pi[s][:, :C], twc_sb[:, j:j + 1], t2[s][:], ALU.mult, ALU.add, r=[("pi", s), "twc", ("t2", s)], w=[("oi", s)])
        P.dma("sp", ar[:, j, :], orr[s][:], r=[("or", s)])
        P.dma("sp", ai[:, j, :], oi[s][:], r=[("oi", s)])
    return P.build()


def build_pf2(NK1=16, C=512, NCTX=256):
    P = Prog()
    ar = P.din("ar", [128, NK1, C], F32); ai = P.din("ai", [128, NK1, C], F32)
    mc = P.din("mc", [128, 128], F32); ms = P.din("ms", [128, 128], F32)
    zcr = P.din("zcr", [NCTX, C], F32); zci = P.din("zci", [NCTX, C], F32)
    cc = P.din("cc", [NCTX, NCTX], F32); cs = P.din("cs", [NCTX, NCTX], F32)
    xr = P.dout("xr", [128, NK1, C], F32)
    xc = P.dout("xc", [NCTX, C], F32)
    ar_sb = P.sb([128, NK1, C]); ai_sb = P.sb([128, NK1, C])
    mc_sb = P.sb([128, 128]); ms_sb = P.sb([128, 128])
    P.dma("sp", mc_sb[:], mc[:], w=["mc"]); P.dma("sp", ms_sb[:], ms[:], w=["ms"])
    H = NK1 // 2
    for hf in range(2):
        P.dma("sp", ar_sb[:, hf * H:(hf + 1) * H, :], ar[:, hf * H:(hf + 1) * H, :], w=[("ar", hf)])
        P.dma("sp", ai_sb[:, hf * H:(hf + 1) * H, :], ai[:, hf * H:(hf + 1) * H, :], w=[("ai", hf)])
    NTC = NCTX // 128
    zcr_sb = P.sb([128, NTC, C]); zci_sb = P.sb([128, NTC, C]); cc_sb = P.sb([128, NTC, NCTX]); cs_sb = P.sb([128, NTC, NCTX])
    P.dma("sp", zcr_sb[:], zcr.rearrange("(a p) c -> p a c", p=128), w=["zcr"])
    P.dma("sp", zci_sb[:], zci.rearrange("(a p) c -> p a c", p=128), w=["zci"])
    P.dma("sp", cc_sb[:], cc.rearrange("(a p) c -> p a c", p=128), w=["cc"])
    P.dma("sp", cs_sb[:], cs.rearrange("(a p) c -> p a c", p=128), w=["cs"])
    px = [P.ps([128, 512]) for _ in range(2)]
    ox = [P.sb([128, C]) for _ in range(2)]
    it = 0
    for j in range(NK1):
        s = it % 2; it += 1; hf = j // H
        mm(P, px[s][:, :C], mc_sb[:], ar_sb[:, j, :], True, False, r=["mc", ("ar", hf)], w=[("px", s)])
        mm(P, px[s][:, :C], ms_sb[:], ai_sb[:, j, :], False, True, r=["ms", ("ai", hf)], w=[("px", s)])
        P.op("act", lambda g, s=s: g.copy(out=ox[s][:], in_=px[s][:, :C]), r=[("px", s)], w=[("ox", s)])
        P.dma("sp", xr[:, j, :], ox[s][:], r=[("ox", s)])
    for kc in range(NTC):
        s = it % 2; it += 1
        n = 0
        for mat, mk, z, zk in ((cc_sb, "cc", zcr_sb, "zcr"), (cs_sb, "cs", zci_sb, "zci")):
            for tcn in range(NTC):
                mm(P, px[s][:, :C], mat[:, tcn, kc * 128:(kc + 1) * 128], z[:, tcn, :], n == 0, n == 2 * NTC - 1,
                   r=[mk, zk], w=[("px", s)])
                n += 1
        P.op("act", lambda g, s=s: g.copy(out=ox[s][:], in_=px[s][:, :C]), r=[("px", s)], w=[("ox", s)])
        P.dma("sp", xc[kc * 128:(kc + 1) * 128, :], ox[s][:], r=[("ox", s)])
    return P.build()


def dft_consts():
    j = np.arange(128)
    ang = 2 * np.pi * np.outer(j, j) / 128
    c128 = np.cos(ang); s128 = np.sin(ang)
    k1 = np.arange(128)[:, None]; t2 = np.arange(128)[None, :]
    ph = 2 * np.pi * (k1 * t2) / 16384.0
    j2 = np.arange(256)
    a2 = 2 * np.pi * np.outer(j2, j2) / 256
    f = np.float32
    return dict(c1=(c128 / 128).astype(f), s1=(s128 / 128).astype(f), ns1=(-s128 / 128).astype(f),
                c2=c128.astype(f), s2=s128.astype(f), twc=np.cos(ph).astype(f), tws=np.sin(ph).astype(f),
                cc=(np.cos(a2) / 16).astype(f), cs=(np.sin(a2) / 16).astype(f))


def run_fourier(run, zr, zi, zcr, zci):
    K = dft_consts()
    Zr = zr.reshape(128, 128, 512); Zi = zi.reshape(128, 128, 512)
    ins = []
    for i in range(8):
        sl = slice(16 * i, 16 * (i + 1))
        ins.append(dict(zr=np.ascontiguousarray(Zr[:, sl]), zi=np.ascontiguousarray(Zi[:, sl]), mc=K["c1"], ms=K["s1"], mns=K["ns1"],
                        twc=np.ascontiguousarray(K["twc"][:, sl]), tws=np.ascontiguousarray(K["tws"][:, sl]),
                        ntws=np.ascontiguousarray(-K["tws"][:, sl])))
    res = run("pf1", lambda: build_pf1(), ins)
    Ar = np.concatenate([r["ar"] for r in res], axis=1)
    Ai = np.concatenate([r["ai"] for r in res], axis=1)
    ArT = Ar.transpose(1, 0, 2); AiT = Ai.transpose(1, 0, 2)
    ins = []
    for i in range(8):
        sl = slice(16 * i, 16 * (i + 1))
        ins.append(dict(ar=np.ascontiguousarray(ArT[:, sl]), ai=np.ascontiguousarray(AiT[:, sl]), mc=K["c2"], ms=K["s2"],
                        zcr=zcr, zci=zci, cc=K["cc"], cs=K["cs"]))
    res = run("pf2", lambda: build_pf2(), ins)
    X = np.concatenate([r["xr"] for r in res], axis=1).reshape(16384, 512)
    return X, res[0]["xc"]


_NC_CACHE = {}


def run_prog(name, builder, ins):
    if name not in _NC_CACHE:
        _NC_CACHE[name] = builder()
    res = run_bass_kernel_spmd(_NC_CACHE[name], ins, core_ids=list(range(len(ins))))
    return [{k: np.asarray(v) for k, v in r.items()} for r in res.results]


FFN_H = 5632


P3_NOWEIGHTS = False


def build_p3(NB, TB):
    P = Prog()
    Tc = NB * TB
    KC = 16
    HC = FFN_H // 128
    xT = P.din("xT", [D_MODEL, Tc], F32)
    attnT = P.din("attnT", [1024, Tc], BF16)
    yfT = P.din("yfT", [512, Tc], F32); ybT = P.din("ybT", [512, Tc], F32); usT = P.din("usT", [512, Tc], F32)
    xfT = P.din("xfT", [512, Tc], F32)
    glu_w = P.din("glu_w", [128, 4, 512], BF16); fw = P.din("fw", [128, 4, 512], BF16)
    w_out = P.din("w_out", [4, 128, KC, 512], BF16)
    w_gate = P.din("w_gate", [HC // 4, 128, KC, 512], BF16); w_up = P.din("w_up", [HC // 4, 128, KC, 512], BF16)
    w_down = P.din("w_down", [4, 4, 128, 11, 512], BF16)
    vnames = ["gpm", "gm", "gpf", "scf", "shf", "gof", "gf"]
    vin = {n: P.din(n, [128, KC], F32) for n in vnames}
    ssm_d = P.din("ssm_d", [128, 4], F32); glu_b = P.din("glu_b", [128, 4], F32)
    xoT = P.dout("xoT", [D_MODEL, Tc], F32)

    vs = {}
    for n in vnames:
        vs[n] = P.sb([128, KC]); P.dma("sp", vs[n][:], vin[n][:], w=[n])
    d_sb = P.sb([128, 4]); gb_sb = P.sb([128, 4])
    P.dma("sp", d_sb[:], ssm_d[:], w=["ssm_d"]); P.dma("sp", gb_sb[:], glu_b[:], w=["glu_b"])
    GM = P.sb([128, KC]); G2 = P.sb([128, KC]); GF = P.sb([128, KC])
    tt(P, "dve", GM[:], vs["gpm"][:], vs["gm"][:], ALU.mult, r=["gpm", "gm"], w=["GM"])
    stt(P, G2[:], vs["scf"][:], 1.0, vs["gpf"][:], ALU.add, ALU.mult, r=["scf", "gpf"], w=["G2"])
    tt(P, "dve", GF[:], vs["gof"][:], vs["gf"][:], ALU.mult, r=["gof", "gf"], w=["GF"])
    ones = P.sb([128, 128], BF16)
    P.op("pool", lambda g: g.memset(ones[:], 1.0), w=["ones"])

    x_sb = P.sb([128, KC, TB], F32)
    cat = P.sb([128, KC, TB], BF16)
    yb_sb = P.sb([128, KC, TB], BF16)
    a_sb = P.sb([128, HC, TB], BF16)
    WB = [P.sb([128, KC, 512], BF16) for _ in range(3)]
    s_us = [P.sb([128, TB]) for _ in range(1)]; s_yf = [P.sb([128, TB]) for _ in range(1)]; s_yb = [P.sb([128, TB]) for _ in range(1)]
    s_xf = [P.sb([128, TB]) for _ in range(1)]
    e1 = P.sb([128, TB]); e2 = P.sb([128, TB]); e3 = P.sb([128, TB])
    gfp = P.sb([128, 4, TB], F32); gbf = P.sb([128, 4, TB], BF16); xfb = P.sb([128, 4, TB], BF16)
    sqc = [P.sb([128, TB], BF16) for _ in range(2)]
    rstd = P.sb([128, TB]); tmpf = [P.sb([128, TB]) for _ in range(2)]
    sg = [P.sb([128, TB]) for _ in range(2)]
    xo = [P.sb([128, TB]) for _ in range(2)]
    PS = [P.ps([128, 512]) for _ in range(8)]
    xv = xT.rearrange("(kc p) t -> p kc t", p=128)
    av = attnT.rearrange("(kc p) t -> p kc t", p=128)
    wcnt = [0]

    def wload(src_ap, shape_kc):
        s = wcnt[0] % 3; wcnt[0] += 1
        if P3_NOWEIGHTS and wcnt[0] > 3:
            return s
        P.dma("sp", WB[s][:, :shape_kc, :], src_ap, w=[("WB", s)])
        return s

    def rms_bc(nsq_key):
        rstd_from_ps(P, rstd[:], PS[4][:, :TB], D_MODEL, EPS, r=["ps4"], w=["rstd"], tmpkey=None)

    nsq = [0]

    pend = []

    def sq_acc(src_ps, src_key, idx, n, defer=False):
        s = nsq[0] % 2; nsq[0] += 1
        act(P, sqc[s][:], src_ps, AF.Square, r=[src_key], w=[("sqc", s)])
        f_ = lambda: mm(P, PS[4][:, :TB], ones[:], sqc[s][:], idx == 0, idx == n - 1, r=["ones", ("sqc", s)], w=["ps4"])
        if defer:
            pend.append(f_)
        else:
            f_()

    def flush():
        while pend:
            pend.pop(0)()

    def stageA1(b):
        tsl = slice(b * TB, (b + 1) * TB)
        P.dma("sp", cat[:, 0:8, :], av[:, :, tsl], w=[("cat", k) for k in range(8)])
        for ch in range(4):
            s = 0
            fsl = slice(ch * 128, (ch + 1) * 128)
            P.dma("sp", s_us[s][:], usT[fsl, tsl], w=[("us", s)])
            P.dma("sp", s_yf[s][:], yfT[fsl, tsl], w=[("yf", s)])
            P.dma("sp", s_yb[s][:], ybT[fsl, tsl], w=[("yb", s)])
            P.dma("sp", s_xf[s][:], xfT[fsl, tsl], w=[("xf", s)])
            stt(P, e1[:], s_us[s][:], d_sb[:, ch:ch + 1], s_yf[s][:], ALU.mult, ALU.add, r=[("us", s), "ssm_d", ("yf", s)], w=["e1"])
            tt(P, "dve", e1[:], e1[:], s_yb[s][:], ALU.add, r=["e1", ("yb", s)], w=["e1"])
            tt(P, "pool", e2[:], e1[:], e1[:], ALU.mult, r=["e1"], w=["e2"])
            ts(P, "dve", e2[:], e2[:], 0.044715, 1.0, ALU.mult, ALU.add, r=["e2"], w=["e2"])
            tt(P, "pool", e3[:], e2[:], e1[:], ALU.mult, r=["e2", "e1"], w=["e3"])
            act(P, e3[:], e3[:], AF.Sigmoid, r=["e3"], w=["e3"], scale=1.5957691216057308)
            tt(P, "dve", gfp[:, ch, :], e3[:], e1[:], ALU.mult, r=["e3", "e1"], w=[("gfp", ch)])
            P.op("act", lambda g, ch=ch: g.copy(out=gbf[:, ch, :], in_=gfp[:, ch, :]), r=[("gfp", ch)], w=[("gbf", ch)])
            P.op("act", lambda g, ch=ch, s=s: g.copy(out=xfb[:, ch, :], in_=s_xf[s][:]), r=[("xf", s)], w=[("xfb", ch)])

    stageA1(0)
    for b in range(NB):
        tsl = slice(b * TB, (b + 1) * TB)
        P.dma("sp", x_sb[:], xv[:, :, tsl], w=["x"])
        sgw = wload(glu_w[:], 4)
        sfw = wload(fw[:], 4)
        for co in range(4):
            pz = PS[5 + co % 2]; pk = ("ps", 5 + co % 2)
            for ci in range(4):
                mm(P, pz[:, :TB], WB[sgw][:, ci, co * 128:(co + 1) * 128], gbf[:, ci, :], ci == 0, ci == 3,
                   r=[("WB", sgw), ("gbf", ci)], w=[pk])
            act(P, e2[:], pz[:, :TB], AF.Sigmoid, r=[pk, "glu_b"], w=["e2"], bias=gb_sb[:, co:co + 1])
            tt(P, "dve", cat[:, 8 + co, :], e2[:], gfp[:, co, :], ALU.mult, r=["e2", ("gfp", co)], w=[("cat", 8 + co)])
        for co in range(4):
            pz = PS[5 + co % 2]; pk = ("ps", 5 + co % 2)
            for ci in range(4):
                mm(P, pz[:, :TB], WB[sfw][:, ci, co * 128:(co + 1) * 128], xfb[:, ci, :], ci == 0, ci == 3,
                   r=[("WB", sfw), ("xfb", ci)], w=[pk])
            P.op("act", lambda g, co=co, pz=pz: g.copy(out=cat[:, 12 + co, :], in_=pz[:, :TB]), r=[pk], w=[("cat", 12 + co)])
        for og in range(4):
            sw = wload(w_out[og], KC)
            for o4 in range(4):
                oc = og * 4 + o4
                pa = PS[oc % 2]; pk = ("ps", oc % 2)
                for kc in range(KC):
                    mm(P, pa[:, :TB], WB[sw][:, kc, o4 * 128:(o4 + 1) * 128], cat[:, kc, :], kc == 0, kc == KC - 1,
                       r=[("WB", sw), ("cat", kc)], w=[pk])
                flush()
                P.op("dve", lambda g, oc=oc, pa=pa: g.tensor_copy(out=yb_sb[:, oc, :], in_=pa[:, :TB]), r=[pk], w=[("yb_sb", oc)])
                sq_acc(pa[:, :TB], pk, oc, KC, defer=True)
        flush()
        rms_bc(None)
        for oc in range(KC):
            s = oc % 2
            stt(P, tmpf[s][:], yb_sb[:, oc, :], GM[:, oc:oc + 1], rstd[:], ALU.mult, ALU.mult,
                r=[("yb_sb", oc), "GM", "rstd"], w=[("tmpf", s)])
            tt(P, "pool" if oc % 2 else "dve", x_sb[:, oc, :], x_sb[:, oc, :], tmpf[s][:], ALU.add, r=["x", ("tmpf", s)], w=[("x1", oc)])
        for kc in range(KC):
            sq_acc(x_sb[:, kc, :], ("x1", kc), kc, KC)
        rms_bc(None)
        for kc in range(KC):
            s = kc % 2
            stt(P, tmpf[s][:], x_sb[:, kc, :], G2[:, kc:kc + 1], rstd[:], ALU.mult, ALU.mult,
                r=[("x1", kc), "G2", "rstd"], w=[("tmpf", s)])
            act(P, cat[:, kc, :], tmpf[s][:], AF.Identity, r=[("tmpf", s), "shf"], w=[("cat", kc)], bias=vs["shf"][:, kc:kc + 1])
        for hg in range(HC // 4):
            swg = wload(w_gate[hg], KC)
            swu = wload(w_up[hg], KC)
            for h4 in range(4):
                hc = hg * 4 + h4
                pg = PS[(hc % 2) * 2]; pgk = ("ps", (hc % 2) * 2)
                pu = PS[(hc % 2) * 2 + 1]; puk = ("ps", (hc % 2) * 2 + 1)
                for kc in range(KC):
                    mm(P, pg[:, :TB], WB[swg][:, kc, h4 * 128:(h4 + 1) * 128], cat[:, kc, :], kc == 0, kc == KC - 1,
                       r=[("WB", swg), ("cat", kc)], w=[pgk])
                for kc in range(KC):
                    mm(P, pu[:, :TB], WB[swu][:, kc, h4 * 128:(h4 + 1) * 128], cat[:, kc, :], kc == 0, kc == KC - 1,
                       r=[("WB", swu), ("cat", kc)], w=[puk])
                s = hc % 2
                act(P, sg[s][:], pg[:, :TB], AF.Silu, r=[pgk], w=[("sg", s)])
                tt(P, "dve", a_sb[:, hc, :], pu[:, :TB], sg[s][:], ALU.mult, r=[puk, ("sg", s)], w=[("a", hc)])
        if b + 1 < NB:
            stageA1(b + 1)
        for og in range(4):
            for q in range(4):
                sw = wload(w_down[og, q], 11)
                for h in range(11):
                    hc = q * 11 + h
                    for o4 in range(4):
                        mm(P, PS[o4][:, :TB], WB[sw][:, h, o4 * 128:(o4 + 1) * 128], a_sb[:, hc, :], hc == 0, hc == HC - 1,
                           r=[("WB", sw), ("a", hc)], w=[("ps", o4)])
            for o4 in range(4):
                oc = og * 4 + o4
                P.op("dve", lambda g, oc=oc, o4=o4: g.tensor_copy(out=yb_sb[:, oc, :], in_=PS[o4][:, :TB]), r=[("ps", o4)], w=[("yb_sb", oc)])
                sq_acc(PS[o4][:, :TB], ("ps", o4), oc, KC)
        rms_bc(None)
        for oc in range(KC):
            s = oc % 2
            stt(P, tmpf[s][:], yb_sb[:, oc, :], GF[:, oc:oc + 1], rstd[:], ALU.mult, ALU.mult,
                r=[("yb_sb", oc), "GF", "rstd"], w=[("tmpf", s)])
            tt(P, "pool" if oc % 2 else "dve", xo[s][:], x_sb[:, oc, :], tmpf[s][:], ALU.add, r=[("x1", oc), ("tmpf", s)], w=[("xo", s)])
            P.dma("sp", xoT[oc * 128:(oc + 1) * 128, tsl], xo[s][:], r=[("xo", s)])
    return P.build()


def _v2(v):
    return np.ascontiguousarray(np.asarray(v, np.float32).reshape(-1, 128).T)


def _cat(res, name, axis):
    return np.concatenate([r[name] for r in res], axis=axis)


BIG_W = ["w_in", "ssm_glu_w", "fourier_w", "w_out", "ffn_w_gate", "ffn_w_up", "ffn_w_down"]


def cast_weights(inp):
    parts = []
    shapes = []
    for l in range(2):
        for n in BIG_W:
            a = np.asarray(inp[n][l], np.float32)
            parts.append(a.ravel()); shapes.append((l, n, a.shape))
    flat = np.concatenate(parts)
    tot = flat.size
    per = tot // (8 * 128)
    assert per * 8 * 128 == tot and per % 5120 == 0, (tot, per)
    flat = flat.reshape(8, 128, per)
    res = run_prog("cast", lambda: build_cast(per, 5120), [{"src": flat[i]} for i in range(8)])
    out = np.concatenate([r["dst"].reshape(-1) for r in res])
    W = [dict(), dict()]
    off = 0
    for (l, n, shp) in shapes:
        sz = int(np.prod(shp))
        W[l][n] = out[off:off + sz].reshape(shp); off += sz
    return W


def kernel(**inp):
    f = np.float32
    x = np.asarray(inp["x"], f)[0]
    xc = np.asarray(inp["ctx"], f)[0]
    T = x.shape[0]; TC = xc.shape[0]
    Wb = cast_weights(inp)
    c2 = np.stack([np.asarray(inp["c"], f)[0], np.asarray(inp["c_ctx"], f)], 0)
    c2T = np.ascontiguousarray(c2.reshape(2, 16, 128).transpose(2, 1, 0))
    ada_w = np.asarray(inp["ada_w"], f); ada_b = np.asarray(inp["ada_b"], f)
    CW = 1536
    ins = [dict(c2T=c2T, w=np.ascontiguousarray(ada_w[:, :, i * CW:(i + 1) * CW]),
                b2=np.ascontiguousarray(np.repeat(ada_b[:, None, i * CW:(i + 1) * CW], 2, axis=1))) for i in range(8)]
    mod = _cat(run_prog("adaln", lambda: build_adaln(), ins), "mod", 2)
    cosT, sinT = rope_tables(np.arange(T))
    cosC = np.ones((128, TC), f); sinC = np.zeros((128, TC), f)
    rotT = rope_rotT(); cd, sdn = chan_dft()
    Tc = T // 8; Cc = TC // 8
    for l in range(2):
        need_ctx = l == 0
        m = mod[l]
        sh_m, sc_m, g_m, sh_f, sc_f, g_f = [m[:, k * 2048:(k + 1) * 2048] for k in range(6)]
        qn = np.asarray(inp["q_norm"][l], f)[:, None].copy(); kn = np.asarray(inp["k_norm"][l], f)[:, None].copy()
        com = dict(gam=_v2(inp["norm_mix_pre"][l]), w_in=Wb[l]["w_in"], qn=qn, kn=kn, rotT=rotT, cd=cd, sdn=sdn)
        ins = [dict(com, xT=np.ascontiguousarray(x[i * Tc:(i + 1) * Tc].T), scv=_v2(sc_m[0]), shv=_v2(sh_m[0]),
                    cosT=np.ascontiguousarray(cosT[:, i * Tc:(i + 1) * Tc]), sinT=np.ascontiguousarray(sinT[:, i * Tc:(i + 1) * Tc]))
               for i in range(8)]
        r1 = run_prog("p1", lambda: build_p1(4, 512), ins)
        ins = [dict(com, xT=np.ascontiguousarray(xc[i * Cc:(i + 1) * Cc].T), scv=_v2(sc_m[1]), shv=_v2(sh_m[1]),
                    cosT=np.ascontiguousarray(cosC[:, i * Cc:(i + 1) * Cc]), sinT=np.ascontiguousarray(sinC[:, i * Cc:(i + 1) * Cc]))
               for i in range(8)]
        r1c = run_prog("p1c", lambda: build_p1(1, 32), ins)
        L1 = {n: _cat(r1, n, 1) for n in ("qT", "kT", "vT", "usT", "zrT", "ziT")}
        C1 = {n: _cat(r1c, n, 1) for n in ("qT", "kT", "vT", "usT", "zrT", "ziT")}
        ins = []
        for i in range(8):
            j = i // 4
            ins.append(dict(qT=np.ascontiguousarray(L1["qT"][i * 128:(i + 1) * 128]),
                            kT=np.ascontiguousarray(np.concatenate([L1["kT"][j * 128:(j + 1) * 128], C1["kT"][j * 128:(j + 1) * 128]], 1)),
                            v=np.ascontiguousarray(np.concatenate([L1["vT"][j * 128:(j + 1) * 128], C1["vT"][j * 128:(j + 1) * 128]], 1).T)))
        attnT = np.ascontiguousarray(np.concatenate([r["o"] for r in run_prog("pa", lambda: build_attn2(T, T + TC, 512), ins)], 1).T)
        if need_ctx:
            ins = []
            for i in range(8):
                j = i // 4
                ins.append(dict(qT=np.ascontiguousarray(C1["qT"][i * 128:(i + 1) * 128]),
                                kT=np.ascontiguousarray(C1["kT"][j * 128:(j + 1) * 128]),
                                v=np.ascontiguousarray(C1["vT"][j * 128:(j + 1) * 128].T)))
            attncT = np.ascontiguousarray(np.concatenate([r["o"] for r in run_prog("pac", lambda: build_attn2(TC, TC, TC), ins)], 1).T)
        us = L1["usT"]; usc = C1["usT"]
        fwd = np.concatenate([usc, us], 1)
        bwd = np.concatenate([usc[:, ::-1], us[:, ::-1]], 1)
        ins = []
        for i in range(8):
            rows = slice(64 * i, 64 * (i + 1))
            u_core = np.ascontiguousarray(np.concatenate([fwd[rows], bwd[rows]], 0))
            idx = [(c // 2, 4 * i + 2 * (c % 2) + gi) for c in range(4) for gi in range(2)]
            pk = ssm_pack(np.stack([np.asarray(inp["ssm_lam_re"], f)[l, d, g] for d, g in idx]),
                          np.stack([np.asarray(inp["ssm_lam_im"], f)[l, d, g] for d, g in idx]),
                          np.array([np.asarray(inp["ssm_log_dt"], f)[l, d, g] for d, g in idx], f),
                          np.stack([np.asarray(inp["ssm_b_re"], f)[l, d, g] for d, g in idx]),
                          np.stack([np.asarray(inp["ssm_b_im"], f)[l, d, g] for d, g in idx]),
                          np.stack([np.asarray(inp["ssm_c_re"], f)[l, d, g] for d, g in idx]),
                          np.stack([np.asarray(inp["ssm_c_im"], f)[l, d, g] for d, g in idx]))
            ins.append(dict(pk, u=u_core))
        rs = run_prog("ps", lambda: build_ssm(T + TC, 512), ins)
        yf_all = np.concatenate([r["y"][0:64] for r in rs], 0)
        yb_all = np.concatenate([r["y"][64:128] for r in rs], 0)
        yfT, yfcT = yf_all[:, TC:], yf_all[:, :TC]
        ybT, ybcT = yb_all[:, TC:][:, ::-1], yb_all[:, :TC][:, ::-1]
        X, Xc = run_fourier(run_prog, np.ascontiguousarray(L1["zrT"].T), np.ascontiguousarray(L1["ziT"].T),
                            np.ascontiguousarray(C1["zrT"].T), np.ascontiguousarray(C1["ziT"].T))
        xfT = X.T; xfcT = Xc.T
        comw = dict(glu_w=tile_kc(Wb[l]["ssm_glu_w"])[0], fw=tile_kc(Wb[l]["fourier_w"])[0], w_out=tile_kc(Wb[l]["w_out"]),
                    w_gate=tile_kc(Wb[l]["ffn_w_gate"]), w_up=tile_kc(Wb[l]["ffn_w_up"]), w_down=tile_down(Wb[l]["ffn_w_down"]), gpm=_v2(inp["norm_mix_post"][l]),
                    gpf=_v2(inp["norm_ffn_pre"][l]), gof=_v2(inp["norm_ffn_post"][l]),
                    ssm_d=_v2(inp["ssm_d"][l]), glu_b=_v2(inp["ssm_glu_b"][l]))

        def p3_ins(xs, at, yf_, yb_, us_, xf_, n, v):
            out = []
            for i in range(8):
                sl = slice(i * n, (i + 1) * n)
                out.append(dict(comw, xT=np.ascontiguousarray(xs[sl].T), attnT=np.ascontiguousarray(at[:, sl]),
                                yfT=np.ascontiguousarray(yf_[:, sl]), ybT=np.ascontiguousarray(yb_[:, sl]),
                                usT=np.ascontiguousarray(us_[:, sl]), xfT=np.ascontiguousarray(xf_[:, sl]),
                                gm=_v2(g_m[v]), scf=_v2(sc_f[v]), shf=_v2(sh_f[v]), gf=_v2(g_f[v])))
            return out
        r3 = run_prog("p3", lambda: build_p3(4, 512), p3_ins(x, attnT, yfT, ybT, us, xfT, Tc, 0))
        x_new = np.ascontiguousarray(_cat(r3, "xoT", 1).T)
        if need_ctx:
            r3c = run_prog("p3c", lambda: build_p3(1, 32), p3_ins(xc, attncT, yfcT, ybcT, usc, xfcT, Cc, 1))
            xc = np.ascontiguousarray(_cat(r3c, "xoT", 1).T)
        x = x_new
    return x[None].astype(np.float32)


def build_attn2(NQ, NK, QB):
    P = Prog()
    NQB = NQ // QB
    NKT = NK // 128
    NPR = NKT // 2
    QC = QB // 128
    qT = P.din("qT", [128, NQ], BF16)
    kT = P.din("kT", [128, NK], BF16)
    v = P.din("v", [NK, 128], BF16)
    o = P.dout("o", [NQ, 128], BF16)
    q_sb = P.sb([128, NQ], BF16); k_sb = P.sb([128, NK], BF16); v_sb = P.sb([128, NKT, 130], BF16)
    KCH = 13 if NKT % 13 == 0 else NKT
    vv = v.rearrange("(kt p) d -> p kt d", p=128)
    QCH = 4 if NQB % 4 == 0 else NQB
    P.dma("sp", q_sb[:, 0:QCH * QB], qT[:, 0:QCH * QB], w=[("Q", 0)])
    for c in range(NKT // KCH):
        P.dma("sp", k_sb[:, c * KCH * 128:(c + 1) * KCH * 128], kT[:, c * KCH * 128:(c + 1) * KCH * 128], w=[("K", c)])
        P.op("pool", lambda g, c=c: g.memset(v_sb[:, c * KCH:(c + 1) * KCH, 128:130], 1.0), w=[("V1", c)])
        P.dma("sp", v_sb[:, c * KCH:(c + 1) * KCH, 0:128], vv[:, c * KCH:(c + 1) * KCH, :], w=[("V", c)])
    for c in range(1, NQB // QCH):
        P.dma("sp", q_sb[:, c * QCH * QB:(c + 1) * QCH * QB], qT[:, c * QCH * QB:(c + 1) * QCH * QB], w=[("Q", c)])
    S = [P.ps([128, 2, 512], F32) for _ in range(2)]
    O = [[P.ps([128, 512], F32) for _ in range(2)] for _ in range(2)]
    PT = [P.sb([128, 2, QB], BF16) for _ in range(2)]
    rs = P.sb([128, 4], F32)
    o_sb = [P.sb([128, QC, 128], BF16) for _ in range(2)]
    ov = o.rearrange("(qb qc p) d -> qb p qc d", qc=QC, p=128)
    scale = 128 ** -0.5
    items = [(qb, pr) for qb in range(NQB) for pr in range(NPR)]

    def emit_S(i):
        qb, pr = items[i]
        s = i % 2
        for j in range(2):
            kt = 2 * pr + j
            mm(P, S[s][:, j, :QB], k_sb[:, kt * 128:(kt + 1) * 128], q_sb[:, qb * QB:(qb + 1) * QB], True, True,
               r=[("K", kt // KCH), ("Q", qb // QCH)], w=[("S", s)])

    emit_S(0)
    for i, (qb, pr) in enumerate(items):
        if i + 1 < len(items):
            emit_S(i + 1)
        s = i % 2
        ob = qb % 2
        act(P, PT[s][:, :, :], S[s][:, :, :QB], AF.Exp, r=[("S", s)], w=[("PT", s)], scale=scale)
        for j in range(2):
            kt = 2 * pr + j
            for qc in range(QC):
                bank = O[ob][qc // 2]
                col = (qc % 2) * 129
                first = (kt == 0 and qc % 2 == 0)
                P.op("pe", lambda g, bank=bank, col=col, s=s, j=j, qc=qc, kt=kt, first=first: g.matmul(
                    bank[:, col:col + 129], PT[s][:, j, qc * 128:(qc + 1) * 128], v_sb[:, kt, 0:129],
                    start=first, stop=(kt == NKT - 1), skip_group_check=True),
                    r=[("V", kt // KCH), ("V1", kt // KCH), ("PT", s)], w=[("O", ob, qc // 2)])
        if pr == NPR - 1:
            for qc in range(QC):
                bank = O[ob][qc // 2]
                col = (qc % 2) * 129
                bk = ("O", ob, qc // 2)
                P.op("dve", lambda g, bank=bank, col=col, qc=qc: g.reciprocal(out=rs[:, qc:qc + 1], in_=bank[:, col + 128:col + 129]),
                     r=[bk], w=[("rs", qc)])
                ts(P, "dve", o_sb[ob][:, qc, :], bank[:, col:col + 128], rs[:, qc:qc + 1], None, ALU.mult, None,
                   r=[bk, ("rs", qc)], w=[("o_sb", ob)])
            P.dma("sp", ov[qb], o_sb[ob][:], r=[("o_sb", ob)])
    return P.build()


def tile_kc(w):
    K_, C_ = w.shape
    return np.ascontiguousarray(w.reshape(K_ // 128, 128, C_ // 512, 512).transpose(2, 1, 0, 3))


def tile_down(w):
    return np.ascontiguousarray(w.reshape(4, 11, 128, 4, 512).transpose(3, 0, 2, 1, 4))
```

```python
from contextlib import ExitStack
import numpy as np
import concourse.bass as bass
import concourse.mybir as mybir
from concourse.bass_utils import run_bass_kernel_spmd

F32 = mybir.dt.float32
BF16 = mybir.dt.bfloat16
I32 = mybir.dt.int32
AF = mybir.ActivationFunctionType
ALU = mybir.AluOpType
AX = mybir.AxisListType


STRICT_WAR = True


class Prog:
    def __init__(self):
        self.nc = bass.Bass("TRN2", target_bir_lowering=False)
        self.es = ExitStack()
        self.ops = []
        self.last_w = {}
        self.readers = {}
        self.n_t = 0
        self.psum_keys = set()

    def din(self, name, shape, dt=F32):
        return self.nc.dram_tensor(name, list(shape), dt, kind="ExternalInput").ap()

    def dout(self, name, shape, dt=F32):
        return self.nc.dram_tensor(name, list(shape), dt, kind="ExternalOutput").ap()

    def sb(self, shape, dt=F32, name=None):
        self.n_t += 1
        return self.es.enter_context(self.nc.sbuf_tensor(name or f"sb{self.n_t}", list(shape), dt))

    def ps(self, shape, dt=F32, name=None):
        self.n_t += 1
        return self.es.enter_context(self.nc.psum_tensor(name or f"ps{self.n_t}", list(shape), dt))

    def op(self, eng, fn, r=(), w=(), dma=False, semkey=None):
        i = len(self.ops)
        raw = set()
        war = set()
        if eng == "pe" and not dma:
            self.psum_keys.update(w)
        for k in r:
            if k in self.last_w:
                raw.add(self.last_w[k])
            if k in self.psum_keys:
                for j in self.readers.get(k, ()):
                    if self.ops[j]["eng"] != eng:
                        raw.add(j)
        for k in w:
            if k in self.last_w:
                raw.add(self.last_w[k])
            war.update(self.readers.get(k, ()))
        self.ops.append(dict(eng=eng, fn=fn, raw=raw, war=war - raw, dma=dma, semkey=semkey))
        for k in r:
            self.readers.setdefault(k, []).append(i)
        for k in w:
            self.last_w[k] = i
            self.readers[k] = []
        return i

    def dma(self, q, out, in_, r=(), w=(), semkey=None):
        if semkey is None:
            semkey = ("ld", w[0]) if w else ("st", r[0])
        return self.op(q, lambda e: e.dma_start(out=out, in_=in_), r=r, w=w, dma=True, semkey=semkey)

    def build(self):
        nc = self.nc
        ops = self.ops
        n = len(ops)
        for i, o in enumerate(ops):
            deps = set()
            for j in o["raw"]:
                pj = ops[j]
                if (not pj["dma"]) and pj["eng"] == o["eng"] and o["eng"] == "pe" and not o["dma"]:
                    continue
                deps.add(j)
            for j in o["war"]:
                pj = ops[j]
                if (not pj["dma"]) and pj["eng"] == o["eng"] and not o["dma"] and (o["eng"] == "pe" or not STRICT_WAR):
                    continue
                if (not pj["dma"]) and pj["eng"] == o["eng"] and o["dma"]:
                    pass
                deps.add(j)
            o["deps"] = deps
        has_dep = [False] * n
        for o in ops:
            for j in o["deps"]:
                has_dep[j] = True
        engs = ["pe", "act", "dve", "pool", "sp"]
        sems = {e: self.es.enter_context(nc.semaphore(f"s_{e}")) for e in engs}
        cnt = {e: 0 for e in engs}
        dsem = {}
        for i, o in enumerate(ops):
            if o["dma"]:
                k = o["semkey"]
                if k not in dsem:
                    dsem[k] = [self.es.enter_context(nc.semaphore(f"d{len(dsem)}")), 0]
                dsem[k][1] += 16
                o["sem"] = dsem[k][0]
                o["ticket"] = dsem[k][1]
                o["inc"] = 16
            elif has_dep[i]:
                cnt[o["eng"]] += 1
                o["sem"] = sems[o["eng"]]
                o["ticket"] = cnt[o["eng"]]
                o["inc"] = 1
            else:
                o["sem"] = None
        self.n_sems = len(dsem) + 5
        per = {e: [i for i, o in enumerate(ops) if o["eng"] == e] for e in engs}

        def replay(e):
            def run(eng):
                seen = {}
                for i in per[e]:
                    o = ops[i]
                    need = {}
                    for j in o["deps"]:
                        pj = ops[j]
                        s = pj["sem"]
                        key = id(s)
                        if seen.get(key, 0) >= pj["ticket"]:
                            continue
                        if key not in need or need[key][1] < pj["ticket"]:
                            need[key] = (s, pj["ticket"])
                    for key, (s, t) in need.items():
                        eng.wait_ge(s, t)
                        seen[key] = t
                    ins = o["fn"](eng)
                    if o["sem"] is not None:
                        ins.then_inc(o["sem"], o["inc"])
                if e == "sp":
                    for k, (s, t) in dsem.items():
                        if seen.get(id(s), 0) < t:
                            eng.wait_ge(s, t)
            return run

        with nc.Block() as block:
            block.sync(replay("sp"))
            block.tensor(replay("pe"))
            block.scalar(replay("act"))
            block.vector(replay("dve"))
            block.gpsimd(replay("pool"))
        self.es.close()
        return nc


def build_cast(N, CH=4096):
    P = Prog()
    src = P.din("src", [128, N], F32)
    dst = P.dout("dst", [128, N], BF16)
    nb = N // CH
    NS = 3
    fin = [P.sb([128, CH], F32) for _ in range(NS)]
    fout = [P.sb([128, CH], BF16) for _ in range(NS)]
    engs = ["dve", "act", "pool"]
    for b in range(nb):
        s = b % NS
        P.dma("sp", fin[s][:], src[:, b * CH:(b + 1) * CH], w=[("fin", s)])
        e = engs[b % 3]
        if e == "act":
            P.op("act", lambda g, s=s: g.copy(out=fout[s][:], in_=fin[s][:]), r=[("fin", s)], w=[("fout", s)])
        else:
            P.op(e, lambda g, s=s: g.tensor_copy(out=fout[s][:], in_=fin[s][:]), r=[("fin", s)], w=[("fout", s)])
        P.dma("sp", dst[:, b * CH:(b + 1) * CH], fout[s][:], r=[("fout", s)])
    return P.build()


def mm(P, out, lhsT, rhs, start, stop, r, w):
    P.op("pe", lambda g: g.matmul(out, lhsT, rhs, start=start, stop=stop), r=r, w=w)


def act(P, out, in_, func, r, w, bias=None, scale=None):
    kw = {}
    if bias is not None:
        kw["bias"] = bias
    if scale is not None:
        kw["scale"] = scale
    P.op("act", lambda g: g.activation(out=out, in_=in_, func=func, **kw), r=r, w=w)


def tt(P, eng, out, a, b, op, r, w):
    P.op(eng, lambda g: g.tensor_tensor(out=out, in0=a, in1=b, op=op), r=r, w=w)


def stt(P, out, in0, scalar, in1, op0, op1, r, w):
    P.op("dve", lambda g: g.scalar_tensor_tensor(out=out, in0=in0, scalar=scalar, in1=in1, op0=op0, op1=op1), r=r, w=w)


def ts(P, eng, out, in0, s1, s2, op0, op1, r, w):
    if op1 is None:
        P.op(eng, lambda g: g.tensor_scalar(out=out, in0=in0, scalar1=s1, scalar2=None, op0=op0), r=r, w=w)
    else:
        P.op(eng, lambda g: g.tensor_scalar(out=out, in0=in0, scalar1=s1, scalar2=s2, op0=op0, op1=op1), r=r, w=w)


def rstd_from_ps(P, out_sb, in_ps, n, eps, r, w, tmpkey):
    act(P, out_sb, in_ps, AF.Ln, r=list(r) + ["epsc"], w=w, bias=eps_ap(P, eps), scale=1.0 / n)
    act(P, out_sb, out_sb, AF.Exp, r=w, w=w, scale=-0.5)


def eps_ap(P, eps):
    if not hasattr(P, "_eps"):
        P._eps = P.sb([128, 1], F32, name="epsc")
        P.op("pool", lambda g: g.memset(P._eps[:], float(eps)), w=["epsc"])
    return P._eps[:]


def build_adaln(NL=2, D=2048, CW=1536):
    P = Prog()
    KC = D // 128
    c2T = P.din("c2T", [128, KC, 2], F32)
    w = P.din("w", [NL, D, CW], F32)
    b2 = P.din("b2", [NL, 2, CW], F32)
    mod = P.dout("mod", [NL, 2, CW], F32)
    cs = P.sb([128, KC, 2], F32)
    ss = P.sb([128, KC, 2], F32)
    bt = P.sb([2, NL, CW], F32)
    P.dma("sp", cs[:], c2T[:], w=["cs"])
    for l in range(NL):
        P.dma("sp", bt[:, l, :], b2[l], w=[("bt", l)])
    act(P, ss[:], cs[:], AF.Silu, r=["cs"], w=["ss"])
    wt = [P.sb([128, KC, 512], F32) for _ in range(2)]
    pst = [P.ps([128, 512], F32) for _ in range(2)]
    ot = [P.sb([2, 512], F32) for _ in range(2)]
    it = 0
    for l in range(NL):
        wv = w[l].rearrange("(kc p) c -> p kc c", p=128)
        for cb in range(CW // 512):
            s = it % 2
            P.dma("sp", wt[s][:], wv[:, :, cb * 512:(cb + 1) * 512], w=[("wt", s)])
            for kc in range(KC):
                mm(P, pst[s][0:2, :], ss[:, kc, :], wt[s][:, kc, :], kc == 0, kc == KC - 1,
                   r=["ss", ("wt", s)], w=[("ps", s)])
            tt(P, "dve", ot[s][:], pst[s][0:2, :], bt[:, l, cb * 512:(cb + 1) * 512], ALU.add,
               r=[("ps", s), ("bt", l)], w=[("ot", s)])
            P.dma("sp", mod[l, :, cb * 512:(cb + 1) * 512], ot[s][:], r=[("ot", s)])
            it += 1
    return P.build()


D_MODEL = 2048
IN_W = 2560
EPS = 1e-6


def build_p1(NB, TB):
    P = Prog()
    Tc = NB * TB
    KC = 16
    xT = P.din("xT", [D_MODEL, Tc], F32)
    gam = P.din("gam", [128, KC], F32)
    scv = P.din("scv", [128, KC], F32)
    shv = P.din("shv", [128, KC], F32)
    w_in = P.din("w_in", [D_MODEL, IN_W], BF16)
    qn = P.din("qn", [128, 1], F32)
    kn = P.din("kn", [128, 1], F32)
    cosT = P.din("cosT", [128, Tc], F32)
    sinT = P.din("sinT", [128, Tc], F32)
    rotT = P.din("rotT", [128, 128], F32)
    cd = P.din("cd", [128, 128], F32)
    sdn = P.din("sdn", [128, 128], F32)
    qT = P.dout("qT", [1024, Tc], BF16)
    kT = P.dout("kT", [256, Tc], BF16)
    vT = P.dout("vT", [256, Tc], BF16)
    usT = P.dout("usT", [512, Tc], F32)
    zrT = P.dout("zrT", [512, Tc], F32)
    ziT = P.dout("ziT", [512, Tc], F32)

    W = P.sb([128, KC, IN_W], BF16)
    wv = w_in.rearrange("(kc p) c -> p kc c", p=128)
    for i in range(4):
        P.dma("sp", W[:, i * 4:(i + 1) * 4, :], wv[:, i * 4:(i + 1) * 4, :], w=[("W", i)])
    Wk = [("W", i) for i in range(4)]
    g_sb = P.sb([128, KC], F32); sc_sb = P.sb([128, KC], F32); sh_sb = P.sb([128, KC], F32)
    G = P.sb([128, KC], F32)
    qn_sb = P.sb([128, 1], F32); kn_sb = P.sb([128, 1], F32)
    rot_sb = P.sb([128, 128], F32); cd_sb = P.sb([128, 128], F32); sd_sb = P.sb([128, 128], F32)
    ones = P.sb([128, 128], BF16)
    P.dma("sp", g_sb[:], gam[:], w=["g"]); P.dma("sp", sc_sb[:], scv[:], w=["sc"]); P.dma("sp", sh_sb[:], shv[:], w=["sh"])
    P.dma("sp", qn_sb[:], qn[:], w=["qn"]); P.dma("sp", kn_sb[:], kn[:], w=["kn"])
    P.dma("sp", rot_sb[:], rotT[:], w=["rot"]); P.dma("sp", cd_sb[:], cd[:], w=["cd"]); P.dma("sp", sd_sb[:], sdn[:], w=["sd"])
    P.op("pool", lambda g: g.memset(ones[:], 1.0), w=["ones"])
    stt(P, G[:], sc_sb[:], 1.0, g_sb[:], ALU.add, ALU.mult, r=["g", "sc"], w=["G"])

    x_sb = [P.sb([128, KC, TB], F32) for _ in range(2)]
    sqs = [P.sb([128, TB], BF16) for _ in range(2)]
    h = P.sb([128, KC, TB], BF16)
    tmp = [P.sb([128, TB], F32) for _ in range(2)]
    rstd = [P.sb([128, TB], F32) for _ in range(2)]
    cs_b = [P.sb([128, TB], F32) for _ in range(2)]; sn_b = [P.sb([128, TB], F32) for _ in range(2)]
    sqh = P.sb([128, TB], BF16)
    r2 = P.sb([128, TB], F32)
    qf = P.sb([128, TB], F32)
    t1 = P.sb([128, TB], F32); t2 = P.sb([128, TB], F32)
    ob = [P.sb([128, TB], BF16) for _ in range(2)]
    of = [P.sb([128, TB], F32) for _ in range(3)]
    ufb = [P.sb([128, TB], F32) for _ in range(2)]
    acc = [P.ps([128, 512], F32) for _ in range(3)]
    ps_ss = P.ps([128, 512], F32); ps_s2 = P.ps([128, 512], F32); ps_rot = P.ps([128, 512], F32)
    ps_z = [P.ps([128, 512], F32) for _ in range(2)]
    xv = xT.rearrange("(kc p) t -> p kc t", p=128)
    nb_o = 0; nf_o = 0; na = 0; nz = 0
    nsq = [0]

    def p_load(b):
        tsl = slice(b * TB, (b + 1) * TB); pb = b % 2
        P.dma("sp", x_sb[pb][:], xv[:, :, tsl], w=[("x", pb)])
        P.dma("sp", cs_b[pb][:], cosT[:, tsl], w=[("cs", pb)]); P.dma("sp", sn_b[pb][:], sinT[:, tsl], w=[("sn", pb)])

    def p_stats(b):
        pb = b % 2
        for kc in range(KC):
            q = nsq[0] % 2; nsq[0] += 1
            act(P, sqs[q][:], x_sb[pb][:, kc, :], AF.Square, r=[("x", pb)], w=[("sqs", q)])
            mm(P, ps_ss[:, :TB], ones[:], sqs[q][:], kc == 0, kc == KC - 1, r=["ones", ("sqs", q)], w=["ps_ss"])
        rstd_from_ps(P, rstd[pb][:], ps_ss[:, :TB], D_MODEL, EPS, r=["ps_ss"], w=[("rstd", pb)], tmpkey=None)

    def p_h(b):
        pb = b % 2
        for kc in range(KC):
            s_ = kc % 2
            stt(P, tmp[s_][:], x_sb[pb][:, kc, :], G[:, kc:kc + 1], rstd[pb][:], ALU.mult, ALU.mult,
                r=[("x", pb), "G", ("rstd", pb)], w=[("tmp", s_)])
            act(P, h[:, kc, :], tmp[s_][:], AF.Identity, r=[("tmp", s_), "sh"], w=[("h", kc)], bias=sh_sb[:, kc:kc + 1])

    p_load(0); p_stats(0); p_h(0)
    for b in range(NB):
        tsl = slice(b * TB, (b + 1) * TB)
        cs_sb = cs_b[b % 2]; sn_sb = sn_b[b % 2]; csk = ("cs", b % 2); snk = ("sn", b % 2)
        if b + 1 < NB:
            p_load(b + 1)
        hk = [("h", kc) for kc in range(KC)]
        NCC = IN_W // 128
        st = {}

        def stage1(cc):
            nonlocal na, nb_o, nf_o, nz
            a = na % 3; na += 1
            for kc in range(KC):
                mm(P, acc[a][:, :TB], W[:, kc, cc * 128:(cc + 1) * 128], h[:, kc, :], kc == 0, kc == KC - 1,
                   r=[Wk[kc // 4], ("h", kc)], w=[("acc", a)])
            A = acc[a][:, :TB]
            st[cc] = (a, A)
            if cc < 10:
                return
            if cc < 12:
                o = nb_o % 2; nb_o += 1
                P.op("act", lambda g, o=o, A=A: g.copy(out=ob[o][:], in_=A), r=[("acc", a)], w=[("ob", o)])
                P.dma("sp", vT[(cc - 10) * 128:(cc - 9) * 128, tsl], ob[o][:], r=[("ob", o)])
            elif cc < 16:
                o = nf_o % 3; nf_o += 1
                P.op("act", lambda g, o=o, A=A: g.copy(out=of[o][:], in_=A), r=[("acc", a)], w=[("of", o)])
                P.dma("sp", usT[(cc - 12) * 128:(cc - 11) * 128, tsl], of[o][:], r=[("of", o)])
            else:
                o = cc % 2
                P.op("act", lambda g, o=o, A=A: g.copy(out=ufb[o][:], in_=A), r=[("acc", a)], w=[("ufb", o)])
                st[cc] = (a, A, o)

        def stage2(cc):
            nonlocal nf_o, nz
            if cc < 10:
                a, A = st[cc]
                gn, gk = (qn_sb, "qn") if cc < 8 else (kn_sb, "kn")
                act(P, sqh[:], A, AF.Square, r=[("acc", a)], w=["sqh"])
                mm(P, ps_s2[:, :TB], ones[:], sqh[:], True, True, r=["ones", "sqh"], w=["ps_s2"])
                rstd_from_ps(P, r2[:], ps_s2[:, :TB], 128, EPS, r=["ps_s2"], w=["r2"], tmpkey=None)
                stt(P, qf[:], A, gn[:, 0:1], r2[:], ALU.mult, ALU.mult, r=[("acc", a), gk, "r2"], w=["qf"])
            elif cc >= 16:
                a, A, o = st[cc]
                for mat, mk, dstT in ((cd_sb, "cd", zrT), (sd_sb, "sd", ziT)):
                    z = nz % 2; nz += 1
                    mm(P, ps_z[z][:, :TB], mat[:], ufb[o][:], True, True, r=[mk, ("ufb", o)], w=[("ps_z", z)])
                    o2 = nf_o % 3; nf_o += 1
                    P.op("dve", lambda g, o2=o2, z=z: g.tensor_copy(out=of[o2][:], in_=ps_z[z][:, :TB]),
                         r=[("ps_z", z)], w=[("of", o2)])
                    P.dma("sp", dstT[(cc - 16) * 128:(cc - 15) * 128, tsl], of[o2][:], r=[("of", o2)])

        def stage3(cc):
            nonlocal nb_o
            if cc >= 10:
                return
            mm(P, ps_rot[:, :TB], rot_sb[:], qf[:], True, True, r=["rot", "qf"], w=["ps_rot"])
            tt(P, "dve", t1[:], qf[:], cs_sb[:], ALU.mult, r=["qf", csk], w=["t1"])
            tt(P, "dve", t2[:], ps_rot[:, :TB], sn_sb[:], ALU.mult, r=["ps_rot", snk], w=["t2"])
            o = nb_o % 2; nb_o += 1
            tt(P, "pool", ob[o][:], t1[:], t2[:], ALU.add, r=["t1", "t2"], w=[("ob", o)])
            dst = qT[cc * 128:(cc + 1) * 128, tsl] if cc < 8 else kT[(cc - 8) * 128:(cc - 7) * 128, tsl]
            P.dma("sp", dst, ob[o][:], r=[("ob", o)])

        for i in range(NCC + 2):
            if i == 3 and b + 1 < NB:
                p_stats(b + 1)
            if i < NCC:
                stage1(i)
            if 0 <= i - 2 < NCC:
                stage3(i - 2)
            if 0 <= i - 1 < NCC:
                stage2(i - 1)
        if b + 1 < NB:
            p_h(b + 1)
    return P.build()


def rope_tables(rows_tok):
    f = (10000.0 ** (-np.arange(32, dtype=np.float32) / 32)).astype(np.float32)
    t = np.asarray(rows_tok)
    row = (t // 64).astype(np.float32); col = (t % 64).astype(np.float32)
    ar = (row[None, :] * f[:, None]).astype(np.float32)
    ac = (col[None, :] * f[:, None]).astype(np.float32)
    cosT = np.concatenate([np.cos(ar), np.cos(ar), np.cos(ac), np.cos(ac)], 0).astype(np.float32)
    sinT = np.concatenate([np.sin(ar), np.sin(ar), np.sin(ac), np.sin(ac)], 0).astype(np.float32)
    return cosT, sinT


def rope_rotT():
    R = np.zeros((128, 128), np.float32)
    for base in (0, 64):
        for i in range(32):
            R[base + i, base + i + 32] = -1.0
            R[base + i + 32, base + i] = 1.0
    return np.ascontiguousarray(R.T)


def chan_dft():
    j = np.arange(128)
    ang = 2 * np.pi * np.outer(j, j) / 128
    return (np.cos(ang) / np.sqrt(128)).astype(np.float32), (-np.sin(ang) / np.sqrt(128)).astype(np.float32)


ATT_VARIANT = ""


def build_attn(NQ, NK, QB):
    P = Prog()
    NQB = NQ // QB
    NKT = NK // 128
    qT = P.din("qT", [128, NQ], BF16)
    kT = P.din("kT", [128, NK], BF16)
    v = P.din("v", [NK, 128], BF16)
    oT = P.dout("oT", [128, NQ], BF16)
    q_sb = P.sb([128, NQ], BF16); k_sb = P.sb([128, NK], BF16); v_sb = P.sb([128, NKT, 128], BF16)
    KCH = 13 if NKT % 13 == 0 else NKT
    vv = v.rearrange("(kt p) d -> p kt d", p=128)
    QCH = 4 if NQB % 4 == 0 else NQB
    P.dma("sp", q_sb[:, 0:QCH * QB], qT[:, 0:QCH * QB], w=[("Q", 0)])
    for c in range(NKT // KCH):
        P.dma("sp", k_sb[:, c * KCH * 128:(c + 1) * KCH * 128], kT[:, c * KCH * 128:(c + 1) * KCH * 128], w=[("K", c)])
        P.dma("sp", v_sb[:, c * KCH:(c + 1) * KCH, :], vv[:, c * KCH:(c + 1) * KCH, :], w=[("V", c)])
    for c in range(1, NQB // QCH):
        P.dma("sp", q_sb[:, c * QCH * QB:(c + 1) * QCH * QB], qT[:, c * QCH * QB:(c + 1) * QCH * QB], w=[("Q", c)])
    ones_b = P.sb([128, 1], BF16); ones_f = P.sb([1, 128], F32)
    P.op("pool", lambda g: g.memset(ones_b[:], 1.0), w=["ones_b"])
    P.op("pool", lambda g: g.memset(ones_f[:], 1.0), w=["ones_f"])
    S = [P.ps([128, 512], F32) for _ in range(3)]
    O = [P.ps([128, 512], F32) for _ in range(2)]
    SM = [P.ps([128, 512], F32) for _ in range(2)]
    BC = P.ps([128, 512], F32)
    PT = [P.sb([128, QB], BF16) for _ in range(3)]
    rs = P.sb([1, QB], F32); bc_sb = P.sb([128, QB], F32)
    o_sb = [P.sb([128, QB], BF16) for _ in range(2)]
    scale = 128 ** -0.5
    items = [(qb, kt) for qb in range(NQB) for kt in range(NKT)]

    def emit_S(i):
        qb, kt = items[i]
        s = i % 3
        mm(P, S[s][:, :QB], k_sb[:, kt * 128:(kt + 1) * 128], q_sb[:, qb * QB:(qb + 1) * QB], True, True,
           r=[("K", kt // KCH), ("Q", qb // QCH)], w=[("S", s)])

    emit_S(0)
    for i, (qb, kt) in enumerate(items):
        if i + 1 < len(items):
            emit_S(i + 1)
        s = i % 3
        o = qb % 2
        act(P, PT[s][:], S[s][:, :QB], AF.Exp, r=[("S", s)], w=[("PT", s)], scale=scale)
        if ATT_VARIANT != "B" or kt in (0, NKT - 1):
            mm(P, O[o][:, :QB], v_sb[:, kt, :], PT[s][:], kt == 0, kt == NKT - 1, r=[("V", kt // KCH), ("PT", s)], w=[("O", o)])
        if ATT_VARIANT == "" or kt in (0, NKT - 1):
            mm(P, SM[o][0:1, :QB], ones_b[:], PT[s][:], kt == 0, kt == NKT - 1, r=["ones_b", ("PT", s)], w=[("SM", o)])
        if kt == NKT - 1:
            P.op("dve", lambda g, o=o: g.reciprocal(out=rs[:], in_=SM[o][0:1, :QB]), r=[("SM", o)], w=["rs"])
            mm(P, BC[:, :QB], ones_f[:], rs[:], True, True, r=["ones_f", "rs"], w=["BC"])
            P.op("act", lambda g: g.copy(out=bc_sb[:], in_=BC[:, :QB]), r=["BC"], w=["bc_sb"])
            tt(P, "dve", o_sb[o][:], O[o][:, :QB], bc_sb[:], ALU.mult, r=[("O", o), "bc_sb"], w=[("o_sb", o)])
            P.dma("sp", oT[:, qb * QB:(qb + 1) * QB], o_sb[o][:], r=[("o_sb", o)])
    return P.build()


def const_ap(P, val):
    if not hasattr(P, "_consts"):
        P._consts = {}
    if val not in P._consts:
        t = P.sb([128, 1], F32, name=f"const{len(P._consts)}")
        key = ("const", len(P._consts))
        P.op("pool", lambda g: g.memset(t[:], float(val)), w=[key])
        P._consts[val] = (t, key)
    return P._consts[val]


def emit_sin(P, out, u, shape, r, w, tag):
    if not hasattr(P, "_sin_tmp"):
        P._sin_tmp = {}
    sk = tuple(shape)
    if sk not in P._sin_tmp:
        P._sin_tmp[sk] = (P.sb(shape, I32), P.sb(shape, F32), P.sb(shape, F32), P.sb(shape, F32))
    ki, kf, fr, mk = P._sin_tmp[sk]
    k = lambda n: ("sintmp", sk, n)
    P.op("dve", lambda g: g.tensor_copy(out=ki[:], in_=u), r=r, w=[k("ki")])
    P.op("dve", lambda g: g.tensor_copy(out=kf[:], in_=ki[:]), r=[k("ki")], w=[k("kf")])
    tt(P, "dve", fr[:], u, kf[:], ALU.subtract, r=r + [k("kf")], w=[k("fr")])
    ts(P, "dve", mk[:], fr[:], 0.0, None, ALU.is_lt, None, r=[k("fr")], w=[k("mk")])
    tt(P, "dve", fr[:], fr[:], mk[:], ALU.add, r=[k("fr"), k("mk")], w=[k("fr")])
    cb, cbk = const_ap(P, -np.pi)
    act(P, out, fr[:], AF.Sin, r=[k("fr"), cbk], w=w, bias=cb[:], scale=2 * np.pi)


def build_ssm(L, TBL=512):
    P = Prog()
    NCT = 4
    u = P.din("u", [128, L], F32)
    lam_re = P.din("lam_re", [128, NCT], F32)
    lam_im = P.din("lam_im", [128, NCT], F32)
    log_dt = P.din("log_dt", [128, NCT], F32)
    b_re = P.din("b_re", [NCT, 128, 128], F32)
    b_im = P.din("b_im", [NCT, 128, 128], F32)
    c_re = P.din("c_re", [NCT, 128, 128], F32)
    c_im = P.din("c_im", [NCT, 128, 128], F32)
    tvec = P.din("tvec", [128, TBL], F32)
    ident = P.din("ident", [128, 128], F32)
    y = P.dout("y", [128, L], F32)

    U = P.sb([128, L], F32)
    nch = 8
    bounds = [(i * L) // nch for i in range(nch + 1)]
    def ukey(t0):
        for i in range(nch):
            if bounds[i] <= t0 < bounds[i + 1]:
                return ("U", i)
    lr = P.sb([128, NCT]); li = P.sb([128, NCT]); ld = P.sb([128, NCT])
    P.dma("sp", lr[:], lam_re[:], w=["lr"]); P.dma("sp", li[:], lam_im[:], w=["li"]); P.dma("sp", ld[:], log_dt[:], w=["ld"])
    tv = P.sb([128, TBL]); idn = P.sb([128, 128])
    P.dma("sp", tv[:], tvec[:], w=["tv"]); P.dma("sp", idn[:], ident[:], w=["idn"])
    Bre = P.sb([128, NCT, 128]); Bim = P.sb([128, NCT, 128]); Cre = P.sb([128, NCT, 128]); Cim = P.sb([128, NCT, 128])
    for c in range(NCT):
        P.dma("sp", Bre[:, c, :], b_re[c], w=[("Bre", c)]); P.dma("sp", Bim[:, c, :], b_im[c], w=[("Bim", c)])
        P.dma("sp", Cre[:, c, :], c_re[c], w=[("Cre", c)]); P.dma("sp", Cim[:, c, :], c_im[c], w=[("Cim", c)])
    for i in range(nch):
        P.dma("sp", U[:, bounds[i]:bounds[i + 1]], u[:, bounds[i]:bounds[i + 1]], w=[("U", i)])
    S4 = [128, NCT]
    dt = P.sb(S4); mag = P.sb(S4); th = P.sb(S4); tmp = P.sb(S4); us = P.sb(S4); uc = P.sb(S4)
    sn = P.sb(S4); cs = P.sb(S4); ar1 = P.sb(S4); ai = P.sb(S4); den = P.sb(S4); t2 = P.sb(S4)
    cr = P.sb(S4); ci = P.sb(S4); nci = P.sb(S4); th2 = P.sb(S4)
    act(P, dt[:], ld[:], AF.Exp, r=["ld"], w=["dt"])
    tt(P, "dve", tmp[:], lr[:], dt[:], ALU.mult, r=["lr", "dt"], w=["tmp"])
    act(P, mag[:], tmp[:], AF.Exp, r=["tmp"], w=["mag"])
    tt(P, "dve", th[:], li[:], dt[:], ALU.mult, r=["li", "dt"], w=["th"])
    ts(P, "dve", th2[:], th[:], float(1.0 / (2 * np.pi)), None, ALU.mult, None, r=["th"], w=["th2"])
    ts(P, "dve", us[:], th2[:], 0.5, None, ALU.add, None, r=["th2"], w=["us"])
    ts(P, "dve", uc[:], th2[:], 0.75, None, ALU.add, None, r=["th2"], w=["uc"])
    emit_sin(P, sn[:], us[:], S4, ["us"], ["sn"], "rs")
    emit_sin(P, cs[:], uc[:], S4, ["uc"], ["cs"], "rc")
    tt(P, "dve", ar1[:], mag[:], cs[:], ALU.mult, r=["mag", "cs"], w=["ar1"])
    ts(P, "dve", ar1[:], ar1[:], -1.0, None, ALU.add, None, r=["ar1"], w=["ar1"])
    tt(P, "dve", ai[:], mag[:], sn[:], ALU.mult, r=["mag", "sn"], w=["ai"])
    tt(P, "dve", den[:], lr[:], lr[:], ALU.mult, r=["lr"], w=["den"])
    tt(P, "dve", t2[:], li[:], li[:], ALU.mult, r=["li"], w=["t2"])
    tt(P, "dve", den[:], den[:], t2[:], ALU.add, r=["den", "t2"], w=["den"])
    P.op("dve", lambda g: g.reciprocal(out=den[:], in_=den[:]), r=["den"], w=["den"])
    tt(P, "dve", cr[:], ar1[:], lr[:], ALU.mult, r=["ar1", "lr"], w=["cr"])
    tt(P, "dve", t2[:], ai[:], li[:], ALU.mult, r=["ai", "li"], w=["t2"])
    tt(P, "dve", cr[:], cr[:], t2[:], ALU.add, r=["cr", "t2"], w=["cr"])
    tt(P, "dve", cr[:], cr[:], den[:], ALU.mult, r=["cr", "den"], w=["cr"])
    tt(P, "dve", ci[:], ai[:], lr[:], ALU.mult, r=["ai", "lr"], w=["ci"])
    tt(P, "dve", t2[:], ar1[:], li[:], ALU.mult, r=["ar1", "li"], w=["t2"])
    tt(P, "dve", ci[:], ci[:], t2[:], ALU.subtract, r=["ci", "t2"], w=["ci"])
    tt(P, "dve", ci[:], ci[:], den[:], ALU.mult, r=["ci", "den"], w=["ci"])
    ts(P, "dve", nci[:], ci[:], -1.0, None, ALU.mult, None, r=["ci"], w=["nci"])
    BrT = P.sb([128, NCT, 128]); BiT = P.sb([128, NCT, 128]); Cimn = P.sb([128, NCT, 128])
    bt = P.sb([128, 128]); bb = [P.sb([128, 128]) for _ in range(2)]
    pst = P.ps([128, 512], F32)
    nb = 0
    for c in range(NCT):
        for which in range(2):
            s = nb % 2; nb += 1
            if which == 0:
                ts(P, "dve", bt[:], Bre[:, c, :], cr[:, c:c + 1], None, ALU.mult, None, r=[("Bre", c), "cr"], w=["bt"])
                stt(P, bb[s][:], Bim[:, c, :], nci[:, c:c + 1], bt[:], ALU.mult, ALU.add, r=[("Bim", c), "nci", "bt"], w=[("bb", s)])
                dst = BrT
            else:
                ts(P, "dve", bt[:], Bim[:, c, :], cr[:, c:c + 1], None, ALU.mult, None, r=[("Bim", c), "cr"], w=["bt"])
                stt(P, bb[s][:], Bre[:, c, :], ci[:, c:c + 1], bt[:], ALU.mult, ALU.add, r=[("Bre", c), "ci", "bt"], w=[("bb", s)])
                dst = BiT
            P.op("pe", lambda g, s=s: g.transpose(pst[:, 0:128], bb[s][:], idn[:]), r=[("bb", s), "idn"], w=["pst"])
            P.op("act", lambda g, dst=dst, c=c: g.copy(out=dst[:, c, :], in_=pst[:, 0:128]), r=["pst"], w=[("BT", c, which)])
        P.op("act", lambda g, c=c: g.mul(out=Cimn[:, c, :], in_=Cim[:, c, :], mul=-1.0), r=[("Cim", c)], w=[("Cimn", c)])
    ct = P.sb([128, NCT, TBL]); st = P.sb([128, NCT, TBL]); magT = P.sb([128, NCT, TBL])
    ut = P.sb([128, TBL])
    for c in range(NCT):
        for which, dstT, off in ((0, st, 0.5), (1, ct, 0.75)):
            ts(P, "dve", ut[:], tv[:], th2[:, c:c + 1], off, ALU.mult, ALU.add, r=["tv", "th2"], w=["ut"])
            emit_sin(P, dstT[:, c, :], ut[:], [128, TBL], ["ut"], [("tab", c, which)], ("tb", c, which))
        ts(P, "dve", magT[:, c, :], tv[:], 0.0, mag[:, c:c + 1], ALU.mult, ALU.add, r=["tv", "mag"], w=[("magT", c)])
    blocks = []
    t0 = 0
    while t0 < L:
        n = min(TBL, L - t0); blocks.append((t0, n)); t0 += n
    psb = [[P.ps([128, 512], F32), P.ps([128, 512], F32)] for _ in range(2)]
    psy = [P.ps([128, 512], F32) for _ in range(2)]
    mi = [P.sb([128, TBL]) for _ in range(4)]
    mo = [[P.sb([128, TBL]) for _ in range(4)] for _ in range(2)]
    gin = [P.sb([128, TBL]) for _ in range(2)]
    g_ = [[P.sb([128, TBL]) for _ in range(2)] for _ in range(2)]
    hh = [[[P.sb([128, TBL]), P.sb([128, TBL])] for _ in range(2)] for _ in range(NCT)]
    ysb = [P.sb([128, TBL]) for _ in range(2)]
    items = [(j, c) for j in range(len(blocks)) for c in range(NCT)]

    def tabs(c, n):
        return ct[:, c, :n], st[:, c, :n], [("tab", c, 0), ("tab", c, 1)]

    def stA(k):
        j, c = items[k]; t0, n = blocks[j]; pb = k % 2
        uk = [("U", i_) for i_ in range(nch) if bounds[i_] < t0 + n and bounds[i_ + 1] > t0]
        br_ps, bi_ps = psb[pb][0][:, :n], psb[pb][1][:, :n]
        mm(P, br_ps, BrT[:, c, :], U[:, t0:t0 + n], True, True, r=[("BT", c, 0)] + uk, w=[("psb", pb, 0)])
        mm(P, bi_ps, BiT[:, c, :], U[:, t0:t0 + n], True, True, r=[("BT", c, 1)] + uk, w=[("psb", pb, 1)])
        ctc, stc, tk = tabs(c, n)
        tt(P, "dve", mi[0][:, :n], br_ps, ctc, ALU.mult, r=[("psb", pb, 0)] + tk, w=["mi0"])
        tt(P, "dve", mi[1][:, :n], bi_ps, stc, ALU.mult, r=[("psb", pb, 1)] + tk, w=["mi1"])
        tt(P, "dve", mi[2][:, :n], bi_ps, ctc, ALU.mult, r=[("psb", pb, 1)] + tk, w=["mi2"])
        tt(P, "dve", mi[3][:, :n], br_ps, stc, ALU.mult, r=[("psb", pb, 0)] + tk, w=["mi3"])
        tt(P, "dve", gin[0][:, :n], mi[0][:, :n], mi[1][:, :n], ALU.add, r=["mi0", "mi1"], w=["gin0"])
        tt(P, "pool", gin[1][:, :n], mi[2][:, :n], mi[3][:, :n], ALU.subtract, r=["mi2", "mi3"], w=["gin1"])

    def stB(k):
        j, c = items[k]; t0, n = blocks[j]; hb = j % 2; gb = k % 2
        for ri in range(2):
            if j == 0:
                init = 0.0; rk = []
            else:
                pn = blocks[j - 1][1]
                init = hh[c][1 - hb][ri][:, pn - 1:pn]; rk = [("hh", c, 1 - hb, ri)]
            P.op("dve", lambda g, ri=ri, init=init, c=c, n=n, gb=gb: g.tensor_tensor_scan(
                out=g_[gb][ri][:, :n], data0=magT[:, c, :n], data1=gin[ri][:, :n], initial=init,
                op0=ALU.mult, op1=ALU.add), r=[("magT", c), f"gin{ri}"] + rk, w=[("g", gb, ri)])

    def stC(k):
        j, c = items[k]; t0, n = blocks[j]; gb = k % 2
        ctc, stc, tk = tabs(c, n)
        tt(P, "pool", mo[gb][0][:, :n], g_[gb][0][:, :n], ctc, ALU.mult, r=[("g", gb, 0)] + tk, w=[("mo", gb, 0)])
        tt(P, "pool", mo[gb][2][:, :n], g_[gb][0][:, :n], stc, ALU.mult, r=[("g", gb, 0)] + tk, w=[("mo", gb, 2)])
        tt(P, "pool", mo[gb][1][:, :n], g_[gb][1][:, :n], stc, ALU.mult, r=[("g", gb, 1)] + tk, w=[("mo", gb, 1)])
        tt(P, "pool", mo[gb][3][:, :n], g_[gb][1][:, :n], ctc, ALU.mult, r=[("g", gb, 1)] + tk, w=[("mo", gb, 3)])

    def stD(k):
        j, c = items[k]; t0, n = blocks[j]; hb = j % 2; gb = k % 2; yb = j % 2
        hr, hi = hh[c][hb][0], hh[c][hb][1]
        tt(P, "dve", hr[:, :n], mo[gb][0][:, :n], mo[gb][1][:, :n], ALU.subtract, r=[("mo", gb, 0), ("mo", gb, 1)], w=[("hh", c, hb, 0)])
        tt(P, "dve", hi[:, :n], mo[gb][2][:, :n], mo[gb][3][:, :n], ALU.add, r=[("mo", gb, 2), ("mo", gb, 3)], w=[("hh", c, hb, 1)])
        mm(P, psy[yb][:, :n], Cre[:, c, :], hr[:, :n], c == 0, False, r=[("Cre", c), ("hh", c, hb, 0)], w=[("psy", yb)])
        mm(P, psy[yb][:, :n], Cimn[:, c, :], hi[:, :n], False, c == NCT - 1, r=[("Cimn", c), ("hh", c, hb, 1)], w=[("psy", yb)])
        if c == NCT - 1:
            P.op("act", lambda g, yb=yb, n=n: g.copy(out=ysb[yb][:, :n], in_=psy[yb][:, :n]), r=[("psy", yb)], w=[("ysb", yb)])
            P.dma("sp", y[:, t0:t0 + n], ysb[yb][:, :n], r=[("ysb", yb)])

    stA(0); stB(0); stC(0)
    for k in range(len(items)):
        if k + 1 < len(items):
            stA(k + 1)
        stD(k)
        if k + 1 < len(items):
            stB(k + 1); stC(k + 1)
    return P.build()


def build_ssm2(L2, TBL=512):
    L = L2
    P = Prog()
    NCT = 4
    ue = P.din("ue", [128, L], F32)
    uo = P.din("uo", [128, L], F32)
    lam_re = P.din("lam_re", [128, NCT], F32)
    lam_im = P.din("lam_im", [128, NCT], F32)
    log_dt = P.din("log_dt", [128, NCT], F32)
    b_re = P.din("b_re", [NCT, 128, 128], F32)
    b_im = P.din("b_im", [NCT, 128, 128], F32)
    c_re = P.din("c_re", [NCT, 128, 128], F32)
    c_im = P.din("c_im", [NCT, 128, 128], F32)
    tvec = P.din("tvec", [128, TBL], F32)
    ident = P.din("ident", [128, 128], F32)
    ye = P.dout("ye", [128, L], F32)
    yo = P.dout("yo", [128, L], F32)

    U = P.sb([128, L], F32)
    UO = P.sb([128, L], F32)
    nch = 8
    bounds = [(i * L) // nch for i in range(nch + 1)]
    def ukey(t0):
        for i in range(nch):
            if bounds[i] <= t0 < bounds[i + 1]:
                return ("U", i)
    lr = P.sb([128, NCT]); li = P.sb([128, NCT]); ld = P.sb([128, NCT])
    P.dma("sp", lr[:], lam_re[:], w=["lr"]); P.dma("sp", li[:], lam_im[:], w=["li"]); P.dma("sp", ld[:], log_dt[:], w=["ld"])
    tv = P.sb([128, TBL]); idn = P.sb([128, 128])
    P.dma("sp", tv[:], tvec[:], w=["tv"]); P.dma("sp", idn[:], ident[:], w=["idn"])
    Bre = P.sb([128, NCT, 128]); Bim = P.sb([128, NCT, 128]); Cre = P.sb([128, NCT, 128]); Cim = P.sb([128, NCT, 128])
    for c in range(NCT):
        P.dma("sp", Bre[:, c, :], b_re[c], w=[("Bre", c)]); P.dma("sp", Bim[:, c, :], b_im[c], w=[("Bim", c)])
        P.dma("sp", Cre[:, c, :], c_re[c], w=[("Cre", c)]); P.dma("sp", Cim[:, c, :], c_im[c], w=[("Cim", c)])
    for i in range(nch):
        P.dma("sp", U[:, bounds[i]:bounds[i + 1]], ue[:, bounds[i]:bounds[i + 1]], w=[("U", i)])
        P.dma("sp", UO[:, bounds[i]:bounds[i + 1]], uo[:, bounds[i]:bounds[i + 1]], w=[("UO", i)])
    S4 = [128, NCT]
    dt = P.sb(S4); mag = P.sb(S4); th = P.sb(S4); tmp = P.sb(S4); us = P.sb(S4); uc = P.sb(S4)
    sn = P.sb(S4); cs = P.sb(S4); ar1 = P.sb(S4); ai = P.sb(S4); den = P.sb(S4); t2 = P.sb(S4)
    cr = P.sb(S4); ci = P.sb(S4); nci = P.sb(S4); th2 = P.sb(S4)
    act(P, dt[:], ld[:], AF.Exp, r=["ld"], w=["dt"])
    tt(P, "dve", tmp[:], lr[:], dt[:], ALU.mult, r=["lr", "dt"], w=["tmp"])
    act(P, mag[:], tmp[:], AF.Exp, r=["tmp"], w=["mag"])
    tt(P, "dve", th[:], li[:], dt[:], ALU.mult, r=["li", "dt"], w=["th"])
    ts(P, "dve", th2[:], th[:], float(1.0 / (2 * np.pi)), None, ALU.mult, None, r=["th"], w=["th2"])
    ts(P, "dve", us[:], th2[:], 0.5, None, ALU.add, None, r=["th2"], w=["us"])
    ts(P, "dve", uc[:], th2[:], 0.75, None, ALU.add, None, r=["th2"], w=["uc"])
    emit_sin(P, sn[:], us[:], S4, ["us"], ["sn"], "rs")
    emit_sin(P, cs[:], uc[:], S4, ["uc"], ["cs"], "rc")
    tt(P, "dve", ar1[:], mag[:], cs[:], ALU.mult, r=["mag", "cs"], w=["ar1"])
    ts(P, "dve", ar1[:], ar1[:], -1.0, None, ALU.add, None, r=["ar1"], w=["ar1"])
    tt(P, "dve", ai[:], mag[:], sn[:], ALU.mult, r=["mag", "sn"], w=["ai"])
    tt(P, "dve", den[:], lr[:], lr[:], ALU.mult, r=["lr"], w=["den"])
    tt(P, "dve", t2[:], li[:], li[:], ALU.mult, r=["li"], w=["t2"])
    tt(P, "dve", den[:], den[:], t2[:], ALU.add, r=["den", "t2"], w=["den"])
    P.op("dve", lambda g: g.reciprocal(out=den[:], in_=den[:]), r=["den"], w=["den"])
    tt(P, "dve", cr[:], ar1[:], lr[:], ALU.mult, r=["ar1", "lr"], w=["cr"])
    tt(P, "dve", t2[:], ai[:], li[:], ALU.mult, r=["ai", "li"], w=["t2"])
    tt(P, "dve", cr[:], cr[:], t2[:], ALU.add, r=["cr", "t2"], w=["cr"])
    tt(P, "dve", cr[:], cr[:], den[:], ALU.mult, r=["cr", "den"], w=["cr"])
    tt(P, "dve", ci[:], ai[:], lr[:], ALU.mult, r=["ai", "lr"], w=["ci"])
    tt(P, "dve", t2[:], ar1[:], li[:], ALU.mult, r=["ar1", "li"], w=["t2"])
    tt(P, "dve", ci[:], ci[:], t2[:], ALU.subtract, r=["ci", "t2"], w=["ci"])
    tt(P, "dve", ci[:], ci[:], den[:], ALU.mult, r=["ci", "den"], w=["ci"])
    ts(P, "dve", nci[:], ci[:], -1.0, None, ALU.mult, None, r=["ci"], w=["nci"])
    arr = P.sb(S4); nai = P.sb(S4); mag2 = P.sb(S4); th2b = P.sb(S4)
    ts(P, "dve", arr[:], ar1[:], 1.0, None, ALU.add, None, r=["ar1"], w=["arr"])
    ts(P, "dve", nai[:], ai[:], -1.0, None, ALU.mult, None, r=["ai"], w=["nai"])
    tt(P, "dve", mag2[:], mag[:], mag[:], ALU.mult, r=["mag"], w=["mag2"])
    ts(P, "dve", th2b[:], th2[:], 2.0, None, ALU.mult, None, r=["th2"], w=["th2b"])
    BrT = P.sb([128, NCT, 128]); BiT = P.sb([128, NCT, 128]); aBrT = P.sb([128, NCT, 128]); aBiT = P.sb([128, NCT, 128])
    Bb0 = P.sb([128, NCT, 128]); Bb1 = P.sb([128, NCT, 128])
    Cimn = P.sb([128, NCT, 128]); CaRe = P.sb([128, NCT, 128]); CaImN = P.sb([128, NCT, 128]); D2T = P.sb([128, 128])
    bt = P.sb([128, 128]); bb = [P.sb([128, 128]) for _ in range(2)]
    psy_e = [P.ps([128, 512], F32) for _ in range(2)]
    psy_o = [P.ps([128, 512], F32) for _ in range(2)]
    pst = psy_e[0]; pstk = ("psye", 0)
    nb = 0
    for c in range(NCT):
        ts(P, "dve", bt[:], Bre[:, c, :], cr[:, c:c + 1], None, ALU.mult, None, r=[("Bre", c), "cr"], w=["bt"])
        stt(P, Bb0[:, c, :], Bim[:, c, :], nci[:, c:c + 1], bt[:], ALU.mult, ALU.add, r=[("Bim", c), "nci", "bt"], w=[("Bb0", c)])
        ts(P, "dve", bt[:], Bim[:, c, :], cr[:, c:c + 1], None, ALU.mult, None, r=[("Bim", c), "cr"], w=["bt"])
        stt(P, Bb1[:, c, :], Bre[:, c, :], ci[:, c:c + 1], bt[:], ALU.mult, ALU.add, r=[("Bre", c), "ci", "bt"], w=[("Bb1", c)])
        P.op("act", lambda g, c=c: g.mul(out=Cimn[:, c, :], in_=Cim[:, c, :], mul=-1.0), r=[("Cim", c)], w=[("Cimn", c)])
        srcs = []
        srcs.append((Bb0[:, c, :], [("Bb0", c)], BrT, 0))
        srcs.append((Bb1[:, c, :], [("Bb1", c)], BiT, 1))
        for which in (2, 3):
            s_ = nb % 2; nb += 1
            if which == 2:
                ts(P, "dve", bt[:], Bb0[:, c, :], arr[:, c:c + 1], None, ALU.mult, None, r=[("Bb0", c), "arr"], w=["bt"])
                stt(P, bb[s_][:], Bb1[:, c, :], nai[:, c:c + 1], bt[:], ALU.mult, ALU.add, r=[("Bb1", c), "nai", "bt"], w=[("bb", s_)])
                srcs.append((bb[s_][:], [("bb", s_)], aBrT, 2))
            else:
                ts(P, "dve", bt[:], Bb1[:, c, :], arr[:, c:c + 1], None, ALU.mult, None, r=[("Bb1", c), "arr"], w=["bt"])
                stt(P, bb[s_][:], Bb0[:, c, :], ai[:, c:c + 1], bt[:], ALU.mult, ALU.add, r=[("Bb0", c), "ai", "bt"], w=[("bb", s_)])
                srcs.append((bb[s_][:], [("bb", s_)], aBiT, 3))
        for src_ap, sk, dst, which in srcs:
            P.op("pe", lambda g, src_ap=src_ap: g.transpose(pst[:, 0:128], src_ap, idn[:]), r=sk + ["idn"], w=[pstk])
            P.op("act", lambda g, dst=dst, c=c: g.copy(out=dst[:, c, :], in_=pst[:, 0:128]), r=[pstk], w=[("BT", c, which)])
        ts(P, "dve", bt[:], Cre[:, c, :], arr[:, c:c + 1], None, ALU.mult, None, r=[("Cre", c), "arr"], w=["bt"])
        stt(P, CaRe[:, c, :], Cimn[:, c, :], ai[:, c:c + 1], bt[:], ALU.mult, ALU.add, r=[("Cimn", c), "ai", "bt"], w=[("CaRe", c)])
        ts(P, "dve", bt[:], Cimn[:, c, :], arr[:, c:c + 1], None, ALU.mult, None, r=[("Cimn", c), "arr"], w=["bt"])
        stt(P, CaImN[:, c, :], Cre[:, c, :], nai[:, c:c + 1], bt[:], ALU.mult, ALU.add, r=[("Cre", c), "nai", "bt"], w=[("CaImN", c)])
    for c in range(NCT):
        mm(P, psy_o[0][:, 0:128], Bb0[:, c, :], Cre[:, c, :], c == 0, False, r=[("Bb0", c), ("Cre", c)], w=[("psyo", 0)])
        mm(P, psy_o[0][:, 0:128], Bb1[:, c, :], Cimn[:, c, :], False, c == NCT - 1, r=[("Bb1", c), ("Cimn", c)], w=[("psyo", 0)])
    P.op("act", lambda g: g.copy(out=D2T[:], in_=psy_o[0][:, 0:128]), r=[("psyo", 0)], w=["D2T"])
    ct = P.sb([128, NCT, TBL]); st = P.sb([128, NCT, TBL]); magT = P.sb([128, NCT, TBL])
    ut = P.sb([128, TBL])
    for c in range(NCT):
        for which, dstT, off in ((0, st, 0.5), (1, ct, 0.75)):
            ts(P, "dve", ut[:], tv[:], th2b[:, c:c + 1], off, ALU.mult, ALU.add, r=["tv", "th2b"], w=["ut"])
            emit_sin(P, dstT[:, c, :], ut[:], [128, TBL], ["ut"], [("tab", c, which)], ("tb", c, which))
        ts(P, "dve", magT[:, c, :], tv[:], 0.0, mag2[:, c:c + 1], ALU.mult, ALU.add, r=["tv", "mag2"], w=[("magT", c)])
    blocks = []
    t0 = 0
    while t0 < L:
        n = min(TBL, L - t0); blocks.append((t0, n)); t0 += n
    psb = [[P.ps([128, 512], F32), P.ps([128, 512], F32)] for _ in range(2)]
    mi = [P.sb([128, TBL]) for _ in range(4)]
    mo = [[P.sb([128, TBL]) for _ in range(4)] for _ in range(2)]
    gin = [P.sb([128, TBL]) for _ in range(2)]
    g_ = [[P.sb([128, TBL]) for _ in range(2)] for _ in range(2)]
    hh = [[[P.sb([128, TBL + 1]), P.sb([128, TBL + 1])] for _ in range(2)] for _ in range(NCT)]
    for c in range(NCT):
        for ri in range(2):
            P.op("pool", lambda g, c=c, ri=ri: g.memset(hh[c][0][ri][:, 0:1], 0.0), w=[("hh0", c, 0, ri)])
    ysb = [P.sb([128, TBL]) for _ in range(4)]
    items = [(j, c) for j in range(len(blocks)) for c in range(NCT)]

    def tabs(c, n):
        return ct[:, c, :n], st[:, c, :n], [("tab", c, 0), ("tab", c, 1)]

    def stA(k):
        j, c = items[k]; t0, n = blocks[j]; pb = k % 2
        uk = [("U", i_) for i_ in range(nch) if bounds[i_] < t0 + n and bounds[i_ + 1] > t0]
        br_ps, bi_ps = psb[pb][0][:, :n], psb[pb][1][:, :n]
        uko = [("UO", k_[1]) for k_ in uk]
        mm(P, br_ps, aBrT[:, c, :], U[:, t0:t0 + n], True, False, r=[("BT", c, 2)] + uk, w=[("psb", pb, 0)])
        mm(P, br_ps, BrT[:, c, :], UO[:, t0:t0 + n], False, True, r=[("BT", c, 0)] + uko, w=[("psb", pb, 0)])
        mm(P, bi_ps, aBiT[:, c, :], U[:, t0:t0 + n], True, False, r=[("BT", c, 3)] + uk, w=[("psb", pb, 1)])
        mm(P, bi_ps, BiT[:, c, :], UO[:, t0:t0 + n], False, True, r=[("BT", c, 1)] + uko, w=[("psb", pb, 1)])
        ctc, stc, tk = tabs(c, n)
        tt(P, "dve", mi[0][:, :n], br_ps, ctc, ALU.mult, r=[("psb", pb, 0)] + tk, w=["mi0"])
        tt(P, "dve", mi[1][:, :n], bi_ps, stc, ALU.mult, r=[("psb", pb, 1)] + tk, w=["mi1"])
        tt(P, "dve", mi[2][:, :n], bi_ps, ctc, ALU.mult, r=[("psb", pb, 1)] + tk, w=["mi2"])
        tt(P, "dve", mi[3][:, :n], br_ps, stc, ALU.mult, r=[("psb", pb, 0)] + tk, w=["mi3"])
        tt(P, "dve", gin[0][:, :n], mi[0][:, :n], mi[1][:, :n], ALU.add, r=["mi0", "mi1"], w=["gin0"])
        tt(P, "pool", gin[1][:, :n], mi[2][:, :n], mi[3][:, :n], ALU.subtract, r=["mi2", "mi3"], w=["gin1"])

    def stB(k):
        j, c = items[k]; t0, n = blocks[j]; hb = j % 2; gb = k % 2
        for ri in range(2):
            if j == 0:
                init = 0.0; rk = []
            else:
                pn = blocks[j - 1][1]
                init = hh[c][1 - hb][ri][:, pn:pn + 1]; rk = [("hh", c, 1 - hb, ri)]
            P.op("dve", lambda g, ri=ri, init=init, c=c, n=n, gb=gb: g.tensor_tensor_scan(
                out=g_[gb][ri][:, :n], data0=magT[:, c, :n], data1=gin[ri][:, :n], initial=init,
                op0=ALU.mult, op1=ALU.add), r=[("magT", c), f"gin{ri}"] + rk, w=[("g", gb, ri)])

    def stC(k):
        j, c = items[k]; t0, n = blocks[j]; gb = k % 2
        ctc, stc, tk = tabs(c, n)
        tt(P, "pool", mo[gb][0][:, :n], g_[gb][0][:, :n], ctc, ALU.mult, r=[("g", gb, 0)] + tk, w=[("mo", gb, 0)])
        tt(P, "pool", mo[gb][2][:, :n], g_[gb][0][:, :n], stc, ALU.mult, r=[("g", gb, 0)] + tk, w=[("mo", gb, 2)])
        tt(P, "pool", mo[gb][1][:, :n], g_[gb][1][:, :n], stc, ALU.mult, r=[("g", gb, 1)] + tk, w=[("mo", gb, 1)])
        tt(P, "pool", mo[gb][3][:, :n], g_[gb][1][:, :n], ctc, ALU.mult, r=[("g", gb, 1)] + tk, w=[("mo", gb, 3)])

    def stD(k):
        j, c = items[k]; t0, n = blocks[j]; hb = j % 2; gb = k % 2; yb = j % 2
        hr, hi = hh[c][hb][0], hh[c][hb][1]
        if j > 0:
            pn = blocks[j - 1][1]
            for ri in range(2):
                P.op("act", lambda g, ri=ri, pn=pn, c=c, hb=hb: g.copy(out=hh[c][hb][ri][:, 0:1], in_=hh[c][1 - hb][ri][:, pn:pn + 1]),
                     r=[("hh", c, 1 - hb, ri)], w=[("hh0", c, hb, ri)])
        tt(P, "dve", hr[:, 1:n + 1], mo[gb][0][:, :n], mo[gb][1][:, :n], ALU.subtract, r=[("mo", gb, 0), ("mo", gb, 1)], w=[("hh", c, hb, 0)])
        tt(P, "dve", hi[:, 1:n + 1], mo[gb][2][:, :n], mo[gb][3][:, :n], ALU.add, r=[("mo", gb, 2), ("mo", gb, 3)], w=[("hh", c, hb, 1)])
        mm(P, psy_o[yb][:, :n], Cre[:, c, :], hr[:, 1:n + 1], c == 0, False, r=[("Cre", c), ("hh", c, hb, 0)], w=[("psyo", yb)])
        mm(P, psy_o[yb][:, :n], Cimn[:, c, :], hi[:, 1:n + 1], False, c == NCT - 1, r=[("Cimn", c), ("hh", c, hb, 1)], w=[("psyo", yb)])
        mm(P, psy_e[yb][:, :n], CaRe[:, c, :], hr[:, 0:n], c == 0, False, r=[("CaRe", c), ("hh", c, hb, 0), ("hh0", c, hb, 0)], w=[("psye", yb)])
        mm(P, psy_e[yb][:, :n], CaImN[:, c, :], hi[:, 0:n], False, False, r=[("CaImN", c), ("hh", c, hb, 1), ("hh0", c, hb, 1)], w=[("psye", yb)])
        if c == NCT - 1:
            uk = [("U", i_) for i_ in range(nch) if bounds[i_] < t0 + n and bounds[i_ + 1] > t0]
            mm(P, psy_e[yb][:, :n], D2T[:], U[:, t0:t0 + n], False, True, r=["D2T"] + uk, w=[("psye", yb)])
            P.op("act", lambda g, yb=yb, n=n: g.copy(out=ysb[yb][:, :n], in_=psy_e[yb][:, :n]), r=[("psye", yb)], w=[("ysb", yb)])
            P.dma("sp", ye[:, t0:t0 + n], ysb[yb][:, :n], r=[("ysb", yb)])
            P.op("act", lambda g, yb=yb, n=n: g.copy(out=ysb[2 + yb][:, :n], in_=psy_o[yb][:, :n]), r=[("psyo", yb)], w=[("ysb", 2 + yb)])
            P.dma("sp", yo[:, t0:t0 + n], ysb[2 + yb][:, :n], r=[("ysb", 2 + yb)])

    stA(0); stB(0); stC(0)
    for k in range(len(items)):
        if k + 1 < len(items):
            stA(k + 1)
        stD(k)
        if k + 1 < len(items):
            stB(k + 1); stC(k + 1)
    return P.build()


def ssm_pack(lam_re, lam_im, log_dt, b_re, b_im, c_re, c_im, TBL=512):
    f = np.float32
    lr = np.zeros((128, 4), f); li = np.zeros((128, 4), f); ld = np.zeros((128, 4), f)
    Bre = np.zeros((4, 128, 128), f); Bim = np.zeros((4, 128, 128), f)
    Cre = np.zeros((4, 128, 128), f); Cim = np.zeros((4, 128, 128), f)
    for c in range(4):
        for gi in range(2):
            g = 2 * c + gi
            sl = slice(gi * 64, (gi + 1) * 64)
            lr[sl, c] = lam_re[g]; li[sl, c] = lam_im[g]; ld[sl, c] = log_dt[g]
            cs = slice(c * 32 + gi * 16, c * 32 + gi * 16 + 16)
            Bre[c, sl, cs] = b_re[g]; Bim[c, sl, cs] = b_im[g]
            Cre[c, sl, cs] = c_re[g].T; Cim[c, sl, cs] = c_im[g].T
    tvec = np.tile(np.arange(1, TBL + 1, dtype=f)[None, :], (128, 1))
    return dict(lam_re=lr, lam_im=li, log_dt=ld, b_re=Bre, b_im=Bim, c_re=Cre, c_im=Cim, tvec=tvec,
                ident=np.eye(128, dtype=f))


def build_pf1(NT2=16, C=512):
    P = Prog()
    zr = P.din("zr", [128, NT2, C], F32); zi = P.din("zi", [128, NT2, C], F32)
    mc = P.din("mc", [128, 128], F32); ms = P.din("ms", [128, 128], F32); mns = P.din("mns", [128, 128], F32)
    twc = P.din("twc", [128, NT2], F32); tws = P.din("tws", [128, NT2], F32); ntws = P.din("ntws", [128, NT2], F32)
    ar = P.dout("ar", [128, NT2, C], F32); ai = P.dout("ai", [128, NT2, C], F32)
    zr_sb = P.sb([128, NT2, C]); zi_sb = P.sb([128, NT2, C])
    mc_sb = P.sb([128, 128]); ms_sb = P.sb([128, 128]); mns_sb = P.sb([128, 128])
    twc_sb = P.sb([128, NT2]); tws_sb = P.sb([128, NT2]); ntws_sb = P.sb([128, NT2])
    for t, d, k in ((mc_sb, mc, "mc"), (ms_sb, ms, "ms"), (mns_sb, mns, "mns"), (twc_sb, twc, "twc"), (tws_sb, tws, "tws"), (ntws_sb, ntws, "ntws")):
        P.dma("sp", t[:], d[:], w=[k])
    H = NT2 // 2
    for hf in range(2):
        P.dma("sp", zr_sb[:, hf * H:(hf + 1) * H, :], zr[:, hf * H:(hf + 1) * H, :], w=[("zr", hf)])
        P.dma("sp", zi_sb[:, hf * H:(hf + 1) * H, :], zi[:, hf * H:(hf + 1) * H, :], w=[("zi", hf)])
    pr = [P.ps([128, 512]) for _ in range(2)]; pi = [P.ps([128, 512]) for _ in range(2)]
    t1 = [P.sb([128, C]) for _ in range(2)]; t2 = [P.sb([128, C]) for _ in range(2)]
    orr = [P.sb([128, C]) for _ in range(2)]; oi = [P.sb([128, C]) for _ in range(2)]
    for j in range(NT2):
        s = j % 2; hf = j // H
        mm(P, pr[s][:, :C], mc_sb[:], zr_sb[:, j, :], True, False, r=["mc", ("zr", hf)], w=[("pr", s)])
        mm(P, pr[s][:, :C], ms_sb[:], zi_sb[:, j, :], False, True, r=["ms", ("zi", hf)], w=[("pr", s)])
        mm(P, pi[s][:, :C], mc_sb[:], zi_sb[:, j, :], True, False, r=["mc", ("zi", hf)], w=[("pi", s)])
        mm(P, pi[s][:, :C], mns_sb[:], zr_sb[:, j, :], False, True, r=["mns", ("zr", hf)], w=[("pi", s)])
        ts(P, "dve", t1[s][:], pi[s][:, :C], tws_sb[:, j:j + 1], None, ALU.mult, None, r=[("pi", s), "tws"], w=[("t1", s)])
        stt(P, orr[s][:], pr[s][:, :C], twc_sb[:, j:j + 1], t1[s][:], ALU.mult, ALU.add, r=[("pr", s), "twc", ("t1", s)], w=[("or", s)])
        ts(P, "dve", t2[s][:], pr[s][:, :C], ntws_sb[:, j:j + 1], None, ALU.mult, None, r=[("pr", s), "ntws"], w=[("t2", s)])
        stt(P, oi[s][:], pi[s][:, :C], twc_sb[:, j:j + 1], t2[s][:], ALU.mult, ALU.add, r=[("pi", s), "twc", ("t2", s)], w=[("oi", s)])
        P.dma("sp", ar[:, j, :], orr[s][:], r=[("or", s)])
        P.dma("sp", ai[:, j, :], oi[s][:], r=[("oi", s)])
    return P.build()


def build_pf2(NK1=16, C=512, NCTX=256):
    P = Prog()
    ar = P.din("ar", [128, NK1, C], F32); ai = P.din("ai", [128, NK1, C], F32)
    mc = P.din("mc", [128, 128], F32); ms = P.din("ms", [128, 128], F32)
    zcr = P.din("zcr", [NCTX, C], F32); zci = P.din("zci", [NCTX, C], F32)
    cc = P.din("cc", [NCTX, NCTX], F32); cs = P.din("cs", [NCTX, NCTX], F32)
    xr = P.dout("xr", [128, NK1, C], F32)
    xc = P.dout("xc", [NCTX, C], F32)
    ar_sb = P.sb([128, NK1, C]); ai_sb = P.sb([128, NK1, C])
    mc_sb = P.sb([128, 128]); ms_sb = P.sb([128, 128])
    P.dma("sp", mc_sb[:], mc[:], w=["mc"]); P.dma("sp", ms_sb[:], ms[:], w=["ms"])
    H = NK1 // 2
    for hf in range(2):
        P.dma("sp", ar_sb[:, hf * H:(hf + 1) * H, :], ar[:, hf * H:(hf + 1) * H, :], w=[("ar", hf)])
        P.dma("sp", ai_sb[:, hf * H:(hf + 1) * H, :], ai[:, hf * H:(hf + 1) * H, :], w=[("ai", hf)])
    NTC = NCTX // 128
    zcr_sb = P.sb([128, NTC, C]); zci_sb = P.sb([128, NTC, C]); cc_sb = P.sb([128, NTC, NCTX]); cs_sb = P.sb([128, NTC, NCTX])
    P.dma("sp", zcr_sb[:], zcr.rearrange("(a p) c -> p a c", p=128), w=["zcr"])
    P.dma("sp", zci_sb[:], zci.rearrange("(a p) c -> p a c", p=128), w=["zci"])
    P.dma("sp", cc_sb[:], cc.rearrange("(a p) c -> p a c", p=128), w=["cc"])
    P.dma("sp", cs_sb[:], cs.rearrange("(a p) c -> p a c", p=128), w=["cs"])
    px = [P.ps([128, 512]) for _ in range(2)]
    ox = [P.sb([128, C]) for _ in range(2)]
    it = 0
    for j in range(NK1):
        s = it % 2; it += 1; hf = j // H
        mm(P, px[s][:, :C], mc_sb[:], ar_sb[:, j, :], True, False, r=["mc", ("ar", hf)], w=[("px", s)])
        mm(P, px[s][:, :C], ms_sb[:], ai_sb[:, j, :], False, True, r=["ms", ("ai", hf)], w=[("px", s)])
        P.op("act", lambda g, s=s: g.copy(out=ox[s][:], in_=px[s][:, :C]), r=[("px", s)], w=[("ox", s)])
        P.dma("sp", xr[:, j, :], ox[s][:], r=[("ox", s)])
    for kc in range(NTC):
        s = it % 2; it += 1
        n = 0
        for mat, mk, z, zk in ((cc_sb, "cc", zcr_sb, "zcr"), (cs_sb, "cs", zci_sb, "zci")):
            for tcn in range(NTC):
                mm(P, px[s][:, :C], mat[:, tcn, kc * 128:(kc + 1) * 128], z[:, tcn, :], n == 0, n == 2 * NTC - 1,
                   r=[mk, zk], w=[("px", s)])
                n += 1
        P.op("act", lambda g, s=s: g.copy(out=ox[s][:], in_=px[s][:, :C]), r=[("px", s)], w=[("ox", s)])
        P.dma("sp", xc[kc * 128:(kc + 1) * 128, :], ox[s][:], r=[("ox", s)])
    return P.build()


def dft_consts():
    j = np.arange(128)
    ang = 2 * np.pi * np.outer(j, j) / 128
    c128 = np.cos(ang); s128 = np.sin(ang)
    k1 = np.arange(128)[:, None]; t2 = np.arange(128)[None, :]
    ph = 2 * np.pi * (k1 * t2) / 16384.0
    j2 = np.arange(256)
    a2 = 2 * np.pi * np.outer(j2, j2) / 256
    f = np.float32
    return dict(c1=(c128 / 128).astype(f), s1=(s128 / 128).astype(f), ns1=(-s128 / 128).astype(f),
                c2=c128.astype(f), s2=s128.astype(f), twc=np.cos(ph).astype(f), tws=np.sin(ph).astype(f),
                cc=(np.cos(a2) / 16).astype(f), cs=(np.sin(a2) / 16).astype(f))


def run_fourier(run, zr, zi, zcr, zci):
    K = dft_consts()
    Zr = zr.reshape(128, 128, 512); Zi = zi.reshape(128, 128, 512)
    ins = []
    for i in range(8):
        sl = slice(16 * i, 16 * (i + 1))
        ins.append(dict(zr=np.ascontiguousarray(Zr[:, sl]), zi=np.ascontiguousarray(Zi[:, sl]), mc=K["c1"], ms=K["s1"], mns=K["ns1"],
                        twc=np.ascontiguousarray(K["twc"][:, sl]), tws=np.ascontiguousarray(K["tws"][:, sl]),
                        ntws=np.ascontiguousarray(-K["tws"][:, sl])))
    res = run("pf1", lambda: build_pf1(), ins)
    Ar = np.concatenate([r["ar"] for r in res], axis=1)
    Ai = np.concatenate([r["ai"] for r in res], axis=1)
    ArT = Ar.transpose(1, 0, 2); AiT = Ai.transpose(1, 0, 2)
    ins = []
    for i in range(8):
        sl = slice(16 * i, 16 * (i + 1))
        ins.append(dict(ar=np.ascontiguousarray(ArT[:, sl]), ai=np.ascontiguousarray(AiT[:, sl]), mc=K["c2"], ms=K["s2"],
                        zcr=zcr, zci=zci, cc=K["cc"], cs=K["cs"]))
    res = run("pf2", lambda: build_pf2(), ins)
    X = np.concatenate([r["xr"] for r in res], axis=1).reshape(16384, 512)
    return X, res[0]["xc"]


_NC_CACHE = {}


def run_prog(name, builder, ins):
    if name not in _NC_CACHE:
        _NC_CACHE[name] = builder()
    res = run_bass_kernel_spmd(_NC_CACHE[name], ins, core_ids=list(range(len(ins))))
    return [{k: np.asarray(v) for k, v in r.items()} for r in res.results]


FFN_H = 5632


P3_NOWEIGHTS = False


def build_p3(NB, TB):
    P = Prog()
    Tc = NB * TB
    KC = 16
    HC = FFN_H // 128
    xT = P.din("xT", [D_MODEL, Tc], F32)
    attnT = P.din("attnT", [1024, Tc], BF16)
    yfT = P.din("yfT", [512, Tc], F32); ybT = P.din("ybT", [512, Tc], F32); usT = P.din("usT", [512, Tc], F32)
    xfT = P.din("xfT", [512, Tc], F32)
    glu_w = P.din("glu_w", [128, 4, 512], BF16); fw = P.din("fw", [128, 4, 512], BF16)
    w_out = P.din("w_out", [4, 128, KC, 512], BF16)
    w_gate = P.din("w_gate", [HC // 4, 128, KC, 512], BF16); w_up = P.din("w_up", [HC // 4, 128, KC, 512], BF16)
    w_down = P.din("w_down", [4, 4, 128, 11, 512], BF16)
    vnames = ["gpm", "gm", "gpf", "scf", "shf", "gof", "gf"]
    vin = {n: P.din(n, [128, KC], F32) for n in vnames}
    ssm_d = P.din("ssm_d", [128, 4], F32); glu_b = P.din("glu_b", [128, 4], F32)
    xoT = P.dout("xoT", [D_MODEL, Tc], F32)

    vs = {}
    for n in vnames:
        vs[n] = P.sb([128, KC]); P.dma("sp", vs[n][:], vin[n][:], w=[n])
    d_sb = P.sb([128, 4]); gb_sb = P.sb([128, 4])
    P.dma("sp", d_sb[:], ssm_d[:], w=["ssm_d"]); P.dma("sp", gb_sb[:], glu_b[:], w=["glu_b"])
    GM = P.sb([128, KC]); G2 = P.sb([128, KC]); GF = P.sb([128, KC])
    tt(P, "dve", GM[:], vs["gpm"][:], vs["gm"][:], ALU.mult, r=["gpm", "gm"], w=["GM"])
    stt(P, G2[:], vs["scf"][:], 1.0, vs["gpf"][:], ALU.add, ALU.mult, r=["scf", "gpf"], w=["G2"])
    tt(P, "dve", GF[:], vs["gof"][:], vs["gf"][:], ALU.mult, r=["gof", "gf"], w=["GF"])
    ones = P.sb([128, 128], BF16)
    P.op("pool", lambda g: g.memset(ones[:], 1.0), w=["ones"])

    x_sb = P.sb([128, KC, TB], F32)
    cat = P.sb([128, KC, TB], BF16)
    yb_sb = P.sb([128, KC, TB], BF16)
    a_sb = P.sb([128, HC, TB], BF16)
    WB = [P.sb([128, KC, 512], BF16) for _ in range(3)]
    s_us = [P.sb([128, TB]) for _ in range(1)]; s_yf = [P.sb([128, TB]) for _ in range(1)]; s_yb = [P.sb([128, TB]) for _ in range(1)]
    s_xf = [P.sb([128, TB]) for _ in range(1)]
    e1 = P.sb([128, TB]); e2 = P.sb([128, TB]); e3 = P.sb([128, TB])
    gfp = P.sb([128, 4, TB], F32); gbf = P.sb([128, 4, TB], BF16); xfb = P.sb([128, 4, TB], BF16)
    sqc = [P.sb([128, TB], BF16) for _ in range(2)]
    rstd = P.sb([128, TB]); tmpf = [P.sb([128, TB]) for _ in range(2)]
    sg = [P.sb([128, TB]) for _ in range(2)]
    xo = [P.sb([128, TB]) for _ in range(2)]
    PS = [P.ps([128, 512]) for _ in range(8)]
    xv = xT.rearrange("(kc p) t -> p kc t", p=128)
    av = attnT.rearrange("(kc p) t -> p kc t", p=128)
    wcnt = [0]

    def wload(src_ap, shape_kc):
        s = wcnt[0] % 3; wcnt[0] += 1
        if P3_NOWEIGHTS and wcnt[0] > 3:
            return s
        P.dma("sp", WB[s][:, :shape_kc, :], src_ap, w=[("WB", s)])
        return s

    def rms_bc(nsq_key):
        rstd_from_ps(P, rstd[:], PS[4][:, :TB], D_MODEL, EPS, r=["ps4"], w=["rstd"], tmpkey=None)

    nsq = [0]

    pend = []

    def sq_acc(src_ps, src_key, idx, n, defer=False):
        s = nsq[0] % 2; nsq[0] += 1
        act(P, sqc[s][:], src_ps, AF.Square, r=[src_key], w=[("sqc", s)])
        f_ = lambda: mm(P, PS[4][:, :TB], ones[:], sqc[s][:], idx == 0, idx == n - 1, r=["ones", ("sqc", s)], w=["ps4"])
        if defer:
            pend.append(f_)
        else:
            f_()

    def flush():
        while pend:
            pend.pop(0)()

    def stageA1(b):
        tsl = slice(b * TB, (b + 1) * TB)
        P.dma("sp", cat[:, 0:8, :], av[:, :, tsl], w=[("cat", k) for k in range(8)])
        for ch in range(4):
            s = 0
            fsl = slice(ch * 128, (ch + 1) * 128)
            P.dma("sp", s_us[s][:], usT[fsl, tsl], w=[("us", s)])
            P.dma("sp", s_yf[s][:], yfT[fsl, tsl], w=[("yf", s)])
            P.dma("sp", s_yb[s][:], ybT[fsl, tsl], w=[("yb", s)])
            P.dma("sp", s_xf[s][:], xfT[fsl, tsl], w=[("xf", s)])
            stt(P, e1[:], s_us[s][:], d_sb[:, ch:ch + 1], s_yf[s][:], ALU.mult, ALU.add, r=[("us", s), "ssm_d", ("yf", s)], w=["e1"])
            tt(P, "dve", e1[:], e1[:], s_yb[s][:], ALU.add, r=["e1", ("yb", s)], w=["e1"])
            tt(P, "pool", e2[:], e1[:], e1[:], ALU.mult, r=["e1"], w=["e2"])
            ts(P, "dve", e2[:], e2[:], 0.044715, 1.0, ALU.mult, ALU.add, r=["e2"], w=["e2"])
            tt(P, "pool", e3[:], e2[:], e1[:], ALU.mult, r=["e2", "e1"], w=["e3"])
            act(P, e3[:], e3[:], AF.Sigmoid, r=["e3"], w=["e3"], scale=1.5957691216057308)
            tt(P, "dve", gfp[:, ch, :], e3[:], e1[:], ALU.mult, r=["e3", "e1"], w=[("gfp", ch)])
            P.op("act", lambda g, ch=ch: g.copy(out=gbf[:, ch, :], in_=gfp[:, ch, :]), r=[("gfp", ch)], w=[("gbf", ch)])
            P.op("act", lambda g, ch=ch, s=s: g.copy(out=xfb[:, ch, :], in_=s_xf[s][:]), r=[("xf", s)], w=[("xfb", ch)])

    stageA1(0)
    for b in range(NB):
        tsl = slice(b * TB, (b + 1) * TB)
        P.dma("sp", x_sb[:], xv[:, :, tsl], w=["x"])
        sgw = wload(glu_w[:], 4)
        sfw = wload(fw[:], 4)
        for co in range(4):
            pz = PS[5 + co % 2]; pk = ("ps", 5 + co % 2)
            for ci in range(4):
                mm(P, pz[:, :TB], WB[sgw][:, ci, co * 128:(co + 1) * 128], gbf[:, ci, :], ci == 0, ci == 3,
                   r=[("WB", sgw), ("gbf", ci)], w=[pk])
            act(P, e2[:], pz[:, :TB], AF.Sigmoid, r=[pk, "glu_b"], w=["e2"], bias=gb_sb[:, co:co + 1])
            tt(P, "dve", cat[:, 8 + co, :], e2[:], gfp[:, co, :], ALU.mult, r=["e2", ("gfp", co)], w=[("cat", 8 + co)])
        for co in range(4):
            pz = PS[5 + co % 2]; pk = ("ps", 5 + co % 2)
            for ci in range(4):
                mm(P, pz[:, :TB], WB[sfw][:, ci, co * 128:(co + 1) * 128], xfb[:, ci, :], ci == 0, ci == 3,
                   r=[("WB", sfw), ("xfb", ci)], w=[pk])
            P.op("act", lambda g, co=co, pz=pz: g.copy(out=cat[:, 12 + co, :], in_=pz[:, :TB]), r=[pk], w=[("cat", 12 + co)])
        for og in range(4):
            sw = wload(w_out[og], KC)
            for o4 in range(4):
                oc = og * 4 + o4
                pa = PS[oc % 2]; pk = ("ps", oc % 2)
                for kc in range(KC):
                    mm(P, pa[:, :TB], WB[sw][:, kc, o4 * 128:(o4 + 1) * 128], cat[:, kc, :], kc == 0, kc == KC - 1,
                       r=[("WB", sw), ("cat", kc)], w=[pk])
                flush()
                P.op("dve", lambda g, oc=oc, pa=pa: g.tensor_copy(out=yb_sb[:, oc, :], in_=pa[:, :TB]), r=[pk], w=[("yb_sb", oc)])
                sq_acc(pa[:, :TB], pk, oc, KC, defer=True)
        flush()
        rms_bc(None)
        for oc in range(KC):
            s = oc % 2
            stt(P, tmpf[s][:], yb_sb[:, oc, :], GM[:, oc:oc + 1], rstd[:], ALU.mult, ALU.mult,
                r=[("yb_sb", oc), "GM", "rstd"], w=[("tmpf", s)])
            tt(P, "pool" if oc % 2 else "dve", x_sb[:, oc, :], x_sb[:, oc, :], tmpf[s][:], ALU.add, r=["x", ("tmpf", s)], w=[("x1", oc)])
        for kc in range(KC):
            sq_acc(x_sb[:, kc, :], ("x1", kc), kc, KC)
        rms_bc(None)
        for kc in range(KC):
            s = kc % 2
            stt(P, tmpf[s][:], x_sb[:, kc, :], G2[:, kc:kc + 1], rstd[:], ALU.mult, ALU.mult,
                r=[("x1", kc), "G2", "rstd"], w=[("tmpf", s)])
            act(P, cat[:, kc, :], tmpf[s][:], AF.Identity, r=[("tmpf", s), "shf"], w=[("cat", kc)], bias=vs["shf"][:, kc:kc + 1])
        for hg in range(HC // 4):
            swg = wload(w_gate[hg], KC)
            swu = wload(w_up[hg], KC)
            for h4 in range(4):
                hc = hg * 4 + h4
                pg = PS[(hc % 2) * 2]; pgk = ("ps", (hc % 2) * 2)
                pu = PS[(hc % 2) * 2 + 1]; puk = ("ps", (hc % 2) * 2 + 1)
                for kc in range(KC):
                    mm(P, pg[:, :TB], WB[swg][:, kc, h4 * 128:(h4 + 1) * 128], cat[:, kc, :], kc == 0, kc == KC - 1,
                       r=[("WB", swg), ("cat", kc)], w=[pgk])
                for kc in range(KC):
                    mm(P, pu[:, :TB], WB[swu][:, kc, h4 * 128:(h4 + 1) * 128], cat[:, kc, :], kc == 0, kc == KC - 1,
                       r=[("WB", swu), ("cat", kc)], w=[puk])
                s = hc % 2
                act(P, sg[s][:], pg[:, :TB], AF.Silu, r=[pgk], w=[("sg", s)])
                tt(P, "dve", a_sb[:, hc, :], pu[:, :TB], sg[s][:], ALU.mult, r=[puk, ("sg", s)], w=[("a", hc)])
        if b + 1 < NB:
            stageA1(b + 1)
        for og in range(4):
            for q in range(4):
                sw = wload(w_down[og, q], 11)
                for h in range(11):
                    hc = q * 11 + h
                    for o4 in range(4):
                        mm(P, PS[o4][:, :TB], WB[sw][:, h, o4 * 128:(o4 + 1) * 128], a_sb[:, hc, :], hc == 0, hc == HC - 1,
                           r=[("WB", sw), ("a", hc)], w=[("ps", o4)])
            for o4 in range(4):
                oc = og * 4 + o4
                P.op("dve", lambda g, oc=oc, o4=o4: g.tensor_copy(out=yb_sb[:, oc, :], in_=PS[o4][:, :TB]), r=[("ps", o4)], w=[("yb_sb", oc)])
                sq_acc(PS[o4][:, :TB], ("ps", o4), oc, KC)
        rms_bc(None)
        for oc in range(KC):
            s = oc % 2
            stt(P, tmpf[s][:], yb_sb[:, oc, :], GF[:, oc:oc + 1], rstd[:], ALU.mult, ALU.mult,
                r=[("yb_sb", oc), "GF", "rstd"], w=[("tmpf", s)])
            tt(P, "pool" if oc % 2 else "dve", xo[s][:], x_sb[:, oc, :], tmpf[s][:], ALU.add, r=[("x1", oc), ("tmpf", s)], w=[("xo", s)])
            P.dma("sp", xoT[oc * 128:(oc + 1) * 128, tsl], xo[s][:], r=[("xo", s)])
    return P.build()


def _v2(v):
    return np.ascontiguousarray(np.asarray(v, np.float32).reshape(-1, 128).T)


def _cat(res, name, axis):
    return np.concatenate([r[name] for r in res], axis=axis)


BIG_W = ["w_in", "ssm_glu_w", "fourier_w", "w_out", "ffn_w_gate", "ffn_w_up", "ffn_w_down"]


def cast_weights(inp):
    parts = []
    shapes = []
    for l in range(2):
        for n in BIG_W:
            a = np.asarray(inp[n][l], np.float32)
            parts.append(a.ravel()); shapes.append((l, n, a.shape))
    flat = np.concatenate(parts)
    tot = flat.size
    per = tot // (8 * 128)
    assert per * 8 * 128 == tot and per % 5120 == 0, (tot, per)
    flat = flat.reshape(8, 128, per)
    res = run_prog("cast", lambda: build_cast(per, 5120), [{"src": flat[i]} for i in range(8)])
    out = np.concatenate([r["dst"].reshape(-1) for r in res])
    W = [dict(), dict()]
    off = 0
    for (l, n, shp) in shapes:
        sz = int(np.prod(shp))
        W[l][n] = out[off:off + sz].reshape(shp); off += sz
    return W


def kernel(**inp):
    f = np.float32
    x = np.asarray(inp["x"], f)[0]
    xc = np.asarray(inp["ctx"], f)[0]
    T = x.shape[0]; TC = xc.shape[0]
    Wb = cast_weights(inp)
    c2 = np.stack([np.asarray(inp["c"], f)[0], np.asarray(inp["c_ctx"], f)], 0)
    c2T = np.ascontiguousarray(c2.reshape(2, 16, 128).transpose(2, 1, 0))
    ada_w = np.asarray(inp["ada_w"], f); ada_b = np.asarray(inp["ada_b"], f)
    CW = 1536
    ins = [dict(c2T=c2T, w=np.ascontiguousarray(ada_w[:, :, i * CW:(i + 1) * CW]),
                b2=np.ascontiguousarray(np.repeat(ada_b[:, None, i * CW:(i + 1) * CW], 2, axis=1))) for i in range(8)]
    mod = _cat(run_prog("adaln", lambda: build_adaln(), ins), "mod", 2)
    cosT, sinT = rope_tables(np.arange(T))
    cosC = np.ones((128, TC), f); sinC = np.zeros((128, TC), f)
    rotT = rope_rotT(); cd, sdn = chan_dft()
    Tc = T // 8; Cc = TC // 8
    for l in range(2):
        need_ctx = l == 0
        m = mod[l]
        sh_m, sc_m, g_m, sh_f, sc_f, g_f = [m[:, k * 2048:(k + 1) * 2048] for k in range(6)]
        qn = np.asarray(inp["q_norm"][l], f)[:, None].copy(); kn = np.asarray(inp["k_norm"][l], f)[:, None].copy()
        com = dict(gam=_v2(inp["norm_mix_pre"][l]), w_in=Wb[l]["w_in"], qn=qn, kn=kn, rotT=rotT, cd=cd, sdn=sdn)
        ins = [dict(com, xT=np.ascontiguousarray(x[i * Tc:(i + 1) * Tc].T), scv=_v2(sc_m[0]), shv=_v2(sh_m[0]),
                    cosT=np.ascontiguousarray(cosT[:, i * Tc:(i + 1) * Tc]), sinT=np.ascontiguousarray(sinT[:, i * Tc:(i + 1) * Tc]))
               for i in range(8)]
        r1 = run_prog("p1", lambda: build_p1(4, 512), ins)
        ins = [dict(com, xT=np.ascontiguousarray(xc[i * Cc:(i + 1) * Cc].T), scv=_v2(sc_m[1]), shv=_v2(sh_m[1]),
                    cosT=np.ascontiguousarray(cosC[:, i * Cc:(i + 1) * Cc]), sinT=np.ascontiguousarray(sinC[:, i * Cc:(i + 1) * Cc]))
               for i in range(8)]
        r1c = run_prog("p1c", lambda: build_p1(1, 32), ins)
        L1 = {n: _cat(r1, n, 1) for n in ("qT", "kT", "vT", "usT", "zrT", "ziT")}
        C1 = {n: _cat(r1c, n, 1) for n in ("qT", "kT", "vT", "usT", "zrT", "ziT")}
        ins = []
        for i in range(8):
            j = i // 4
            ins.append(dict(qT=np.ascontiguousarray(L1["qT"][i * 128:(i + 1) * 128]),
                            kT=np.ascontiguousarray(np.concatenate([L1["kT"][j * 128:(j + 1) * 128], C1["kT"][j * 128:(j + 1) * 128]], 1)),
                            v=np.ascontiguousarray(np.concatenate([L1["vT"][j * 128:(j + 1) * 128], C1["vT"][j * 128:(j + 1) * 128]], 1).T)))
        attnT = np.ascontiguousarray(np.concatenate([r["o"] for r in run_prog("pa", lambda: build_attn2(T, T + TC, 512), ins)], 1).T)
        if need_ctx:
            ins = []
            for i in range(8):
                j = i // 4
                ins.append(dict(qT=np.ascontiguousarray(C1["qT"][i * 128:(i + 1) * 128]),
                                kT=np.ascontiguousarray(C1["kT"][j * 128:(j + 1) * 128]),
                                v=np.ascontiguousarray(C1["vT"][j * 128:(j + 1) * 128].T)))
            attncT = np.ascontiguousarray(np.concatenate([r["o"] for r in run_prog("pac", lambda: build_attn2(TC, TC, TC), ins)], 1).T)
        us = L1["usT"]; usc = C1["usT"]
        fwd = np.concatenate([usc, us], 1)
        bwd = np.concatenate([usc[:, ::-1], us[:, ::-1]], 1)
        ins = []
        for i in range(8):
            rows = slice(64 * i, 64 * (i + 1))
            u_core = np.ascontiguousarray(np.concatenate([fwd[rows], bwd[rows]], 0))
            idx = [(c // 2, 4 * i + 2 * (c % 2) + gi) for c in range(4) for gi in range(2)]
            pk = ssm_pack(np.stack([np.asarray(inp["ssm_lam_re"], f)[l, d, g] for d, g in idx]),
                          np.stack([np.asarray(inp["ssm_lam_im"], f)[l, d, g] for d, g in idx]),
                          np.array([np.asarray(inp["ssm_log_dt"], f)[l, d, g] for d, g in idx], f),
                          np.stack([np.asarray(inp["ssm_b_re"], f)[l, d, g] for d, g in idx]),
                          np.stack([np.asarray(inp["ssm_b_im"], f)[l, d, g] for d, g in idx]),
                          np.stack([np.asarray(inp["ssm_c_re"], f)[l, d, g] for d, g in idx]),
                          np.stack([np.asarray(inp["ssm_c_im"], f)[l, d, g] for d, g in idx]))
            ins.append(dict(pk, ue=np.ascontiguousarray(u_core[:, 0::2]), uo=np.ascontiguousarray(u_core[:, 1::2])))
        rs = run_prog("ps", lambda: build_ssm2((T + TC) // 2, 512), ins)
        ys = []
        for r in rs:
            yy = np.empty((128, T + TC), f)
            yy[:, 0::2] = r["ye"]; yy[:, 1::2] = r["yo"]
            ys.append(yy)
        yf_all = np.concatenate([yy[0:64] for yy in ys], 0)
        yb_all = np.concatenate([yy[64:128] for yy in ys], 0)
        yfT, yfcT = yf_all[:, TC:], yf_all[:, :TC]
        ybT, ybcT = yb_all[:, TC:][:, ::-1], yb_all[:, :TC][:, ::-1]
        X, Xc = run_fourier(run_prog, np.ascontiguousarray(L1["zrT"].T), np.ascontiguousarray(L1["ziT"].T),
                            np.ascontiguousarray(C1["zrT"].T), np.ascontiguousarray(C1["ziT"].T))
        xfT = X.T; xfcT = Xc.T
        comw = dict(glu_w=tile_kc(Wb[l]["ssm_glu_w"])[0], fw=tile_kc(Wb[l]["fourier_w"])[0], w_out=tile_kc(Wb[l]["w_out"]),
                    w_gate=tile_kc(Wb[l]["ffn_w_gate"]), w_up=tile_kc(Wb[l]["ffn_w_up"]), w_down=tile_down(Wb[l]["ffn_w_down"]), gpm=_v2(inp["norm_mix_post"][l]),
                    gpf=_v2(inp["norm_ffn_pre"][l]), gof=_v2(inp["norm_ffn_post"][l]),
                    ssm_d=_v2(inp["ssm_d"][l]), glu_b=_v2(inp["ssm_glu_b"][l]))

        def p3_ins(xs, at, yf_, yb_, us_, xf_, n, v):
            out = []
            for i in range(8):
                sl = slice(i * n, (i + 1) * n)
                out.append(dict(comw, xT=np.ascontiguousarray(xs[sl].T), attnT=np.ascontiguousarray(at[:, sl]),
                                yfT=np.ascontiguousarray(yf_[:, sl]), ybT=np.ascontiguousarray(yb_[:, sl]),
                                usT=np.ascontiguousarray(us_[:, sl]), xfT=np.ascontiguousarray(xf_[:, sl]),
                                gm=_v2(g_m[v]), scf=_v2(sc_f[v]), shf=_v2(sh_f[v]), gf=_v2(g_f[v])))
            return out
        r3 = run_prog("p3", lambda: build_p3(4, 512), p3_ins(x, attnT, yfT, ybT, us, xfT, Tc, 0))
        x_new = np.ascontiguousarray(_cat(r3, "xoT", 1).T)
        if need_ctx:
            r3c = run_prog("p3c", lambda: build_p3(1, 32), p3_ins(xc, attncT, yfcT, ybcT, usc, xfcT, Cc, 1))
            xc = np.ascontiguousarray(_cat(r3c, "xoT", 1).T)
        x = x_new
    return x[None].astype(np.float32)


def build_attn2(NQ, NK, QB):
    P = Prog()
    NQB = NQ // QB
    NKT = NK // 128
    NPR = NKT // 2
    QC = QB // 128
    qT = P.din("qT", [128, NQ], BF16)
    kT = P.din("kT", [128, NK], BF16)
    v = P.din("v", [NK, 128], BF16)
    o = P.dout("o", [NQ, 128], BF16)
    q_sb = P.sb([128, NQ], BF16); k_sb = P.sb([128, NK], BF16); v_sb = P.sb([128, NKT, 130], BF16)
    KCH = 13 if NKT % 13 == 0 else NKT
    vv = v.rearrange("(kt p) d -> p kt d", p=128)
    QCH = 4 if NQB % 4 == 0 else NQB
    P.dma("sp", q_sb[:, 0:QCH * QB], qT[:, 0:QCH * QB], w=[("Q", 0)])
    for c in range(NKT // KCH):
        P.dma("sp", k_sb[:, c * KCH * 128:(c + 1) * KCH * 128], kT[:, c * KCH * 128:(c + 1) * KCH * 128], w=[("K", c)])
        P.op("pool", lambda g, c=c: g.memset(v_sb[:, c * KCH:(c + 1) * KCH, 128:130], 1.0), w=[("V1", c)])
        P.dma("sp", v_sb[:, c * KCH:(c + 1) * KCH, 0:128], vv[:, c * KCH:(c + 1) * KCH, :], w=[("V", c)])
    for c in range(1, NQB // QCH):
        P.dma("sp", q_sb[:, c * QCH * QB:(c + 1) * QCH * QB], qT[:, c * QCH * QB:(c + 1) * QCH * QB], w=[("Q", c)])
    S = [P.ps([128, 2, 512], F32) for _ in range(2)]
    O = [[P.ps([128, 512], F32) for _ in range(2)] for _ in range(2)]
    PT = [P.sb([128, 2, QB], BF16) for _ in range(2)]
    rs = P.sb([128, 4], F32)
    o_sb = [P.sb([128, QC, 128], BF16) for _ in range(2)]
    ov = o.rearrange("(qb qc p) d -> qb p qc d", qc=QC, p=128)
    scale = 128 ** -0.5
    items = [(qb, pr) for qb in range(NQB) for pr in range(NPR)]

    def emit_S(i):
        qb, pr = items[i]
        s = i % 2
        for j in range(2):
            kt = 2 * pr + j
            mm(P, S[s][:, j, :QB], k_sb[:, kt * 128:(kt + 1) * 128], q_sb[:, qb * QB:(qb + 1) * QB], True, True,
               r=[("K", kt // KCH), ("Q", qb // QCH)], w=[("S", s)])

    emit_S(0)
    for i, (qb, pr) in enumerate(items):
        if i + 1 < len(items):
            emit_S(i + 1)
        s = i % 2
        ob = qb % 2
        act(P, PT[s][:, :, :], S[s][:, :, :QB], AF.Exp, r=[("S", s)], w=[("PT", s)], scale=scale)
        for j in range(2):
            kt = 2 * pr + j
            for qc in range(QC):
                bank = O[ob][qc // 2]
                col = (qc % 2) * 129
                first = (kt == 0 and qc % 2 == 0)
                P.op("pe", lambda g, bank=bank, col=col, s=s, j=j, qc=qc, kt=kt, first=first: g.matmul(
                    bank[:, col:col + 129], PT[s][:, j, qc * 128:(qc + 1) * 128], v_sb[:, kt, 0:129],
                    start=first, stop=(kt == NKT - 1), skip_group_check=True),
                    r=[("V", kt // KCH), ("V1", kt // KCH), ("PT", s)], w=[("O", ob, qc // 2)])
        if pr == NPR - 1:
            for qc in range(QC):
                bank = O[ob][qc // 2]
                col = (qc % 2) * 129
                bk = ("O", ob, qc // 2)
                P.op("dve", lambda g, bank=bank, col=col, qc=qc: g.reciprocal(out=rs[:, qc:qc + 1], in_=bank[:, col + 128:col + 129]),
                     r=[bk], w=[("rs", qc)])
                ts(P, "dve", o_sb[ob][:, qc, :], bank[:, col:col + 128], rs[:, qc:qc + 1], None, ALU.mult, None,
                   r=[bk, ("rs", qc)], w=[("o_sb", ob)])
            P.dma("sp", ov[qb], o_sb[ob][:], r=[("o_sb", ob)])
    return P.build()


def tile_kc(w):
    K_, C_ = w.shape
    return np.ascontiguousarray(w.reshape(K_ // 128, 128, C_ // 512, 512).transpose(2, 1, 0, 3))


def tile_down(w):
    return np.ascontiguousarray(w.reshape(4, 11, 128, 4, 512).transpose(3, 0, 2, 1, 4))
```

```python
from contextlib import ExitStack
import numpy as np
import concourse.bass as bass
import concourse.mybir as mybir
from concourse.bass_utils import run_bass_kernel_spmd

F32 = mybir.dt.float32
BF16 = mybir.dt.bfloat16
I32 = mybir.dt.int32
AF = mybir.ActivationFunctionType
ALU = mybir.AluOpType
AX = mybir.AxisListType


STRICT_WAR = True


class Prog:
    def __init__(self):
        self.nc = bass.Bass("TRN2", target_bir_lowering=False)
        self.es = ExitStack()
        self.ops = []
        self.last_w = {}
        self.readers = {}
        self.n_t = 0
        self.psum_keys = set()

    def din(self, name, shape, dt=F32):
        return self.nc.dram_tensor(name, list(shape), dt, kind="ExternalInput").ap()

    def dout(self, name, shape, dt=F32):
        return self.nc.dram_tensor(name, list(shape), dt, kind="ExternalOutput").ap()

    def sb(self, shape, dt=F32, name=None):
        self.n_t += 1
        return self.es.enter_context(self.nc.sbuf_tensor(name or f"sb{self.n_t}", list(shape), dt))

    def ps(self, shape, dt=F32, name=None):
        self.n_t += 1
        return self.es.enter_context(self.nc.psum_tensor(name or f"ps{self.n_t}", list(shape), dt))

    def op(self, eng, fn, r=(), w=(), dma=False, semkey=None):
        i = len(self.ops)
        raw = set()
        war = set()
        if eng == "pe" and not dma:
            self.psum_keys.update(w)
        for k in r:
            if k in self.last_w:
                raw.add(self.last_w[k])
            if k in self.psum_keys:
                for j in self.readers.get(k, ()):
                    if self.ops[j]["eng"] != eng:
                        raw.add(j)
        for k in w:
            if k in self.last_w:
                raw.add(self.last_w[k])
            war.update(self.readers.get(k, ()))
        self.ops.append(dict(eng=eng, fn=fn, raw=raw, war=war - raw, dma=dma, semkey=semkey))
        for k in r:
            self.readers.setdefault(k, []).append(i)
        for k in w:
            self.last_w[k] = i
            self.readers[k] = []
        return i

    def dma(self, q, out, in_, r=(), w=(), semkey=None):
        if semkey is None:
            semkey = ("ld", w[0]) if w else ("st", r[0])
        return self.op(q, lambda e: e.dma_start(out=out, in_=in_), r=r, w=w, dma=True, semkey=semkey)

    def build(self):
        nc = self.nc
        ops = self.ops
        n = len(ops)
        for i, o in enumerate(ops):
            deps = set()
            for j in o["raw"]:
                pj = ops[j]
                if (not pj["dma"]) and pj["eng"] == o["eng"] and o["eng"] == "pe" and not o["dma"]:
                    continue
                deps.add(j)
            for j in o["war"]:
                pj = ops[j]
                if (not pj["dma"]) and pj["eng"] == o["eng"] and not o["dma"] and (o["eng"] == "pe" or not STRICT_WAR):
                    continue
                if (not pj["dma"]) and pj["eng"] == o["eng"] and o["dma"]:
                    pass
                deps.add(j)
            o["deps"] = deps
        has_dep = [False] * n
        for o in ops:
            for j in o["deps"]:
                has_dep[j] = True
        engs = ["pe", "act", "dve", "pool", "sp"]
        sems = {e: self.es.enter_context(nc.semaphore(f"s_{e}")) for e in engs}
        cnt = {e: 0 for e in engs}
        dsem = {}
        for i, o in enumerate(ops):
            if o["dma"]:
                k = o["semkey"]
                if k not in dsem:
                    dsem[k] = [self.es.enter_context(nc.semaphore(f"d{len(dsem)}")), 0]
                dsem[k][1] += 16
                o["sem"] = dsem[k][0]
                o["ticket"] = dsem[k][1]
                o["inc"] = 16
            elif has_dep[i]:
                cnt[o["eng"]] += 1
                o["sem"] = sems[o["eng"]]
                o["ticket"] = cnt[o["eng"]]
                o["inc"] = 1
            else:
                o["sem"] = None
        self.n_sems = len(dsem) + 5
        per = {e: [i for i, o in enumerate(ops) if o["eng"] == e] for e in engs}

        def replay(e):
            def run(eng):
                seen = {}
                for i in per[e]:
                    o = ops[i]
                    need = {}
                    for j in o["deps"]:
                        pj = ops[j]
                        s = pj["sem"]
                        key = id(s)
                        if seen.get(key, 0) >= pj["ticket"]:
                            continue
                        if key not in need or need[key][1] < pj["ticket"]:
                            need[key] = (s, pj["ticket"])
                    for key, (s, t) in need.items():
                        eng.wait_ge(s, t)
                        seen[key] = t
                    ins = o["fn"](eng)
                    if o["sem"] is not None:
                        ins.then_inc(o["sem"], o["inc"])
                if e == "sp":
                    for k, (s, t) in dsem.items():
                        if seen.get(id(s), 0) < t:
                            eng.wait_ge(s, t)
            return run

        with nc.Block() as block:
            block.sync(replay("sp"))
            block.tensor(replay("pe"))
            block.scalar(replay("act"))
            block.vector(replay("dve"))
            block.gpsimd(replay("pool"))
        self.es.close()
        return nc


def build_cast(N, CH=4096):
    P = Prog()
    src = P.din("src", [128, N], F32)
    dst = P.dout("dst", [128, N], BF16)
    nb = N // CH
    NS = 3
    fin = [P.sb([128, CH], F32) for _ in range(NS)]
    fout = [P.sb([128, CH], BF16) for _ in range(NS)]
    engs = ["dve", "act", "pool"]
    for b in range(nb):
        s = b % NS
        P.dma("sp", fin[s][:], src[:, b * CH:(b + 1) * CH], w=[("fin", s)])
        e = engs[b % 3]
        if e == "act":
            P.op("act", lambda g, s=s: g.copy(out=fout[s][:], in_=fin[s][:]), r=[("fin", s)], w=[("fout", s)])
        else:
            P.op(e, lambda g, s=s: g.tensor_copy(out=fout[s][:], in_=fin[s][:]), r=[("fin", s)], w=[("fout", s)])
        P.dma("sp", dst[:, b * CH:(b + 1) * CH], fout[s][:], r=[("fout", s)])
    return P.build()


def mm(P, out, lhsT, rhs, start, stop, r, w):
    P.op("pe", lambda g: g.matmul(out, lhsT, rhs, start=start, stop=stop), r=r, w=w)


def act(P, out, in_, func, r, w, bias=None, scale=None):
    kw = {}
    if bias is not None:
        kw["bias"] = bias
    if scale is not None:
        kw["scale"] = scale
    P.op("act", lambda g: g.activation(out=out, in_=in_, func=func, **kw), r=r, w=w)


def tt(P, eng, out, a, b, op, r, w):
    P.op(eng, lambda g: g.tensor_tensor(out=out, in0=a, in1=b, op=op), r=r, w=w)


def stt(P, out, in0, scalar, in1, op0, op1, r, w):
    P.op("dve", lambda g: g.scalar_tensor_tensor(out=out, in0=in0, scalar=scalar, in1=in1, op0=op0, op1=op1), r=r, w=w)


def ts(P, eng, out, in0, s1, s2, op0, op1, r, w):
    if op1 is None:
        P.op(eng, lambda g: g.tensor_scalar(out=out, in0=in0, scalar1=s1, scalar2=None, op0=op0), r=r, w=w)
    else:
        P.op(eng, lambda g: g.tensor_scalar(out=out, in0=in0, scalar1=s1, scalar2=s2, op0=op0, op1=op1), r=r, w=w)


def rstd_from_ps(P, out_sb, in_ps, n, eps, r, w, tmpkey):
    act(P, out_sb, in_ps, AF.Ln, r=list(r) + ["epsc"], w=w, bias=eps_ap(P, eps), scale=1.0 / n)
    act(P, out_sb, out_sb, AF.Exp, r=w, w=w, scale=-0.5)


def eps_ap(P, eps):
    if not hasattr(P, "_eps"):
        P._eps = P.sb([128, 1], F32, name="epsc")
        P.op("pool", lambda g: g.memset(P._eps[:], float(eps)), w=["epsc"])
    return P._eps[:]


def build_adaln(NL=2, D=2048, CW=1536):
    P = Prog()
    KC = D // 128
    c2T = P.din("c2T", [128, KC, 2], F32)
    w = P.din("w", [NL, D, CW], F32)
    b2 = P.din("b2", [NL, 2, CW], F32)
    mod = P.dout("mod", [NL, 2, CW], F32)
    cs = P.sb([128, KC, 2], F32)
    ss = P.sb([128, KC, 2], F32)
    bt = P.sb([2, NL, CW], F32)
    P.dma("sp", cs[:], c2T[:], w=["cs"])
    for l in range(NL):
        P.dma("sp", bt[:, l, :], b2[l], w=[("bt", l)])
    act(P, ss[:], cs[:], AF.Silu, r=["cs"], w=["ss"])
    wt = [P.sb([128, KC, 512], F32) for _ in range(2)]
    pst = [P.ps([128, 512], F32) for _ in range(2)]
    ot = [P.sb([2, 512], F32) for _ in range(2)]
    it = 0
    for l in range(NL):
        wv = w[l].rearrange("(kc p) c -> p kc c", p=128)
        for cb in range(CW // 512):
            s = it % 2
            P.dma("sp", wt[s][:], wv[:, :, cb * 512:(cb + 1) * 512], w=[("wt", s)])
            for kc in range(KC):
                mm(P, pst[s][0:2, :], ss[:, kc, :], wt[s][:, kc, :], kc == 0, kc == KC - 1,
                   r=["ss", ("wt", s)], w=[("ps", s)])
            tt(P, "dve", ot[s][:], pst[s][0:2, :], bt[:, l, cb * 512:(cb + 1) * 512], ALU.add,
               r=[("ps", s), ("bt", l)], w=[("ot", s)])
            P.dma("sp", mod[l, :, cb * 512:(cb + 1) * 512], ot[s][:], r=[("ot", s)])
            it += 1
    return P.build()


D_MODEL = 2048
IN_W = 2560
EPS = 1e-6


def build_p1(NB, TB):
    P = Prog()
    Tc = NB * TB
    KC = 16
    xT = P.din("xT", [D_MODEL, Tc], F32)
    gam = P.din("gam", [128, KC], F32)
    scv = P.din("scv", [128, KC], F32)
    shv = P.din("shv", [128, KC], F32)
    w_in = P.din("w_in", [D_MODEL, IN_W], BF16)
    qn = P.din("qn", [128, 1], F32)
    kn = P.din("kn", [128, 1], F32)
    cosT = P.din("cosT", [128, Tc], F32)
    sinT = P.din("sinT", [128, Tc], F32)
    rotT = P.din("rotT", [128, 128], F32)
    cd = P.din("cd", [128, 128], F32)
    sdn = P.din("sdn", [128, 128], F32)
    qT = P.dout("qT", [1024, Tc], BF16)
    kT = P.dout("kT", [256, Tc], BF16)
    vT = P.dout("vT", [256, Tc], BF16)
    usT = P.dout("usT", [512, Tc], F32)
    zrT = P.dout("zrT", [512, Tc], F32)
    ziT = P.dout("ziT", [512, Tc], F32)

    W = P.sb([128, KC, IN_W], BF16)
    wv = w_in.rearrange("(kc p) c -> p kc c", p=128)
    for i in range(4):
        P.dma("sp", W[:, i * 4:(i + 1) * 4, :], wv[:, i * 4:(i + 1) * 4, :], w=[("W", i)])
    Wk = [("W", i) for i in range(4)]
    g_sb = P.sb([128, KC], F32); sc_sb = P.sb([128, KC], F32); sh_sb = P.sb([128, KC], F32)
    G = P.sb([128, KC], F32)
    qn_sb = P.sb([128, 1], F32); kn_sb = P.sb([128, 1], F32)
    rot_sb = P.sb([128, 128], F32); cd_sb = P.sb([128, 128], F32); sd_sb = P.sb([128, 128], F32)
    ones = P.sb([128, 128], BF16)
    P.dma("sp", g_sb[:], gam[:], w=["g"]); P.dma("sp", sc_sb[:], scv[:], w=["sc"]); P.dma("sp", sh_sb[:], shv[:], w=["sh"])
    P.dma("sp", qn_sb[:], qn[:], w=["qn"]); P.dma("sp", kn_sb[:], kn[:], w=["kn"])
    P.dma("sp", rot_sb[:], rotT[:], w=["rot"]); P.dma("sp", cd_sb[:], cd[:], w=["cd"]); P.dma("sp", sd_sb[:], sdn[:], w=["sd"])
    P.op("pool", lambda g: g.memset(ones[:], 1.0), w=["ones"])
    stt(P, G[:], sc_sb[:], 1.0, g_sb[:], ALU.add, ALU.mult, r=["g", "sc"], w=["G"])

    x_sb = [P.sb([128, KC, TB], F32) for _ in range(2)]
    sqs = [P.sb([128, TB], BF16) for _ in range(2)]
    h = P.sb([128, KC, TB], BF16)
    tmp = [P.sb([128, TB], F32) for _ in range(2)]
    rstd = [P.sb([128, TB], F32) for _ in range(2)]
    cs_b = [P.sb([128, TB], F32) for _ in range(2)]; sn_b = [P.sb([128, TB], F32) for _ in range(2)]
    sqh = P.sb([128, TB], BF16)
    r2 = P.sb([128, TB], F32)
    qf = P.sb([128, TB], F32)
    t1 = P.sb([128, TB], F32); t2 = P.sb([128, TB], F32)
    ob = [P.sb([128, TB], BF16) for _ in range(2)]
    of = [P.sb([128, TB], F32) for _ in range(3)]
    ufb = [P.sb([128, TB], F32) for _ in range(2)]
    acc = [P.ps([128, 512], F32) for _ in range(3)]
    ps_ss = P.ps([128, 512], F32); ps_s2 = P.ps([128, 512], F32); ps_rot = P.ps([128, 512], F32)
    ps_z = [P.ps([128, 512], F32) for _ in range(2)]
    xv = xT.rearrange("(kc p) t -> p kc t", p=128)
    nb_o = 0; nf_o = 0; na = 0; nz = 0
    nsq = [0]

    def p_load(b):
        tsl = slice(b * TB, (b + 1) * TB); pb = b % 2
        P.dma("sp", x_sb[pb][:], xv[:, :, tsl], w=[("x", pb)])
        P.dma("sp", cs_b[pb][:], cosT[:, tsl], w=[("cs", pb)]); P.dma("sp", sn_b[pb][:], sinT[:, tsl], w=[("sn", pb)])

    def p_stats(b):
        pb = b % 2
        for kc in range(KC):
            q = nsq[0] % 2; nsq[0] += 1
            act(P, sqs[q][:], x_sb[pb][:, kc, :], AF.Square, r=[("x", pb)], w=[("sqs", q)])
            mm(P, ps_ss[:, :TB], ones[:], sqs[q][:], kc == 0, kc == KC - 1, r=["ones", ("sqs", q)], w=["ps_ss"])
        rstd_from_ps(P, rstd[pb][:], ps_ss[:, :TB], D_MODEL, EPS, r=["ps_ss"], w=[("rstd", pb)], tmpkey=None)

    def p_h(b):
        pb = b % 2
        for kc in range(KC):
            s_ = kc % 2
            stt(P, tmp[s_][:], x_sb[pb][:, kc, :], G[:, kc:kc + 1], rstd[pb][:], ALU.mult, ALU.mult,
                r=[("x", pb), "G", ("rstd", pb)], w=[("tmp", s_)])
            act(P, h[:, kc, :], tmp[s_][:], AF.Identity, r=[("tmp", s_), "sh"], w=[("h", kc)], bias=sh_sb[:, kc:kc + 1])

    p_load(0); p_stats(0); p_h(0)
    for b in range(NB):
        tsl = slice(b * TB, (b + 1) * TB)
        cs_sb = cs_b[b % 2]; sn_sb = sn_b[b % 2]; csk = ("cs", b % 2); snk = ("sn", b % 2)
        if b + 1 < NB:
            p_load(b + 1)
        hk = [("h", kc) for kc in range(KC)]
        NCC = IN_W // 128
        st = {}

        def stage1(cc):
            nonlocal na, nb_o, nf_o, nz
            a = na % 3; na += 1
            for kc in range(KC):
                mm(P, acc[a][:, :TB], W[:, kc, cc * 128:(cc + 1) * 128], h[:, kc, :], kc == 0, kc == KC - 1,
                   r=[Wk[kc // 4], ("h", kc)], w=[("acc", a)])
            A = acc[a][:, :TB]
            st[cc] = (a, A)
            if cc < 10:
                return
            if cc < 12:
                o = nb_o % 2; nb_o += 1
                P.op("act", lambda g, o=o, A=A: g.copy(out=ob[o][:], in_=A), r=[("acc", a)], w=[("ob", o)])
                P.dma("sp", vT[(cc - 10) * 128:(cc - 9) * 128, tsl], ob[o][:], r=[("ob", o)])
            elif cc < 16:
                o = nf_o % 3; nf_o += 1
                P.op("act", lambda g, o=o, A=A: g.copy(out=of[o][:], in_=A), r=[("acc", a)], w=[("of", o)])
                P.dma("sp", usT[(cc - 12) * 128:(cc - 11) * 128, tsl], of[o][:], r=[("of", o)])
            else:
                o = cc % 2
                P.op("act", lambda g, o=o, A=A: g.copy(out=ufb[o][:], in_=A), r=[("acc", a)], w=[("ufb", o)])
                st[cc] = (a, A, o)

        def stage2(cc):
            nonlocal nf_o, nz
            if cc < 10:
                a, A = st[cc]
                gn, gk = (qn_sb, "qn") if cc < 8 else (kn_sb, "kn")
                act(P, sqh[:], A, AF.Square, r=[("acc", a)], w=["sqh"])
                mm(P, ps_s2[:, :TB], ones[:], sqh[:], True, True, r=["ones", "sqh"], w=["ps_s2"])
                rstd_from_ps(P, r2[:], ps_s2[:, :TB], 128, EPS, r=["ps_s2"], w=["r2"], tmpkey=None)
                stt(P, qf[:], A, gn[:, 0:1], r2[:], ALU.mult, ALU.mult, r=[("acc", a), gk, "r2"], w=["qf"])
            elif cc >= 16:
                a, A, o = st[cc]
                for mat, mk, dstT in ((cd_sb, "cd", zrT), (sd_sb, "sd", ziT)):
                    z = nz % 2; nz += 1
                    mm(P, ps_z[z][:, :TB], mat[:], ufb[o][:], True, True, r=[mk, ("ufb", o)], w=[("ps_z", z)])
                    o2 = nf_o % 3; nf_o += 1
                    P.op("dve", lambda g, o2=o2, z=z: g.tensor_copy(out=of[o2][:], in_=ps_z[z][:, :TB]),
                         r=[("ps_z", z)], w=[("of", o2)])
                    P.dma("sp", dstT[(cc - 16) * 128:(cc - 15) * 128, tsl], of[o2][:], r=[("of", o2)])

        def stage3(cc):
            nonlocal nb_o
            if cc >= 10:
                return
            mm(P, ps_rot[:, :TB], rot_sb[:], qf[:], True, True, r=["rot", "qf"], w=["ps_rot"])
            tt(P, "dve", t1[:], qf[:], cs_sb[:], ALU.mult, r=["qf", csk], w=["t1"])
            tt(P, "dve", t2[:], ps_rot[:, :TB], sn_sb[:], ALU.mult, r=["ps_rot", snk], w=["t2"])
            o = nb_o % 2; nb_o += 1
            tt(P, "pool", ob[o][:], t1[:], t2[:], ALU.add, r=["t1", "t2"], w=[("ob", o)])
            dst = qT[cc * 128:(cc + 1) * 128, tsl] if cc < 8 else kT[(cc - 8) * 128:(cc - 7) * 128, tsl]
            P.dma("sp", dst, ob[o][:], r=[("ob", o)])

        for i in range(NCC + 2):
            if i == 3 and b + 1 < NB:
                p_stats(b + 1)
            if i < NCC:
                stage1(i)
            if 0 <= i - 2 < NCC:
                stage3(i - 2)
            if 0 <= i - 1 < NCC:
                stage2(i - 1)
        if b + 1 < NB:
            p_h(b + 1)
    return P.build()


def rope_tables(rows_tok):
    f = (10000.0 ** (-np.arange(32, dtype=np.float32) / 32)).astype(np.float32)
    t = np.asarray(rows_tok)
    row = (t // 64).astype(np.float32); col = (t % 64).astype(np.float32)
    ar = (row[None, :] * f[:, None]).astype(np.float32)
    ac = (col[None, :] * f[:, None]).astype(np.float32)
    cosT = np.concatenate([np.cos(ar), np.cos(ar), np.cos(ac), np.cos(ac)], 0).astype(np.float32)
    sinT = np.concatenate([np.sin(ar), np.sin(ar), np.sin(ac), np.sin(ac)], 0).astype(np.float32)
    return cosT, sinT


def rope_rotT():
    R = np.zeros((128, 128), np.float32)
    for base in (0, 64):
        for i in range(32):
            R[base + i, base + i + 32] = -1.0
            R[base + i + 32, base + i] = 1.0
    return np.ascontiguousarray(R.T)


def chan_dft():
    j = np.arange(128)
    ang = 2 * np.pi * np.outer(j, j) / 128
    return (np.cos(ang) / np.sqrt(128)).astype(np.float32), (-np.sin(ang) / np.sqrt(128)).astype(np.float32)


ATT_VARIANT = ""


def build_attn(NQ, NK, QB):
    P = Prog()
    NQB = NQ // QB
    NKT = NK // 128
    qT = P.din("qT", [128, NQ], BF16)
    kT = P.din("kT", [128, NK], BF16)
    v = P.din("v", [NK, 128], BF16)
    oT = P.dout("oT", [128, NQ], BF16)
    q_sb = P.sb([128, NQ], BF16); k_sb = P.sb([128, NK], BF16); v_sb = P.sb([128, NKT, 128], BF16)
    KCH = 13 if NKT % 13 == 0 else NKT
    vv = v.rearrange("(kt p) d -> p kt d", p=128)
    QCH = 4 if NQB % 4 == 0 else NQB
    P.dma("sp", q_sb[:, 0:QCH * QB], qT[:, 0:QCH * QB], w=[("Q", 0)])
    for c in range(NKT // KCH):
        P.dma("sp", k_sb[:, c * KCH * 128:(c + 1) * KCH * 128], kT[:, c * KCH * 128:(c + 1) * KCH * 128], w=[("K", c)])
        P.dma("sp", v_sb[:, c * KCH:(c + 1) * KCH, :], vv[:, c * KCH:(c + 1) * KCH, :], w=[("V", c)])
    for c in range(1, NQB // QCH):
        P.dma("sp", q_sb[:, c * QCH * QB:(c + 1) * QCH * QB], qT[:, c * QCH * QB:(c + 1) * QCH * QB], w=[("Q", c)])
    ones_b = P.sb([128, 1], BF16); ones_f = P.sb([1, 128], F32)
    P.op("pool", lambda g: g.memset(ones_b[:], 1.0), w=["ones_b"])
    P.op("pool", lambda g: g.memset(ones_f[:], 1.0), w=["ones_f"])
    S = [P.ps([128, 512], F32) for _ in range(3)]
    O = [P.ps([128, 512], F32) for _ in range(2)]
    SM = [P.ps([128, 512], F32) for _ in range(2)]
    BC = P.ps([128, 512], F32)
    PT = [P.sb([128, QB], BF16) for _ in range(3)]
    rs = P.sb([1, QB], F32); bc_sb = P.sb([128, QB], F32)
    o_sb = [P.sb([128, QB], BF16) for _ in range(2)]
    scale = 128 ** -0.5
    items = [(qb, kt) for qb in range(NQB) for kt in range(NKT)]

    def emit_S(i):
        qb, kt = items[i]
        s = i % 3
        mm(P, S[s][:, :QB], k_sb[:, kt * 128:(kt + 1) * 128], q_sb[:, qb * QB:(qb + 1) * QB], True, True,
           r=[("K", kt // KCH), ("Q", qb // QCH)], w=[("S", s)])

    emit_S(0)
    for i, (qb, kt) in enumerate(items):
        if i + 1 < len(items):
            emit_S(i + 1)
        s = i % 3
        o = qb % 2
        act(P, PT[s][:], S[s][:, :QB], AF.Exp, r=[("S", s)], w=[("PT", s)], scale=scale)
        if ATT_VARIANT != "B" or kt in (0, NKT - 1):
            mm(P, O[o][:, :QB], v_sb[:, kt, :], PT[s][:], kt == 0, kt == NKT - 1, r=[("V", kt // KCH), ("PT", s)], w=[("O", o)])
        if ATT_VARIANT == "" or kt in (0, NKT - 1):
            mm(P, SM[o][0:1, :QB], ones_b[:], PT[s][:], kt == 0, kt == NKT - 1, r=["ones_b", ("PT", s)], w=[("SM", o)])
        if kt == NKT - 1:
            P.op("dve", lambda g, o=o: g.reciprocal(out=rs[:], in_=SM[o][0:1, :QB]), r=[("SM", o)], w=["rs"])
            mm(P, BC[:, :QB], ones_f[:], rs[:], True, True, r=["ones_f", "rs"], w=["BC"])
            P.op("act", lambda g: g.copy(out=bc_sb[:], in_=BC[:, :QB]), r=["BC"], w=["bc_sb"])
            tt(P, "dve", o_sb[o][:], O[o][:, :QB], bc_sb[:], ALU.mult, r=[("O", o), "bc_sb"], w=[("o_sb", o)])
            P.dma("sp", oT[:, qb * QB:(qb + 1) * QB], o_sb[o][:], r=[("o_sb", o)])
    return P.build()


def const_ap(P, val):
    if not hasattr(P, "_consts"):
        P._consts = {}
    if val not in P._consts:
        t = P.sb([128, 1], F32, name=f"const{len(P._consts)}")
        key = ("const", len(P._consts))
        P.op("pool", lambda g: g.memset(t[:], float(val)), w=[key])
        P._consts[val] = (t, key)
    return P._consts[val]


def emit_sin(P, out, u, shape, r, w, tag):
    if not hasattr(P, "_sin_tmp"):
        P._sin_tmp = {}
    sk = tuple(shape)
    if sk not in P._sin_tmp:
        P._sin_tmp[sk] = (P.sb(shape, I32), P.sb(shape, F32), P.sb(shape, F32), P.sb(shape, F32))
    ki, kf, fr, mk = P._sin_tmp[sk]
    k = lambda n: ("sintmp", sk, n)
    P.op("dve", lambda g: g.tensor_copy(out=ki[:], in_=u), r=r, w=[k("ki")])
    P.op("dve", lambda g: g.tensor_copy(out=kf[:], in_=ki[:]), r=[k("ki")], w=[k("kf")])
    tt(P, "dve", fr[:], u, kf[:], ALU.subtract, r=r + [k("kf")], w=[k("fr")])
    ts(P, "dve", mk[:], fr[:], 0.0, None, ALU.is_lt, None, r=[k("fr")], w=[k("mk")])
    tt(P, "dve", fr[:], fr[:], mk[:], ALU.add, r=[k("fr"), k("mk")], w=[k("fr")])
    cb, cbk = const_ap(P, -np.pi)
    act(P, out, fr[:], AF.Sin, r=[k("fr"), cbk], w=w, bias=cb[:], scale=2 * np.pi)


def build_ssm(L, TBL=512):
    P = Prog()
    NCT = 4
    u = P.din("u", [128, L], F32)
    lam_re = P.din("lam_re", [128, NCT], F32)
    lam_im = P.din("lam_im", [128, NCT], F32)
    log_dt = P.din("log_dt", [128, NCT], F32)
    b_re = P.din("b_re", [NCT, 128, 128], F32)
    b_im = P.din("b_im", [NCT, 128, 128], F32)
    c_re = P.din("c_re", [NCT, 128, 128], F32)
    c_im = P.din("c_im", [NCT, 128, 128], F32)
    tvec = P.din("tvec", [128, TBL], F32)
    ident = P.din("ident", [128, 128], F32)
    y = P.dout("y", [128, L], F32)

    U = P.sb([128, L], F32)
    nch = 8
    bounds = [(i * L) // nch for i in range(nch + 1)]
    def ukey(t0):
        for i in range(nch):
            if bounds[i] <= t0 < bounds[i + 1]:
                return ("U", i)
    lr = P.sb([128, NCT]); li = P.sb([128, NCT]); ld = P.sb([128, NCT])
    P.dma("sp", lr[:], lam_re[:], w=["lr"]); P.dma("sp", li[:], lam_im[:], w=["li"]); P.dma("sp", ld[:], log_dt[:], w=["ld"])
    tv = P.sb([128, TBL]); idn = P.sb([128, 128])
    P.dma("sp", tv[:], tvec[:], w=["tv"]); P.dma("sp", idn[:], ident[:], w=["idn"])
    Bre = P.sb([128, NCT, 128]); Bim = P.sb([128, NCT, 128]); Cre = P.sb([128, NCT, 128]); Cim = P.sb([128, NCT, 128])
    for c in range(NCT):
        P.dma("sp", Bre[:, c, :], b_re[c], w=[("Bre", c)]); P.dma("sp", Bim[:, c, :], b_im[c], w=[("Bim", c)])
        P.dma("sp", Cre[:, c, :], c_re[c], w=[("Cre", c)]); P.dma("sp", Cim[:, c, :], c_im[c], w=[("Cim", c)])
    for i in range(nch):
        P.dma("sp", U[:, bounds[i]:bounds[i + 1]], u[:, bounds[i]:bounds[i + 1]], w=[("U", i)])
    S4 = [128, NCT]
    dt = P.sb(S4); mag = P.sb(S4); th = P.sb(S4); tmp = P.sb(S4); us = P.sb(S4); uc = P.sb(S4)
    sn = P.sb(S4); cs = P.sb(S4); ar1 = P.sb(S4); ai = P.sb(S4); den = P.sb(S4); t2 = P.sb(S4)
    cr = P.sb(S4); ci = P.sb(S4); nci = P.sb(S4); th2 = P.sb(S4)
    act(P, dt[:], ld[:], AF.Exp, r=["ld"], w=["dt"])
    tt(P, "dve", tmp[:], lr[:], dt[:], ALU.mult, r=["lr", "dt"], w=["tmp"])
    act(P, mag[:], tmp[:], AF.Exp, r=["tmp"], w=["mag"])
    tt(P, "dve", th[:], li[:], dt[:], ALU.mult, r=["li", "dt"], w=["th"])
    ts(P, "dve", th2[:], th[:], float(1.0 / (2 * np.pi)), None, ALU.mult, None, r=["th"], w=["th2"])
    ts(P, "dve", us[:], th2[:], 0.5, None, ALU.add, None, r=["th2"], w=["us"])
    ts(P, "dve", uc[:], th2[:], 0.75, None, ALU.add, None, r=["th2"], w=["uc"])
    emit_sin(P, sn[:], us[:], S4, ["us"], ["sn"], "rs")
    emit_sin(P, cs[:], uc[:], S4, ["uc"], ["cs"], "rc")
    tt(P, "dve", ar1[:], mag[:], cs[:], ALU.mult, r=["mag", "cs"], w=["ar1"])
    ts(P, "dve", ar1[:], ar1[:], -1.0, None, ALU.add, None, r=["ar1"], w=["ar1"])
    tt(P, "dve", ai[:], mag[:], sn[:], ALU.mult, r=["mag", "sn"], w=["ai"])
    tt(P, "dve", den[:], lr[:], lr[:], ALU.mult, r=["lr"], w=["den"])
    tt(P, "dve", t2[:], li[:], li[:], ALU.mult, r=["li"], w=["t2"])
    tt(P, "dve", den[:], den[:], t2[:], ALU.add, r=["den", "t2"], w=["den"])
    P.op("dve", lambda g: g.reciprocal(out=den[:], in_=den[:]), r=["den"], w=["den"])
    tt(P, "dve", cr[:], ar1[:], lr[:], ALU.mult, r=["ar1", "lr"], w=["cr"])
    tt(P, "dve", t2[:], ai[:], li[:], ALU.mult, r=["ai", "li"], w=["t2"])
    tt(P, "dve", cr[:], cr[:], t2[:], ALU.add, r=["cr", "t2"], w=["cr"])
    tt(P, "dve", cr[:], cr[:], den[:], ALU.mult, r=["cr", "den"], w=["cr"])
    tt(P, "dve", ci[:], ai[:], lr[:], ALU.mult, r=["ai", "lr"], w=["ci"])
    tt(P, "dve", t2[:], ar1[:], li[:], ALU.mult, r=["ar1", "li"], w=["t2"])
    tt(P, "dve", ci[:], ci[:], t2[:], ALU.subtract, r=["ci", "t2"], w=["ci"])
    tt(P, "dve", ci[:], ci[:], den[:], ALU.mult, r=["ci", "den"], w=["ci"])
    ts(P, "dve", nci[:], ci[:], -1.0, None, ALU.mult, None, r=["ci"], w=["nci"])
    BrT = P.sb([128, NCT, 128]); BiT = P.sb([128, NCT, 128]); Cimn = P.sb([128, NCT, 128])
    bt = P.sb([128, 128]); bb = [P.sb([128, 128]) for _ in range(2)]
    pst = P.ps([128, 512], F32)
    nb = 0
    for c in range(NCT):
        for which in range(2):
            s = nb % 2; nb += 1
            if which == 0:
                ts(P, "dve", bt[:], Bre[:, c, :], cr[:, c:c + 1], None, ALU.mult, None, r=[("Bre", c), "cr"], w=["bt"])
                stt(P, bb[s][:], Bim[:, c, :], nci[:, c:c + 1], bt[:], ALU.mult, ALU.add, r=[("Bim", c), "nci", "bt"], w=[("bb", s)])
                dst = BrT
            else:
                ts(P, "dve", bt[:], Bim[:, c, :], cr[:, c:c + 1], None, ALU.mult, None, r=[("Bim", c), "cr"], w=["bt"])
                stt(P, bb[s][:], Bre[:, c, :], ci[:, c:c + 1], bt[:], ALU.mult, ALU.add, r=[("Bre", c), "ci", "bt"], w=[("bb", s)])
                dst = BiT
            P.op("pe", lambda g, s=s: g.transpose(pst[:, 0:128], bb[s][:], idn[:]), r=[("bb", s), "idn"], w=["pst"])
            P.op("act", lambda g, dst=dst, c=c: g.copy(out=dst[:, c, :], in_=pst[:, 0:128]), r=["pst"], w=[("BT", c, which)])
        P.op("act", lambda g, c=c: g.mul(out=Cimn[:, c, :], in_=Cim[:, c, :], mul=-1.0), r=[("Cim", c)], w=[("Cimn", c)])
    ct = P.sb([128, NCT, TBL]); st = P.sb([128, NCT, TBL]); magT = P.sb([128, NCT, TBL])
    ut = P.sb([128, TBL])
    for c in range(NCT):
        for which, dstT, off in ((0, st, 0.5), (1, ct, 0.75)):
            ts(P, "dve", ut[:], tv[:], th2[:, c:c + 1], off, ALU.mult, ALU.add, r=["tv", "th2"], w=["ut"])
            emit_sin(P, dstT[:, c, :], ut[:], [128, TBL], ["ut"], [("tab", c, which)], ("tb", c, which))
        ts(P, "dve", magT[:, c, :], tv[:], 0.0, mag[:, c:c + 1], ALU.mult, ALU.add, r=["tv", "mag"], w=[("magT", c)])
    blocks = []
    t0 = 0
    while t0 < L:
        n = min(TBL, L - t0); blocks.append((t0, n)); t0 += n
    psb = [[P.ps([128, 512], F32), P.ps([128, 512], F32)] for _ in range(2)]
    psy = [P.ps([128, 512], F32) for _ in range(2)]
    mi = [P.sb([128, TBL]) for _ in range(4)]
    mo = [[P.sb([128, TBL]) for _ in range(4)] for _ in range(2)]
    gin = [P.sb([128, TBL]) for _ in range(2)]
    g_ = [[P.sb([128, TBL]) for _ in range(2)] for _ in range(2)]
    hh = [[[P.sb([128, TBL]), P.sb([128, TBL])] for _ in range(2)] for _ in range(NCT)]
    ysb = [P.sb([128, TBL]) for _ in range(2)]
    items = [(j, c) for j in range(len(blocks)) for c in range(NCT)]

    def tabs(c, n):
        return ct[:, c, :n], st[:, c, :n], [("tab", c, 0), ("tab", c, 1)]

    def stA(k):
        j, c = items[k]; t0, n = blocks[j]; pb = k % 2
        uk = [("U", i_) for i_ in range(nch) if bounds[i_] < t0 + n and bounds[i_ + 1] > t0]
        br_ps, bi_ps = psb[pb][0][:, :n], psb[pb][1][:, :n]
        mm(P, br_ps, BrT[:, c, :], U[:, t0:t0 + n], True, True, r=[("BT", c, 0)] + uk, w=[("psb", pb, 0)])
        mm(P, bi_ps, BiT[:, c, :], U[:, t0:t0 + n], True, True, r=[("BT", c, 1)] + uk, w=[("psb", pb, 1)])
        ctc, stc, tk = tabs(c, n)
        tt(P, "dve", mi[0][:, :n], br_ps, ctc, ALU.mult, r=[("psb", pb, 0)] + tk, w=["mi0"])
        tt(P, "dve", mi[1][:, :n], bi_ps, stc, ALU.mult, r=[("psb", pb, 1)] + tk, w=["mi1"])
        tt(P, "dve", mi[2][:, :n], bi_ps, ctc, ALU.mult, r=[("psb", pb, 1)] + tk, w=["mi2"])
        tt(P, "dve", mi[3][:, :n], br_ps, stc, ALU.mult, r=[("psb", pb, 0)] + tk, w=["mi3"])
        tt(P, "dve", gin[0][:, :n], mi[0][:, :n], mi[1][:, :n], ALU.add, r=["mi0", "mi1"], w=["gin0"])
        tt(P, "pool", gin[1][:, :n], mi[2][:, :n], mi[3][:, :n], ALU.subtract, r=["mi2", "mi3"], w=["gin1"])

    def stB(k):
        j, c = items[k]; t0, n = blocks[j]; hb = j % 2; gb = k % 2
        for ri in range(2):
            if j == 0:
                init = 0.0; rk = []
            else:
                pn = blocks[j - 1][1]
                init = hh[c][1 - hb][ri][:, pn - 1:pn]; rk = [("hh", c, 1 - hb, ri)]
            P.op("dve", lambda g, ri=ri, init=init, c=c, n=n, gb=gb: g.tensor_tensor_scan(
                out=g_[gb][ri][:, :n], data0=magT[:, c, :n], data1=gin[ri][:, :n], initial=init,
                op0=ALU.mult, op1=ALU.add), r=[("magT", c), f"gin{ri}"] + rk, w=[("g", gb, ri)])

    def stC(k):
        j, c = items[k]; t0, n = blocks[j]; gb = k % 2
        ctc, stc, tk = tabs(c, n)
        tt(P, "pool", mo[gb][0][:, :n], g_[gb][0][:, :n], ctc, ALU.mult, r=[("g", gb, 0)] + tk, w=[("mo", gb, 0)])
        tt(P, "pool", mo[gb][2][:, :n], g_[gb][0][:, :n], stc, ALU.mult, r=[("g", gb, 0)] + tk, w=[("mo", gb, 2)])
        tt(P, "pool", mo[gb][1][:, :n], g_[gb][1][:, :n], stc, ALU.mult, r=[("g", gb, 1)] + tk, w=[("mo", gb, 1)])
        tt(P, "pool", mo[gb][3][:, :n], g_[gb][1][:, :n], ctc, ALU.mult, r=[("g", gb, 1)] + tk, w=[("mo", gb, 3)])

    def stD(k):
        j, c = items[k]; t0, n = blocks[j]; hb = j % 2; gb = k % 2; yb = j % 2
        hr, hi = hh[c][hb][0], hh[c][hb][1]
        tt(P, "dve", hr[:, :n], mo[gb][0][:, :n], mo[gb][1][:, :n], ALU.subtract, r=[("mo", gb, 0), ("mo", gb, 1)], w=[("hh", c, hb, 0)])
        tt(P, "dve", hi[:, :n], mo[gb][2][:, :n], mo[gb][3][:, :n], ALU.add, r=[("mo", gb, 2), ("mo", gb, 3)], w=[("hh", c, hb, 1)])
        mm(P, psy[yb][:, :n], Cre[:, c, :], hr[:, :n], c == 0, False, r=[("Cre", c), ("hh", c, hb, 0)], w=[("psy", yb)])
        mm(P, psy[yb][:, :n], Cimn[:, c, :], hi[:, :n], False, c == NCT - 1, r=[("Cimn", c), ("hh", c, hb, 1)], w=[("psy", yb)])
        if c == NCT - 1:
            P.op("act", lambda g, yb=yb, n=n: g.copy(out=ysb[yb][:, :n], in_=psy[yb][:, :n]), r=[("psy", yb)], w=[("ysb", yb)])
            P.dma("sp", y[:, t0:t0 + n], ysb[yb][:, :n], r=[("ysb", yb)])

    stA(0); stB(0); stC(0)
    for k in range(len(items)):
        if k + 1 < len(items):
            stA(k + 1)
        stD(k)
        if k + 1 < len(items):
            stB(k + 1); stC(k + 1)
    return P.build()


def build_ssm2(L2, TBL=512):
    L = L2
    P = Prog()
    NCT = 4
    ue = P.din("ue", [128, L], F32)
    uo = P.din("uo", [128, L], F32)
    lam_re = P.din("lam_re", [128, NCT], F32)
    lam_im = P.din("lam_im", [128, NCT], F32)
    log_dt = P.din("log_dt", [128, NCT], F32)
    b_re = P.din("b_re", [NCT, 128, 128], F32)
    b_im = P.din("b_im", [NCT, 128, 128], F32)
    c_re = P.din("c_re", [NCT, 128, 128], F32)
    c_im = P.din("c_im", [NCT, 128, 128], F32)
    tvec = P.din("tvec", [128, TBL], F32)
    ident = P.din("ident", [128, 128], F32)
    ye = P.dout("ye", [128, L], F32)
    yo = P.dout("yo", [128, L], F32)

    U = P.sb([128, L], F32)
    UO = P.sb([128, L], F32)
    nch = 8
    bounds = [(i * L) // nch for i in range(nch + 1)]
    def ukey(t0):
        for i in range(nch):
            if bounds[i] <= t0 < bounds[i + 1]:
                return ("U", i)
    lr = P.sb([128, NCT]); li = P.sb([128, NCT]); ld = P.sb([128, NCT])
    P.dma("sp", lr[:], lam_re[:], w=["lr"]); P.dma("sp", li[:], lam_im[:], w=["li"]); P.dma("sp", ld[:], log_dt[:], w=["ld"])
    tv = P.sb([128, TBL]); idn = P.sb([128, 128])
    P.dma("sp", tv[:], tvec[:], w=["tv"]); P.dma("sp", idn[:], ident[:], w=["idn"])
    Bre = P.sb([128, NCT, 128]); Bim = P.sb([128, NCT, 128]); Cre = P.sb([128, NCT, 128]); Cim = P.sb([128, NCT, 128])
    for c in range(NCT):
        P.dma("sp", Bre[:, c, :], b_re[c], w=[("Bre", c)]); P.dma("sp", Bim[:, c, :], b_im[c], w=[("Bim", c)])
        P.dma("sp", Cre[:, c, :], c_re[c], w=[("Cre", c)]); P.dma("sp", Cim[:, c, :], c_im[c], w=[("Cim", c)])
    for i in range(nch):
        P.dma("sp", U[:, bounds[i]:bounds[i + 1]], ue[:, bounds[i]:bounds[i + 1]], w=[("U", i)])
        P.dma("sp", UO[:, bounds[i]:bounds[i + 1]], uo[:, bounds[i]:bounds[i + 1]], w=[("UO", i)])
    S4 = [128, NCT]
    dt = P.sb(S4); mag = P.sb(S4); th = P.sb(S4); tmp = P.sb(S4); us = P.sb(S4); uc = P.sb(S4)
    sn = P.sb(S4); cs = P.sb(S4); ar1 = P.sb(S4); ai = P.sb(S4); den = P.sb(S4); t2 = P.sb(S4)
    cr = P.sb(S4); ci = P.sb(S4); nci = P.sb(S4); th2 = P.sb(S4)
    act(P, dt[:], ld[:], AF.Exp, r=["ld"], w=["dt"])
    tt(P, "dve", tmp[:], lr[:], dt[:], ALU.mult, r=["lr", "dt"], w=["tmp"])
    act(P, mag[:], tmp[:], AF.Exp, r=["tmp"], w=["mag"])
    tt(P, "dve", th[:], li[:], dt[:], ALU.mult, r=["li", "dt"], w=["th"])
    ts(P, "dve", th2[:], th[:], float(1.0 / (2 * np.pi)), None, ALU.mult, None, r=["th"], w=["th2"])
    ts(P, "dve", us[:], th2[:], 0.5, None, ALU.add, None, r=["th2"], w=["us"])
    ts(P, "dve", uc[:], th2[:], 0.75, None, ALU.add, None, r=["th2"], w=["uc"])
    emit_sin(P, sn[:], us[:], S4, ["us"], ["sn"], "rs")
    emit_sin(P, cs[:], uc[:], S4, ["uc"], ["cs"], "rc")
    tt(P, "dve", ar1[:], mag[:], cs[:], ALU.mult, r=["mag", "cs"], w=["ar1"])
    ts(P, "dve", ar1[:], ar1[:], -1.0, None, ALU.add, None, r=["ar1"], w=["ar1"])
    tt(P, "dve", ai[:], mag[:], sn[:], ALU.mult, r=["mag", "sn"], w=["ai"])
    tt(P, "dve", den[:], lr[:], lr[:], ALU.mult, r=["lr"], w=["den"])
    tt(P, "dve", t2[:], li[:], li[:], ALU.mult, r=["li"], w=["t2"])
    tt(P, "dve", den[:], den[:], t2[:], ALU.add, r=["den", "t2"], w=["den"])
    P.op("dve", lambda g: g.reciprocal(out=den[:], in_=den[:]), r=["den"], w=["den"])
    tt(P, "dve", cr[:], ar1[:], lr[:], ALU.mult, r=["ar1", "lr"], w=["cr"])
    tt(P, "dve", t2[:], ai[:], li[:], ALU.mult, r=["ai", "li"], w=["t2"])
    tt(P, "dve", cr[:], cr[:], t2[:], ALU.add, r=["cr", "t2"], w=["cr"])
    tt(P, "dve", cr[:], cr[:], den[:], ALU.mult, r=["cr", "den"], w=["cr"])
    tt(P, "dve", ci[:], ai[:], lr[:], ALU.mult, r=["ai", "lr"], w=["ci"])
    tt(P, "dve", t2[:], ar1[:], li[:], ALU.mult, r=["ar1", "li"], w=["t2"])
    tt(P, "dve", ci[:], ci[:], t2[:], ALU.subtract, r=["ci", "t2"], w=["ci"])
    tt(P, "dve", ci[:], ci[:], den[:], ALU.mult, r=["ci", "den"], w=["ci"])
    ts(P, "dve", nci[:], ci[:], -1.0, None, ALU.mult, None, r=["ci"], w=["nci"])
    arr = P.sb(S4); nai = P.sb(S4); mag2 = P.sb(S4); th2b = P.sb(S4)
    ts(P, "dve", arr[:], ar1[:], 1.0, None, ALU.add, None, r=["ar1"], w=["arr"])
    ts(P, "dve", nai[:], ai[:], -1.0, None, ALU.mult, None, r=["ai"], w=["nai"])
    tt(P, "dve", mag2[:], mag[:], mag[:], ALU.mult, r=["mag"], w=["mag2"])
    ts(P, "dve", th2b[:], th2[:], 2.0, None, ALU.mult, None, r=["th2"], w=["th2b"])
    BrT = P.sb([128, NCT, 128]); BiT = P.sb([128, NCT, 128]); aBrT = P.sb([128, NCT, 128]); aBiT = P.sb([128, NCT, 128])
    Bb0 = P.sb([128, NCT, 128]); Bb1 = P.sb([128, NCT, 128])
    Cimn = P.sb([128, NCT, 128]); CaRe = P.sb([128, NCT, 128]); CaImN = P.sb([128, NCT, 128]); D2T = P.sb([128, 128])
    bt = P.sb([128, 128]); bb = [P.sb([128, 128]) for _ in range(2)]
    psy_e = [P.ps([128, 512], F32) for _ in range(2)]
    psy_o = [P.ps([128, 512], F32) for _ in range(2)]
    pst = psy_e[0]; pstk = ("psye", 0)
    nb = 0
    for c in range(NCT):
        ts(P, "dve", bt[:], Bre[:, c, :], cr[:, c:c + 1], None, ALU.mult, None, r=[("Bre", c), "cr"], w=["bt"])
        stt(P, Bb0[:, c, :], Bim[:, c, :], nci[:, c:c + 1], bt[:], ALU.mult, ALU.add, r=[("Bim", c), "nci", "bt"], w=[("Bb0", c)])
        ts(P, "dve", bt[:], Bim[:, c, :], cr[:, c:c + 1], None, ALU.mult, None, r=[("Bim", c), "cr"], w=["bt"])
        stt(P, Bb1[:, c, :], Bre[:, c, :], ci[:, c:c + 1], bt[:], ALU.mult, ALU.add, r=[("Bre", c), "ci", "bt"], w=[("Bb1", c)])
        P.op("act", lambda g, c=c: g.mul(out=Cimn[:, c, :], in_=Cim[:, c, :], mul=-1.0), r=[("Cim", c)], w=[("Cimn", c)])
        srcs = []
        srcs.append((Bb0[:, c, :], [("Bb0", c)], BrT, 0))
        srcs.append((Bb1[:, c, :], [("Bb1", c)], BiT, 1))
        for which in (2, 3):
            s_ = nb % 2; nb += 1
            if which == 2:
                ts(P, "dve", bt[:], Bb0[:, c, :], arr[:, c:c + 1], None, ALU.mult, None, r=[("Bb0", c), "arr"], w=["bt"])
                stt(P, bb[s_][:], Bb1[:, c, :], nai[:, c:c + 1], bt[:], ALU.mult, ALU.add, r=[("Bb1", c), "nai", "bt"], w=[("bb", s_)])
                srcs.append((bb[s_][:], [("bb", s_)], aBrT, 2))
            else:
                ts(P, "dve", bt[:], Bb1[:, c, :], arr[:, c:c + 1], None, ALU.mult, None, r=[("Bb1", c), "arr"], w=["bt"])
                stt(P, bb[s_][:], Bb0[:, c, :], ai[:, c:c + 1], bt[:], ALU.mult, ALU.add, r=[("Bb0", c), "ai", "bt"], w=[("bb", s_)])
                srcs.append((bb[s_][:], [("bb", s_)], aBiT, 3))
        for src_ap, sk, dst, which in srcs:
            P.op("pe", lambda g, src_ap=src_ap: g.transpose(pst[:, 0:128], src_ap, idn[:]), r=sk + ["idn"], w=[pstk])
            P.op("act", lambda g, dst=dst, c=c: g.copy(out=dst[:, c, :], in_=pst[:, 0:128]), r=[pstk], w=[("BT", c, which)])
        ts(P, "dve", bt[:], Cre[:, c, :], arr[:, c:c + 1], None, ALU.mult, None, r=[("Cre", c), "arr"], w=["bt"])
        stt(P, CaRe[:, c, :], Cimn[:, c, :], ai[:, c:c + 1], bt[:], ALU.mult, ALU.add, r=[("Cimn", c), "ai", "bt"], w=[("CaRe", c)])
        ts(P, "dve", bt[:], Cimn[:, c, :], arr[:, c:c + 1], None, ALU.mult, None, r=[("Cimn", c), "arr"], w=["bt"])
        stt(P, CaImN[:, c, :], Cre[:, c, :], nai[:, c:c + 1], bt[:], ALU.mult, ALU.add, r=[("Cre", c), "nai", "bt"], w=[("CaImN", c)])
    for c in range(NCT):
        mm(P, psy_o[0][:, 0:128], Bb0[:, c, :], Cre[:, c, :], c == 0, False, r=[("Bb0", c), ("Cre", c)], w=[("psyo", 0)])
        mm(P, psy_o[0][:, 0:128], Bb1[:, c, :], Cimn[:, c, :], False, c == NCT - 1, r=[("Bb1", c), ("Cimn", c)], w=[("psyo", 0)])
    P.op("act", lambda g: g.copy(out=D2T[:], in_=psy_o[0][:, 0:128]), r=[("psyo", 0)], w=["D2T"])
    ct = P.sb([128, NCT, TBL]); st = P.sb([128, NCT, TBL]); magT = P.sb([128, NCT, TBL])
    ut = P.sb([128, TBL])
    for c in range(NCT):
        for which, dstT, off in ((0, st, 0.5), (1, ct, 0.75)):
            ts(P, "dve", ut[:], tv[:], th2b[:, c:c + 1], off, ALU.mult, ALU.add, r=["tv", "th2b"], w=["ut"])
            emit_sin(P, dstT[:, c, :], ut[:], [128, TBL], ["ut"], [("tab", c, which)], ("tb", c, which))
        ts(P, "dve", magT[:, c, :], tv[:], 0.0, mag2[:, c:c + 1], ALU.mult, ALU.add, r=["tv", "mag2"], w=[("magT", c)])
    blocks = []
    t0 = 0
    while t0 < L:
        n = min(TBL, L - t0); blocks.append((t0, n)); t0 += n
    psb = [[P.ps([128, 512], F32), P.ps([128, 512], F32)] for _ in range(2)]
    mi = [P.sb([128, TBL]) for _ in range(4)]
    mo = [[P.sb([128, TBL]) for _ in range(4)] for _ in range(2)]
    gin = [P.sb([128, TBL]) for _ in range(2)]
    g_ = [[P.sb([128, TBL]) for _ in range(2)] for _ in range(2)]
    hh = [[[P.sb([128, TBL + 1]), P.sb([128, TBL + 1])] for _ in range(2)] for _ in range(NCT)]
    for c in range(NCT):
        for ri in range(2):
            P.op("pool", lambda g, c=c, ri=ri: g.memset(hh[c][0][ri][:, 0:1], 0.0), w=[("hh0", c, 0, ri)])
    ysb = [P.sb([128, TBL]) for _ in range(4)]
    items = [(j, c) for j in range(len(blocks)) for c in range(NCT)]

    def tabs(c, n):
        return ct[:, c, :n], st[:, c, :n], [("tab", c, 0), ("tab", c, 1)]

    def stA(k):
        j, c = items[k]; t0, n = blocks[j]; pb = k % 2
        uk = [("U", i_) for i_ in range(nch) if bounds[i_] < t0 + n and bounds[i_ + 1] > t0]
        br_ps, bi_ps = psb[pb][0][:, :n], psb[pb][1][:, :n]
        uko = [("UO", k_[1]) for k_ in uk]
        mm(P, br_ps, aBrT[:, c, :], U[:, t0:t0 + n], True, False, r=[("BT", c, 2)] + uk, w=[("psb", pb, 0)])
        mm(P, br_ps, BrT[:, c, :], UO[:, t0:t0 + n], False, True, r=[("BT", c, 0)] + uko, w=[("psb", pb, 0)])
        mm(P, bi_ps, aBiT[:, c, :], U[:, t0:t0 + n], True, False, r=[("BT", c, 3)] + uk, w=[("psb", pb, 1)])
        mm(P, bi_ps, BiT[:, c, :], UO[:, t0:t0 + n], False, True, r=[("BT", c, 1)] + uko, w=[("psb", pb, 1)])
        ctc, stc, tk = tabs(c, n)
        tt(P, "dve", mi[0][:, :n], br_ps, ctc, ALU.mult, r=[("psb", pb, 0)] + tk, w=["mi0"])
        tt(P, "dve", mi[1][:, :n], bi_ps, stc, ALU.mult, r=[("psb", pb, 1)] + tk, w=["mi1"])
        tt(P, "dve", mi[2][:, :n], bi_ps, ctc, ALU.mult, r=[("psb", pb, 1)] + tk, w=["mi2"])
        tt(P, "dve", mi[3][:, :n], br_ps, stc, ALU.mult, r=[("psb", pb, 0)] + tk, w=["mi3"])
        tt(P, "dve", gin[0][:, :n], mi[0][:, :n], mi[1][:, :n], ALU.add, r=["mi0", "mi1"], w=["gin0"])
        tt(P, "pool", gin[1][:, :n], mi[2][:, :n], mi[3][:, :n], ALU.subtract, r=["mi2", "mi3"], w=["gin1"])

    def stB(k):
        j, c = items[k]; t0, n = blocks[j]; hb = j % 2; gb = k % 2
        for ri in range(2):
            if j == 0:
                init = 0.0; rk = []
            else:
                pn = blocks[j - 1][1]
                init = hh[c][1 - hb][ri][:, pn:pn + 1]; rk = [("hh", c, 1 - hb, ri)]
            P.op("dve", lambda g, ri=ri, init=init, c=c, n=n, gb=gb: g.tensor_tensor_scan(
                out=g_[gb][ri][:, :n], data0=magT[:, c, :n], data1=gin[ri][:, :n], initial=init,
                op0=ALU.mult, op1=ALU.add), r=[("magT", c), f"gin{ri}"] + rk, w=[("g", gb, ri)])

    def stC(k):
        j, c = items[k]; t0, n = blocks[j]; gb = k % 2
        ctc, stc, tk = tabs(c, n)
        tt(P, "pool", mo[gb][0][:, :n], g_[gb][0][:, :n], ctc, ALU.mult, r=[("g", gb, 0)] + tk, w=[("mo", gb, 0)])
        tt(P, "pool", mo[gb][2][:, :n], g_[gb][0][:, :n], stc, ALU.mult, r=[("g", gb, 0)] + tk, w=[("mo", gb, 2)])
        tt(P, "pool", mo[gb][1][:, :n], g_[gb][1][:, :n], stc, ALU.mult, r=[("g", gb, 1)] + tk, w=[("mo", gb, 1)])
        tt(P, "pool", mo[gb][3][:, :n], g_[gb][1][:, :n], ctc, ALU.mult, r=[("g", gb, 1)] + tk, w=[("mo", gb, 3)])

    def stD(k):
        j, c = items[k]; t0, n = blocks[j]; hb = j % 2; gb = k % 2; yb = j % 2
        hr, hi = hh[c][hb][0], hh[c][hb][1]
        if j > 0:
            pn = blocks[j - 1][1]
            for ri in range(2):
                P.op("act", lambda g, ri=ri, pn=pn, c=c, hb=hb: g.copy(out=hh[c][hb][ri][:, 0:1], in_=hh[c][1 - hb][ri][:, pn:pn + 1]),
                     r=[("hh", c, 1 - hb, ri)], w=[("hh0", c, hb, ri)])
        tt(P, "dve", hr[:, 1:n + 1], mo[gb][0][:, :n], mo[gb][1][:, :n], ALU.subtract, r=[("mo", gb, 0), ("mo", gb, 1)], w=[("hh", c, hb, 0)])
        tt(P, "dve", hi[:, 1:n + 1], mo[gb][2][:, :n], mo[gb][3][:, :n], ALU.add, r=[("mo", gb, 2), ("mo", gb, 3)], w=[("hh", c, hb, 1)])
        mm(P, psy_o[yb][:, :n], Cre[:, c, :], hr[:, 1:n + 1], c == 0, False, r=[("Cre", c), ("hh", c, hb, 0)], w=[("psyo", yb)])
        mm(P, psy_o[yb][:, :n], Cimn[:, c, :], hi[:, 1:n + 1], False, c == NCT - 1, r=[("Cimn", c), ("hh", c, hb, 1)], w=[("psyo", yb)])
        mm(P, psy_e[yb][:, :n], CaRe[:, c, :], hr[:, 0:n], c == 0, False, r=[("CaRe", c), ("hh", c, hb, 0), ("hh0", c, hb, 0)], w=[("psye", yb)])
        mm(P, psy_e[yb][:, :n], CaImN[:, c, :], hi[:, 0:n], False, False, r=[("CaImN", c), ("hh", c, hb, 1), ("hh0", c, hb, 1)], w=[("psye", yb)])
        if c == NCT - 1:
            uk = [("U", i_) for i_ in range(nch) if bounds[i_] < t0 + n and bounds[i_ + 1] > t0]
            mm(P, psy_e[yb][:, :n], D2T[:], U[:, t0:t0 + n], False, True, r=["D2T"] + uk, w=[("psye", yb)])
            P.op("act", lambda g, yb=yb, n=n: g.copy(out=ysb[yb][:, :n], in_=psy_e[yb][:, :n]), r=[("psye", yb)], w=[("ysb", yb)])
            P.dma("sp", ye[:, t0:t0 + n], ysb[yb][:, :n], r=[("ysb", yb)])
            P.op("act", lambda g, yb=yb, n=n: g.copy(out=ysb[2 + yb][:, :n], in_=psy_o[yb][:, :n]), r=[("psyo", yb)], w=[("ysb", 2 + yb)])
            P.dma("sp", yo[:, t0:t0 + n], ysb[2 + yb][:, :n], r=[("ysb", 2 + yb)])

    stA(0); stB(0); stC(0)
    for k in range(len(items)):
        if k + 1 < len(items):
            stA(k + 1)
        stD(k)
        if k + 1 < len(items):
            stB(k + 1); stC(k + 1)
    return P.build()


def ssm_pack(lam_re, lam_im, log_dt, b_re, b_im, c_re, c_im, TBL=512):
    f = np.float32
    lr = np.zeros((128, 4), f); li = np.zeros((128, 4), f); ld = np.zeros((128, 4), f)
    Bre = np.zeros((4, 128, 128), f); Bim = np.zeros((4, 128, 128), f)
    Cre = np.zeros((4, 128, 128), f); Cim = np.zeros((4, 128, 128), f)
    for c in range(4):
        for gi in range(2):
            g = 2 * c + gi
            sl = slice(gi * 64, (gi + 1) * 64)
            lr[sl, c] = lam_re[g]; li[sl, c] = lam_im[g]; ld[sl, c] = log_dt[g]
            cs = slice(c * 32 + gi * 16, c * 32 + gi * 16 + 16)
            Bre[c, sl, cs] = b_re[g]; Bim[c, sl, cs] = b_im[g]
            Cre[c, sl, cs] = c_re[g].T; Cim[c, sl, cs] = c_im[g].T
    tvec = np.tile(np.arange(1, TBL + 1, dtype=f)[None, :], (128, 1))
    return dict(lam_re=lr, lam_im=li, log_dt=ld, b_re=Bre, b_im=Bim, c_re=Cre, c_im=Cim, tvec=tvec,
                ident=np.eye(128, dtype=f))


def build_pf1(NT2=16, C=512):
    P = Prog()
    zr = P.din("zr", [128, NT2, C], F32); zi = P.din("zi", [128, NT2, C], F32)
    mc = P.din("mc", [128, 128], F32); ms = P.din("ms", [128, 128], F32); mns = P.din("mns", [128, 128], F32)
    twc = P.din("twc", [128, NT2], F32); tws = P.din("tws", [128, NT2], F32); ntws = P.din("ntws", [128, NT2], F32)
    ar = P.dout("ar", [128, NT2, C], F32); ai = P.dout("ai", [128, NT2, C], F32)
    zr_sb = P.sb([128, NT2, C]); zi_sb = P.sb([128, NT2, C])
    mc_sb = P.sb([128, 128]); ms_sb = P.sb([128, 128]); mns_sb = P.sb([128, 128])
    twc_sb = P.sb([128, NT2]); tws_sb = P.sb([128, NT2]); ntws_sb = P.sb([128, NT2])
    for t, d, k in ((mc_sb, mc, "mc"), (ms_sb, ms, "ms"), (mns_sb, mns, "mns"), (twc_sb, twc, "twc"), (tws_sb, tws, "tws"), (ntws_sb, ntws, "ntws")):
        P.dma("sp", t[:], d[:], w=[k])
    H = NT2 // 2
    for hf in range(2):
        P.dma("sp", zr_sb[:, hf * H:(hf + 1) * H, :], zr[:, hf * H:(hf + 1) * H, :], w=[("zr", hf)])
        P.dma("sp", zi_sb[:, hf * H:(hf + 1) * H, :], zi[:, hf * H:(hf + 1) * H, :], w=[("zi", hf)])
    pr = [P.ps([128, 512]) for _ in range(2)]; pi = [P.ps([128, 512]) for _ in range(2)]
    t1 = [P.sb([128, C]) for _ in range(2)]; t2 = [P.sb([128, C]) for _ in range(2)]
    orr = [P.sb([128, C]) for _ in range(2)]; oi = [P.sb([128, C]) for _ in range(2)]
    for j in range(NT2):
        s = j % 2; hf = j // H
        mm(P, pr[s][:, :C], mc_sb[:], zr_sb[:, j, :], True, False, r=["mc", ("zr", hf)], w=[("pr", s)])
        mm(P, pr[s][:, :C], ms_sb[:], zi_sb[:, j, :], False, True, r=["ms", ("zi", hf)], w=[("pr", s)])
        mm(P, pi[s][:, :C], mc_sb[:], zi_sb[:, j, :], True, False, r=["mc", ("zi", hf)], w=[("pi", s)])
        mm(P, pi[s][:, :C], mns_sb[:], zr_sb[:, j, :], False, True, r=["mns", ("zr", hf)], w=[("pi", s)])
        ts(P, "dve", t1[s][:], pi[s][:, :C], tws_sb[:, j:j + 1], None, ALU.mult, None, r=[("pi", s), "tws"], w=[("t1", s)])
        stt(P, orr[s][:], pr[s][:, :C], twc_sb[:, j:j + 1], t1[s][:], ALU.mult, ALU.add, r=[("pr", s), "twc", ("t1", s)], w=[("or", s)])
        ts(P, "dve", t2[s][:], pr[s][:, :C], ntws_sb[:, j:j + 1], None, ALU.mult, None, r=[("pr", s), "ntws"], w=[("t2", s)])
        stt(P, oi[s][:], pi[s][:, :C], twc_sb[:, j:j + 1], t2[s][:], ALU.mult, ALU.add, r=[("pi", s), "twc", ("t2", s)], w=[("oi", s)])
        P.dma("sp", ar[:, j, :], orr[s][:], r=[("or", s)])
        P.dma("sp", ai[:, j, :], oi[s][:], r=[("oi", s)])
    return P.build()


def build_pf2(NK1=16, C=512, NCTX=256):
    P = Prog()
    ar = P.din("ar", [128, NK1, C], F32); ai = P.din("ai", [128, NK1, C], F32)
    mc = P.din("mc", [128, 128], F32); ms = P.din("ms", [128, 128], F32)
    zcr = P.din("zcr", [NCTX, C], F32); zci = P.din("zci", [NCTX, C], F32)
    cc = P.din("cc", [NCTX, NCTX], F32); cs = P.din("cs", [NCTX, NCTX], F32)
    xr = P.dout("xr", [128, NK1, C], F32)
    xc = P.dout("xc", [NCTX, C], F32)
    ar_sb = P.sb([128, NK1, C]); ai_sb = P.sb([128, NK1, C])
    mc_sb = P.sb([128, 128]); ms_sb = P.sb([128, 128])
    P.dma("sp", mc_sb[:], mc[:], w=["mc"]); P.dma("sp", ms_sb[:], ms[:], w=["ms"])
    H = NK1 // 2
    for hf in range(2):
        P.dma("sp", ar_sb[:, hf * H:(hf + 1) * H, :], ar[:, hf * H:(hf + 1) * H, :], w=[("ar", hf)])
        P.dma("sp", ai_sb[:, hf * H:(hf + 1) * H, :], ai[:, hf * H:(hf + 1) * H, :], w=[("ai", hf)])
    NTC = NCTX // 128
    zcr_sb = P.sb([128, NTC, C]); zci_sb = P.sb([128, NTC, C]); cc_sb = P.sb([128, NTC, NCTX]); cs_sb = P.sb([128, NTC, NCTX])
    P.dma("sp", zcr_sb[:], zcr.rearrange("(a p) c -> p a c", p=128), w=["zcr"])
    P.dma("sp", zci_sb[:], zci.rearrange("(a p) c -> p a c", p=128), w=["zci"])
    P.dma("sp", cc_sb[:], cc.rearrange("(a p) c -> p a c", p=128), w=["cc"])
    P.dma("sp", cs_sb[:], cs.rearrange("(a p) c -> p a c", p=128), w=["cs"])
    px = [P.ps([128, 512]) for _ in range(2)]
    ox = [P.sb([128, C]) for _ in range(2)]
    it = 0
    for j in range(NK1):
        s = it % 2; it += 1; hf = j // H
        mm(P, px[s][:, :C], mc_sb[:], ar_sb[:, j, :], True, False, r=["mc", ("ar", hf)], w=[("px", s)])
        mm(P, px[s][:, :C], ms_sb[:], ai_sb[:, j, :], False, True, r=["ms", ("ai", hf)], w=[("px", s)])
        P.op("act", lambda g, s=s: g.copy(out=ox[s][:], in_=px[s][:, :C]), r=[("px", s)], w=[("ox", s)])
        P.dma("sp", xr[:, j, :], ox[s][:], r=[("ox", s)])
    for kc in range(NTC):
        s = it % 2; it += 1
        n = 0
        for mat, mk, z, zk in ((cc_sb, "cc", zcr_sb, "zcr"), (cs_sb, "cs", zci_sb, "zci")):
            for tcn in range(NTC):
                mm(P, px[s][:, :C], mat[:, tcn, kc * 128:(kc + 1) * 128], z[:, tcn, :], n == 0, n == 2 * NTC - 1,
                   r=[mk, zk], w=[("px", s)])
                n += 1
        P.op("act", lambda g, s=s: g.copy(out=ox[s][:], in_=px[s][:, :C]), r=[("px", s)], w=[("ox", s)])
        P.dma("sp", xc[kc * 128:(kc + 1) * 128, :], ox[s][:], r=[("ox", s)])
    return P.build()


def dft_consts():
    j = np.arange(128)
    ang = 2 * np.pi * np.outer(j, j) / 128
    c128 = np.cos(ang); s128 = np.sin(ang)
    k1 = np.arange(128)[:, None]; t2 = np.arange(128)[None, :]
    ph = 2 * np.pi * (k1 * t2) / 16384.0
    j2 = np.arange(256)
    a2 = 2 * np.pi * np.outer(j2, j2) / 256
    f = np.float32
    return dict(c1=(c128 / 128).astype(f), s1=(s128 / 128).astype(f), ns1=(-s128 / 128).astype(f),
                c2=c128.astype(f), s2=s128.astype(f), twc=np.cos(ph).astype(f), tws=np.sin(ph).astype(f),
                cc=(np.cos(a2) / 16).astype(f), cs=(np.sin(a2) / 16).astype(f))


def run_fourier(run, zr, zi, zcr, zci):
    K = dft_consts()
    Zr = zr.reshape(128, 128, 512); Zi = zi.reshape(128, 128, 512)
    ins = []
    for i in range(8):
        sl = slice(16 * i, 16 * (i + 1))
        ins.append(dict(zr=np.ascontiguousarray(Zr[:, sl]), zi=np.ascontiguousarray(Zi[:, sl]), mc=K["c1"], ms=K["s1"], mns=K["ns1"],
                        twc=np.ascontiguousarray(K["twc"][:, sl]), tws=np.ascontiguousarray(K["tws"][:, sl]),
                        ntws=np.ascontiguousarray(-K["tws"][:, sl])))
    res = run("pf1", lambda: build_pf1(), ins)
    Ar = np.concatenate([r["ar"] for r in res], axis=1)
    Ai = np.concatenate([r["ai"] for r in res], axis=1)
    ArT = Ar.transpose(1, 0, 2); AiT = Ai.transpose(1, 0, 2)
    ins = []
    for i in range(8):
        sl = slice(16 * i, 16 * (i + 1))
        ins.append(dict(ar=np.ascontiguousarray(ArT[:, sl]), ai=np.ascontiguousarray(AiT[:, sl]), mc=K["c2"], ms=K["s2"],
                        zcr=zcr, zci=zci, cc=K["cc"], cs=K["cs"]))
    res = run("pf2", lambda: build_pf2(), ins)
    X = np.concatenate([r["xr"] for r in res], axis=1).reshape(16384, 512)
    return X, res[0]["xc"]


_NC_CACHE = {}


def run_prog(name, builder, ins):
    if name not in _NC_CACHE:
        _NC_CACHE[name] = builder()
    res = run_bass_kernel_spmd(_NC_CACHE[name], ins, core_ids=list(range(len(ins))))
    return [{k: np.asarray(v) for k, v in r.items()} for r in res.results]


FFN_H = 5632


P3_NOWEIGHTS = False


def build_p3(NB, TB):
    P = Prog()
    Tc = NB * TB
    KC = 16
    HC = FFN_H // 128
    xT = P.din("xT", [D_MODEL, Tc], F32)
    attnT = P.din("attnT", [1024, Tc], BF16)
    yfT = P.din("yfT", [512, Tc], F32); ybT = P.din("ybT", [512, Tc], F32); usT = P.din("usT", [512, Tc], F32)
    xfT = P.din("xfT", [512, Tc], F32)
    glu_w = P.din("glu_w", [128, 4, 512], BF16); fw = P.din("fw", [128, 4, 512], BF16)
    w_out = P.din("w_out", [4, 128, KC, 512], BF16)
    w_gate = P.din("w_gate", [HC // 4, 128, KC, 512], BF16); w_up = P.din("w_up", [HC // 4, 128, KC, 512], BF16)
    w_down = P.din("w_down", [4, 4, 128, 11, 512], BF16)
    vnames = ["gpm", "gm", "gpf", "scf", "shf", "gof", "gf"]
    vin = {n: P.din(n, [128, KC], F32) for n in vnames}
    ssm_d = P.din("ssm_d", [128, 4], F32); glu_b = P.din("glu_b", [128, 4], F32)
    xoT = P.dout("xoT", [D_MODEL, Tc], F32)

    vs = {}
    for n in vnames:
        vs[n] = P.sb([128, KC]); P.dma("sp", vs[n][:], vin[n][:], w=[n])
    d_sb = P.sb([128, 4]); gb_sb = P.sb([128, 4])
    P.dma("sp", d_sb[:], ssm_d[:], w=["ssm_d"]); P.dma("sp", gb_sb[:], glu_b[:], w=["glu_b"])
    GM = P.sb([128, KC]); G2 = P.sb([128, KC]); GF = P.sb([128, KC])
    tt(P, "dve", GM[:], vs["gpm"][:], vs["gm"][:], ALU.mult, r=["gpm", "gm"], w=["GM"])
    stt(P, G2[:], vs["scf"][:], 1.0, vs["gpf"][:], ALU.add, ALU.mult, r=["scf", "gpf"], w=["G2"])
    tt(P, "dve", GF[:], vs["gof"][:], vs["gf"][:], ALU.mult, r=["gof", "gf"], w=["GF"])
    ones = P.sb([128, 128], BF16)
    P.op("pool", lambda g: g.memset(ones[:], 1.0), w=["ones"])

    x_sb = P.sb([128, KC, TB], F32)
    cat = P.sb([128, KC, TB], BF16)
    yb_sb = P.sb([128, KC, TB], BF16)
    a_sb = P.sb([128, HC, TB], BF16)
    WB = [P.sb([128, KC, 512], BF16) for _ in range(3)]
    s_us = [P.sb([128, TB]) for _ in range(1)]; s_yf = [P.sb([128, TB]) for _ in range(1)]; s_yb = [P.sb([128, TB]) for _ in range(1)]
    s_xf = [P.sb([128, TB]) for _ in range(1)]
    e1 = P.sb([128, TB]); e2 = P.sb([128, TB]); e3 = P.sb([128, TB])
    gfp = P.sb([128, 4, TB], F32); gbf = P.sb([128, 4, TB], BF16); xfb = P.sb([128, 4, TB], BF16)
    sqc = [P.sb([128, TB], BF16) for _ in range(4)]
    rstd = P.sb([128, TB]); tmpf = [P.sb([128, TB]) for _ in range(2)]
    sg = [P.sb([128, TB]) for _ in range(2)]
    xo = [P.sb([128, TB]) for _ in range(2)]
    PS = [P.ps([128, 512]) for _ in range(8)]
    xv = xT.rearrange("(kc p) t -> p kc t", p=128)
    av = attnT.rearrange("(kc p) t -> p kc t", p=128)
    wcnt = [0]

    def wload(src_ap, shape_kc):
        s = wcnt[0] % 3; wcnt[0] += 1
        if P3_NOWEIGHTS and wcnt[0] > 3:
            return s
        P.dma("sp", WB[s][:, :shape_kc, :], src_ap, w=[("WB", s)])
        return s

    def rms_bc(nsq_key):
        rstd_from_ps(P, rstd[:], PS[4][:, :TB], D_MODEL, EPS, r=["ps4"], w=["rstd"], tmpkey=None)

    nsq = [0]

    pend = []

    def sq_acc(src_ps, src_key, idx, n, defer=False):
        s = nsq[0] % 4; nsq[0] += 1
        act(P, sqc[s][:], src_ps, AF.Square, r=[src_key], w=[("sqc", s)])
        f_ = lambda: mm(P, PS[4][:, :TB], ones[:], sqc[s][:], idx == 0, idx == n - 1, r=["ones", ("sqc", s)], w=["ps4"])
        if defer:
            pend.append(f_)
        else:
            f_()

    def flush():
        while pend:
            pend.pop(0)()

    def stageA1(b):
        tsl = slice(b * TB, (b + 1) * TB)
        P.dma("sp", cat[:, 0:8, :], av[:, :, tsl], w=[("cat", k) for k in range(8)])
        for ch in range(4):
            s = 0
            fsl = slice(ch * 128, (ch + 1) * 128)
            P.dma("sp", s_us[s][:], usT[fsl, tsl], w=[("us", s)])
            P.dma("sp", s_yf[s][:], yfT[fsl, tsl], w=[("yf", s)])
            P.dma("sp", s_yb[s][:], ybT[fsl, tsl], w=[("yb", s)])
            P.dma("sp", s_xf[s][:], xfT[fsl, tsl], w=[("xf", s)])
            stt(P, e1[:], s_us[s][:], d_sb[:, ch:ch + 1], s_yf[s][:], ALU.mult, ALU.add, r=[("us", s), "ssm_d", ("yf", s)], w=["e1"])
            tt(P, "dve", e1[:], e1[:], s_yb[s][:], ALU.add, r=["e1", ("yb", s)], w=["e1"])
            tt(P, "pool", e2[:], e1[:], e1[:], ALU.mult, r=["e1"], w=["e2"])
            ts(P, "dve", e2[:], e2[:], 0.044715, 1.0, ALU.mult, ALU.add, r=["e2"], w=["e2"])
            tt(P, "pool", e3[:], e2[:], e1[:], ALU.mult, r=["e2", "e1"], w=["e3"])
            act(P, e3[:], e3[:], AF.Sigmoid, r=["e3"], w=["e3"], scale=1.5957691216057308)
            tt(P, "dve", gfp[:, ch, :], e3[:], e1[:], ALU.mult, r=["e3", "e1"], w=[("gfp", ch)])
            P.op("act", lambda g, ch=ch: g.copy(out=gbf[:, ch, :], in_=gfp[:, ch, :]), r=[("gfp", ch)], w=[("gbf", ch)])
            P.op("act", lambda g, ch=ch, s=s: g.copy(out=xfb[:, ch, :], in_=s_xf[s][:]), r=[("xf", s)], w=[("xfb", ch)])

    stageA1(0)
    for b in range(NB):
        tsl = slice(b * TB, (b + 1) * TB)
        P.dma("sp", x_sb[:], xv[:, :, tsl], w=["x"])
        sgw = wload(glu_w[:], 4)
        sfw = wload(fw[:], 4)
        for co in range(4):
            pz = PS[5 + co % 2]; pk = ("ps", 5 + co % 2)
            for ci in range(4):
                mm(P, pz[:, :TB], WB[sgw][:, ci, co * 128:(co + 1) * 128], gbf[:, ci, :], ci == 0, ci == 3,
                   r=[("WB", sgw), ("gbf", ci)], w=[pk])
            act(P, e2[:], pz[:, :TB], AF.Sigmoid, r=[pk, "glu_b"], w=["e2"], bias=gb_sb[:, co:co + 1])
            tt(P, "dve", cat[:, 8 + co, :], e2[:], gfp[:, co, :], ALU.mult, r=["e2", ("gfp", co)], w=[("cat", 8 + co)])
        for co in range(4):
            pz = PS[5 + co % 2]; pk = ("ps", 5 + co % 2)
            for ci in range(4):
                mm(P, pz[:, :TB], WB[sfw][:, ci, co * 128:(co + 1) * 128], xfb[:, ci, :], ci == 0, ci == 3,
                   r=[("WB", sfw), ("xfb", ci)], w=[pk])
            P.op("act", lambda g, co=co, pz=pz: g.copy(out=cat[:, 12 + co, :], in_=pz[:, :TB]), r=[pk], w=[("cat", 12 + co)])
        for og in range(4):
            sw = wload(w_out[og], KC)
            for o4 in range(4):
                oc = og * 4 + o4
                pa = PS[oc % 2]; pk = ("ps", oc % 2)
                for kc in range(KC):
                    mm(P, pa[:, :TB], WB[sw][:, kc, o4 * 128:(o4 + 1) * 128], cat[:, kc, :], kc == 0, kc == KC - 1,
                       r=[("WB", sw), ("cat", kc)], w=[pk])
                flush()
                P.op("dve", lambda g, oc=oc, pa=pa: g.tensor_copy(out=yb_sb[:, oc, :], in_=pa[:, :TB]), r=[pk], w=[("yb_sb", oc)])
                sq_acc(pa[:, :TB], pk, oc, KC, defer=True)
        flush()
        rms_bc(None)
        for oc in range(KC):
            s = oc % 2
            stt(P, tmpf[s][:], yb_sb[:, oc, :], GM[:, oc:oc + 1], rstd[:], ALU.mult, ALU.mult,
                r=[("yb_sb", oc), "GM", "rstd"], w=[("tmpf", s)])
            tt(P, "pool" if oc % 2 else "dve", x_sb[:, oc, :], x_sb[:, oc, :], tmpf[s][:], ALU.add, r=["x", ("tmpf", s)], w=[("x1", oc)])
        for kc in range(KC):
            sq_acc(x_sb[:, kc, :], ("x1", kc), kc, KC)
        rms_bc(None)
        for kc in range(KC):
            s = kc % 2
            stt(P, tmpf[s][:], x_sb[:, kc, :], G2[:, kc:kc + 1], rstd[:], ALU.mult, ALU.mult,
                r=[("x1", kc), "G2", "rstd"], w=[("tmpf", s)])
            act(P, cat[:, kc, :], tmpf[s][:], AF.Identity, r=[("tmpf", s), "shf"], w=[("cat", kc)], bias=vs["shf"][:, kc:kc + 1])
        for hg in range(HC // 4):
            swg = wload(w_gate[hg], KC)
            swu = wload(w_up[hg], KC)
            for h4 in range(4):
                hc = hg * 4 + h4
                pg = PS[(hc % 2) * 2]; pgk = ("ps", (hc % 2) * 2)
                pu = PS[(hc % 2) * 2 + 1]; puk = ("ps", (hc % 2) * 2 + 1)
                for kc in range(KC):
                    mm(P, pg[:, :TB], WB[swg][:, kc, h4 * 128:(h4 + 1) * 128], cat[:, kc, :], kc == 0, kc == KC - 1,
                       r=[("WB", swg), ("cat", kc)], w=[pgk])
                for kc in range(KC):
                    mm(P, pu[:, :TB], WB[swu][:, kc, h4 * 128:(h4 + 1) * 128], cat[:, kc, :], kc == 0, kc == KC - 1,
                       r=[("WB", swu), ("cat", kc)], w=[puk])
                s = hc % 2
                act(P, sg[s][:], pg[:, :TB], AF.Silu, r=[pgk], w=[("sg", s)])
                tt(P, "dve", a_sb[:, hc, :], pu[:, :TB], sg[s][:], ALU.mult, r=[puk, ("sg", s)], w=[("a", hc)])
        if b + 1 < NB:
            stageA1(b + 1)
        for og in range(4):
            for q in range(4):
                sw = wload(w_down[og, q], 11)
                for h in range(11):
                    hc = q * 11 + h
                    for o4 in range(4):
                        mm(P, PS[o4][:, :TB], WB[sw][:, h, o4 * 128:(o4 + 1) * 128], a_sb[:, hc, :], hc == 0, hc == HC - 1,
                           r=[("WB", sw), ("a", hc)], w=[("ps", o4)])
                if q == 0:
                    flush()
            for o4 in range(4):
                oc = og * 4 + o4
                P.op("dve", lambda g, oc=oc, o4=o4: g.tensor_copy(out=yb_sb[:, oc, :], in_=PS[o4][:, :TB]), r=[("ps", o4)], w=[("yb_sb", oc)])
                sq_acc(PS[o4][:, :TB], ("ps", o4), oc, KC, defer=True)
        flush()
        rms_bc(None)
        for oc in range(KC):
            s = oc % 2
            stt(P, tmpf[s][:], yb_sb[:, oc, :], GF[:, oc:oc + 1], rstd[:], ALU.mult, ALU.mult,
                r=[("yb_sb", oc), "GF", "rstd"], w=[("tmpf", s)])
            tt(P, "pool" if oc % 2 else "dve", xo[s][:], x_sb[:, oc, :], tmpf[s][:], ALU.add, r=[("x1", oc), ("tmpf", s)], w=[("xo", s)])
            P.dma("sp", xoT[oc * 128:(oc + 1) * 128, tsl], xo[s][:], r=[("xo", s)])
    return P.build()


def _v2(v):
    return np.ascontiguousarray(np.asarray(v, np.float32).reshape(-1, 128).T)


def _cat(res, name, axis):
    return np.concatenate([r[name] for r in res], axis=axis)


BIG_W = ["w_in", "ssm_glu_w", "fourier_w", "w_out", "ffn_w_gate", "ffn_w_up", "ffn_w_down"]


def cast_weights(inp):
    parts = []
    shapes = []
    for l in range(2):
        for n in BIG_W:
            a = np.asarray(inp[n][l], np.float32)
            parts.append(a.ravel()); shapes.append((l, n, a.shape))
    flat = np.concatenate(parts)
    tot = flat.size
    per = tot // (8 * 128)
    assert per * 8 * 128 == tot and per % 5120 == 0, (tot, per)
    flat = flat.reshape(8, 128, per)
    res = run_prog("cast", lambda: build_cast(per, 5120), [{"src": flat[i]} for i in range(8)])
    out = np.concatenate([r["dst"].reshape(-1) for r in res])
    W = [dict(), dict()]
    off = 0
    for (l, n, shp) in shapes:
        sz = int(np.prod(shp))
        W[l][n] = out[off:off + sz].reshape(shp); off += sz
    return W


def kernel(**inp):
    f = np.float32
    x = np.asarray(inp["x"], f)[0]
    xc = np.asarray(inp["ctx"], f)[0]
    T = x.shape[0]; TC = xc.shape[0]
    Wb = cast_weights(inp)
    c2 = np.stack([np.asarray(inp["c"], f)[0], np.asarray(inp["c_ctx"], f)], 0)
    c2T = np.ascontiguousarray(c2.reshape(2, 16, 128).transpose(2, 1, 0))
    ada_w = np.asarray(inp["ada_w"], f); ada_b = np.asarray(inp["ada_b"], f)
    CW = 1536
    ins = [dict(c2T=c2T, w=np.ascontiguousarray(ada_w[:, :, i * CW:(i + 1) * CW]),
                b2=np.ascontiguousarray(np.repeat(ada_b[:, None, i * CW:(i + 1) * CW], 2, axis=1))) for i in range(8)]
    mod = _cat(run_prog("adaln", lambda: build_adaln(), ins), "mod", 2)
    cosT, sinT = rope_tables(np.arange(T))
    cosC = np.ones((128, TC), f); sinC = np.zeros((128, TC), f)
    rotT = rope_rotT(); cd, sdn = chan_dft()
    Tc = T // 8; Cc = TC // 8
    for l in range(2):
        need_ctx = l == 0
        m = mod[l]
        sh_m, sc_m, g_m, sh_f, sc_f, g_f = [m[:, k * 2048:(k + 1) * 2048] for k in range(6)]
        qn = np.asarray(inp["q_norm"][l], f)[:, None].copy(); kn = np.asarray(inp["k_norm"][l], f)[:, None].copy()
        com = dict(gam=_v2(inp["norm_mix_pre"][l]), w_in=Wb[l]["w_in"], qn=qn, kn=kn, rotT=rotT, cd=cd, sdn=sdn)
        ins = [dict(com, xT=np.ascontiguousarray(x[i * Tc:(i + 1) * Tc].T), scv=_v2(sc_m[0]), shv=_v2(sh_m[0]),
                    cosT=np.ascontiguousarray(cosT[:, i * Tc:(i + 1) * Tc]), sinT=np.ascontiguousarray(sinT[:, i * Tc:(i + 1) * Tc]))
               for i in range(8)]
        r1 = run_prog("p1", lambda: build_p1(4, 512), ins)
        ins = [dict(com, xT=np.ascontiguousarray(xc[i * Cc:(i + 1) * Cc].T), scv=_v2(sc_m[1]), shv=_v2(sh_m[1]),
                    cosT=np.ascontiguousarray(cosC[:, i * Cc:(i + 1) * Cc]), sinT=np.ascontiguousarray(sinC[:, i * Cc:(i + 1) * Cc]))
               for i in range(8)]
        r1c = run_prog("p1c", lambda: build_p1(1, 32), ins)
        L1 = {n: _cat(r1, n, 1) for n in ("qT", "kT", "vT", "usT", "zrT", "ziT")}
        C1 = {n: _cat(r1c, n, 1) for n in ("qT", "kT", "vT", "usT", "zrT", "ziT")}
        ins = []
        for i in range(8):
            j = i // 4
            ins.append(dict(qT=np.ascontiguousarray(L1["qT"][i * 128:(i + 1) * 128]),
                            kT=np.ascontiguousarray(np.concatenate([L1["kT"][j * 128:(j + 1) * 128], C1["kT"][j * 128:(j + 1) * 128]], 1)),
                            v=np.ascontiguousarray(np.concatenate([L1["vT"][j * 128:(j + 1) * 128], C1["vT"][j * 128:(j + 1) * 128]], 1).T)))
        attnT = np.ascontiguousarray(np.concatenate([r["o"] for r in run_prog("pa", lambda: build_attn2(T, T + TC, 512), ins)], 1).T)
        if need_ctx:
            ins = []
            for i in range(8):
                j = i // 4
                ins.append(dict(qT=np.ascontiguousarray(C1["qT"][i * 128:(i + 1) * 128]),
                                kT=np.ascontiguousarray(C1["kT"][j * 128:(j + 1) * 128]),
                                v=np.ascontiguousarray(C1["vT"][j * 128:(j + 1) * 128].T)))
            attncT = np.ascontiguousarray(np.concatenate([r["o"] for r in run_prog("pac", lambda: build_attn2(TC, TC, TC), ins)], 1).T)
        us = L1["usT"]; usc = C1["usT"]
        fwd = np.concatenate([usc, us], 1)
        bwd = np.concatenate([usc[:, ::-1], us[:, ::-1]], 1)
        ins = []
        for i in range(8):
            rows = slice(64 * i, 64 * (i + 1))
            u_core = np.ascontiguousarray(np.concatenate([fwd[rows], bwd[rows]], 0))
            idx = [(c // 2, 4 * i + 2 * (c % 2) + gi) for c in range(4) for gi in range(2)]
            pk = ssm_pack(np.stack([np.asarray(inp["ssm_lam_re"], f)[l, d, g] for d, g in idx]),
                          np.stack([np.asarray(inp["ssm_lam_im"], f)[l, d, g] for d, g in idx]),
                          np.array([np.asarray(inp["ssm_log_dt"], f)[l, d, g] for d, g in idx], f),
                          np.stack([np.asarray(inp["ssm_b_re"], f)[l, d, g] for d, g in idx]),
                          np.stack([np.asarray(inp["ssm_b_im"], f)[l, d, g] for d, g in idx]),
                          np.stack([np.asarray(inp["ssm_c_re"], f)[l, d, g] for d, g in idx]),
                          np.stack([np.asarray(inp["ssm_c_im"], f)[l, d, g] for d, g in idx]))
            ins.append(dict(pk, ue=np.ascontiguousarray(u_core[:, 0::2]), uo=np.ascontiguousarray(u_core[:, 1::2])))
        rs = run_prog("ps", lambda: build_ssm2((T + TC) // 2, 512), ins)
        ys = []
        for r in rs:
            yy = np.empty((128, T + TC), f)
            yy[:, 0::2] = r["ye"]; yy[:, 1::2] = r["yo"]
            ys.append(yy)
        yf_all = np.concatenate([yy[0:64] for yy in ys], 0)
        yb_all = np.concatenate([yy[64:128] for yy in ys], 0)
        yfT, yfcT = yf_all[:, TC:], yf_all[:, :TC]
        ybT, ybcT = yb_all[:, TC:][:, ::-1], yb_all[:, :TC][:, ::-1]
        X, Xc = run_fourier(run_prog, np.ascontiguousarray(L1["zrT"].T), np.ascontiguousarray(L1["ziT"].T),
                            np.ascontiguousarray(C1["zrT"].T), np.ascontiguousarray(C1["ziT"].T))
        xfT = X.T; xfcT = Xc.T
        comw = dict(glu_w=tile_kc(Wb[l]["ssm_glu_w"])[0], fw=tile_kc(Wb[l]["fourier_w"])[0], w_out=tile_kc(Wb[l]["w_out"]),
                    w_gate=tile_kc(Wb[l]["ffn_w_gate"]), w_up=tile_kc(Wb[l]["ffn_w_up"]), w_down=tile_down(Wb[l]["ffn_w_down"]), gpm=_v2(inp["norm_mix_post"][l]),
                    gpf=_v2(inp["norm_ffn_pre"][l]), gof=_v2(inp["norm_ffn_post"][l]),
                    ssm_d=_v2(inp["ssm_d"][l]), glu_b=_v2(inp["ssm_glu_b"][l]))

        def p3_ins(xs, at, yf_, yb_, us_, xf_, n, v):
            out = []
            for i in range(8):
                sl = slice(i * n, (i + 1) * n)
                out.append(dict(comw, xT=np.ascontiguousarray(xs[sl].T), attnT=np.ascontiguousarray(at[:, sl]),
                                yfT=np.ascontiguousarray(yf_[:, sl]), ybT=np.ascontiguousarray(yb_[:, sl]),
                                usT=np.ascontiguousarray(us_[:, sl]), xfT=np.ascontiguousarray(xf_[:, sl]),
                                gm=_v2(g_m[v]), scf=_v2(sc_f[v]), shf=_v2(sh_f[v]), gf=_v2(g_f[v])))
            return out
        r3 = run_prog("p3", lambda: build_p3(4, 512), p3_ins(x, attnT, yfT, ybT, us, xfT, Tc, 0))
        x_new = np.ascontiguousarray(_cat(r3, "xoT", 1).T)
        if need_ctx:
            r3c = run_prog("p3c", lambda: build_p3(1, 32), p3_ins(xc, attncT, yfcT, ybcT, usc, xfcT, Cc, 1))
            xc = np.ascontiguousarray(_cat(r3c, "xoT", 1).T)
        x = x_new
    return x[None].astype(np.float32)


def build_attn2(NQ, NK, QB):
    P = Prog()
    NQB = NQ // QB
    NKT = NK // 128
    NPR = NKT // 2
    QC = QB // 128
    qT = P.din("qT", [128, NQ], BF16)
    kT = P.din("kT", [128, NK], BF16)
    v = P.din("v", [NK, 128], BF16)
    o = P.dout("o", [NQ, 128], BF16)
    q_sb = P.sb([128, NQ], BF16); k_sb = P.sb([128, NK], BF16); v_sb = P.sb([128, NKT, 130], BF16)
    KCH = 13 if NKT % 13 == 0 else NKT
    vv = v.rearrange("(kt p) d -> p kt d", p=128)
    QCH = 4 if NQB % 4 == 0 else NQB
    P.dma("sp", q_sb[:, 0:QCH * QB], qT[:, 0:QCH * QB], w=[("Q", 0)])
    for c in range(NKT // KCH):
        P.dma("sp", k_sb[:, c * KCH * 128:(c + 1) * KCH * 128], kT[:, c * KCH * 128:(c + 1) * KCH * 128], w=[("K", c)])
        P.op("pool", lambda g, c=c: g.memset(v_sb[:, c * KCH:(c + 1) * KCH, 128:130], 1.0), w=[("V1", c)])
        P.dma("sp", v_sb[:, c * KCH:(c + 1) * KCH, 0:128], vv[:, c * KCH:(c + 1) * KCH, :], w=[("V", c)])
    for c in range(1, NQB // QCH):
        P.dma("sp", q_sb[:, c * QCH * QB:(c + 1) * QCH * QB], qT[:, c * QCH * QB:(c + 1) * QCH * QB], w=[("Q", c)])
    S = [P.ps([128, 2, 512], F32) for _ in range(2)]
    O = [[P.ps([128, 512], F32) for _ in range(2)] for _ in range(2)]
    PT = [P.sb([128, 2, QB], BF16) for _ in range(2)]
    rs = P.sb([128, 4], F32)
    o_sb = [P.sb([128, QC, 128], BF16) for _ in range(2)]
    ov = o.rearrange("(qb qc p) d -> qb p qc d", qc=QC, p=128)
    scale = 128 ** -0.5
    items = [(qb, pr) for qb in range(NQB) for pr in range(NPR)]

    def emit_S(i):
        qb, pr = items[i]
        s = i % 2
        for j in range(2):
            kt = 2 * pr + j
            mm(P, S[s][:, j, :QB], k_sb[:, kt * 128:(kt + 1) * 128], q_sb[:, qb * QB:(qb + 1) * QB], True, True,
               r=[("K", kt // KCH), ("Q", qb // QCH)], w=[("S", s)])

    emit_S(0)
    for i, (qb, pr) in enumerate(items):
        if i + 1 < len(items):
            emit_S(i + 1)
        s = i % 2
        ob = qb % 2
        act(P, PT[s][:, :, :], S[s][:, :, :QB], AF.Exp, r=[("S", s)], w=[("PT", s)], scale=scale)
        for j in range(2):
            kt = 2 * pr + j
            for qc in range(QC):
                bank = O[ob][qc // 2]
                col = (qc % 2) * 129
                first = (kt == 0 and qc % 2 == 0)
                P.op("pe", lambda g, bank=bank, col=col, s=s, j=j, qc=qc, kt=kt, first=first: g.matmul(
                    bank[:, col:col + 129], PT[s][:, j, qc * 128:(qc + 1) * 128], v_sb[:, kt, 0:129],
                    start=first, stop=(kt == NKT - 1), skip_group_check=True),
                    r=[("V", kt // KCH), ("V1", kt // KCH), ("PT", s)], w=[("O", ob, qc // 2)])
        if pr == NPR - 1:
            for qc in range(QC):
                bank = O[ob][qc // 2]
                col = (qc % 2) * 129
                bk = ("O", ob, qc // 2)
                P.op("dve", lambda g, bank=bank, col=col, qc=qc: g.reciprocal(out=rs[:, qc:qc + 1], in_=bank[:, col + 128:col + 129]),
                     r=[bk], w=[("rs", qc)])
                ts(P, "dve", o_sb[ob][:, qc, :], bank[:, col:col + 128], rs[:, qc:qc + 1], None, ALU.mult, None,
                   r=[bk, ("rs", qc)], w=[("o_sb", ob)])
            P.dma("sp", ov[qb], o_sb[ob][:], r=[("o_sb", ob)])
    return P.build()


def tile_kc(w):
    K_, C_ = w.shape
    return np.ascontiguousarray(w.reshape(K_ // 128, 128, C_ // 512, 512).transpose(2, 1, 0, 3))


def tile_down(w):
    return np.ascontiguousarray(w.reshape(4, 11, 128, 4, 512).transpose(3, 0, 2, 1, 4))
```
